# Optimizing a Trainium2 kernel written in Bass

```python
import math
import jax
import jax.numpy as jnp
from jax import lax
import numpy as np

D_MODEL = 2048
BATCH = 32
SEQ = 256
DEPTH = 2
DEC_BATCH = 2
DEC_SEQ = 1024
PAST_LEN = 512

GRID_W = 64
D_MIX = D_MODEL
W_A = D_MIX // 4
HD_A = 64
H_A = W_A // HD_A
W_LORA = 64
A_LORA = 64
W_B = D_MIX // 4
H_B = 4
HD_B = W_B // H_B
CHUNK = 64
W_C = D_MIX // 2
H_C = 8
DV_C = W_C // H_C
DK_C = DV_C // 2
Q_BLOCK = 128
ROPE_BASE = 10000.0
CONV_W = 3
N_DIR = 2
LN_EPS = 1e-5
GN_EPS_A = 64e-5
GN_EPS = 1e-5
RMS_EPS = 1e-5
DEEPNORM_ALPHA = (2 * DEPTH) ** 0.25
DEEPNORM_BETA = (8 * DEPTH) ** -0.25
PROJ_SPLITS = (
    3 * W_A, W_LORA, W_LORA, A_LORA, A_LORA, W_A,
    2 * W_B, W_B, W_B, N_DIR * H_B, N_DIR * H_B, W_B,
    2 * H_C * DK_C, 2 * H_C * DK_C, H_C * DV_C, H_C * DV_C,
)
P_IN = sum(PROJ_SPLITS)

kernel_name = 'hybrid_rwkv7_mlstm_diffattn_prefix_dit_step'


def layer_norm(x, g, b):
    xf = x.astype(jnp.float32)
    mu = jnp.mean(xf, axis=-1, keepdims=True)
    var = jnp.mean(jnp.square(xf - mu), axis=-1, keepdims=True)
    return ((xf - mu) * lax.rsqrt(var + LN_EPS) * g + b).astype(x.dtype)


def head_layer_norm(y, g, b, eps):
    mu = jnp.mean(y, axis=-1, keepdims=True)
    var = jnp.mean(jnp.square(y - mu), axis=-1, keepdims=True)
    yn = (y - mu) * lax.rsqrt(var + eps)
    return yn.reshape(y.shape[0], y.shape[1], -1) * g + b


def centred_conv(x, w, b):
    xp = jnp.pad(x, ((0, 0), (1, 1), (0, 0)))
    return xp[:, :-2] * w[0] + xp[:, 1:-1] * w[1] + xp[:, 2:] * w[2] + b


def axial_rope(x):
    n_tok = x.shape[1]
    rows = n_tok // GRID_W
    row = jnp.repeat(jnp.arange(rows, dtype=jnp.float32), GRID_W)
    col = jnp.tile(jnp.arange(GRID_W, dtype=jnp.float32), rows)
    half = DK_C // 2
    inv = 1.0 / (ROPE_BASE ** (jnp.arange(0, half, 2, dtype=jnp.float32) / half))

    def rot(xa, pos):
        ang = pos[:, None] * inv
        cos = jnp.cos(ang)[None, :, None, :]
        sin = jnp.sin(ang)[None, :, None, :]
        x1, x2 = jnp.split(xa, 2, axis=-1)
        return jnp.concatenate([x1 * cos - x2 * sin, x2 * cos + x1 * sin], axis=-1)

    xr, xc = jnp.split(x, 2, axis=-1)
    return jnp.concatenate([rot(xr, row), rot(xc, col)], axis=-1)


def rwkv_scan(s0, r, w, k, v, kk, a, reverse):
    def step(S, inp):
        r_t, w_t, k_t, v_t, kk_t, a_t = inp
        sk = jnp.einsum('bhvk,bhk->bhv', S, kk_t)
        S = (S * w_t[:, :, None, :] - sk[..., None] * (kk_t * a_t)[:, :, None, :]
             + v_t[..., None] * k_t[:, :, None, :])
        return S, jnp.einsum('bhvk,bhk->bhv', S, r_t)

    xs = tuple(jnp.swapaxes(t, 0, 1) for t in (r, w, k, v, kk, a))
    s_fin, ys = lax.scan(step, s0.astype(jnp.float32), xs, reverse=reverse)
    return jnp.swapaxes(ys, 0, 1), s_fin


def rwkv_mixer(u_rkv, wds, ads, u_gate, lp, s0):
    B, T, _ = u_rkv.shape

    def heads(t):
        return t.reshape(B, T, H_A, HD_A)

    r, k, v = (heads(t) for t in jnp.split(centred_conv(u_rkv, lp['conv_a_w'], lp['conv_a_b']), 3, axis=-1))
    kk = k * lp['rwkv_k_k'].reshape(H_A, HD_A)
    kk = kk * lax.rsqrt(jnp.sum(kk * kk, axis=-1, keepdims=True) + 1e-12)
    k_a = lp['rwkv_k_a'].reshape(H_A, HD_A)
    y = 0.0
    bonus = 0.0
    states = []
    for d in range(N_DIR):
        z = lp['rwkv_w0'][d] + jnp.tanh(wds[d]) @ lp['rwkv_w_up'][d]
        decay = heads(jnp.exp(-jnp.exp(-jax.nn.softplus(-z) - 0.5)))
        a = heads(jax.nn.sigmoid(lp['rwkv_a0'][d] + ads[d] @ lp['rwkv_a_up'][d]))
        kt = k * (1.0 + (a - 1.0) * k_a)
        yd, sd = rwkv_scan(s0[:, d], r, decay, kt, v, kk, a, reverse=(d == 1))
        y = y + yd
        bonus = bonus + jnp.sum(r * kt * lp['rwkv_r_k'], axis=-1, keepdims=True) * v
        states.append(sd)
    y = head_layer_norm(y, lp['rwkv_gn_g'], lp['rwkv_gn_b'], GN_EPS_A) + bonus.reshape(B, T, W_A)
    return y * jax.nn.silu(u_gate), jnp.stack(states, axis=1)


def mlstm_chunkwise(q, k, v, li, lf, C0, n0, m0):
    B, H, T, d = q.shape
    nc = T // CHUNK

    def to_chunks(t):
        t = t.reshape(B, H, nc, CHUNK, *t.shape[3:])
        return jnp.moveaxis(t, 2, 0)

    xs = tuple(to_chunks(t) for t in (q, k, v, li, lf))
    tri = jnp.tril(jnp.ones((CHUNK, CHUNK), dtype=bool))

    def step(carry, inp):
        C, n, m = carry
        qc, kc, vc, ic, fc = inp
        b = jnp.cumsum(fc, axis=-1)
        inter = b + m[..., None]
        D = b[..., :, None] - b[..., None, :] + ic[..., None, :]
        D = jnp.where(tri, D, -jnp.inf)
        mt = jnp.maximum(inter, jnp.max(D, axis=-1))
        wi = jnp.exp(inter - mt)
        P = jnp.exp(D - mt[..., None]) * jnp.einsum('bhtd,bhsd->bhts', qc, kc)
        num = wi[..., None] * jnp.einsum('bhvk,bhtk->bhtv', C, qc) + jnp.einsum('bhts,bhsv->bhtv', P, vc)
        den = wi * jnp.einsum('bhk,bhtk->bht', n, qc) + jnp.sum(P, axis=-1)
        h = num / jnp.maximum(jnp.abs(den), jnp.exp(-mt))[..., None]
        bL = b[..., -1]
        g = bL[..., None] - b + ic
        m_new = jnp.maximum(bL + m, jnp.max(g, axis=-1))
        sc = jnp.exp(bL + m - m_new)
        wg = jnp.exp(g - m_new[..., None])
        C = sc[..., None, None] * C + jnp.einsum('bhs,bhsv,bhsk->bhvk', wg, vc, kc)
        n = sc[..., None] * n + jnp.einsum('bhs,bhsk->bhk', wg, kc)
        return (C, n, m_new), h

    init = (C0.astype(jnp.float32), n0.astype(jnp.float32), m0.astype(jnp.float32))
    (C, n, m), hs = lax.scan(step, init, xs)
    h = jnp.moveaxis(hs, 0, 2).reshape(B, H, T, d)
    return h, C, n, m


def mlstm_mixer(u_qk, u_v, u_o, u_i, u_f, u_gate, lp, state0):
    B, T, _ = u_qk.shape
    C0, n0, m0 = state0
    qk = jax.nn.silu(centred_conv(u_qk, lp['conv_b_w'], lp['conv_b_b']))
    q, k = jnp.split(qk, 2, axis=-1)

    def heads(t):
        return t.reshape(B, T, H_B, HD_B).transpose(0, 2, 1, 3)

    q, k, v = heads(q), heads(k) * (HD_B ** -0.5), heads(u_v)
    ig = (u_i.reshape(B, T, N_DIR, H_B) + lp['mlstm_b_i']).transpose(2, 0, 3, 1)
    lf = jax.nn.log_sigmoid(u_f.reshape(B, T, N_DIR, H_B) + lp['mlstm_b_f']).transpose(2, 0, 3, 1)
    h = 0.0
    Cs, ns, ms = [], [], []
    for d in range(N_DIR):
        if d == 0:
            flip = lambda t: t
        else:
            flip = lambda t: jnp.flip(t, axis=2)
        hd, Cd, nd, md = mlstm_chunkwise(flip(q), flip(k), flip(v), flip(ig[d]), flip(lf[d]),
                                         C0[:, d], n0[:, d], m0[:, d])
        h = h + flip(hd)
        Cs.append(Cd)
        ns.append(nd)
        ms.append(md)
    o = jax.nn.sigmoid(u_o).reshape(B, T, H_B, HD_B)
    h = h.transpose(0, 2, 1, 3) * o
    y = head_layer_norm(h, lp['mlstm_gn_g'], lp['mlstm_gn_b'], GN_EPS) * jax.nn.silu(u_gate)
    return y, (jnp.stack(Cs, axis=1), jnp.stack(ns, axis=1), jnp.stack(ms, axis=1))


def diff_softmax_attention(q1, q2, k1, k2, v, lam):
    B, H, T, _ = q1.shape
    nb = T // Q_BLOCK
    scale = DK_C ** -0.5

    def blocks(t):
        return jnp.moveaxis(t.reshape(B, H, nb, Q_BLOCK, t.shape[-1]), 2, 0)

    def one_block(qs):
        qb1, qb2 = qs
        p1 = jax.nn.softmax(jnp.einsum('bhqd,bhkd->bhqk', qb1, k1) * scale, axis=-1)
        p2 = jax.nn.softmax(jnp.einsum('bhqd,bhkd->bhqk', qb2, k2) * scale, axis=-1)
        return jnp.einsum('bhqk,bhkd->bhqd', p1 - lam * p2, v)

    out = lax.map(one_block, (blocks(q1), blocks(q2)))
    return jnp.moveaxis(out, 0, 2).reshape(B, H, T, v.shape[-1])


def diff_attn_mixer(u_q, u_k, u_v, u_gate, lp, lam_init, ctx_kv):
    B, T, _ = u_q.shape
    q = u_q.reshape(B, T, 2 * H_C, DK_C)
    k = u_k.reshape(B, T, 2 * H_C, DK_C)
    if ctx_kv is not None:
        q, k = axial_rope(q), axial_rope(k)
    q = q.reshape(B, T, H_C, 2, DK_C).transpose(3, 0, 2, 1, 4)
    k_own = k.reshape(B, T, H_C, 2 * DK_C).transpose(0, 2, 1, 3)
    v_own = u_v.reshape(B, T, H_C, DV_C).transpose(0, 2, 1, 3)
    if ctx_kv is None:
        k_all, v_all = k_own, v_own
    else:
        k_all = jnp.concatenate([ctx_kv[0].astype(jnp.float32), k_own], axis=2)
        v_all = jnp.concatenate([ctx_kv[1].astype(jnp.float32), v_own], axis=2)
    k1, k2 = jnp.split(k_all, 2, axis=-1)
    lam = (jnp.exp(jnp.sum(lp['diff_lq1'] * lp['diff_lk1'])) - jnp.exp(jnp.sum(lp['diff_lq2'] * lp['diff_lk2']))
           + lam_init).astype(jnp.float32)
    o = diff_softmax_attention(q[0], q[1], k1, k2, v_all, lam)
    o = o * lax.rsqrt(jnp.mean(jnp.square(o), axis=-1, keepdims=True) + RMS_EPS) * lp['diff_subln_g'] * (1.0 - lam_init)
    o = o.transpose(0, 2, 1, 3).reshape(B, T, W_C)
    return o * jax.nn.silu(u_gate), (k_own, v_own)


def trunk_layer(x, mod, lp, lam_init, cached):
    B, T, _ = x.shape
    f32 = jnp.float32
    shift, scale, gate = jnp.split(mod, 3, axis=-1)
    u = x * (1.0 + scale[:, None]) + shift[:, None]
    proj = jnp.matmul(u, lp['w_in']).astype(f32)
    cuts = [int(i) for i in np.cumsum(PROJ_SPLITS)[:-1]]
    (a_rkv, a_wdf, a_wdb, a_adf, a_adb, a_gate,
     b_qk, b_v, b_o, b_i, b_f, b_gate,
     c_q, c_k, c_v, c_gate) = jnp.split(proj, cuts, axis=-1)
    if cached is None:
        ctx_kv = None
        s_rwkv0 = jnp.zeros((B, N_DIR, H_A, HD_A, HD_A), f32)
        mstate0 = (jnp.zeros((B, N_DIR, H_B, HD_B, HD_B), f32),
                   jnp.zeros((B, N_DIR, H_B, HD_B), f32),
                   jnp.zeros((B, N_DIR, H_B), f32))
    else:
        ctx_kv = (cached[0], cached[1])
        s_rwkv0 = cached[2]
        mstate0 = (cached[3], cached[4], cached[5])
    y_a, s_rwkv = rwkv_mixer(a_rkv, (a_wdf, a_wdb), (a_adf, a_adb), a_gate, lp, s_rwkv0)
    y_b, mstate = mlstm_mixer(b_qk, b_v, b_o, b_i, b_f, b_gate, lp, mstate0)
    y_c, kv_own = diff_attn_mixer(c_q, c_k, c_v, c_gate, lp, lam_init, ctx_kv)
    mixed = jnp.concatenate([y_a, y_b, y_c], axis=-1).astype(x.dtype)
    out = jnp.matmul(mixed, lp['w_out'])
    x_new = layer_norm(DEEPNORM_ALPHA * x + gate[:, None] * out, lp['ln_g'], lp['ln_b'])
    return x_new, (kv_own[0], kv_own[1], s_rwkv, mstate[0], mstate[1], mstate[2])


def setup_inputs(seed: int = 0) -> dict:
    key = jax.random.key(seed)
    keys = iter(jax.random.split(key, 48))

    def nrm(shape, s=1.0):
        return jax.random.normal(next(keys), shape, jnp.float32) * s

    L = DEPTH
    return {
        'x_prompt': nrm((BATCH, SEQ, D_MODEL)),
        'x_sample': nrm((DEC_BATCH, DEC_SEQ, D_MODEL)),
        'cache_attn_k': nrm((DEC_BATCH, L, H_C, PAST_LEN, 2 * DK_C)),
        'cache_attn_v': nrm((DEC_BATCH, L, H_C, PAST_LEN, DV_C)),
        'state_rwkv': nrm((DEC_BATCH, L, N_DIR, H_A, HD_A, HD_A), 0.5),
        'state_mlstm_c': nrm((DEC_BATCH, L, N_DIR, H_B, HD_B, HD_B), 0.1),
        'state_mlstm_n': nrm((DEC_BATCH, L, N_DIR, H_B, HD_B), 0.3),
        'state_mlstm_m': nrm((DEC_BATCH, L, N_DIR, H_B), 0.5),
        'c': nrm((DEC_BATCH, D_MODEL)),
        'c_ctx': nrm((D_MODEL,)),
        'w_ada': nrm((L, D_MODEL, 3 * D_MODEL), 0.5 * D_MODEL ** -0.5),
        'b_ada': nrm((L, 3 * D_MODEL), 0.01),
        'w_in': nrm((L, D_MODEL, P_IN), D_MODEL ** -0.5),
        'conv_a_w': nrm((L, CONV_W, 3 * W_A), 0.2).at[:, 1].add(1.0),
        'conv_a_b': nrm((L, 3 * W_A), 0.01),
        'rwkv_w0': nrm((L, N_DIR, W_A), 0.5) - 1.0,
        'rwkv_w_up': nrm((L, N_DIR, W_LORA, W_A), 0.5 * W_LORA ** -0.5),
        'rwkv_a0': nrm((L, N_DIR, W_A), 0.3),
        'rwkv_a_up': nrm((L, N_DIR, A_LORA, W_A), 0.5 * A_LORA ** -0.5),
        'rwkv_k_k': 0.85 + nrm((L, W_A), 0.05),
        'rwkv_k_a': 1.0 + nrm((L, W_A), 0.05),
        'rwkv_r_k': nrm((L, H_A, HD_A), 0.1),
        'rwkv_gn_g': 1.0 + nrm((L, W_A), 0.05),
        'rwkv_gn_b': nrm((L, W_A), 0.01),
        'conv_b_w': nrm((L, CONV_W, 2 * W_B), 0.2).at[:, 1].add(1.0),
        'conv_b_b': nrm((L, 2 * W_B), 0.01),
        'mlstm_b_i': nrm((L, N_DIR, H_B), 0.1),
        'mlstm_b_f': jax.random.uniform(next(keys), (L, N_DIR, H_B), jnp.float32, 3.0, 6.0),
        'mlstm_gn_g': 1.0 + nrm((L, W_B), 0.05),
        'mlstm_gn_b': nrm((L, W_B), 0.01),
        'diff_lq1': nrm((L, DK_C), 0.1),
        'diff_lk1': nrm((L, DK_C), 0.1),
        'diff_lq2': nrm((L, DK_C), 0.1),
        'diff_lk2': nrm((L, DK_C), 0.1),
        'diff_subln_g': 1.0 + nrm((L, DV_C), 0.05),
        'w_out': nrm((L, D_MIX, D_MODEL), DEEPNORM_BETA * D_MIX ** -0.5),
        'ln_g': 1.0 + nrm((L, D_MODEL), 0.05),
        'ln_b': nrm((L, D_MODEL), 0.01),
    }


def reference(x_prompt, x_sample, cache_attn_k, cache_attn_v, state_rwkv, state_mlstm_c, state_mlstm_n,
              state_mlstm_m, c, c_ctx, w_ada, b_ada, w_in, conv_a_w, conv_a_b, rwkv_w0, rwkv_w_up, rwkv_a0,
              rwkv_a_up, rwkv_k_k, rwkv_k_a, rwkv_r_k, rwkv_gn_g, rwkv_gn_b, conv_b_w, conv_b_b, mlstm_b_i,
              mlstm_b_f, mlstm_gn_g, mlstm_gn_b, diff_lq1, diff_lk1, diff_lq2, diff_lk2, diff_subln_g, w_out,
              ln_g, ln_b):
    silu_ctx = jax.nn.silu(c_ctx)[None, :]
    silu_c = jax.nn.silu(c)
    y_prompt = x_prompt
    y_sample = x_sample
    ctx_layers = []
    for l in range(DEPTH):
        lp = {
            'w_in': w_in[l], 'conv_a_w': conv_a_w[l], 'conv_a_b': conv_a_b[l],
            'rwkv_w0': rwkv_w0[l], 'rwkv_w_up': rwkv_w_up[l], 'rwkv_a0': rwkv_a0[l], 'rwkv_a_up': rwkv_a_up[l],
            'rwkv_k_k': rwkv_k_k[l], 'rwkv_k_a': rwkv_k_a[l], 'rwkv_r_k': rwkv_r_k[l],
            'rwkv_gn_g': rwkv_gn_g[l], 'rwkv_gn_b': rwkv_gn_b[l],
            'conv_b_w': conv_b_w[l], 'conv_b_b': conv_b_b[l], 'mlstm_b_i': mlstm_b_i[l], 'mlstm_b_f': mlstm_b_f[l],
            'mlstm_gn_g': mlstm_gn_g[l], 'mlstm_gn_b': mlstm_gn_b[l],
            'diff_lq1': diff_lq1[l], 'diff_lk1': diff_lk1[l], 'diff_lq2': diff_lq2[l], 'diff_lk2': diff_lk2[l],
            'diff_subln_g': diff_subln_g[l], 'w_out': w_out[l], 'ln_g': ln_g[l], 'ln_b': ln_b[l],
        }
        lam_init = 0.8 - 0.6 * math.exp(-0.3 * l)
        mod_ctx = jnp.matmul(silu_ctx, w_ada[l]) + b_ada[l]
        y_prompt, ctx_t = trunk_layer(y_prompt, mod_ctx, lp, lam_init, None)
        ctx_layers.append(ctx_t)
        mod_lat = jnp.matmul(silu_c, w_ada[l]) + b_ada[l]
        cached = (cache_attn_k[:, l], cache_attn_v[:, l], state_rwkv[:, l],
                  state_mlstm_c[:, l], state_mlstm_n[:, l], state_mlstm_m[:, l])
        y_sample, _ = trunk_layer(y_sample, mod_lat, lp, lam_init, cached)
    new_k, new_v, new_rwkv, new_c, new_n, new_m = (jnp.stack([t[i] for t in ctx_layers], axis=1) for i in range(6))
    return (y_prompt, y_sample, new_k, new_v, new_rwkv, new_c, new_n, new_m)
```

```python
import math
from contextlib import ExitStack

import numpy as np
import concourse.bass as bass
import concourse.mybir as mybir
from concourse.bass_utils import run_bass_kernel_spmd

F32 = mybir.dt.float32
BF16 = mybir.dt.bfloat16
AF = mybir.ActivationFunctionType
ALU = mybir.AluOpType
AX = mybir.AxisListType

D = 2048
KC = 16
DEPTH = 2
NSEQ_P = 4
TSEQ_P = 256
TG = 1024
NT = 8
NCH = 16
PAST = 512
P_IN = 8976
ALPHA = (2 * DEPTH) ** 0.25
LN_EPS = 1e-5
GN_EPS_A = 64e-5
GN_EPS = 1e-5
RMS_EPS = 1e-5
WDECAY = math.exp(-0.5)

U_LORA = 0
U_RW = [256 + 512 * p for p in range(4)]
U_ML = [2304 + 640 * h for h in range(4)]
U_MG = 2304 + 2560
U_AT = [4880 + 512 * h for h in range(8)]


def _perm_cols():
    perm = []
    perm += list(range(1536, 1792))
    for p in range(4):
        perm += list(range(p * 128, p * 128 + 128))
        perm += list(range(512 + p * 128, 512 + p * 128 + 128))
        perm += list(range(1024 + p * 128, 1024 + p * 128 + 128))
        perm += list(range(1792 + p * 128, 1792 + p * 128 + 128))
    b0 = 2304
    for h in range(4):
        perm += list(range(b0 + h * 128, b0 + h * 128 + 128))
        perm += list(range(b0 + 512 + h * 128, b0 + 512 + h * 128 + 128))
        perm += list(range(4368 + h * 128, 4368 + h * 128 + 128))
        perm += list(range(3328 + h * 128, 3328 + h * 128 + 128))
        perm += list(range(3840 + h * 128, 3840 + h * 128 + 128))
    perm += list(range(4352, 4368))
    for h in range(8):
        perm += list(range(4880 + h * 128, 4880 + h * 128 + 128))
        perm += list(range(5904 + h * 128, 5904 + h * 128 + 128))
        perm += list(range(6928 + h * 128, 6928 + h * 128 + 128))
        perm += list(range(7952 + h * 128, 7952 + h * 128 + 128))
    assert len(perm) == P_IN and len(set(perm)) == P_IN
    return np.array(perm)


C_IDENT = 0
C_MASK = 128
C_ONESBD = C_MASK + 8 * 128
C_ONES = C_ONESBD + 128
C_HM = C_ONES + 128
C_COS = C_HM + 2
C_SIN = C_COS + 512
NCONST = C_SIN + 512


def _consts():
    c = np.zeros((128, NCONST), np.float32)
    c[:, C_IDENT:C_IDENT + 128] = np.eye(128)
    r = np.arange(128)[:, None]
    q = np.arange(128)[None, :]
    same = (r // 64) == (q // 64)
    us = (same & (r < q)).astype(np.float32)
    ui = (same & (r <= q)).astype(np.float32)
    ls = (same & (r > q)).astype(np.float32)
    li = (same & (r >= q)).astype(np.float32)
    for i, m in enumerate([-us, ui, us, ui, -ls, li, ls, li]):
        c[:, C_MASK + i * 128:C_MASK + (i + 1) * 128] = m
    c[:, C_ONESBD:C_ONESBD + 128] = same.astype(np.float32)
    c[:, C_ONES:C_ONES + 128] = 1.0
    c[:64, C_HM] = 1.0
    c[64:, C_HM + 1] = 1.0
    half = 32
    inv = 1.0 / (10000.0 ** (np.arange(0, half, 2, dtype=np.float32) / half))
    t = (np.arange(8)[None, :] * 128 + np.arange(128)[:, None]).astype(np.float32)
    row = np.floor(t / 64.0)
    col = t - row * 64.0
    cos = np.zeros((128, 8, 4, 16), np.float32)
    sin = np.zeros((128, 8, 4, 16), np.float32)
    for br in range(2):
        for rc, pos in enumerate([row, col]):
            ang = pos[:, :, None] * inv[None, None, :]
            cos[:, :, br * 2 + rc, :] = np.cos(ang)
            sin[:, :, br * 2 + rc, :] = np.sin(ang)
    c[:, C_COS:C_COS + 512] = cos.reshape(128, 512)
    c[:, C_SIN:C_SIN + 512] = sin.reshape(128, 512)
    return c


PF = {}
_n = 0
for _p in range(4):
    for _j in range(3):
        for _tap in range(3):
            PF[("ca_w", _p, _j, _tap)] = _n; _n += 1
        PF[("ca_b", _p, _j)] = _n; _n += 1
    for _d in range(2):
        PF[("w0", _p, _d)] = _n; _n += 1
        PF[("a0", _p, _d)] = _n; _n += 1
    for _nm in ("k_k", "k_a", "omk_a", "r_k", "gn_g", "gn_b"):
        PF[("rw_" + _nm, _p)] = _n; _n += 1
for _h in range(4):
    for _j in range(2):
        for _tap in range(3):
            PF[("cb_w", _h, _j, _tap)] = _n; _n += 1
        PF[("cb_b", _h, _j)] = _n; _n += 1
    PF[("ml_gn_g", _h)] = _n; _n += 1
    PF[("ml_gn_b", _h)] = _n; _n += 1
PF[("subln",)] = _n; _n += 1
NPF = _n

PR_BI = 0
PR_BF = 8
PR_LQ1 = 16
PR_LK1 = 80
PR_LQ2 = 144
PR_LK2 = 208
NPR = 272


def _pack_pfm(inp, l):
    o = np.zeros((128, NPF), np.float32)
    for p in range(4):
        sl = slice(p * 128, p * 128 + 128)
        for j in range(3):
            for tap in range(3):
                o[:, PF[("ca_w", p, j, tap)]] = inp["conv_a_w"][l, tap, j * 512 + p * 128: j * 512 + p * 128 + 128]
            o[:, PF[("ca_b", p, j)]] = inp["conv_a_b"][l, j * 512 + p * 128: j * 512 + p * 128 + 128]
        for d in range(2):
            o[:, PF[("w0", p, d)]] = inp["rwkv_w0"][l, d, sl]
            o[:, PF[("a0", p, d)]] = inp["rwkv_a0"][l, d, sl]
        o[:, PF[("rw_k_k", p)]] = inp["rwkv_k_k"][l, sl]
        o[:, PF[("rw_k_a", p)]] = inp["rwkv_k_a"][l, sl]
        o[:, PF[("rw_r_k", p)]] = inp["rwkv_r_k"][l].reshape(512)[sl]
        o[:, PF[("rw_gn_g", p)]] = inp["rwkv_gn_g"][l, sl]
        o[:, PF[("rw_gn_b", p)]] = inp["rwkv_gn_b"][l, sl]
    for h in range(4):
        sl = slice(h * 128, h * 128 + 128)
        for j in range(2):
            for tap in range(3):
                o[:, PF[("cb_w", h, j, tap)]] = inp["conv_b_w"][l, tap, j * 512 + h * 128: j * 512 + h * 128 + 128]
            o[:, PF[("cb_b", h, j)]] = inp["conv_b_b"][l, j * 512 + h * 128: j * 512 + h * 128 + 128]
        o[:, PF[("ml_gn_g", h)]] = inp["mlstm_gn_g"][l, sl]
        o[:, PF[("ml_gn_b", h)]] = inp["mlstm_gn_b"][l, sl]
    o[:, PF[("subln",)]] = inp["diff_subln_g"][l]
    return o


def _pack_prow(inp, l):
    o = np.zeros((128, NPR), np.float32)
    o[:, PR_BI:PR_BI + 8] = inp["mlstm_b_i"][l].reshape(8)[None, :]
    o[:, PR_BF:PR_BF + 8] = inp["mlstm_b_f"][l].reshape(8)[None, :]
    o[:, PR_LQ1:PR_LQ1 + 64] = inp["diff_lq1"][l][None, :]
    o[:, PR_LK1:PR_LK1 + 64] = inp["diff_lk1"][l][None, :]
    o[:, PR_LQ2:PR_LQ2 + 64] = inp["diff_lq2"][l][None, :]
    o[:, PR_LK2:PR_LK2 + 64] = inp["diff_lk2"][l][None, :]
    return o


ENGS = ("pe", "act", "dve", "pool", "sp")
SAME_ENGINE_SYNC = True
SELF_RAW_ONLY = True


class Buf:
    __slots__ = ("name", "w", "r", "dsem", "excl")

    def __init__(self, name, init=None):
        self.name = name
        self.excl = False
        self.w = dict(init) if init else {}
        self.r = {}
        self.dsem = None


class Sched:
    def __init__(self, nc, es, n_dma_sems=90):
        self.nc = nc
        self.eng = {"pe": nc.tensor, "act": nc.scalar, "dve": nc.vector, "pool": nc.gpsimd, "sp": nc.sync}
        self.cnt = {}
        self.waited = {e: {} for e in ENGS}
        self.sem = {}
        for e in ENGS:
            self.sem[e] = es.enter_context(nc.semaphore("s_" + e))
            self.cnt[e] = 0
        self.free_dsems = [es.enter_context(nc.semaphore("d%d" % i)) for i in range(n_dma_sems)]
        self.n_dsem = 0
        self.nops = {e: 0 for e in ENGS}
        self.nwaits = {e: 0 for e in ENGS}
        self.fence = {}
        self.all_dma_events = {}
        self.recycled = []

    def newbuf(self, name, fenced=True):
        return Buf(name, self.fence if fenced else None)

    def close_scope(self, bufs):
        for b in bufs:
            for dct in (b.w, b.r):
                for k, v in dct.items():
                    if self.fence.get(k, 0) < v:
                        self.fence[k] = v
            if b.dsem is not None:
                self.recycled.append(b.dsem)

    def _dsem_for(self, b):
        if b.dsem is None:
            if self.recycled:
                key = self.recycled.pop()
            else:
                key = "D%d" % self.n_dsem
                self.sem[key] = self.free_dsems[self.n_dsem]
                self.n_dsem += 1
                self.cnt[key] = 0
            b.dsem = key
        return b.dsem

    def _emit_wait(self, ename, k, v):
        self.eng[ename].wait_ge(self.sem[k], v)
        self.nwaits[ename] += 1

    def _waits(self, eng, reads, writes):
        deps = {}
        for b in reads:
            for k, v in b.w.items():
                if deps.get(k, 0) < v:
                    deps[k] = v
        for b in writes:
            for k, v in b.w.items():
                if k == eng and SELF_RAW_ONLY:
                    continue
                if deps.get(k, 0) < v:
                    deps[k] = v
            for k, v in b.r.items():
                if k == eng and SELF_RAW_ONLY:
                    continue
                if deps.get(k, 0) < v:
                    deps[k] = v
        wd = self.waited[eng]
        for k, v in deps.items():
            if k == eng and (eng == "pe" or not SAME_ENGINE_SYNC):
                continue
            if wd.get(k, 0) >= v:
                continue
            wd[k] = v
            self._emit_wait(eng, k, v)

    def op(self, eng, fn, reads=(), writes=()):
        writes = [b for b in writes if b is not None] + [b for b in reads if b is not None and b.excl]
        reads = [b for b in reads if b is not None and not b.excl]
        self._waits(eng, reads, writes)
        self.cnt[eng] += 1
        n = self.cnt[eng]
        inst = fn(self.eng[eng])
        inst.then_inc(self.sem[eng], 1)
        self.nops[eng] += 1
        for b in reads:
            if b.r.get(eng, 0) < n:
                b.r[eng] = n
        for b in writes:
            b.w = {eng: n}
            b.r = {}

    def dma(self, qeng, fn, reads=(), writes=(), sembuf=None):
        reads = [b for b in reads if b is not None]
        writes = [b for b in writes if b is not None]
        self._waits(qeng, reads, writes)
        if sembuf is None:
            sembuf = writes[0] if writes else reads[0]
        key = self._dsem_for(sembuf)
        self.cnt[key] += 16
        n = self.cnt[key]
        inst = fn(self.eng[qeng])
        inst.then_inc(self.sem[key], 16)
        self.nops[qeng] += 1
        self.all_dma_events[key] = n
        for b in reads:
            if b.r.get(key, 0) < n:
                b.r[key] = n
        for b in writes:
            b.w = {key: n}
            b.r = {}

    def finish(self):
        for k, v in self.all_dma_events.items():
            if self.waited["sp"].get(k, 0) < v:
                self.waited["sp"][k] = v
                self._emit_wait("sp", k, v)
        for e in ("pe", "act", "dve", "pool"):
            if self.cnt[e] > 0 and self.waited["sp"].get(e, 0) < self.cnt[e]:
                self._emit_wait("sp", e, self.cnt[e])


class V:
    __slots__ = ("ap", "b")

    def __init__(self, ap, b):
        self.ap = ap
        self.b = b

    def __getitem__(self, idx):
        return V(self.ap[idx], self.b)

    def rr(self, pat, **kw):
        return V(self.ap.rearrange(pat, **kw), self.b)

    def bc(self, shape):
        return V(self.ap.broadcast_to(shape), self.b)

    def us(self, axis):
        return V(self.ap.unsqueeze(axis), self.b)

    def bitcast(self, dt):
        return V(self.ap.bitcast(dt), self.b)


class K:
    def __init__(self, nc, es):
        self.nc = nc
        self.S = Sched(nc, es)
        self.scopes = []

    def open_scope(self):
        es = ExitStack()
        es.__enter__()
        self.scopes.append((es, []))

    def close_scope(self):
        es, bufs = self.scopes.pop()
        self.S.close_scope(bufs)
        es.__exit__(None, None, None)

    _uid = 0

    def sb(self, name, shape, dt=F32):
        es, bufs = self.scopes[-1]
        K._uid += 1
        name = "%s_%d" % (name, K._uid)
        h = es.enter_context(self.nc.sbuf_tensor(name, list(shape), dt))
        b = self.S.newbuf(name)
        bufs.append(b)
        return V(h.ap(), b)

    def ps(self, name, shape, dt=F32):
        es, bufs = self.scopes[-1]
        h = es.enter_context(self.nc.psum_tensor(name, list(shape), dt))
        b = self.S.newbuf(name)
        bufs.append(b)
        return V(h.ap(), b)

    def dram(self, ap, tracked=False, name="dram"):
        return V(ap, self.S.newbuf(name, fenced=False) if tracked else None)

    def act(self, out, in_, func, bias=None, scale=None, accum=None, eng="act"):
        kw = {}
        reads = [in_.b]
        if bias is not None:
            if isinstance(bias, V):
                kw["bias"] = bias.ap
                reads.append(bias.b)
            else:
                kw["bias"] = float(bias)
        if scale is not None:
            if isinstance(scale, V):
                kw["scale"] = scale.ap
                reads.append(scale.b)
            else:
                kw["scale"] = float(scale)
        writes = [out.b]
        if accum is not None:
            kw["accum_out"] = accum.ap
            writes.append(accum.b)
        self.S.op("act", lambda e: e.activation(out=out.ap, in_=in_.ap, func=func, **kw), reads, writes)

    def tt(self, out, in0, in1, op, eng="dve"):
        self.S.op(eng, lambda e: e.tensor_tensor(out=out.ap, in0=in0.ap, in1=in1.ap, op=op), [in0.b, in1.b], [out.b])

    def ts(self, out, in0, s1, op0, s2=None, op1=None, accum=None, eng="dve"):
        reads = [in0.b]
        a1 = s1.ap if isinstance(s1, V) else float(s1)
        if isinstance(s1, V):
            reads.append(s1.b)
        kw = {}
        if s2 is not None:
            a2 = s2.ap if isinstance(s2, V) else float(s2)
            if isinstance(s2, V):
                reads.append(s2.b)
            kw["op1"] = op1
        else:
            a2 = None
        writes = [out.b]
        if accum is not None:
            kw["accum_out"] = accum.ap
            writes.append(accum.b)
            if op1 is not None:
                kw["op1"] = op1
        self.S.op(eng, lambda e: e.tensor_scalar(out=out.ap, in0=in0.ap, scalar1=a1, scalar2=a2, op0=op0, **kw), reads, writes)

    def stt(self, out, in0, scalar, in1, op0, op1):
        reads = [in0.b, in1.b]
        sc = scalar.ap if isinstance(scalar, V) else float(scalar)
        if isinstance(scalar, V):
            reads.append(scalar.b)
        self.S.op("dve", lambda e: e.scalar_tensor_tensor(out=out.ap, in0=in0.ap, scalar=sc, in1=in1.ap, op0=op0, op1=op1), reads, [out.b])

    def copy(self, out, in_, eng="dve"):
        if eng == "act":
            self.act(out, in_, AF.Copy)
        else:
            self.S.op(eng, lambda e: e.tensor_copy(out=out.ap, in_=in_.ap), [in_.b], [out.b])

    def memset(self, out, val, eng="pool"):
        self.S.op(eng, lambda e: e.memset(out.ap, float(val)), [], [out.b])

    def reduce(self, out, in_, op, axis=AX.X, eng="dve"):
        self.S.op(eng, lambda e: e.tensor_reduce(out=out.ap, in_=in_.ap, axis=axis, op=op), [in_.b], [out.b])

    def recip(self, out, in_):
        self.S.op("dve", lambda e: e.reciprocal(out=out.ap, in_=in_.ap), [in_.b], [out.b])

    def scan(self, out, d0, d1, initial, op0, op1):
        self.S.op("dve", lambda e: e.tensor_tensor_scan(out=out.ap, data0=d0.ap, data1=d1.ap, initial=float(initial), op0=op0, op1=op1), [d0.b, d1.b], [out.b])

    def mm(self, out, lhsT, rhs, start=True, stop=True):
        self.S.op("pe", lambda e: e.matmul(out.ap, lhsT=lhsT.ap, rhs=rhs.ap, start=start, stop=stop), [lhsT.b, rhs.b], [out.b])

    def tr(self, out, in_, ident):
        self.S.op("pe", lambda e: e.transpose(out=out.ap, in_=in_.ap, identity=ident.ap), [in_.b, ident.b], [out.b])

    def dma(self, out, in_, q="sp"):
        wr = [out.b]
        rd = [in_.b]
        out_is_dram = "DRAM" in str(out.ap.space).upper()
        sembuf = in_.b if (out_is_dram and in_.b is not None) else out.b
        self.S.dma(q, lambda e: e.dma_start(out=out.ap, in_=in_.ap), rd, wr, sembuf=sembuf)


class _Stop(Exception):
    pass


class Prog:
    stop_stage = None

    def stage(self, name):
        if self.stop_stage is not None and name == self.stop_stage:
            raise _Stop()

    def run_unit(self, fn, *a):
        depth = len(self.k.scopes)
        try:
            fn(*a)
        except _Stop:
            while len(self.k.scopes) > depth:
                self.k.close_scope()

    def __init__(self, dbg=False, layers=(0, 1), groups=(0, 1), parts=None):
        self.dbg = dbg
        self.layers = layers
        self.groups = groups
        self.parts = parts
        self.dumps = {}
        self.bank_i = 0
        self.tbank = 0
        self.abank = 0

    def want(self, part):
        return self.parts is None or part in self.parts

    def declare(self):
        nc = self.nc
        di = lambda n, s: nc.dram_tensor(n, list(s), F32, kind="ExternalInput").ap()
        do = lambda n, s: nc.dram_tensor(n, list(s), F32, kind="ExternalOutput").ap()
        dint = lambda n, s: nc.dram_tensor(n, list(s), F32, kind="Internal").ap()
        self.d_xin = di("xin", [2, TG, D])
        self.d_cT = di("cT", [128, 32])
        self.d_wada = di("w_ada", [2, D, 3 * D])
        self.d_bada = di("b_ada", [2, 3 * D])
        self.d_win = di("w_in", [2, D, P_IN])
        self.d_wout = di("w_out", [2, D, D])
        self.d_pfm = di("pfm", [2, 128, NPF])
        self.d_prow = di("prow", [2, 128, NPR])
        self.d_lng = di("ln_g", [2, D])
        self.d_lnb = di("ln_b", [2, D])
        self.d_wup = di("wup", [2, 2, 128, 512])
        self.d_rw0 = di("rw0", [2, 2, 4, 128, 128])
        self.d_ml0 = di("ml0", [2, 2, 4, 128, 130])
        self.d_mm0 = di("mm0", [2, 128, 8])
        self.d_ck = di("ck", [2, 8, PAST, 128])
        self.d_cv = di("cv", [2, 8, PAST, 128])
        self.d_consts = di("consts", [128, NCONST])
        self.d_yout = do("yout", [2, TG, D])
        self.d_nk = do("nk", [4, 2, 8, TSEQ_P, 128])
        self.d_nv = do("nv", [4, 2, 8, TSEQ_P, 128])
        self.d_nrw = do("nrw", [4, 2, 2, 4, 128, 128])
        self.d_nmc = do("nmc", [4, 2, 2, 4, 128, 130])
        self.d_nmm = do("nmm", [2, 32])
        self.d_x1 = dint("x1s", [2, TG, D])
        self.d_modg = dint("modg", [2, 2, D])

    def dump(self, name, v, shape):
        if not self.dbg:
            return
        ap = self.nc.dram_tensor("dbg_" + name, list(shape), F32, kind="ExternalOutput").ap()
        self.dumps[name] = list(shape)
        k = self.k
        if v.ap.dtype != F32:
            k.open_scope()
            t = k.sb("dbgt_" + name, shape, F32)
            k.copy(t, v)
            k.dma(V(ap, None), t)
            k.close_scope()
        else:
            k.dma(V(ap, None), v)

    def bank(self):
        b = self.PB[self.bank_i % 8]
        self.bank_i += 1
        return b

    def build(self):
        self.nc = nc = bass.Bass("TRN2", target_bir_lowering=False)
        self.declare()
        with ExitStack() as es:
            self.k = k = K(nc, es)
            k.open_scope()
            self.setup()
            for l in self.layers:
                self.layer_setup(l)
                for g in self.groups:
                    self.group(l, g)
            k.S.finish()
            k.close_scope()
        return nc

    def setup(self):
        k = self.k
        self.CON = k.sb("CON", [128, NCONST])
        k.dma(self.CON, V(self.d_consts, None))
        C = self.CON
        self.IDF = C[:, C_IDENT:C_IDENT + 128]
        self.MASK = C[:, C_MASK:C_MASK + 1024].rr("p (a b) -> p a b", a=8)
        self.ONESBD = C[:, C_ONESBD:C_ONESBD + 128]
        self.ONES = C[:, C_ONES:C_ONES + 128]
        self.HM = C[:, C_HM:C_HM + 2]
        self.COS = C[:, C_COS:C_COS + 512].rr("p (i a b) -> p i a b", i=8, a=4)
        self.SIN = C[:, C_SIN:C_SIN + 512].rr("p (i a b) -> p i a b", i=8, a=4)
        self.IDB = k.sb("IDB", [128, 128], BF16)
        k.copy(self.IDB, self.IDF)
        es, bufs = k.scopes[-1]
        h = es.enter_context(self.nc.psum_tensor("PS", [128, 8, 512], F32))
        self.PB = []
        for i in range(8):
            b = k.S.newbuf("bank%d" % i)
            b.excl = True
            bufs.append(b)
            self.PB.append(V(h.ap()[:, i, :], b))
        self.mixT = k.sb("mixT", [128, KC, TG], BF16)
        self.WS = k.sb("WS", [128, KC, 656], BF16)
        self.PFM = k.sb("PFM", [128, NPF])
        self.PROW = k.sb("PROW", [128, NPR])
        self.WUP = k.sb("WUP", [128, 2, 512], BF16)
        self.MODT = k.sb("MODT", [128, 2, 32])
        self.LAM = k.sb("LAM", [128, 2])
        self.SUBG = k.sb("SUBG", [128, 1])
        self.RMASK = k.sb("RMASK", [128, TG])
        k.memset(self.RMASK, 1.0)
        k.memset(self.RMASK.rr("p (c t) -> p c t", t=64)[:, :, 0:1], 0.0)
        self.EPSC = k.sb("EPSC", [128, 4])
        k.memset(self.EPSC[:, 0:1], 1e-12)
        k.memset(self.EPSC[:, 1:2], GN_EPS_A)
        k.memset(self.EPSC[:, 2:3], GN_EPS)
        k.memset(self.EPSC[:, 3:4], 1.0)
        self.EPS12 = self.EPSC[:, 0:1]
        self.EPSA = self.EPSC[:, 1:2]
        self.EPSB = self.EPSC[:, 2:3]
        self.ONE1 = self.EPSC[:, 3:4]
        self.x1bufs = [[k.S.newbuf("x1_%d_%d" % (g, i), fenced=False) for i in range(NT)] for g in range(2)]
        self.modgbuf = [[k.S.newbuf("modg%d%d" % (l, j), fenced=False) for j in range(2)] for l in range(2)]

    def pf(self, *key):
        c = PF[key]
        return self.PFM[:, c:c + 1]

    def layer_setup(self, l):
        k = self.k
        k.dma(self.PFM, V(self.d_pfm[l], None))
        k.dma(self.PROW, V(self.d_prow[l], None))
        for p in range(4):
            k.ts(self.pf("rw_omk_a", p), self.pf("rw_k_a", p), -1.0, ALU.mult, 1.0, ALU.add)
        k.dma(self.WUP, V(self.d_wup[l].rearrange("a p c -> p a c"), None), q="pool")
        lam_init = 0.8 - 0.6 * math.exp(-0.3 * l)
        k.open_scope()
        t = k.sb("lamt", [128, 64])
        s = k.sb("lams", [128, 2])
        k.tt(t, self.PROW[:, PR_LQ1:PR_LQ1 + 64], self.PROW[:, PR_LK1:PR_LK1 + 64], ALU.mult)
        k.reduce(s[:, 0:1], t, ALU.add)
        k.tt(t, self.PROW[:, PR_LQ2:PR_LQ2 + 64], self.PROW[:, PR_LK2:PR_LK2 + 64], ALU.mult)
        k.reduce(s[:, 1:2], t, ALU.add)
        k.act(s, s, AF.Exp)
        k.tt(self.LAM[:, 0:1], s[:, 0:1], s[:, 1:2], ALU.subtract)
        k.ts(self.LAM[:, 0:1], self.LAM[:, 0:1], lam_init, ALU.add)
        k.ts(self.SUBG, self.pf("subln"), 1.0 - lam_init, ALU.mult)
        k.close_scope()
        if self.want("ada"):
            self.ada(l)

    def load_w(self, src_rows, c0, ncols):
        k = self.k
        src = src_rows.rearrange("(kc p) c -> p kc c", p=128)[:, :, c0:c0 + ncols]
        k.dma(self.WS[:, :, 0:ncols], V(src, None), q="pool")

    def ada(self, l):
        k = self.k
        k.open_scope()
        ct = k.sb("ada_c", [128, 32])
        cb = k.sb("ada_cb", [128, 32], BF16)
        k.dma(ct, V(self.d_cT, None))
        k.act(cb, ct, AF.Silu)
        brow = k.sb("ada_brow", [1, 512])
        rowt = [k.sb("ada_row%d" % j, [1, 512]) for j in range(2)]
        pm = self.PB[7]
        for s in range(12):
            self.load_w(self.d_wada[l], s * 512, 512)
            k.dma(brow, V(self.d_bada[l:l + 1, s * 512:(s + 1) * 512], None))
            for j in range(2):
                pb = self.PB[j]
                for kc in range(KC):
                    k.mm(pb[0:1, 0:512], cb[:, kc * 2 + j:kc * 2 + j + 1], self.WS[:, kc, 0:512], start=(kc == 0), stop=(kc == KC - 1))
                k.tt(rowt[j], pb[0:1, 0:512], brow, ALU.add)
                if s < 8:
                    for q in range(4):
                        c = s * 4 + q
                        k.mm(pm[:, (j * 32 + c) * 2:(j * 32 + c) * 2 + 2], rowt[j][0:1, q * 128:(q + 1) * 128], self.ONES[0:1, 0:2])
                else:
                    k.dma(V(self.d_modg[l, j:j + 1, (s - 8) * 512:(s - 7) * 512], self.modgbuf[l][j]), rowt[j])
        k.copy(self.MODT.rr("p a b -> p (a b)"), pm[:, 0:128].rr("p (c t) -> p c t", t=2)[:, :, 0])
        k.ts(self.MODT[:, :, 16:32], self.MODT[:, :, 16:32], 1.0, ALU.add)
        self.dump("modT%d" % l, self.MODT, [128, 2, 32])
        k.close_scope()

    def group(self, l, g):
        k = self.k
        self.l, self.g = l, g
        self.nseq, self.tseq = (NSEQ_P, TSEQ_P) if g == 0 else (1, TG)
        tag = "%d%d" % (l, g)
        k.open_scope()
        self.uT = k.sb("uT" + tag, [128, KC, TG], BF16)
        self.LW = k.sb("LW" + tag, [128, TG], BF16)
        self.LA = k.sb("LA" + tag, [128, TG], BF16)
        self.OMG = k.sb("OMG" + tag, [128, NT, 8])
        self.CLAMP = k.sb("CLAMP" + tag, [128, NT, 8])
        self.SC = k.sb("SC" + tag, [128, NCH, 8])
        if self.want("u"):
            self.uphase(l, g)
        if self.want("lora"):
            self.unit_lora(l, g)
        for p in range(4):
            if self.want("rw%d" % p):
                self.run_unit(self.unit_rwkv, l, g, p)
        if self.want("mg"):
            self.run_unit(self.unit_mg, l, g)
        for h in range(4):
            if self.want("ml%d" % h):
                self.run_unit(self.unit_mlstm, l, g, h)
        for h in range(8):
            if self.want("at%d" % h):
                self.run_unit(self.unit_attn3, l, g, h)
        k.close_scope()
        if self.want("o"):
            self.phase_o(l, g)

    def xsrc(self, l, g, i):
        if l == 0:
            return V(self.d_xin[g, i * 128:(i + 1) * 128, :], None)
        return V(self.d_x1[g, i * 128:(i + 1) * 128, :], self.x1bufs[g][i])

    def uphase(self, l, g):
        k = self.k
        k.open_scope()
        xts = [k.sb("xt%d" % j, [128, D]) for j in range(2)]
        for i in range(NT):
            xt = xts[i % 2]
            k.dma(xt, self.xsrc(l, g, i))
            for q in range(4):
                pb = self.bank()
                for kk in range(4):
                    kc = q * 4 + kk
                    k.tr(pb[:, kk * 128:(kk + 1) * 128], xt[:, kc * 128:(kc + 1) * 128], self.IDF)
                for kk in range(4):
                    kc = q * 4 + kk
                    k.act(self.uT[:, kc, i * 128:(i + 1) * 128], pb[:, kk * 128:(kk + 1) * 128], AF.Identity,
                          scale=self.MODT[:, g, 16 + kc:17 + kc], bias=self.MODT[:, g, kc:kc + 1])
        k.close_scope()
        self.dump("uT%d%d" % (l, g), self.uT[:, :, 0:256], [128, KC, 256])

    def proj_fm(self, col, evac):
        k = self.k
        for nb in range(2):
            pb = self.bank()
            for kc in range(KC):
                k.mm(pb[:, 0:512], self.WS[:, kc, col:col + 128], self.uT[:, kc, nb * 512:(nb + 1) * 512], start=(kc == 0), stop=(kc == KC - 1))
            evac(nb, pb[:, 0:512])

    def proj_tm(self, col, ncols, evac):
        k = self.k
        for i in range(NT):
            pb = self.bank()
            for kc in range(KC):
                k.mm(pb[:, 0:ncols], self.uT[:, kc, i * 128:(i + 1) * 128], self.WS[:, kc, col:col + ncols], start=(kc == 0), stop=(kc == KC - 1))
            evac(i, pb[:, 0:ncols])

    def pad_evac(self, PRE, j):
        k = self.k
        nseq, tseq = self.nseq, self.tseq

        def ev(nb, ps):
            if nseq == 1:
                k.act(PRE[:, j, 0, 1 + nb * 512:1 + (nb + 1) * 512], ps, AF.Copy)
            else:
                k.act(PRE[:, j, 2 * nb:2 * nb + 2, 1:tseq + 1], ps.rr("p (s t) -> p s t", s=2), AF.Copy)
        return ev

    def conv(self, X, PRE, j, w0, w1, w2, b):
        k = self.k
        nseq, tseq = self.nseq, self.tseq
        xv = X.rr("p (s t) -> p s t", s=nseq)
        k.act(xv, PRE[:, j, :, 1:tseq + 1], AF.Identity, scale=w1, bias=b)
        k.stt(xv, PRE[:, j, :, 0:tseq], w0, xv, ALU.mult, ALU.add)
        k.stt(xv, PRE[:, j, :, 2:tseq + 2], w2, xv, ALU.mult, ALU.add)

    def unit_lora(self, l, g):
        k = self.k
        self.load_w(self.d_win[l], U_LORA, 256)
        self.proj_fm(0, lambda nb, ps: k.act(self.LW[:, nb * 512:(nb + 1) * 512], ps, AF.Tanh))
        self.proj_fm(128, lambda nb, ps: k.act(self.LA[:, nb * 512:(nb + 1) * 512], ps, AF.Copy))
        self.dump("LW%d%d" % (l, g), self.LW, [128, TG])

    def unit_rwkv(self, l, g, p):
        k = self.k
        nseq, tseq = self.nseq, self.tseq
        cps = tseq // 64
        tag = "%d%d%d" % (l, g, p)
        self.load_w(self.d_win[l], U_RW[p], 512)
        k.open_scope()
        GT = k.sb("rGT" + tag, [128, TG])
        BON = k.sb("rBON" + tag, [128, TG])
        KR = [k.sb("rKR%d" % d + tag, [128, NT, 2, 128], BF16) for d in range(2)]
        AH = [k.sb("rAH%d" % d + tag, [128, TG], BF16) for d in range(2)]
        KH = [k.sb("rKH%d" % d + tag, [128, TG], BF16) for d in range(2)]
        VB = k.sb("rVB" + tag, [128, TG], BF16)
        GL = [k.sb("rGL%d" % d + tag, [128, NCH]) for d in range(2)]
        YTM = k.sb("rY" + tag, [128, NT, 128])

        k.open_scope()
        R = k.sb("rR" + tag, [128, TG])
        Kk = k.sb("rK" + tag, [128, TG])
        V32 = k.sb("rV" + tag, [128, TG])
        k.open_scope()
        PRE = k.sb("rPRE" + tag, [128, 3, nseq, tseq + 2])
        k.memset(PRE[:, :, :, 0:1], 0.0)
        k.memset(PRE[:, :, :, tseq + 1:tseq + 2], 0.0)
        for j in range(3):
            self.proj_fm(j * 128, self.pad_evac(PRE, j))
        self.proj_fm(384, lambda nb, ps: k.act(GT[:, nb * 512:(nb + 1) * 512], ps, AF.Silu))
        for j, X in enumerate((R, Kk, V32)):
            self.conv(X, PRE, j, self.pf("ca_w", p, j, 0), self.pf("ca_w", p, j, 1), self.pf("ca_w", p, j, 2), self.pf("ca_b", p, j))
        k.close_scope()
        self.stage("rw_conv")
        B = [k.sb("rB%d" % i + tag, [128, TG]) for i in range(9)]
        k.copy(VB, V32, eng="pool")
        KK, SQ, RS = B[0], B[1], B[2]
        k.ts(KK, Kk, self.pf("rw_k_k", p), ALU.mult)
        k.act(SQ, KK, AF.Square)
        for nb in range(2):
            pb = self.bank()
            k.mm(pb[:, 0:512], self.ONESBD, SQ[:, nb * 512:(nb + 1) * 512])
            k.act(RS[:, nb * 512:(nb + 1) * 512], pb[:, 0:512], AF.Sqrt, bias=self.EPS12)
        k.recip(RS, RS)
        k.tt(KK, KK, RS, ALU.mult)
        self.stage("rw_kk")
        KS = B[8]
        v3 = lambda X: X.rr("p (i t) -> p i t", t=128)
        c3 = lambda X: X.rr("p (c t) -> p c t", t=64)
        for d in range(2):
            SIG, AA, KT, AL, CS, T1, T2 = B[1], B[2], B[3], B[4], B[5], B[6], B[7]
            dr = slice(d * 64, d * 64 + 64)
            for nb in range(2):
                ns = slice(nb * 512, (nb + 1) * 512)
                pb = self.bank()
                k.mm(pb[:, 0:512], self.WUP[dr, 0, p * 128:(p + 1) * 128], self.LW[dr, ns])
                k.act(SIG[:, ns], pb[:, 0:512], AF.Sigmoid, bias=self.pf("w0", p, d))
                pb = self.bank()
                k.mm(pb[:, 0:512], self.WUP[dr, 1, p * 128:(p + 1) * 128], self.LA[dr, ns])
                k.act(AA[:, ns], pb[:, 0:512], AF.Sigmoid, bias=self.pf("a0", p, d))
            k.ts(T1, AA, self.pf("rw_k_a", p), ALU.mult, self.pf("rw_omk_a", p), ALU.add)
            k.tt(KT, T1, Kk, ALU.mult)
            if d == 0:
                k.copy(KS, KT, eng="pool")
            else:
                k.tt(KS, KS, KT, ALU.add, eng="pool")
            k.tt(AL, AA, KK, ALU.mult)
            k.scan(CS, self.RMASK, SIG, 0.0, ALU.mult, ALU.add)
            if d == 1:
                k.tt(c3(T1), c3(CS), c3(CS)[:, :, 63:64].bc([128, NCH, 64]), ALU.subtract)
                k.tt(CS, SIG, T1, ALU.subtract)
            k.tt(T2, CS, SIG, ALU.subtract)
            k.act(T2, T2, AF.Exp, scale=-WDECAY)
            k.tt(KR[d][:, :, 0, :], v3(KK), v3(T2), ALU.mult)
            EP = T1
            k.act(EP, CS, AF.Exp, scale=-WDECAY)
            k.tt(KR[d][:, :, 1, :], v3(R), v3(EP), ALU.mult)
            k.copy(GL[d], c3(EP)[:, :, 63 if d == 0 else 0])
            EM = AA
            k.act(EM, CS, AF.Exp, scale=WDECAY)
            k.tt(AH[d], AL, EM, ALU.mult)
            k.tt(KH[d], KT, EM, ALU.mult)
        k.ts(B[1], R, self.pf("rw_r_k", p), ALU.mult)
        k.tt(B[1], B[1], KS, ALU.mult)
        for nb in range(2):
            ns = slice(nb * 512, (nb + 1) * 512)
            pb = self.bank()
            k.mm(pb[:, 0:512], self.ONESBD, B[1][:, ns])
            k.tt(BON[:, ns], pb[:, 0:512], V32[:, ns], ALU.mult)
        if p == 0:
            self.dump("rw_R%d%d" % (l, g), R, [128, TG])
            self.dump("rw_KK%d%d" % (l, g), KK, [128, TG])
            self.dump("rw_BON%d%d" % (l, g), BON, [128, TG])
            self.dump("rw_GL0%d%d" % (l, g), GL[0], [128, NCH])
            self.dump("rw_GL1%d%d" % (l, g), GL[1], [128, NCH])
        k.close_scope()

        self.stage("rw_prep")
        k.open_scope()
        TMB = k.sb("rTMB" + tag, [128, NT, 5, 128], BF16)
        for i in range(NT):
            pbb = self.bank().bitcast(BF16)
            ts_ = slice(i * 128, (i + 1) * 128)
            for s, src in enumerate((VB, KH[0], KH[1], AH[0], AH[1])):
                k.tr(pbb[:, s * 128:(s + 1) * 128], src[:, ts_], self.IDB)
            k.copy(TMB[:, i].rr("p a b -> p (a b)"), pbb[:, 0:640], eng=("act" if i % 2 else "dve"))
        self.stage("rw_tmb")
        k.memset(YTM, 0.0)
        self.RB = [k.sb("rRB%d" % i + tag, [128, 128], BF16) for i in range(2)]
        self.UB = [k.sb("rUB%d" % i + tag, [128, 128], BF16) for i in range(2)]
        self.HT = [k.sb("rHT%d" % i + tag, [128, 128]) for i in range(2)]
        Hf = {}
        Hb = {}
        for s in range(nseq):
            for d in range(2):
                Hf[(s, d)] = k.sb("rHf%d%d" % (s, d) + tag, [128, 128])
                Hb[(s, d)] = k.sb("rHb%d%d" % (s, d) + tag, [128, 128], BF16)
        for d in range(2):
            k.open_scope()
            GMQ = k.sb("rGMQ%d" % d + tag, [128, NCH, 4, 128], BF16)
            P0 = k.sb("rP0%d" % d + tag, [128, NCH, 128], BF16)
            QA = k.sb("rQA%d" % d + tag, [128, NCH, 128], BF16)
            PA = k.sb("rPA%d" % d + tag, [128, NCH, 128], BF16)
            X = k.sb("rX%d" % d + tag, [128, NCH, 128], BF16)
            R1 = k.sb("rR1%d" % d + tag, [128, NT, 128])
            mP = 4 if d == 0 else 0
            for i in range(NT):
                ts_ = slice(i * 128, (i + 1) * 128)
                for hh in range(2):
                    j = i * 2 + hh
                    hr = slice(hh * 64, hh * 64 + 64)
                    pb = self.bank()
                    krv = KR[d][hr, i].rr("p a b -> p (a b)")
                    k.mm(pb[:, 0:256], AH[d][hr, ts_], krv)
                    k.mm(pb[:, 256:512], KH[d][hr, ts_], krv)
                    k.tt(GMQ[:, j], pb[:, 0:512].rr("p (a b) -> p a b", a=4), self.MASK[:, 4 * d:4 * d + 4, :], ALU.mult)
            for hh in range(2):
                hr = slice(hh * 64, hh * 64 + 64)
                for q in range(2):
                    pbP = self.bank()
                    for ii in range(4):
                        i = q * 4 + ii
                        k.mm(pbP[:, ii * 128:(ii + 1) * 128], KR[d][hr, i, 0, :], AH[d][hr, i * 128:(i + 1) * 128])
                    k.tt(P0[:, q * 8 + hh:q * 8 + 8:2, :], pbP[:, 0:512].rr("p (a b) -> p a b", a=4),
                         self.MASK[:, mP:mP + 1, :].bc([128, 4, 128]), ALU.mult)
            self.stage("rw_gram")
            k.tt(X, GMQ[:, :, 0, :], self.IDB.us(1).bc([128, NCH, 128]), ALU.add)
            Qc, Pc = GMQ[:, :, 0, :], P0
            Qn, Pn = QA, PA
            for lev in range(1, 6):
                for grp in range(4):
                    js = range(grp * 4, grp * 4 + 4)
                    gsl = slice(grp * 4, grp * 4 + 4)
                    pbP = self.bank()
                    for jj, j in enumerate(js):
                        k.mm(pbP[:, jj * 128:(jj + 1) * 128], Qc[:, j, :], Pc[:, j, :])
                    k.copy(Pn[:, gsl, :], pbP[:, 0:512].rr("p (a b) -> p a b", a=4), eng="act")
                    if lev < 5:
                        pbQ = self.bank()
                        for jj, j in enumerate(js):
                            k.mm(pbQ[:, jj * 128:(jj + 1) * 128], Pc[:, j, :], Qc[:, j, :])
                        k.copy(Qn[:, gsl, :], pbQ[:, 0:512].rr("p (a b) -> p a b", a=4), eng="dve")
                    pbX = self.bank()
                    for jj, j in enumerate(js):
                        k.mm(pbX[:, jj * 128:(jj + 1) * 128], Pn[:, j, :], X[:, j, :])
                    k.tt(X[:, gsl, :], X[:, gsl, :], pbX[:, 0:512].rr("p (a b) -> p a b", a=4), ALU.add)
                if lev == 1:
                    Qc, Pc, Qn, Pn = QA, PA, k.sb("rQB%d" % d + tag, [128, NCH, 128], BF16), P0
                else:
                    Qc, Pc, Qn, Pn = Qn, Pn, Qc, Pc
            self.stage("rw_inv")
            for i in range(NT):
                pb = self.bank()
                for hh in range(2):
                    j = i * 2 + hh
                    vs = TMB[:, i, 0, hh * 64:(hh + 1) * 64]
                    k.mm(pb[:, hh * 64:(hh + 1) * 64], GMQ[:, j, 2, :], vs)
                    k.mm(pb[:, 128 + hh * 64:128 + (hh + 1) * 64], GMQ[:, j, 3, :], vs)
                k.copy(R1[:, i, :], pb[:, 0:128], eng="act")
                k.tt(YTM[:, i, :], YTM[:, i, :], pb[:, 128:256], ALU.add)
            self.stage("rw_r1")
            for s in range(nseq):
                if g == 0:
                    k.memset(Hf[(s, d)], 0.0)
                else:
                    k.dma(Hf[(s, d)], V(self.d_rw0[l, d, p], None))
                k.copy(Hb[(s, d)], Hf[(s, d)], eng="act")
            for cs in range(cps):
                for s in range(nseq):
                    c = s * cps + (cs if d == 0 else cps - 1 - cs)
                    i, half = c // 2, c % 2
                    tr_ = slice(half * 64, half * 64 + 64)
                    hf, hb = Hf[(s, d)], Hb[(s, d)]
                    pbR = self.bank()
                    k.mm(pbR[tr_, 0:128], KR[d][:, i, 0, tr_], hb)
                    Rb = self.RB[(s + d) % 2]
                    k.tt(Rb[tr_, :], pbR[tr_, 0:128], R1[tr_, i, :], ALU.add)
                    self.stage("sc_R%d" % c)
                    pbU = self.bank()
                    for hh in range(2):
                        j = i * 2 + hh
                        k.mm(pbU[tr_, hh * 64:(hh + 1) * 64], X[tr_, j, tr_], Rb[tr_, hh * 64:(hh + 1) * 64])
                    Ub = self.UB[(s + d) % 2]
                    k.act(Ub[tr_, :], pbU[tr_, 0:128], AF.Copy, scale=-1.0)
                    self.stage("sc_U%d" % c)
                    pbY = self.bank()
                    k.mm(pbY[tr_, 0:128], KR[d][:, i, 1, tr_], hb)
                    pbH = self.bank()
                    k.mm(pbH[:, 0:128], TMB[tr_, i, 1 + d, :], TMB[tr_, i, 0, :], start=True, stop=False)
                    k.mm(pbH[:, 0:128], TMB[tr_, i, 3 + d, :], Ub[tr_, :], start=False, stop=True)
                    pbY2 = self.bank()
                    for hh in range(2):
                        j = i * 2 + hh
                        k.mm(pbY2[tr_, hh * 64:(hh + 1) * 64], GMQ[tr_, j, 1, tr_], Ub[tr_, hh * 64:(hh + 1) * 64])
                    HT = self.HT[(s + d) % 2]
                    k.tt(HT, pbH[:, 0:128], self.ONESBD, ALU.mult)
                    k.tt(HT, HT, hf, ALU.add)
                    k.ts(hf, HT, GL[d][:, c:c + 1], ALU.mult)
                    k.copy(hb, hf, eng="act")
                    k.tt(YTM[tr_, i, :], YTM[tr_, i, :], pbY[tr_, 0:128], ALU.add)
                    k.tt(YTM[tr_, i, :], YTM[tr_, i, :], pbY2[tr_, 0:128], ALU.add)
                    self.stage("sc_Y%d" % c)
                    self.stage("sc_H%d" % c)
            self.stage("sc_end")
            if g == 0:
                for s in range(nseq):
                    k.dma(V(self.d_nrw[s, l, d, p], None), Hf[(s, d)])
            self.stage("sc_out%d" % d)
            if p == 0 and d == 0:
                self.dump("rw_X%d%d" % (l, g), X[:, 0:2, :], [128, 2, 128])
                self.dump("rw_R1%d%d" % (l, g), R1, [128, NT, 128])
            k.close_scope()
        if p == 0:
            self.dump("rw_Y%d%d" % (l, g), YTM, [128, NT, 128])
        self.stage("rw_scan")
        YN = k.sb("rYN" + tag, [128, NT, 128])
        ST = k.sb("rST" + tag, [128, 4, 16])
        yv = YTM.rr("p i (h v) -> p (i h) v", h=2)
        ynv = YN.rr("p i (h v) -> p (i h) v", h=2)
        k.reduce(ST[:, 0, :], yv, ALU.add)
        k.act(YN, YTM, AF.Square)
        k.reduce(ST[:, 1, :], ynv, ALU.add)
        k.ts(ST[:, 0, :], ST[:, 0, :], 1.0 / 64, ALU.mult)
        k.tt(ST[:, 2, :], ST[:, 0, :], ST[:, 0, :], ALU.mult)
        k.stt(ST[:, 1, :], ST[:, 1, :], 1.0 / 64, ST[:, 2, :], ALU.mult, ALU.subtract)
        k.act(ST[:, 1, :], ST[:, 1, :], AF.Sqrt, bias=self.EPSA)
        k.recip(ST[:, 1, :], ST[:, 1, :])
        k.tt(ynv, yv, ST[:, 0, :].us(2).bc([128, 16, 64]), ALU.subtract)
        k.tt(ynv, ynv, ST[:, 1, :].us(2).bc([128, 16, 64]), ALU.mult)
        self.stage("rw_ln")
        OUTF = k.sb("rOUT" + tag, [128, TG])
        for q in range(2):
            pb = self.bank()
            for ii in range(4):
                i = q * 4 + ii
                k.tr(pb[:, ii * 128:(ii + 1) * 128], YN[:, i, :], self.IDF)
            k.act(OUTF[:, q * 512:(q + 1) * 512], pb[:, 0:512], AF.Identity, scale=self.pf("rw_gn_g", p), bias=self.pf("rw_gn_b", p))
        k.tt(OUTF, OUTF, BON, ALU.add)
        k.tt(self.mixT[:, p, :], OUTF, GT, ALU.mult)
        if p == 0:
            self.dump("rw_out%d%d" % (l, g), self.mixT[:, 0, :], [128, TG])
        k.close_scope()
        k.close_scope()

    def unit_mg(self, l, g):
        k = self.k
        nseq, tseq = self.nseq, self.tseq
        cps = tseq // 64
        tag = "%d%d" % (l, g)
        self.load_w(self.d_win[l], U_MG, 16)
        k.open_scope()
        GI = k.sb("gGI" + tag, [128, NT, 16])
        self.proj_tm(0, 16, lambda i, ps: k.act(GI[:, i, :], ps, AF.Copy))
        gi = k.sb("ggi" + tag, [128, NT, 8])
        LFN = k.sb("gLFN" + tag, [128, NT, 8])
        NB = k.sb("gNB" + tag, [128, NT, 8])
        AG = k.sb("gAG" + tag, [128, NT, 8])
        k.tt(gi, GI[:, :, 0:8], self.PROW[:, PR_BI:PR_BI + 8].us(1).bc([128, NT, 8]), ALU.add)
        k.tt(LFN, GI[:, :, 8:16], self.PROW[:, PR_BF:PR_BF + 8].us(1).bc([128, NT, 8]), ALU.add)
        k.act(LFN, LFN, AF.Exp, scale=-1.0)
        k.act(LFN, LFN, AF.Ln, bias=self.ONE1)
        self.stage("mg_a")
        pb = self.bank()
        for d in range(2):
            k.mm(pb[:, d * 32:(d + 1) * 32], self.MASK[:, 1 if d == 0 else 5, :], LFN[:, :, d * 4:(d + 1) * 4])
        for d in range(2):
            k.copy(NB[:, :, d * 4:(d + 1) * 4], pb[:, d * 32:(d + 1) * 32].rr("p (i h) -> p i h", h=4))
        k.tt(AG, gi, NB, ALU.add)
        self.stage("mg_b")
        pbt = self.bank()
        k.tr(pbt[0:64, 0:128], AG.rr("p i k -> p (i k)"), self.IDF)
        self.stage("mg_c")
        MXT = k.sb("gMXT" + tag, [64, 2])
        k.reduce(MXT, pbt[0:64, 0:128].rr("p (f t) -> p f t", f=2), ALU.max)
        RH = k.sb("gRH" + tag, [64, 64, 2])
        k.tt(RH, self.IDF[0:64, 0:64].us(2).bc([64, 64, 2]), MXT.us(1).bc([64, 64, 2]), ALU.mult)
        self.stage("mg_d")
        pbm = self.bank()
        k.mm(pbm[:, 0:128], self.ONES[0:64, :], RH.rr("p a b -> p (a b)"))
        MXF = k.sb("gMXF" + tag, [128, NT, 8, 2])
        k.copy(MXF.rr("p i k f -> p (i k f)"), pbm[:, 0:128])
        self.stage("mg_e")
        R2 = k.sb("gR2" + tag, [128, 64, 2])
        k.tt(R2, LFN.rr("p i k -> p (i k)").us(2).bc([128, 64, 2]), self.HM.us(1).bc([128, 64, 2]), ALU.mult)
        pbl = self.bank()
        k.mm(pbl[:, 0:128], self.ONES, R2.rr("p a b -> p (a b)"))
        NBL = k.sb("gNBL" + tag, [128, NT, 8, 2])
        k.copy(NBL.rr("p i k f -> p (i k f)"), pbl[:, 0:128])
        self.stage("mg_f")
        M0 = k.sb("gM0" + tag, [128, nseq, 8])
        MBAR = k.sb("gMBAR" + tag, [128, NT, 2, 8])
        if g == 0:
            k.memset(M0, 0.0)
        else:
            k.dma(M0[:, 0, :], V(self.d_mm0[l], None))
        ipseq = NT // nseq
        mxv = MXF.rr("p (s i) k f -> p s i k f", s=nseq)
        nbv = NBL.rr("p (s i) k f -> p s i k f", s=nseq)
        mbv = MBAR.rr("p (s i) f k -> p s i f k", s=nseq)
        scv = self.SC.rr("p (s i f) k -> p s i f k", s=nseq, f=2)
        for cs in range(cps):
            for d in range(2):
                cc = cs if d == 0 else cps - 1 - cs
                ii, half = cc // 2, cc % 2
                ds = slice(d * 4, d * 4 + 4)
                m0 = M0[:, :, ds]
                mb = mbv[:, :, ii, half, ds]
                k.tt(mb, m0, mxv[:, :, ii, ds, half], ALU.max)
                k.tt(scv[:, :, ii, half, ds], m0, mb, ALU.subtract)
                k.tt(m0, mb, nbv[:, :, ii, ds, half], ALU.subtract)
        self.stage("mg_g")
        k.act(self.SC, self.SC, AF.Exp)
        self.stage("mg_h")
        if g == 0:
            k.dma(V(self.d_nmm[l:l + 1, :], None), M0[0:1].rr("p s k -> p (s k)"))
        self.stage("mg_i")
        MT = k.sb("gMT" + tag, [128, NT, 8])
        k.ts(MT, MBAR[:, :, 0, :], self.HM[:, 0:1], ALU.mult)
        k.stt(MT, MBAR[:, :, 1, :], self.HM[:, 1:2], MT, ALU.mult, ALU.add)
        k.tt(self.OMG, AG, MT, ALU.subtract)
        k.act(self.OMG, self.OMG, AF.Exp)
        k.tt(self.CLAMP, NB, MT, ALU.subtract)
        k.act(self.CLAMP, self.CLAMP, AF.Exp)
        self.stage("mg_j")
        self.dump("mg_AG%d%d" % (l, g), AG, [128, NT, 8])
        self.stage("mg_k")
        self.dump("mg_MBAR%d%d" % (l, g), MBAR.rr("p i f k -> p (i f k)"), [128, NT * 16])
        self.dump("mg_SC%d%d" % (l, g), self.SC, [128, NCH, 8])
        self.dump("mg_OMG%d%d" % (l, g), self.OMG, [128, NT, 8])
        k.close_scope()

    def unit_mlstm(self, l, g, h):
        k = self.k
        nseq, tseq = self.nseq, self.tseq
        cps = tseq // 64
        tag = "%d%d%d" % (l, g, h)
        self.load_w(self.d_win[l], U_ML[h], 640)
        k.open_scope()
        GT = k.sb("mGT" + tag, [128, TG])
        VT = k.sb("mVT" + tag, [128, NT, 128])
        OT = k.sb("mOT" + tag, [128, NT, 128])
        QB = k.sb("mQB" + tag, [128, TG], BF16)
        KB = k.sb("mKB" + tag, [128, TG], BF16)
        k.open_scope()
        PRE = k.sb("mPRE" + tag, [128, 2, nseq, tseq + 2])
        self.stage("ml_a")
        k.memset(PRE[:, :, :, 0:1], 0.0)
        k.memset(PRE[:, :, :, tseq + 1:tseq + 2], 0.0)
        self.stage("ml_b")
        for j in range(2):
            self.proj_fm(j * 128, self.pad_evac(PRE, j))
        self.stage("ml_c")
        self.proj_fm(256, lambda nb, ps: k.act(GT[:, nb * 512:(nb + 1) * 512], ps, AF.Silu))
        self.stage("ml_d")

        def ev_vo(i, ps):
            k.copy(VT[:, i, :], ps[:, 0:128])
            k.act(OT[:, i, :], ps[:, 128:256], AF.Sigmoid)
        self.proj_tm(384, 256, ev_vo)
        self.stage("ml_proj")
        X = k.sb("mX" + tag, [128, TG])
        for j, dst in enumerate((QB, KB)):
            self.conv(X, PRE, j, self.pf("cb_w", h, j, 0), self.pf("cb_w", h, j, 1), self.pf("cb_w", h, j, 2), self.pf("cb_b", h, j))
            if j == 0:
                k.act(dst, X, AF.Silu)
            else:
                k.act(X, X, AF.Silu)
                k.ts(dst, X, 128.0 ** -0.5, ALU.mult)
        k.close_scope()
        self.stage("ml_conv")
        KTM = k.sb("mKTM" + tag, [128, NT, 128], BF16)
        pbb = self.bank().bitcast(BF16)
        for i in range(NT):
            k.tr(pbb[:, i * 128:(i + 1) * 128], KB[:, i * 128:(i + 1) * 128], self.IDB)
        k.copy(KTM.rr("p i c -> p (i c)"), pbb[:, 0:1024])
        self.stage("ml_ktm")
        MTd = [k.sb("mMT%d" % d + tag, [128, NT, 128], BF16) for d in range(2)]
        for q in range(2):
            pb = self.bank()
            for ii in range(4):
                i = q * 4 + ii
                ts_ = slice(i * 128, (i + 1) * 128)
                k.mm(pb[:, ii * 128:(ii + 1) * 128], KB[:, ts_], QB[:, ts_])
            pv = pb[:, 0:512].rr("p (a b) -> p a b", a=4)
            k.tt(MTd[0][:, q * 4:q * 4 + 4, :], pv, self.MASK[:, 1:2, :].bc([128, 4, 128]), ALU.mult)
            k.tt(MTd[1][:, q * 4:q * 4 + 4, :], pv, self.MASK[:, 5:6, :].bc([128, 4, 128]), ALU.mult)
        self.stage("ml_mt")
        WV = [k.sb("mWV%d" % d + tag, [128, NT, 130], BF16) for d in range(2)]
        HI = [k.sb("mHI%d" % d + tag, [128, NT, 130]) for d in range(2)]
        for d in range(2):
            om = self.OMG[:, :, d * 4 + h:d * 4 + h + 1]
            k.tt(WV[d][:, :, 0:128], VT, om.bc([128, NT, 128]), ALU.mult)
            k.copy(WV[d][:, :, 128:129], om)
            k.memset(WV[d][:, :, 129:130], 0.0)
            for i in range(NT):
                pb = self.bank()
                k.mm(pb[:, 0:130], MTd[d][:, i, :], WV[d][:, i, :])
                k.copy(HI[d][:, i, :], pb[:, 0:130], eng=("act" if i % 2 else "dve"))
        self.stage("ml_hi")
        HS = k.sb("mHS" + tag, [128, NT, 128])
        TOTS = [k.sb("mTOTS%d" % i + tag, [128, NT, 130]) for i in range(2)]
        Z = {}
        Zb = {}
        for s in range(nseq):
            for d in range(2):
                Z[(s, d)] = k.sb("mZ%d%d" % (s, d) + tag, [128, 130])
                Zb[(s, d)] = k.sb("mZb%d%d" % (s, d) + tag, [128, 130], BF16)
                if g == 0:
                    k.memset(Z[(s, d)], 0.0)
                else:
                    k.dma(Z[(s, d)], V(self.d_ml0[l, d, h], None))
        for cs in range(cps):
            for s in range(nseq):
                for d in range(2):
                    c = s * cps + (cs if d == 0 else cps - 1 - cs)
                    i, half = c // 2, c % 2
                    tr_ = slice(half * 64, half * 64 + 64)
                    z, zb = Z[(s, d)], Zb[(s, d)]
                    dh = d * 4 + h
                    k.ts(z, z, self.SC[:, c, dh:dh + 1], ALU.mult)
                    k.copy(zb, z, eng="act")
                    pbZ = self.bank()
                    k.mm(pbZ[:, 0:130], KTM[tr_, i, :], WV[d][tr_, i, :])
                    pbS = self.bank()
                    k.mm(pbS[tr_, 0:130], QB[:, c * 64:(c + 1) * 64], zb)
                    k.tt(z, z, pbZ[:, 0:130], ALU.add)
                    k.tt(TOTS[d][tr_, i, :], pbS[tr_, 0:130], HI[d][tr_, i, :], ALU.add)
                    self.stage("ml_c%d_%d" % (c, d))
        DNb = k.sb("mDNb" + tag, [128, 2, NT])
        for d in range(2):
            k.act(DNb[:, d, :], TOTS[d][:, :, 128], AF.Abs)
            k.tt(DNb[:, d, :], DNb[:, d, :], self.CLAMP[:, :, d * 4 + h], ALU.max)
        k.recip(DNb, DNb)
        for d in range(2):
            k.tt(TOTS[d][:, :, 0:128], TOTS[d][:, :, 0:128], DNb[:, d, :].us(2).bc([128, NT, 128]), ALU.mult, eng=("pool" if d else "dve"))
        k.tt(HS, TOTS[0][:, :, 0:128], TOTS[1][:, :, 0:128], ALU.add)
        self.stage("ml_chain")
        if g == 0:
            for s in range(nseq):
                for d in range(2):
                    k.dma(V(self.d_nmc[s, l, d, h], None), Z[(s, d)])
        self.stage("ml_nmc")
        if h == 0:
            self.dump("ml_HS%d%d" % (l, g), HS, [128, NT, 128])
            self.dump("ml_HI%d%d" % (l, g), HI[0], [128, NT, 130])
        self.stage("ml_dump")
        k.tt(HS, HS, OT, ALU.mult)
        HN = k.sb("mHN" + tag, [128, NT, 128])
        ST = k.sb("mST" + tag, [128, 3, NT])
        k.reduce(ST[:, 0, :], HS, ALU.add)
        k.act(HN, HS, AF.Square)
        k.reduce(ST[:, 1, :], HN, ALU.add)
        k.ts(ST[:, 0, :], ST[:, 0, :], 1.0 / 128, ALU.mult)
        k.tt(ST[:, 2, :], ST[:, 0, :], ST[:, 0, :], ALU.mult)
        k.stt(ST[:, 1, :], ST[:, 1, :], 1.0 / 128, ST[:, 2, :], ALU.mult, ALU.subtract)
        k.act(ST[:, 1, :], ST[:, 1, :], AF.Sqrt, bias=self.EPSB)
        k.recip(ST[:, 1, :], ST[:, 1, :])
        k.tt(HN, HS, ST[:, 0, :].us(2).bc([128, NT, 128]), ALU.subtract)
        k.tt(HN, HN, ST[:, 1, :].us(2).bc([128, NT, 128]), ALU.mult)
        OUTF = k.sb("mOUT" + tag, [128, TG])
        for q in range(2):
            pb = self.bank()
            for ii in range(4):
                k.tr(pb[:, ii * 128:(ii + 1) * 128], HN[:, q * 4 + ii, :], self.IDF)
            k.act(OUTF[:, q * 512:(q + 1) * 512], pb[:, 0:512], AF.Identity, scale=self.pf("ml_gn_g", h), bias=self.pf("ml_gn_b", h))
        k.tt(self.mixT[:, 4 + h, :], OUTF, GT, ALU.mult)
        if h == 0:
            self.dump("ml_out%d%d" % (l, g), self.mixT[:, 4, :], [128, TG])
        k.close_scope()

    def attn_prep(self, l, g, h):
        k = self.k
        nseq = self.nseq
        tag = "%d%d%d" % (l, g, h)
        npast = 0 if g == 0 else PAST // 128
        nkt = npast + NT
        c = {"h": h, "nkt": nkt, "npast": npast}
        c["GT"] = GT = k.sb("aGT" + tag, [128, TG])
        c["VBk"] = VBk = k.sb("aVB" + tag, [128, nkt, 128], BF16)
        c["QT"] = QT = k.sb("aQT" + tag, [128, TG], BF16)
        c["KTa"] = KTa = k.sb("aKT" + tag, [128, nkt * 128], BF16)
        self.load_w(self.d_win[l], U_AT[h], 512)
        k.open_scope()
        QKV = k.sb("aQKV" + tag, [128, NT, 384])
        self.proj_tm(0, 384, lambda i, ps: k.copy(QKV[:, i, :], ps, eng=("act" if i % 2 else "dve")))
        self.proj_fm(384, lambda nb, ps: k.act(GT[:, nb * 512:(nb + 1) * 512], ps, AF.Silu))
        if g == 0:
            for s in range(nseq):
                k.dma(V(self.d_nk[s, l, h].rearrange("(i p) c -> p i c", p=128), None), QKV[:, 2 * s:2 * s + 2, 128:256])
                k.dma(V(self.d_nv[s, l, h].rearrange("(i p) c -> p i c", p=128), None), QKV[:, 2 * s:2 * s + 2, 256:384])
        else:
            T = [k.sb("aT%d" % i + tag, [128, NT, 4, 16]) for i in range(4)]
            for off in (0, 128):
                xv = QKV[:, :, off:off + 128].rr("p i (a x t) -> p i a x t", a=4, x=2)
                x1, x2 = xv[:, :, :, 0, :], xv[:, :, :, 1, :]
                k.tt(T[0], x1, self.COS, ALU.mult)
                k.tt(T[1], x2, self.SIN, ALU.mult)
                k.tt(T[2], x2, self.COS, ALU.mult)
                k.tt(T[3], x1, self.SIN, ALU.mult)
                k.tt(x1, T[0], T[1], ALU.subtract)
                k.tt(x2, T[2], T[3], ALU.add)
        QKB = k.sb("aQKB" + tag, [128, NT, 256], BF16)
        k.copy(QKB, QKV[:, :, 0:256])
        k.copy(VBk[:, npast:nkt, :], QKV[:, :, 256:384], eng="pool")
        for which, dst, c0 in ((0, QT, 0), (1, KTa, npast * 128)):
            pbb = self.bank().bitcast(BF16)
            for i in range(NT):
                k.tr(pbb[:, i * 128:(i + 1) * 128], QKB[:, i, which * 128:(which + 1) * 128], self.IDB)
            k.copy(dst[:, c0:c0 + TG], pbb[:, 0:1024], eng=("act" if which else "dve"))
        if g == 1:
            CKB = k.sb("aCKB" + tag, [128, npast, 128], BF16)
            k.dma(CKB, V(self.d_ck[l, h].rearrange("(i p) c -> p i c", p=128), None), q="pool")
            k.dma(VBk[:, 0:npast, :], V(self.d_cv[l, h].rearrange("(i p) c -> p i c", p=128), None), q="pool")
            pbb = self.bank().bitcast(BF16)
            for i in range(npast):
                k.tr(pbb[:, i * 128:(i + 1) * 128], CKB[:, i, :], self.IDB)
            k.copy(KTa[:, 0:npast * 128], pbb[:, 0:npast * 128])
        k.close_scope()
        return c

    def unit_attn2(self, l, g, h0):
        k = self.k
        nseq = self.nseq
        scale = 64.0 ** -0.5
        k.open_scope()
        ctxs = [self.attn_prep(l, g, h0 + j) for j in range(2)]
        for j, c in enumerate(ctxs):
            tag = "%d%d%d" % (l, g, c["h"])
            nkt = c["nkt"]
            c["E"] = [k.sb("aE%d" % b + tag, [128, nkt * 128], BF16) for b in range(2)]
            c["ET"] = [k.sb("aET%d" % b + tag, [128, nkt, 128], BF16) for b in range(2)]
            c["SMq"] = [[k.sb("aSM%d%d" % (a, b) + tag, [128, 4]) for b in range(2)] for a in range(2)]
            c["MXp"] = [k.sb("aMX%d" % b + tag, [128, 4]) for b in range(2)]
            c["NBp"] = [k.sb("aNB%d" % b + tag, [128, 1]) for b in range(2)]
            c["RS"] = k.sb("aRS" + tag, [128, 4])
            c["O2"] = k.sb("aO2" + tag, [128, 128])
            c["OD"] = k.sb("aOD" + tag, [128, 128])
            c["JK"] = k.sb("aJK" + tag, [128, 128])
            c["OT"] = k.sb("aOT" + tag, [128, 128])
            c["abank"] = j * 3
            c["ocol"] = j * 256
        pbO = self.PB[7]
        pbT = self.PB[6]
        items = [(i, br) for i in range(NT) for br in range(2)]

        def keys_of(c, i):
            if g == 0:
                sq = i // (NT // nseq)
                return [2 * sq, 2 * sq + 1]
            return list(range(c["nkt"]))

        def stage_a(c, kidx):
            i, br = items[kidx]
            par = kidx % 2
            kts = keys_of(c, i)
            k0 = kts[0] * 128
            ncols = len(kts) * 128
            chunks = [(c0, min(512, ncols - c0)) for c0 in range(0, ncols, 512)]
            brs = slice(br * 64, br * 64 + 64)
            banks = [self.PB[c["abank"] + ci] for ci in range(len(chunks))]
            for ci, (c0, cn) in enumerate(chunks):
                k.mm(banks[ci][:, 0:cn], c["QT"][brs, i * 128:(i + 1) * 128], c["KTa"][brs, k0 + c0:k0 + c0 + cn])
            for ci, (c0, cn) in enumerate(chunks):
                k.reduce(c["MXp"][par][:, ci:ci + 1], banks[ci][:, 0:cn], ALU.max)
            k.reduce(c["NBp"][par], c["MXp"][par][:, 0:len(chunks)], ALU.max)
            k.ts(c["NBp"][par], c["NBp"][par], -scale, ALU.mult)
            for ci, (c0, cn) in enumerate(chunks):
                k.act(c["E"][par][:, c0:c0 + cn], banks[ci][:, 0:cn], AF.Exp, scale=scale, bias=c["NBp"][par],
                      accum=c["SMq"][i % 2][br][:, ci:ci + 1])

        def stage_b(c, kidx):
            i, br = items[kidx]
            par = kidx % 2
            kts = keys_of(c, i)
            nk = len(kts)
            oc = c["ocol"] + br * 128
            for q0 in range(0, nk, 8):
                qn = min(8, nk - q0)
                pbb = pbT.bitcast(BF16)
                for jj in range(qn):
                    k.tr(pbb[:, jj * 128:(jj + 1) * 128], c["E"][par][:, (q0 + jj) * 128:(q0 + jj + 1) * 128], self.IDB)
                k.copy(c["ET"][par][:, q0:q0 + qn, :].rr("p a b -> p (a b)"), pbb[:, 0:qn * 128], eng=("act" if qn == 8 else "dve"))
            for jj in range(nk):
                k.mm(pbO[:, oc:oc + 128], c["ET"][par][:, jj, :], c["VBk"][:, kts[jj], :], start=(jj == 0), stop=(jj == nk - 1))

        def tail(c, i):
            qp = i % 2
            RS, O2, OD, JK, OTt = c["RS"], c["O2"], c["OD"], c["JK"], c["OT"]
            oc = c["ocol"]
            nch = (len(keys_of(c, i)) * 128 + 511) // 512
            for br in range(2):
                k.reduce(RS[:, br:br + 1], c["SMq"][qp][br][:, 0:nch], ALU.add)
            k.recip(RS[:, 0:2], RS[:, 0:2])
            k.tt(RS[:, 1:2], RS[:, 1:2], self.LAM[:, 0:1], ALU.mult)
            k.act(O2, pbO[:, oc + 128:oc + 256], AF.Identity, scale=RS[:, 1:2])
            k.stt(OD, pbO[:, oc:oc + 128], RS[:, 0:1], O2, ALU.mult, ALU.subtract)
            k.act(JK, OD, AF.Square, accum=RS[:, 2:3])
            k.act(RS[:, 2:3], RS[:, 2:3], AF.Sqrt, scale=1.0 / 128, bias=self.EPSB)
            k.recip(RS[:, 2:3], RS[:, 2:3])
            k.ts(OD, OD, RS[:, 2:3], ALU.mult)
            k.tr(pbT[:, 0:128], OD, self.IDF)
            k.act(OTt, pbT[:, 0:128], AF.Identity, scale=self.SUBG)
            k.tt(self.mixT[:, 8 + c["h"], i * 128:(i + 1) * 128], OTt, c["GT"][:, i * 128:(i + 1) * 128], ALU.mult, eng="pool")

        for c in ctxs:
            stage_a(c, 0)
        for kidx in range(len(items)):
            if kidx + 1 < len(items):
                for c in ctxs:
                    stage_a(c, kidx + 1)
            for c in ctxs:
                stage_b(c, kidx)
                if items[kidx][1] == 1:
                    tail(c, items[kidx][0])
        if h0 == 0:
            self.dump("at_out%d%d" % (l, g), self.mixT[:, 8, :], [128, TG])
        k.close_scope()

    def unit_attn3(self, l, g, h):
        k = self.k
        nseq = self.nseq
        scale = 64.0 ** -0.5
        tag = "%d%d%d" % (l, g, h)
        npast = 0 if g == 0 else PAST // 128
        nkt = npast + NT
        nkl = 2 if g == 0 else nkt
        k.open_scope()
        GT = k.sb("aGT" + tag, [128, TG])
        VP = k.sb("aVP" + tag, [128, nkt, 130], BF16)
        QT = k.sb("aQT" + tag, [128, TG], BF16)
        KTa = k.sb("aKT" + tag, [128, nkt * 128], BF16)
        NB = k.sb("aNB" + tag, [128, 2])
        self.load_w(self.d_win[l], U_AT[h], 512)
        k.open_scope()
        QKV = k.sb("aQKV" + tag, [128, NT, 384])
        self.proj_tm(0, 384, lambda i, ps: k.copy(QKV[:, i, :], ps, eng=("act" if i % 2 else "dve")))
        self.proj_fm(384, lambda nb, ps: k.act(GT[:, nb * 512:(nb + 1) * 512], ps, AF.Silu))
        if g == 0:
            for s in range(nseq):
                k.dma(V(self.d_nk[s, l, h].rearrange("(i p) c -> p i c", p=128), None), QKV[:, 2 * s:2 * s + 2, 128:256])
                k.dma(V(self.d_nv[s, l, h].rearrange("(i p) c -> p i c", p=128), None), QKV[:, 2 * s:2 * s + 2, 256:384])
        else:
            T = [k.sb("aT%d" % i + tag, [128, NT, 4, 16]) for i in range(4)]
            for off in (0, 128):
                xv = QKV[:, :, off:off + 128].rr("p i (a x t) -> p i a x t", a=4, x=2)
                x1, x2 = xv[:, :, :, 0, :], xv[:, :, :, 1, :]
                k.tt(T[0], x1, self.COS, ALU.mult)
                k.tt(T[1], x2, self.SIN, ALU.mult)
                k.tt(T[2], x2, self.COS, ALU.mult)
                k.tt(T[3], x1, self.SIN, ALU.mult)
                k.tt(x1, T[0], T[1], ALU.subtract)
                k.tt(x2, T[2], T[3], ALU.add)
        QKB = k.sb("aQKB" + tag, [128, NT, 256], BF16)
        k.copy(QKB, QKV[:, :, 0:256])
        k.copy(VP[:, npast:nkt, 0:128], QKV[:, :, 256:384], eng="pool")
        k.memset(VP[:, :, 128:129], 1.0)
        k.memset(VP[:, :, 129:130], 0.0)
        for which, dst, c0 in ((0, QT, 0), (1, KTa, npast * 128)):
            pbb = self.bank().bitcast(BF16)
            for i in range(NT):
                k.tr(pbb[:, i * 128:(i + 1) * 128], QKB[:, i, which * 128:(which + 1) * 128], self.IDB)
            k.copy(dst[:, c0:c0 + TG], pbb[:, 0:1024], eng=("act" if which else "dve"))
        SQ = k.sb("aSQ" + tag, [128, NT, 256])
        N2 = k.sb("aN2" + tag, [128, NT, 4])
        M4 = k.sb("aM4" + tag, [128, 4])
        k.act(SQ, QKV[:, :, 0:256], AF.Square)
        k.reduce(N2, SQ.rr("p i (a d) -> p i a d", a=4), ALU.add)
        k.reduce(M4, N2.rr("p i a -> p a i"), ALU.max)
        if g == 1:
            CKB = k.sb("aCKB" + tag, [128, npast, 128], BF16)
            k.dma(CKB, V(self.d_ck[l, h].rearrange("(i p) c -> p i c", p=128), None), q="pool")
            k.dma(VP[:, 0:npast, 0:128], V(self.d_cv[l, h].rearrange("(i p) c -> p i c", p=128), None), q="pool")
            pbb = self.bank().bitcast(BF16)
            for i in range(npast):
                k.tr(pbb[:, i * 128:(i + 1) * 128], CKB[:, i, :], self.IDB)
            k.copy(KTa[:, 0:npast * 128], pbb[:, 0:npast * 128])
            CSQ = k.sb("aCSQ" + tag, [128, npast, 128])
            CN2 = k.sb("aCN2" + tag, [128, npast, 2])
            CM = k.sb("aCM" + tag, [128, 2])
            k.act(CSQ, CKB, AF.Square)
            k.reduce(CN2, CSQ.rr("p i (a d) -> p i a d", a=2), ALU.add)
            k.reduce(CM, CN2.rr("p i a -> p a i"), ALU.max)
            k.tt(M4[:, 2:4], M4[:, 2:4], CM, ALU.max)
        pbm = self.bank()
        k.tr(pbm[0:4, 0:128], M4, self.IDF)
        MC = k.sb("aMC" + tag, [4, 1])
        k.reduce(MC, pbm[0:4, 0:128], ALU.max)
        RH = k.sb("aRH" + tag, [4, 4])
        k.ts(RH, self.IDF[0:4, 0:4], MC[0:4, 0:1], ALU.mult)
        pbr = self.bank()
        k.mm(pbr[:, 0:4], self.ONES[0:4, :], RH)
        MR = k.sb("aMR" + tag, [128, 4])
        k.copy(MR, pbr[:, 0:4])
        k.tt(NB, MR[:, 0:2], MR[:, 2:4], ALU.mult)
        k.act(NB, NB, AF.Sqrt)
        k.ts(NB, NB, -scale, ALU.mult)
        k.close_scope()
        ET = k.sb("aET" + tag, [128, nkl, TG], BF16)
        O1S = k.sb("aO1" + tag, [128, NT, 130])
        RS = k.sb("aRS" + tag, [128, 4])
        O2 = k.sb("aO2" + tag, [128, 128])
        OD = k.sb("aOD" + tag, [128, 128])
        JK = k.sb("aJK" + tag, [128, 128])
        OTt = k.sb("aOT" + tag, [128, 128])
        qblk = 512 if g == 1 else TSEQ_P
        for br in range(2):
            brs = slice(br * 64, br * 64 + 64)
            for qb in range(TG // qblk):
                qs = slice(qb * qblk, (qb + 1) * qblk)
                kt0 = 0 if g == 1 else 2 * qb
                for j in range(nkl):
                    pb = self.PB[self.abank % 6]
                    self.abank += 1
                    k.mm(pb[:, 0:qblk], KTa[brs, (kt0 + j) * 128:(kt0 + j + 1) * 128], QT[brs, qs])
                    k.act(ET[:, j, qs], pb[:, 0:qblk], AF.Exp, scale=scale, bias=NB[:, br:br + 1])
            for i in range(NT):
                kt0 = 0 if g == 1 else 2 * (i // 2)
                pbO = self.PB[6 + i % 2]
                for j in range(nkl):
                    k.mm(pbO[:, 0:130], ET[:, j, i * 128:(i + 1) * 128], VP[:, kt0 + j, :], start=(j == 0), stop=(j == nkl - 1))
                if br == 0:
                    k.copy(O1S[:, i, :], pbO[:, 0:130], eng=("act" if i % 2 else "dve"))
                else:
                    k.copy(RS[:, 0:1], O1S[:, i, 128:129])
                    k.copy(RS[:, 1:2], pbO[:, 128:129])
                    k.recip(RS[:, 0:2], RS[:, 0:2])
                    k.tt(RS[:, 1:2], RS[:, 1:2], self.LAM[:, 0:1], ALU.mult)
                    k.act(O2, pbO[:, 0:128], AF.Identity, scale=RS[:, 1:2])
                    k.stt(OD, O1S[:, i, 0:128], RS[:, 0:1], O2, ALU.mult, ALU.subtract)
                    k.act(JK, OD, AF.Square, accum=RS[:, 2:3])
                    k.act(RS[:, 2:3], RS[:, 2:3], AF.Sqrt, scale=1.0 / 128, bias=self.EPSB)
                    k.recip(RS[:, 2:3], RS[:, 2:3])
                    k.ts(OD, OD, RS[:, 2:3], ALU.mult)
                    pbt = self.PB[self.abank % 6]
                    self.abank += 1
                    k.tr(pbt[:, 0:128], OD, self.IDF)
                    k.act(OTt, pbt[:, 0:128], AF.Identity, scale=self.SUBG)
                    k.tt(self.mixT[:, 8 + h, i * 128:(i + 1) * 128], OTt, GT[:, i * 128:(i + 1) * 128], ALU.mult, eng="pool")
        if h == 0:
            self.dump("at_out%d%d" % (l, g), self.mixT[:, 8, :], [128, TG])
        k.close_scope()

    def phase_o(self, l, g):
        k = self.k
        k.open_scope()
        WO = k.sb("oWO", [128, KC, D], BF16)
        for s in range(4):
            src = self.d_wout[l].rearrange("(kc p) c -> p kc c", p=128)[:, :, s * 512:(s + 1) * 512]
            k.dma(WO[:, :, s * 512:(s + 1) * 512], V(src, None), q="pool")
        GBC = k.sb("oGBC", [128, D])
        LG = k.sb("oLG", [128, D])
        LB = k.sb("oLB", [128, D])
        k.dma(GBC, V(self.d_modg[l, g:g + 1, :].partition_broadcast(128).rearrange("p a d -> p (a d)"), self.modgbuf[l][g]))
        k.dma(LG, V(self.d_lng[l:l + 1, :].partition_broadcast(128).rearrange("p a d -> p (a d)"), None))
        k.dma(LB, V(self.d_lnb[l:l + 1, :].partition_broadcast(128).rearrange("p a d -> p (a d)"), None))
        XT = k.sb("oXT", [128, D])
        VV = k.sb("oVV", [128, D])
        JK = k.sb("oJK", [128, D], BF16)
        ST = k.sb("oST", [128, 4])
        for i in range(NT):
            k.dma(XT, self.xsrc(l, g, i))
            for s in range(4):
                pb = self.bank()
                for kc in range(KC):
                    k.mm(pb[:, 0:512], self.mixT[:, kc, i * 128:(i + 1) * 128], WO[:, kc, s * 512:(s + 1) * 512], start=(kc == 0), stop=(kc == KC - 1))
                k.tt(VV[:, s * 512:(s + 1) * 512], pb[:, 0:512], GBC[:, s * 512:(s + 1) * 512], ALU.mult)
            k.stt(VV, XT, ALPHA, VV, ALU.mult, ALU.add)
            k.act(JK, VV, AF.Identity, accum=ST[:, 0:1])
            k.act(JK, VV, AF.Square, accum=ST[:, 1:2])
            k.ts(ST[:, 0:1], ST[:, 0:1], 1.0 / D, ALU.mult)
            k.tt(ST[:, 2:3], ST[:, 0:1], ST[:, 0:1], ALU.mult)
            k.stt(ST[:, 1:2], ST[:, 1:2], 1.0 / D, ST[:, 2:3], ALU.mult, ALU.subtract)
            k.act(ST[:, 1:2], ST[:, 1:2], AF.Sqrt, bias=self.EPSB)
            k.recip(ST[:, 1:2], ST[:, 1:2])
            k.ts(VV, VV, ST[:, 0:1], ALU.subtract, ST[:, 1:2], ALU.mult)
            k.tt(VV, VV, LG, ALU.mult)
            k.tt(VV, VV, LB, ALU.add)
            if l == DEPTH - 1:
                dst = V(self.d_yout[g, i * 128:(i + 1) * 128, :], None)
            else:
                dst = V(self.d_x1[g, i * 128:(i + 1) * 128, :], self.x1bufs[g][i])
            k.dma(dst, VV)
        k.close_scope()


def _shared_inputs(inp):
    perm = _perm_cols()
    sh = {}
    sh["w_ada"] = np.ascontiguousarray(inp["w_ada"], dtype=np.float32)
    sh["b_ada"] = np.ascontiguousarray(inp["b_ada"], dtype=np.float32)
    sh["w_in"] = np.ascontiguousarray(inp["w_in"][:, :, perm], dtype=np.float32)
    sh["w_out"] = np.ascontiguousarray(inp["w_out"], dtype=np.float32)
    sh["pfm"] = np.stack([_pack_pfm(inp, l) for l in range(DEPTH)])
    sh["prow"] = np.stack([_pack_prow(inp, l) for l in range(DEPTH)])
    sh["ln_g"] = np.ascontiguousarray(inp["ln_g"], dtype=np.float32)
    sh["ln_b"] = np.ascontiguousarray(inp["ln_b"], dtype=np.float32)
    wup = np.zeros((DEPTH, 2, 128, 512), np.float32)
    wup[:, 0] = inp["rwkv_w_up"].reshape(DEPTH, 128, 512)
    wup[:, 1] = inp["rwkv_a_up"].reshape(DEPTH, 128, 512)
    sh["wup"] = wup
    sh["consts"] = _consts()
    return sh


def _core_inputs(inp, c, sh):
    sb = c % 2
    m = dict(sh)
    xin = np.empty((2, TG, D), np.float32)
    xin[0] = inp["x_prompt"][4 * c:4 * c + 4].reshape(TG, D)
    xin[1] = inp["x_sample"][sb]
    m["xin"] = xin
    cT = np.empty((128, KC, 2), np.float32)
    cT[:, :, 0] = inp["c_ctx"].reshape(KC, 128).T
    cT[:, :, 1] = inp["c"][sb].reshape(KC, 128).T
    m["cT"] = cT.reshape(128, 32)
    rw = inp["state_rwkv"][sb]
    rw0 = np.zeros((DEPTH, 2, 4, 128, 128), np.float32)
    for p in range(4):
        for hh in range(2):
            rw0[:, :, p, hh * 64:(hh + 1) * 64, hh * 64:(hh + 1) * 64] = np.swapaxes(rw[:, :, 2 * p + hh], -1, -2)
    m["rw0"] = rw0
    ml0 = np.zeros((DEPTH, 2, 4, 128, 130), np.float32)
    ml0[..., 0:128] = np.swapaxes(inp["state_mlstm_c"][sb], -1, -2)
    ml0[..., 128] = inp["state_mlstm_n"][sb]
    m["ml0"] = ml0
    m["mm0"] = np.ascontiguousarray(np.broadcast_to(inp["state_mlstm_m"][sb].reshape(DEPTH, 1, 8), (DEPTH, 128, 8)), dtype=np.float32)
    m["ck"] = np.ascontiguousarray(inp["cache_attn_k"][sb], dtype=np.float32)
    m["cv"] = np.ascontiguousarray(inp["cache_attn_v"][sb], dtype=np.float32)
    return m


def _assemble(results):
    B = 8 * NSEQ_P
    y_prompt = np.empty((B, TSEQ_P, D), np.float32)
    y_sample = np.empty((2, TG, D), np.float32)
    new_k = np.empty((B, DEPTH, 8, TSEQ_P, 128), np.float32)
    new_v = np.empty((B, DEPTH, 8, TSEQ_P, 128), np.float32)
    new_rw = np.empty((B, DEPTH, 2, 8, 64, 64), np.float32)
    new_c = np.empty((B, DEPTH, 2, 4, 128, 128), np.float32)
    new_n = np.empty((B, DEPTH, 2, 4, 128), np.float32)
    new_m = np.empty((B, DEPTH, 2, 4), np.float32)
    for c, r in enumerate(results):
        bs = slice(4 * c, 4 * c + 4)
        y_prompt[bs] = r["yout"][0].reshape(4, TSEQ_P, D)
        if c < 2:
            y_sample[c] = r["yout"][1]
        new_k[bs] = r["nk"]
        new_v[bs] = r["nv"]
        nrw = r["nrw"]
        for p in range(4):
            for hh in range(2):
                blk = nrw[:, :, :, p, hh * 64:(hh + 1) * 64, hh * 64:(hh + 1) * 64]
                new_rw[bs, :, :, 2 * p + hh] = np.swapaxes(blk, -1, -2)
        nmc = r["nmc"]
        new_c[bs] = np.swapaxes(nmc[..., 0:128], -1, -2)
        new_n[bs] = nmc[..., 128]
        new_m[bs] = np.transpose(r["nmm"].reshape(DEPTH, 4, 2, 4), (1, 0, 2, 3))
    return (y_prompt, y_sample, new_k, new_v, new_rw, new_c, new_n, new_m)


def kernel(**inputs):
    inp = {k: np.asarray(v) for k, v in inputs.items()}
    prog = Prog()
    nc = prog.build()
    sh = _shared_inputs(inp)
    in_maps = [_core_inputs(inp, c, sh) for c in range(8)]
    res = run_bass_kernel_spmd(nc, in_maps, core_ids=list(range(8)))
    return _assemble(res.results)
```

```python
import math
from contextlib import ExitStack

import numpy as np
import concourse.bass as bass
import concourse.mybir as mybir
from concourse.bass_utils import run_bass_kernel_spmd

F32 = mybir.dt.float32
BF16 = mybir.dt.bfloat16
AF = mybir.ActivationFunctionType
ALU = mybir.AluOpType
AX = mybir.AxisListType

D = 2048
KC = 16
DEPTH = 2
NSEQ_P = 4
TSEQ_P = 256
TG = 1024
NT = 8
NCH = 16
PAST = 512
P_IN = 8976
ALPHA = (2 * DEPTH) ** 0.25
LN_EPS = 1e-5
GN_EPS_A = 64e-5
GN_EPS = 1e-5
RMS_EPS = 1e-5
WDECAY = math.exp(-0.5)

U_LORA = 0
U_RW = [256 + 512 * p for p in range(4)]
U_ML = [2304 + 640 * h for h in range(4)]
U_MG = 2304 + 2560
U_AT = [4880 + 512 * h for h in range(8)]


def _perm_cols():
    perm = []
    perm += list(range(1536, 1792))
    for p in range(4):
        perm += list(range(p * 128, p * 128 + 128))
        perm += list(range(512 + p * 128, 512 + p * 128 + 128))
        perm += list(range(1024 + p * 128, 1024 + p * 128 + 128))
        perm += list(range(1792 + p * 128, 1792 + p * 128 + 128))
    b0 = 2304
    for h in range(4):
        perm += list(range(b0 + h * 128, b0 + h * 128 + 128))
        perm += list(range(b0 + 512 + h * 128, b0 + 512 + h * 128 + 128))
        perm += list(range(4368 + h * 128, 4368 + h * 128 + 128))
        perm += list(range(3328 + h * 128, 3328 + h * 128 + 128))
        perm += list(range(3840 + h * 128, 3840 + h * 128 + 128))
    perm += list(range(4352, 4368))
    for h in range(8):
        perm += list(range(4880 + h * 128, 4880 + h * 128 + 128))
        perm += list(range(5904 + h * 128, 5904 + h * 128 + 128))
        perm += list(range(6928 + h * 128, 6928 + h * 128 + 128))
        perm += list(range(7952 + h * 128, 7952 + h * 128 + 128))
    assert len(perm) == P_IN and len(set(perm)) == P_IN
    return np.array(perm)


C_IDENT = 0
C_MASK = 128
C_ONESBD = C_MASK + 8 * 128
C_ONES = C_ONESBD + 128
C_HM = C_ONES + 128
C_COS = C_HM + 2
C_SIN = C_COS + 512
NCONST = C_SIN + 512


def _consts():
    c = np.zeros((128, NCONST), np.float32)
    c[:, C_IDENT:C_IDENT + 128] = np.eye(128)
    r = np.arange(128)[:, None]
    q = np.arange(128)[None, :]
    same = (r // 64) == (q // 64)
    us = (same & (r < q)).astype(np.float32)
    ui = (same & (r <= q)).astype(np.float32)
    ls = (same & (r > q)).astype(np.float32)
    li = (same & (r >= q)).astype(np.float32)
    for i, m in enumerate([-us, ui, us, ui, -ls, li, ls, li]):
        c[:, C_MASK + i * 128:C_MASK + (i + 1) * 128] = m
    c[:, C_ONESBD:C_ONESBD + 128] = same.astype(np.float32)
    c[:, C_ONES:C_ONES + 128] = 1.0
    c[:64, C_HM] = 1.0
    c[64:, C_HM + 1] = 1.0
    half = 32
    inv = 1.0 / (10000.0 ** (np.arange(0, half, 2, dtype=np.float32) / half))
    t = (np.arange(8)[None, :] * 128 + np.arange(128)[:, None]).astype(np.float32)
    row = np.floor(t / 64.0)
    col = t - row * 64.0
    cos = np.zeros((128, 8, 4, 16), np.float32)
    sin = np.zeros((128, 8, 4, 16), np.float32)
    for br in range(2):
        for rc, pos in enumerate([row, col]):
            ang = pos[:, :, None] * inv[None, None, :]
            cos[:, :, br * 2 + rc, :] = np.cos(ang)
            sin[:, :, br * 2 + rc, :] = np.sin(ang)
    c[:, C_COS:C_COS + 512] = cos.reshape(128, 512)
    c[:, C_SIN:C_SIN + 512] = sin.reshape(128, 512)
    return c


PF = {}
_n = 0
for _p in range(4):
    for _j in range(3):
        for _tap in range(3):
            PF[("ca_w", _p, _j, _tap)] = _n; _n += 1
        PF[("ca_b", _p, _j)] = _n; _n += 1
    for _d in range(2):
        PF[("w0", _p, _d)] = _n; _n += 1
        PF[("a0", _p, _d)] = _n; _n += 1
    for _nm in ("k_k", "k_a", "omk_a", "r_k", "gn_g", "gn_b"):
        PF[("rw_" + _nm, _p)] = _n; _n += 1
for _h in range(4):
    for _j in range(2):
        for _tap in range(3):
            PF[("cb_w", _h, _j, _tap)] = _n; _n += 1
        PF[("cb_b", _h, _j)] = _n; _n += 1
    PF[("ml_gn_g", _h)] = _n; _n += 1
    PF[("ml_gn_b", _h)] = _n; _n += 1
PF[("subln",)] = _n; _n += 1
NPF = _n

PR_BI = 0
PR_BF = 8
PR_LQ1 = 16
PR_LK1 = 80
PR_LQ2 = 144
PR_LK2 = 208
NPR = 272


def _pack_pfm(inp, l):
    o = np.zeros((128, NPF), np.float32)
    for p in range(4):
        sl = slice(p * 128, p * 128 + 128)
        for j in range(3):
            for tap in range(3):
                o[:, PF[("ca_w", p, j, tap)]] = inp["conv_a_w"][l, tap, j * 512 + p * 128: j * 512 + p * 128 + 128]
            o[:, PF[("ca_b", p, j)]] = inp["conv_a_b"][l, j * 512 + p * 128: j * 512 + p * 128 + 128]
        for d in range(2):
            o[:, PF[("w0", p, d)]] = inp["rwkv_w0"][l, d, sl]
            o[:, PF[("a0", p, d)]] = inp["rwkv_a0"][l, d, sl]
        o[:, PF[("rw_k_k", p)]] = inp["rwkv_k_k"][l, sl]
        o[:, PF[("rw_k_a", p)]] = inp["rwkv_k_a"][l, sl]
        o[:, PF[("rw_r_k", p)]] = inp["rwkv_r_k"][l].reshape(512)[sl]
        o[:, PF[("rw_gn_g", p)]] = inp["rwkv_gn_g"][l, sl]
        o[:, PF[("rw_gn_b", p)]] = inp["rwkv_gn_b"][l, sl]
    for h in range(4):
        sl = slice(h * 128, h * 128 + 128)
        for j in range(2):
            for tap in range(3):
                o[:, PF[("cb_w", h, j, tap)]] = inp["conv_b_w"][l, tap, j * 512 + h * 128: j * 512 + h * 128 + 128]
            o[:, PF[("cb_b", h, j)]] = inp["conv_b_b"][l, j * 512 + h * 128: j * 512 + h * 128 + 128]
        o[:, PF[("ml_gn_g", h)]] = inp["mlstm_gn_g"][l, sl]
        o[:, PF[("ml_gn_b", h)]] = inp["mlstm_gn_b"][l, sl]
    o[:, PF[("subln",)]] = inp["diff_subln_g"][l]
    return o


def _pack_prow(inp, l):
    o = np.zeros((128, NPR), np.float32)
    o[:, PR_BI:PR_BI + 8] = inp["mlstm_b_i"][l].reshape(8)[None, :]
    o[:, PR_BF:PR_BF + 8] = inp["mlstm_b_f"][l].reshape(8)[None, :]
    o[:, PR_LQ1:PR_LQ1 + 64] = inp["diff_lq1"][l][None, :]
    o[:, PR_LK1:PR_LK1 + 64] = inp["diff_lk1"][l][None, :]
    o[:, PR_LQ2:PR_LQ2 + 64] = inp["diff_lq2"][l][None, :]
    o[:, PR_LK2:PR_LK2 + 64] = inp["diff_lk2"][l][None, :]
    return o


ENGS = ("pe", "act", "dve", "pool", "sp")
SAME_ENGINE_SYNC = True
SELF_RAW_ONLY = True


class Buf:
    __slots__ = ("name", "w", "r", "dsem", "excl")

    def __init__(self, name, init=None):
        self.name = name
        self.excl = False
        self.w = dict(init) if init else {}
        self.r = {}
        self.dsem = None


class Sched:
    def __init__(self, nc, es, n_dma_sems=90):
        self.nc = nc
        self.eng = {"pe": nc.tensor, "act": nc.scalar, "dve": nc.vector, "pool": nc.gpsimd, "sp": nc.sync}
        self.cnt = {}
        self.waited = {e: {} for e in ENGS}
        self.sem = {}
        for e in ENGS:
            self.sem[e] = es.enter_context(nc.semaphore("s_" + e))
            self.cnt[e] = 0
        self.free_dsems = [es.enter_context(nc.semaphore("d%d" % i)) for i in range(n_dma_sems)]
        self.n_dsem = 0
        self.nops = {e: 0 for e in ENGS}
        self.nwaits = {e: 0 for e in ENGS}
        self.fence = {}
        self.all_dma_events = {}
        self.recycled = []

    def newbuf(self, name, fenced=True):
        return Buf(name, self.fence if fenced else None)

    def close_scope(self, bufs):
        for b in bufs:
            for dct in (b.w, b.r):
                for k, v in dct.items():
                    if self.fence.get(k, 0) < v:
                        self.fence[k] = v
            if b.dsem is not None:
                self.recycled.append(b.dsem)

    def _dsem_for(self, b):
        if b.dsem is None:
            if self.recycled:
                key = self.recycled.pop()
            else:
                key = "D%d" % self.n_dsem
                self.sem[key] = self.free_dsems[self.n_dsem]
                self.n_dsem += 1
                self.cnt[key] = 0
            b.dsem = key
        return b.dsem

    def _emit_wait(self, ename, k, v):
        self.eng[ename].wait_ge(self.sem[k], v)
        self.nwaits[ename] += 1

    def _waits(self, eng, reads, writes):
        deps = {}
        for b in reads:
            for k, v in b.w.items():
                if deps.get(k, 0) < v:
                    deps[k] = v
        for b in writes:
            for k, v in b.w.items():
                if k == eng and SELF_RAW_ONLY:
                    continue
                if deps.get(k, 0) < v:
                    deps[k] = v
            for k, v in b.r.items():
                if k == eng and SELF_RAW_ONLY:
                    continue
                if deps.get(k, 0) < v:
                    deps[k] = v
        wd = self.waited[eng]
        for k, v in deps.items():
            if k == eng and (eng == "pe" or not SAME_ENGINE_SYNC):
                continue
            if wd.get(k, 0) >= v:
                continue
            wd[k] = v
            self._emit_wait(eng, k, v)

    def op(self, eng, fn, reads=(), writes=()):
        writes = [b for b in writes if b is not None] + [b for b in reads if b is not None and b.excl]
        reads = [b for b in reads if b is not None and not b.excl]
        self._waits(eng, reads, writes)
        self.cnt[eng] += 1
        n = self.cnt[eng]
        inst = fn(self.eng[eng])
        inst.then_inc(self.sem[eng], 1)
        self.nops[eng] += 1
        for b in reads:
            if b.r.get(eng, 0) < n:
                b.r[eng] = n
        for b in writes:
            b.w = {eng: n}
            b.r = {}

    def dma(self, qeng, fn, reads=(), writes=(), sembuf=None):
        reads = [b for b in reads if b is not None]
        writes = [b for b in writes if b is not None]
        self._waits(qeng, reads, writes)
        if sembuf is None:
            sembuf = writes[0] if writes else reads[0]
        key = self._dsem_for(sembuf)
        self.cnt[key] += 16
        n = self.cnt[key]
        inst = fn(self.eng[qeng])
        inst.then_inc(self.sem[key], 16)
        self.nops[qeng] += 1
        self.all_dma_events[key] = n
        for b in reads:
            if b.r.get(key, 0) < n:
                b.r[key] = n
        for b in writes:
            b.w = {key: n}
            b.r = {}

    def finish(self):
        for k, v in self.all_dma_events.items():
            if self.waited["sp"].get(k, 0) < v:
                self.waited["sp"][k] = v
                self._emit_wait("sp", k, v)
        for e in ("pe", "act", "dve", "pool"):
            if self.cnt[e] > 0 and self.waited["sp"].get(e, 0) < self.cnt[e]:
                self._emit_wait("sp", e, self.cnt[e])


class V:
    __slots__ = ("ap", "b")

    def __init__(self, ap, b):
        self.ap = ap
        self.b = b

    def __getitem__(self, idx):
        return V(self.ap[idx], self.b)

    def rr(self, pat, **kw):
        return V(self.ap.rearrange(pat, **kw), self.b)

    def bc(self, shape):
        return V(self.ap.broadcast_to(shape), self.b)

    def us(self, axis):
        return V(self.ap.unsqueeze(axis), self.b)

    def bitcast(self, dt):
        return V(self.ap.bitcast(dt), self.b)


class K:
    def __init__(self, nc, es):
        self.nc = nc
        self.S = Sched(nc, es)
        self.scopes = []

    def open_scope(self):
        es = ExitStack()
        es.__enter__()
        self.scopes.append((es, []))

    def close_scope(self):
        es, bufs = self.scopes.pop()
        self.S.close_scope(bufs)
        es.__exit__(None, None, None)

    _uid = 0

    def sb(self, name, shape, dt=F32):
        es, bufs = self.scopes[-1]
        K._uid += 1
        name = "%s_%d" % (name, K._uid)
        h = es.enter_context(self.nc.sbuf_tensor(name, list(shape), dt))
        b = self.S.newbuf(name)
        bufs.append(b)
        return V(h.ap(), b)

    def ps(self, name, shape, dt=F32):
        es, bufs = self.scopes[-1]
        h = es.enter_context(self.nc.psum_tensor(name, list(shape), dt))
        b = self.S.newbuf(name)
        bufs.append(b)
        return V(h.ap(), b)

    def dram(self, ap, tracked=False, name="dram"):
        return V(ap, self.S.newbuf(name, fenced=False) if tracked else None)

    def act(self, out, in_, func, bias=None, scale=None, accum=None, eng="act"):
        kw = {}
        reads = [in_.b]
        if bias is not None:
            if isinstance(bias, V):
                kw["bias"] = bias.ap
                reads.append(bias.b)
            else:
                kw["bias"] = float(bias)
        if scale is not None:
            if isinstance(scale, V):
                kw["scale"] = scale.ap
                reads.append(scale.b)
            else:
                kw["scale"] = float(scale)
        writes = [out.b]
        if accum is not None:
            kw["accum_out"] = accum.ap
            writes.append(accum.b)
        self.S.op("act", lambda e: e.activation(out=out.ap, in_=in_.ap, func=func, **kw), reads, writes)

    def tt(self, out, in0, in1, op, eng="dve"):
        self.S.op(eng, lambda e: e.tensor_tensor(out=out.ap, in0=in0.ap, in1=in1.ap, op=op), [in0.b, in1.b], [out.b])

    def ts(self, out, in0, s1, op0, s2=None, op1=None, accum=None, eng="dve"):
        reads = [in0.b]
        a1 = s1.ap if isinstance(s1, V) else float(s1)
        if isinstance(s1, V):
            reads.append(s1.b)
        kw = {}
        if s2 is not None:
            a2 = s2.ap if isinstance(s2, V) else float(s2)
            if isinstance(s2, V):
                reads.append(s2.b)
            kw["op1"] = op1
        else:
            a2 = None
        writes = [out.b]
        if accum is not None:
            kw["accum_out"] = accum.ap
            writes.append(accum.b)
            if op1 is not None:
                kw["op1"] = op1
        self.S.op(eng, lambda e: e.tensor_scalar(out=out.ap, in0=in0.ap, scalar1=a1, scalar2=a2, op0=op0, **kw), reads, writes)

    def stt(self, out, in0, scalar, in1, op0, op1):
        reads = [in0.b, in1.b]
        sc = scalar.ap if isinstance(scalar, V) else float(scalar)
        if isinstance(scalar, V):
            reads.append(scalar.b)
        self.S.op("dve", lambda e: e.scalar_tensor_tensor(out=out.ap, in0=in0.ap, scalar=sc, in1=in1.ap, op0=op0, op1=op1), reads, [out.b])

    def copy(self, out, in_, eng="dve"):
        if eng == "act":
            self.act(out, in_, AF.Copy)
        else:
            self.S.op(eng, lambda e: e.tensor_copy(out=out.ap, in_=in_.ap), [in_.b], [out.b])

    def memset(self, out, val, eng="pool"):
        self.S.op(eng, lambda e: e.memset(out.ap, float(val)), [], [out.b])

    def reduce(self, out, in_, op, axis=AX.X, eng="dve"):
        self.S.op(eng, lambda e: e.tensor_reduce(out=out.ap, in_=in_.ap, axis=axis, op=op), [in_.b], [out.b])

    def recip(self, out, in_):
        self.S.op("dve", lambda e: e.reciprocal(out=out.ap, in_=in_.ap), [in_.b], [out.b])

    def scan(self, out, d0, d1, initial, op0, op1):
        self.S.op("dve", lambda e: e.tensor_tensor_scan(out=out.ap, data0=d0.ap, data1=d1.ap, initial=float(initial), op0=op0, op1=op1), [d0.b, d1.b], [out.b])

    def mm(self, out, lhsT, rhs, start=True, stop=True):
        self.S.op("pe", lambda e: e.matmul(out.ap, lhsT=lhsT.ap, rhs=rhs.ap, start=start, stop=stop), [lhsT.b, rhs.b], [out.b])

    def tr(self, out, in_, ident):
        self.S.op("pe", lambda e: e.transpose(out=out.ap, in_=in_.ap, identity=ident.ap), [in_.b, ident.b], [out.b])

    def dma(self, out, in_, q="sp"):
        wr = [out.b]
        rd = [in_.b]
        out_is_dram = "DRAM" in str(out.ap.space).upper()
        sembuf = in_.b if (out_is_dram and in_.b is not None) else out.b
        self.S.dma(q, lambda e: e.dma_start(out=out.ap, in_=in_.ap), rd, wr, sembuf=sembuf)


class _Stop(Exception):
    pass


class Prog:
    stop_stage = None

    def stage(self, name):
        if self.stop_stage is not None and name == self.stop_stage:
            raise _Stop()

    def run_unit(self, fn, *a):
        depth = len(self.k.scopes)
        try:
            fn(*a)
        except _Stop:
            while len(self.k.scopes) > depth:
                self.k.close_scope()

    def __init__(self, dbg=False, layers=(0, 1), groups=(0, 1), parts=None):
        self.dbg = dbg
        self.layers = layers
        self.groups = groups
        self.parts = parts
        self.dumps = {}
        self.bank_i = 0
        self.tbank = 0
        self.abank = 0

    def want(self, part):
        return self.parts is None or part in self.parts

    def declare(self):
        nc = self.nc
        di = lambda n, s: nc.dram_tensor(n, list(s), F32, kind="ExternalInput").ap()
        do = lambda n, s: nc.dram_tensor(n, list(s), F32, kind="ExternalOutput").ap()
        dint = lambda n, s: nc.dram_tensor(n, list(s), F32, kind="Internal").ap()
        self.d_xin = di("xin", [2, TG, D])
        self.d_cT = di("cT", [128, 32])
        self.d_wada = di("w_ada", [2, D, 3 * D])
        self.d_bada = di("b_ada", [2, 3 * D])
        self.d_win = di("w_in", [2, D, P_IN])
        self.d_wout = di("w_out", [2, D, D])
        self.d_pfm = di("pfm", [2, 128, NPF])
        self.d_prow = di("prow", [2, 128, NPR])
        self.d_lng = di("ln_g", [2, D])
        self.d_lnb = di("ln_b", [2, D])
        self.d_wup = di("wup", [2, 2, 128, 512])
        self.d_rw0 = di("rw0", [2, 2, 4, 128, 128])
        self.d_ml0 = di("ml0", [2, 2, 4, 128, 130])
        self.d_mm0 = di("mm0", [2, 128, 8])
        self.d_ck = di("ck", [2, 8, PAST, 128])
        self.d_cv = di("cv", [2, 8, PAST, 128])
        self.d_consts = di("consts", [128, NCONST])
        self.d_yout = do("yout", [2, TG, D])
        self.d_nk = do("nk", [4, 2, 8, TSEQ_P, 128])
        self.d_nv = do("nv", [4, 2, 8, TSEQ_P, 128])
        self.d_nrw = do("nrw", [4, 2, 2, 4, 128, 128])
        self.d_nmc = do("nmc", [4, 2, 2, 4, 128, 130])
        self.d_nmm = do("nmm", [2, 32])
        self.d_x1 = dint("x1s", [2, TG, D])
        self.d_modg = dint("modg", [2, 2, D])

    def dump(self, name, v, shape):
        if not self.dbg:
            return
        ap = self.nc.dram_tensor("dbg_" + name, list(shape), F32, kind="ExternalOutput").ap()
        self.dumps[name] = list(shape)
        k = self.k
        if v.ap.dtype != F32:
            k.open_scope()
            t = k.sb("dbgt_" + name, shape, F32)
            k.copy(t, v)
            k.dma(V(ap, None), t)
            k.close_scope()
        else:
            k.dma(V(ap, None), v)

    def bank(self):
        b = self.PB[self.bank_i % 8]
        self.bank_i += 1
        return b

    def build(self):
        self.nc = nc = bass.Bass("TRN2", target_bir_lowering=False)
        self.declare()
        with ExitStack() as es:
            self.k = k = K(nc, es)
            k.open_scope()
            self.setup()
            for l in self.layers:
                self.layer_setup(l)
                for g in self.groups:
                    self.group(l, g)
            k.S.finish()
            k.close_scope()
        return nc

    def setup(self):
        k = self.k
        self.CON = k.sb("CON", [128, NCONST])
        k.dma(self.CON, V(self.d_consts, None))
        C = self.CON
        self.IDF = C[:, C_IDENT:C_IDENT + 128]
        self.MASK = C[:, C_MASK:C_MASK + 1024].rr("p (a b) -> p a b", a=8)
        self.ONESBD = C[:, C_ONESBD:C_ONESBD + 128]
        self.ONES = C[:, C_ONES:C_ONES + 128]
        self.HM = C[:, C_HM:C_HM + 2]
        self.COS = C[:, C_COS:C_COS + 512].rr("p (i a b) -> p i a b", i=8, a=4)
        self.SIN = C[:, C_SIN:C_SIN + 512].rr("p (i a b) -> p i a b", i=8, a=4)
        self.IDB = k.sb("IDB", [128, 128], BF16)
        k.copy(self.IDB, self.IDF)
        es, bufs = k.scopes[-1]
        h = es.enter_context(self.nc.psum_tensor("PS", [128, 8, 512], F32))
        self.PB = []
        for i in range(8):
            b = k.S.newbuf("bank%d" % i)
            b.excl = True
            bufs.append(b)
            self.PB.append(V(h.ap()[:, i, :], b))
        self.mixT = k.sb("mixT", [128, KC, TG], BF16)
        self.WS = k.sb("WS", [128, KC, 656], BF16)
        self.PFM = k.sb("PFM", [128, NPF])
        self.PROW = k.sb("PROW", [128, NPR])
        self.WUP = k.sb("WUP", [128, 2, 512], BF16)
        self.MODT = k.sb("MODT", [128, 2, 32])
        self.LAM = k.sb("LAM", [128, 2])
        self.SUBG = k.sb("SUBG", [128, 1])
        self.RMASK = k.sb("RMASK", [128, TG])
        k.memset(self.RMASK, 1.0)
        k.memset(self.RMASK.rr("p (c t) -> p c t", t=64)[:, :, 0:1], 0.0)
        self.EPSC = k.sb("EPSC", [128, 4])
        k.memset(self.EPSC[:, 0:1], 1e-12)
        k.memset(self.EPSC[:, 1:2], GN_EPS_A)
        k.memset(self.EPSC[:, 2:3], GN_EPS)
        k.memset(self.EPSC[:, 3:4], 1.0)
        self.EPS12 = self.EPSC[:, 0:1]
        self.EPSA = self.EPSC[:, 1:2]
        self.EPSB = self.EPSC[:, 2:3]
        self.ONE1 = self.EPSC[:, 3:4]
        self.x1bufs = [[k.S.newbuf("x1_%d_%d" % (g, i), fenced=False) for i in range(NT)] for g in range(2)]
        self.modgbuf = [[k.S.newbuf("modg%d%d" % (l, j), fenced=False) for j in range(2)] for l in range(2)]

    def pf(self, *key):
        c = PF[key]
        return self.PFM[:, c:c + 1]

    def layer_setup(self, l):
        k = self.k
        k.dma(self.PFM, V(self.d_pfm[l], None))
        k.dma(self.PROW, V(self.d_prow[l], None))
        for p in range(4):
            k.ts(self.pf("rw_omk_a", p), self.pf("rw_k_a", p), -1.0, ALU.mult, 1.0, ALU.add)
        k.dma(self.WUP, V(self.d_wup[l].rearrange("a p c -> p a c"), None), q="pool")
        lam_init = 0.8 - 0.6 * math.exp(-0.3 * l)
        k.open_scope()
        t = k.sb("lamt", [128, 64])
        s = k.sb("lams", [128, 2])
        k.tt(t, self.PROW[:, PR_LQ1:PR_LQ1 + 64], self.PROW[:, PR_LK1:PR_LK1 + 64], ALU.mult)
        k.reduce(s[:, 0:1], t, ALU.add)
        k.tt(t, self.PROW[:, PR_LQ2:PR_LQ2 + 64], self.PROW[:, PR_LK2:PR_LK2 + 64], ALU.mult)
        k.reduce(s[:, 1:2], t, ALU.add)
        k.act(s, s, AF.Exp)
        k.tt(self.LAM[:, 0:1], s[:, 0:1], s[:, 1:2], ALU.subtract)
        k.ts(self.LAM[:, 0:1], self.LAM[:, 0:1], lam_init, ALU.add)
        k.ts(self.SUBG, self.pf("subln"), 1.0 - lam_init, ALU.mult)
        k.close_scope()
        if self.want("ada"):
            self.ada(l)

    def load_w(self, src_rows, c0, ncols):
        k = self.k
        src = src_rows.rearrange("(kc p) c -> p kc c", p=128)[:, :, c0:c0 + ncols]
        k.dma(self.WS[:, :, 0:ncols], V(src, None), q="pool")

    def ada(self, l):
        k = self.k
        k.open_scope()
        ct = k.sb("ada_c", [128, 32])
        cb = k.sb("ada_cb", [128, 32], BF16)
        k.dma(ct, V(self.d_cT, None))
        k.act(cb, ct, AF.Silu)
        brow = k.sb("ada_brow", [1, 512])
        rowt = [k.sb("ada_row%d" % j, [1, 512]) for j in range(2)]
        pm = self.PB[7]
        for s in range(12):
            self.load_w(self.d_wada[l], s * 512, 512)
            k.dma(brow, V(self.d_bada[l:l + 1, s * 512:(s + 1) * 512], None))
            for j in range(2):
                pb = self.PB[j]
                for kc in range(KC):
                    k.mm(pb[0:1, 0:512], cb[:, kc * 2 + j:kc * 2 + j + 1], self.WS[:, kc, 0:512], start=(kc == 0), stop=(kc == KC - 1))
                k.tt(rowt[j], pb[0:1, 0:512], brow, ALU.add)
                if s < 8:
                    for q in range(4):
                        c = s * 4 + q
                        k.mm(pm[:, (j * 32 + c) * 2:(j * 32 + c) * 2 + 2], rowt[j][0:1, q * 128:(q + 1) * 128], self.ONES[0:1, 0:2])
                else:
                    k.dma(V(self.d_modg[l, j:j + 1, (s - 8) * 512:(s - 7) * 512], self.modgbuf[l][j]), rowt[j])
        k.copy(self.MODT.rr("p a b -> p (a b)"), pm[:, 0:128].rr("p (c t) -> p c t", t=2)[:, :, 0])
        k.ts(self.MODT[:, :, 16:32], self.MODT[:, :, 16:32], 1.0, ALU.add)
        self.dump("modT%d" % l, self.MODT, [128, 2, 32])
        k.close_scope()

    def group(self, l, g):
        k = self.k
        self.l, self.g = l, g
        self.nseq, self.tseq = (NSEQ_P, TSEQ_P) if g == 0 else (1, TG)
        tag = "%d%d" % (l, g)
        k.open_scope()
        self.uT = k.sb("uT" + tag, [128, KC, TG], BF16)
        self.LW = k.sb("LW" + tag, [128, TG], BF16)
        self.LA = k.sb("LA" + tag, [128, TG], BF16)
        self.OMG = k.sb("OMG" + tag, [128, NT, 8])
        self.CLAMP = k.sb("CLAMP" + tag, [128, NT, 8])
        self.SC = k.sb("SC" + tag, [128, NCH, 8])
        if self.want("u"):
            self.uphase(l, g)
        if self.want("lora"):
            self.unit_lora(l, g)
        for p in range(4):
            if self.want("rw%d" % p):
                self.run_unit(self.unit_rwkv, l, g, p)
        if self.want("mg"):
            self.run_unit(self.unit_mg, l, g)
        for h in range(4):
            if self.want("ml%d" % h):
                self.run_unit(self.unit_mlstm, l, g, h)
        for h in range(8):
            if self.want("at%d" % h):
                self.run_unit(self.unit_attn3, l, g, h)
        k.close_scope()
        if self.want("o"):
            self.phase_o(l, g)

    def xsrc(self, l, g, i):
        if l == 0:
            return V(self.d_xin[g, i * 128:(i + 1) * 128, :], None)
        return V(self.d_x1[g, i * 128:(i + 1) * 128, :], self.x1bufs[g][i])

    def uphase(self, l, g):
        k = self.k
        k.open_scope()
        xts = [k.sb("xt%d" % j, [128, D]) for j in range(2)]
        for i in range(NT):
            xt = xts[i % 2]
            k.dma(xt, self.xsrc(l, g, i))
            for q in range(4):
                pb = self.bank()
                for kk in range(4):
                    kc = q * 4 + kk
                    k.tr(pb[:, kk * 128:(kk + 1) * 128], xt[:, kc * 128:(kc + 1) * 128], self.IDF)
                for kk in range(4):
                    kc = q * 4 + kk
                    k.act(self.uT[:, kc, i * 128:(i + 1) * 128], pb[:, kk * 128:(kk + 1) * 128], AF.Identity,
                          scale=self.MODT[:, g, 16 + kc:17 + kc], bias=self.MODT[:, g, kc:kc + 1])
        k.close_scope()
        self.dump("uT%d%d" % (l, g), self.uT[:, :, 0:256], [128, KC, 256])

    def proj_fm(self, col, evac):
        k = self.k
        for nb in range(2):
            pb = self.bank()
            for kc in range(KC):
                k.mm(pb[:, 0:512], self.WS[:, kc, col:col + 128], self.uT[:, kc, nb * 512:(nb + 1) * 512], start=(kc == 0), stop=(kc == KC - 1))
            evac(nb, pb[:, 0:512])

    def proj_tm(self, col, ncols, evac):
        k = self.k
        for i in range(NT):
            pb = self.bank()
            for kc in range(KC):
                k.mm(pb[:, 0:ncols], self.uT[:, kc, i * 128:(i + 1) * 128], self.WS[:, kc, col:col + ncols], start=(kc == 0), stop=(kc == KC - 1))
            evac(i, pb[:, 0:ncols])

    def pad_evac(self, PRE, j):
        k = self.k
        nseq, tseq = self.nseq, self.tseq

        def ev(nb, ps):
            if nseq == 1:
                k.act(PRE[:, j, 0, 1 + nb * 512:1 + (nb + 1) * 512], ps, AF.Copy)
            else:
                k.act(PRE[:, j, 2 * nb:2 * nb + 2, 1:tseq + 1], ps.rr("p (s t) -> p s t", s=2), AF.Copy)
        return ev

    def conv(self, X, PRE, j, w0, w1, w2, b):
        k = self.k
        nseq, tseq = self.nseq, self.tseq
        xv = X.rr("p (s t) -> p s t", s=nseq)
        k.act(xv, PRE[:, j, :, 1:tseq + 1], AF.Identity, scale=w1, bias=b)
        k.stt(xv, PRE[:, j, :, 0:tseq], w0, xv, ALU.mult, ALU.add)
        k.stt(xv, PRE[:, j, :, 2:tseq + 2], w2, xv, ALU.mult, ALU.add)

    def unit_lora(self, l, g):
        k = self.k
        self.load_w(self.d_win[l], U_LORA, 256)
        self.proj_fm(0, lambda nb, ps: k.act(self.LW[:, nb * 512:(nb + 1) * 512], ps, AF.Tanh))
        self.proj_fm(128, lambda nb, ps: k.act(self.LA[:, nb * 512:(nb + 1) * 512], ps, AF.Copy))
        self.dump("LW%d%d" % (l, g), self.LW, [128, TG])

    def unit_rwkv(self, l, g, p):
        k = self.k
        nseq, tseq = self.nseq, self.tseq
        cps = tseq // 64
        tag = "%d%d%d" % (l, g, p)
        self.load_w(self.d_win[l], U_RW[p], 512)
        k.open_scope()
        GT = k.sb("rGT" + tag, [128, TG])
        BON = k.sb("rBON" + tag, [128, TG])
        KR = [k.sb("rKR%d" % d + tag, [128, NT, 2, 128], BF16) for d in range(2)]
        AH = [k.sb("rAH%d" % d + tag, [128, TG], BF16) for d in range(2)]
        KH = [k.sb("rKH%d" % d + tag, [128, TG], BF16) for d in range(2)]
        VB = k.sb("rVB" + tag, [128, TG], BF16)
        GL = [k.sb("rGL%d" % d + tag, [128, NCH]) for d in range(2)]
        YTM = k.sb("rY" + tag, [128, NT, 128])

        k.open_scope()
        R = k.sb("rR" + tag, [128, TG])
        Kk = k.sb("rK" + tag, [128, TG])
        V32 = k.sb("rV" + tag, [128, TG])
        k.open_scope()
        PRE = k.sb("rPRE" + tag, [128, 3, nseq, tseq + 2])
        k.memset(PRE[:, :, :, 0:1], 0.0)
        k.memset(PRE[:, :, :, tseq + 1:tseq + 2], 0.0)
        for j in range(3):
            self.proj_fm(j * 128, self.pad_evac(PRE, j))
        self.proj_fm(384, lambda nb, ps: k.act(GT[:, nb * 512:(nb + 1) * 512], ps, AF.Silu))
        for j, X in enumerate((R, Kk, V32)):
            self.conv(X, PRE, j, self.pf("ca_w", p, j, 0), self.pf("ca_w", p, j, 1), self.pf("ca_w", p, j, 2), self.pf("ca_b", p, j))
        k.close_scope()
        self.stage("rw_conv")
        B = [k.sb("rB%d" % i + tag, [128, TG]) for i in range(9)]
        k.copy(VB, V32, eng="pool")
        KK, SQ, RS = B[0], B[1], B[2]
        k.ts(KK, Kk, self.pf("rw_k_k", p), ALU.mult)
        k.act(SQ, KK, AF.Square)
        for nb in range(2):
            pb = self.bank()
            k.mm(pb[:, 0:512], self.ONESBD, SQ[:, nb * 512:(nb + 1) * 512])
            k.act(RS[:, nb * 512:(nb + 1) * 512], pb[:, 0:512], AF.Sqrt, bias=self.EPS12)
        k.recip(RS, RS)
        k.tt(KK, KK, RS, ALU.mult)
        self.stage("rw_kk")
        KS = B[8]
        v3 = lambda X: X.rr("p (i t) -> p i t", t=128)
        c3 = lambda X: X.rr("p (c t) -> p c t", t=64)
        for d in range(2):
            SIG, AA, KT, AL, CS, T1, T2 = B[1], B[2], B[3], B[4], B[5], B[6], B[7]
            dr = slice(d * 64, d * 64 + 64)
            for nb in range(2):
                ns = slice(nb * 512, (nb + 1) * 512)
                pb = self.bank()
                k.mm(pb[:, 0:512], self.WUP[dr, 0, p * 128:(p + 1) * 128], self.LW[dr, ns])
                k.act(SIG[:, ns], pb[:, 0:512], AF.Sigmoid, bias=self.pf("w0", p, d))
                pb = self.bank()
                k.mm(pb[:, 0:512], self.WUP[dr, 1, p * 128:(p + 1) * 128], self.LA[dr, ns])
                k.act(AA[:, ns], pb[:, 0:512], AF.Sigmoid, bias=self.pf("a0", p, d))
            k.ts(T1, AA, self.pf("rw_k_a", p), ALU.mult, self.pf("rw_omk_a", p), ALU.add)
            k.tt(KT, T1, Kk, ALU.mult)
            if d == 0:
                k.copy(KS, KT, eng="pool")
            else:
                k.tt(KS, KS, KT, ALU.add, eng="pool")
            k.tt(AL, AA, KK, ALU.mult)
            k.scan(CS, self.RMASK, SIG, 0.0, ALU.mult, ALU.add)
            if d == 1:
                k.tt(c3(T1), c3(CS), c3(CS)[:, :, 63:64].bc([128, NCH, 64]), ALU.subtract)
                k.tt(CS, SIG, T1, ALU.subtract)
            k.tt(T2, CS, SIG, ALU.subtract)
            k.act(T2, T2, AF.Exp, scale=-WDECAY)
            k.tt(KR[d][:, :, 0, :], v3(KK), v3(T2), ALU.mult)
            EP = T1
            k.act(EP, CS, AF.Exp, scale=-WDECAY)
            k.tt(KR[d][:, :, 1, :], v3(R), v3(EP), ALU.mult)
            k.copy(GL[d], c3(EP)[:, :, 63 if d == 0 else 0])
            EM = AA
            k.act(EM, CS, AF.Exp, scale=WDECAY)
            k.tt(AH[d], AL, EM, ALU.mult)
            k.tt(KH[d], KT, EM, ALU.mult)
        k.ts(B[1], R, self.pf("rw_r_k", p), ALU.mult)
        k.tt(B[1], B[1], KS, ALU.mult)
        for nb in range(2):
            ns = slice(nb * 512, (nb + 1) * 512)
            pb = self.bank()
            k.mm(pb[:, 0:512], self.ONESBD, B[1][:, ns])
            k.tt(BON[:, ns], pb[:, 0:512], V32[:, ns], ALU.mult)
        if p == 0:
            self.dump("rw_R%d%d" % (l, g), R, [128, TG])
            self.dump("rw_KK%d%d" % (l, g), KK, [128, TG])
            self.dump("rw_BON%d%d" % (l, g), BON, [128, TG])
            self.dump("rw_GL0%d%d" % (l, g), GL[0], [128, NCH])
            self.dump("rw_GL1%d%d" % (l, g), GL[1], [128, NCH])
        k.close_scope()

        self.stage("rw_prep")
        k.open_scope()
        TMB = k.sb("rTMB" + tag, [128, NT, 5, 128], BF16)
        for i in range(NT):
            pbb = self.bank().bitcast(BF16)
            ts_ = slice(i * 128, (i + 1) * 128)
            for s, src in enumerate((VB, KH[0], KH[1], AH[0], AH[1])):
                k.tr(pbb[:, s * 128:(s + 1) * 128], src[:, ts_], self.IDB)
            k.copy(TMB[:, i].rr("p a b -> p (a b)"), pbb[:, 0:640], eng=("act" if i % 2 else "dve"))
        self.stage("rw_tmb")
        k.memset(YTM, 0.0)
        self.RB = [k.sb("rRB%d" % i + tag, [128, 128], BF16) for i in range(2)]
        self.UB = [k.sb("rUB%d" % i + tag, [128, 128], BF16) for i in range(2)]
        self.HT = [k.sb("rHT%d" % i + tag, [128, 128]) for i in range(2)]
        Hf = {}
        Hb = {}
        for s in range(nseq):
            for d in range(2):
                Hf[(s, d)] = k.sb("rHf%d%d" % (s, d) + tag, [128, 128])
                Hb[(s, d)] = k.sb("rHb%d%d" % (s, d) + tag, [128, 128], BF16)
        for d in range(2):
            k.open_scope()
            GMQ = k.sb("rGMQ%d" % d + tag, [128, NCH, 4, 128], BF16)
            P0 = k.sb("rP0%d" % d + tag, [128, NCH, 128], BF16)
            QA = k.sb("rQA%d" % d + tag, [128, NCH, 128], BF16)
            PA = k.sb("rPA%d" % d + tag, [128, NCH, 128], BF16)
            X = k.sb("rX%d" % d + tag, [128, NCH, 128], BF16)
            R1 = k.sb("rR1%d" % d + tag, [128, NT, 128])
            mP = 4 if d == 0 else 0
            for i in range(NT):
                ts_ = slice(i * 128, (i + 1) * 128)
                for hh in range(2):
                    j = i * 2 + hh
                    hr = slice(hh * 64, hh * 64 + 64)
                    pb = self.bank()
                    krv = KR[d][hr, i].rr("p a b -> p (a b)")
                    k.mm(pb[:, 0:256], AH[d][hr, ts_], krv)
                    k.mm(pb[:, 256:512], KH[d][hr, ts_], krv)
                    k.tt(GMQ[:, j], pb[:, 0:512].rr("p (a b) -> p a b", a=4), self.MASK[:, 4 * d:4 * d + 4, :], ALU.mult)
            for hh in range(2):
                hr = slice(hh * 64, hh * 64 + 64)
                for q in range(2):
                    pbP = self.bank()
                    for ii in range(4):
                        i = q * 4 + ii
                        k.mm(pbP[:, ii * 128:(ii + 1) * 128], KR[d][hr, i, 0, :], AH[d][hr, i * 128:(i + 1) * 128])
                    k.tt(P0[:, q * 8 + hh:q * 8 + 8:2, :], pbP[:, 0:512].rr("p (a b) -> p a b", a=4),
                         self.MASK[:, mP:mP + 1, :].bc([128, 4, 128]), ALU.mult)
            self.stage("rw_gram")
            k.tt(X, GMQ[:, :, 0, :], self.IDB.us(1).bc([128, NCH, 128]), ALU.add)
            Qc, Pc = GMQ[:, :, 0, :], P0
            Qn, Pn = QA, PA
            for lev in range(1, 6):
                for grp in range(4):
                    js = range(grp * 4, grp * 4 + 4)
                    gsl = slice(grp * 4, grp * 4 + 4)
                    pbP = self.bank()
                    for jj, j in enumerate(js):
                        k.mm(pbP[:, jj * 128:(jj + 1) * 128], Qc[:, j, :], Pc[:, j, :])
                    k.copy(Pn[:, gsl, :], pbP[:, 0:512].rr("p (a b) -> p a b", a=4), eng="act")
                    if lev < 5:
                        pbQ = self.bank()
                        for jj, j in enumerate(js):
                            k.mm(pbQ[:, jj * 128:(jj + 1) * 128], Pc[:, j, :], Qc[:, j, :])
                        k.copy(Qn[:, gsl, :], pbQ[:, 0:512].rr("p (a b) -> p a b", a=4), eng="dve")
                    pbX = self.bank()
                    for jj, j in enumerate(js):
                        k.mm(pbX[:, jj * 128:(jj + 1) * 128], Pn[:, j, :], X[:, j, :])
                    k.tt(X[:, gsl, :], X[:, gsl, :], pbX[:, 0:512].rr("p (a b) -> p a b", a=4), ALU.add)
                if lev == 1:
                    Qc, Pc, Qn, Pn = QA, PA, k.sb("rQB%d" % d + tag, [128, NCH, 128], BF16), P0
                else:
                    Qc, Pc, Qn, Pn = Qn, Pn, Qc, Pc
            self.stage("rw_inv")
            for i in range(NT):
                pb = self.bank()
                for hh in range(2):
                    j = i * 2 + hh
                    vs = TMB[:, i, 0, hh * 64:(hh + 1) * 64]
                    k.mm(pb[:, hh * 64:(hh + 1) * 64], GMQ[:, j, 2, :], vs)
                    k.mm(pb[:, 128 + hh * 64:128 + (hh + 1) * 64], GMQ[:, j, 3, :], vs)
                k.copy(R1[:, i, :], pb[:, 0:128], eng="act")
                k.tt(YTM[:, i, :], YTM[:, i, :], pb[:, 128:256], ALU.add)
            self.stage("rw_r1")
            for s in range(nseq):
                if g == 0:
                    k.memset(Hf[(s, d)], 0.0)
                else:
                    k.dma(Hf[(s, d)], V(self.d_rw0[l, d, p], None))
                k.copy(Hb[(s, d)], Hf[(s, d)], eng="act")
            for cs in range(cps):
                for s in range(nseq):
                    c = s * cps + (cs if d == 0 else cps - 1 - cs)
                    i, half = c // 2, c % 2
                    tr_ = slice(half * 64, half * 64 + 64)
                    hf, hb = Hf[(s, d)], Hb[(s, d)]
                    pbR = self.bank()
                    k.mm(pbR[tr_, 0:128], KR[d][:, i, 0, tr_], hb)
                    Rb = self.RB[(s + d) % 2]
                    k.tt(Rb[tr_, :], pbR[tr_, 0:128], R1[tr_, i, :], ALU.add)
                    self.stage("sc_R%d" % c)
                    pbU = self.bank()
                    for hh in range(2):
                        j = i * 2 + hh
                        k.mm(pbU[tr_, hh * 64:(hh + 1) * 64], X[tr_, j, tr_], Rb[tr_, hh * 64:(hh + 1) * 64])
                    Ub = self.UB[(s + d) % 2]
                    k.act(Ub[tr_, :], pbU[tr_, 0:128], AF.Copy, scale=-1.0)
                    self.stage("sc_U%d" % c)
                    pbY = self.bank()
                    k.mm(pbY[tr_, 0:128], KR[d][:, i, 1, tr_], hb)
                    pbH = self.bank()
                    k.mm(pbH[:, 0:128], TMB[tr_, i, 1 + d, :], TMB[tr_, i, 0, :], start=True, stop=False)
                    k.mm(pbH[:, 0:128], TMB[tr_, i, 3 + d, :], Ub[tr_, :], start=False, stop=True)
                    pbY2 = self.bank()
                    for hh in range(2):
                        j = i * 2 + hh
                        k.mm(pbY2[tr_, hh * 64:(hh + 1) * 64], GMQ[tr_, j, 1, tr_], Ub[tr_, hh * 64:(hh + 1) * 64])
                    HT = self.HT[(s + d) % 2]
                    k.tt(HT, pbH[:, 0:128], self.ONESBD, ALU.mult)
                    k.tt(HT, HT, hf, ALU.add)
                    k.ts(hf, HT, GL[d][:, c:c + 1], ALU.mult)
                    k.copy(hb, hf, eng="act")
                    k.tt(YTM[tr_, i, :], YTM[tr_, i, :], pbY[tr_, 0:128], ALU.add)
                    k.tt(YTM[tr_, i, :], YTM[tr_, i, :], pbY2[tr_, 0:128], ALU.add)
                    self.stage("sc_Y%d" % c)
                    self.stage("sc_H%d" % c)
            self.stage("sc_end")
            if g == 0:
                for s in range(nseq):
                    k.dma(V(self.d_nrw[s, l, d, p], None), Hf[(s, d)])
            self.stage("sc_out%d" % d)
            if p == 0 and d == 0:
                self.dump("rw_X%d%d" % (l, g), X[:, 0:2, :], [128, 2, 128])
                self.dump("rw_R1%d%d" % (l, g), R1, [128, NT, 128])
            k.close_scope()
        if p == 0:
            self.dump("rw_Y%d%d" % (l, g), YTM, [128, NT, 128])
        self.stage("rw_scan")
        YN = k.sb("rYN" + tag, [128, NT, 128])
        ST = k.sb("rST" + tag, [128, 4, 16])
        yv = YTM.rr("p i (h v) -> p (i h) v", h=2)
        ynv = YN.rr("p i (h v) -> p (i h) v", h=2)
        k.reduce(ST[:, 0, :], yv, ALU.add)
        k.act(YN, YTM, AF.Square)
        k.reduce(ST[:, 1, :], ynv, ALU.add)
        k.ts(ST[:, 0, :], ST[:, 0, :], 1.0 / 64, ALU.mult)
        k.tt(ST[:, 2, :], ST[:, 0, :], ST[:, 0, :], ALU.mult)
        k.stt(ST[:, 1, :], ST[:, 1, :], 1.0 / 64, ST[:, 2, :], ALU.mult, ALU.subtract)
        k.act(ST[:, 1, :], ST[:, 1, :], AF.Sqrt, bias=self.EPSA)
        k.recip(ST[:, 1, :], ST[:, 1, :])
        k.tt(ynv, yv, ST[:, 0, :].us(2).bc([128, 16, 64]), ALU.subtract)
        k.tt(ynv, ynv, ST[:, 1, :].us(2).bc([128, 16, 64]), ALU.mult)
        self.stage("rw_ln")
        OUTF = k.sb("rOUT" + tag, [128, TG])
        for q in range(2):
            pb = self.bank()
            for ii in range(4):
                i = q * 4 + ii
                k.tr(pb[:, ii * 128:(ii + 1) * 128], YN[:, i, :], self.IDF)
            k.act(OUTF[:, q * 512:(q + 1) * 512], pb[:, 0:512], AF.Identity, scale=self.pf("rw_gn_g", p), bias=self.pf("rw_gn_b", p))
        k.tt(OUTF, OUTF, BON, ALU.add)
        k.tt(self.mixT[:, p, :], OUTF, GT, ALU.mult)
        if p == 0:
            self.dump("rw_out%d%d" % (l, g), self.mixT[:, 0, :], [128, TG])
        k.close_scope()
        k.close_scope()

    def unit_mg(self, l, g):
        k = self.k
        nseq, tseq = self.nseq, self.tseq
        cps = tseq // 64
        tag = "%d%d" % (l, g)
        self.load_w(self.d_win[l], U_MG, 16)
        k.open_scope()
        GI = k.sb("gGI" + tag, [128, NT, 16])
        self.proj_tm(0, 16, lambda i, ps: k.act(GI[:, i, :], ps, AF.Copy))
        gi = k.sb("ggi" + tag, [128, NT, 8])
        LFN = k.sb("gLFN" + tag, [128, NT, 8])
        NB = k.sb("gNB" + tag, [128, NT, 8])
        AG = k.sb("gAG" + tag, [128, NT, 8])
        k.tt(gi, GI[:, :, 0:8], self.PROW[:, PR_BI:PR_BI + 8].us(1).bc([128, NT, 8]), ALU.add)
        k.tt(LFN, GI[:, :, 8:16], self.PROW[:, PR_BF:PR_BF + 8].us(1).bc([128, NT, 8]), ALU.add)
        k.act(LFN, LFN, AF.Exp, scale=-1.0)
        k.act(LFN, LFN, AF.Ln, bias=self.ONE1)
        self.stage("mg_a")
        pb = self.bank()
        for d in range(2):
            k.mm(pb[:, d * 32:(d + 1) * 32], self.MASK[:, 1 if d == 0 else 5, :], LFN[:, :, d * 4:(d + 1) * 4])
        for d in range(2):
            k.copy(NB[:, :, d * 4:(d + 1) * 4], pb[:, d * 32:(d + 1) * 32].rr("p (i h) -> p i h", h=4))
        k.tt(AG, gi, NB, ALU.add)
        self.stage("mg_b")
        pbt = self.bank()
        k.tr(pbt[0:64, 0:128], AG.rr("p i k -> p (i k)"), self.IDF)
        self.stage("mg_c")
        MXT = k.sb("gMXT" + tag, [64, 2])
        k.reduce(MXT, pbt[0:64, 0:128].rr("p (f t) -> p f t", f=2), ALU.max)
        RH = k.sb("gRH" + tag, [64, 64, 2])
        k.tt(RH, self.IDF[0:64, 0:64].us(2).bc([64, 64, 2]), MXT.us(1).bc([64, 64, 2]), ALU.mult)
        self.stage("mg_d")
        pbm = self.bank()
        k.mm(pbm[:, 0:128], self.ONES[0:64, :], RH.rr("p a b -> p (a b)"))
        MXF = k.sb("gMXF" + tag, [128, NT, 8, 2])
        k.copy(MXF.rr("p i k f -> p (i k f)"), pbm[:, 0:128])
        self.stage("mg_e")
        R2 = k.sb("gR2" + tag, [128, 64, 2])
        k.tt(R2, LFN.rr("p i k -> p (i k)").us(2).bc([128, 64, 2]), self.HM.us(1).bc([128, 64, 2]), ALU.mult)
        pbl = self.bank()
        k.mm(pbl[:, 0:128], self.ONES, R2.rr("p a b -> p (a b)"))
        NBL = k.sb("gNBL" + tag, [128, NT, 8, 2])
        k.copy(NBL.rr("p i k f -> p (i k f)"), pbl[:, 0:128])
        self.stage("mg_f")
        M0 = k.sb("gM0" + tag, [128, nseq, 8])
        MBAR = k.sb("gMBAR" + tag, [128, NT, 2, 8])
        if g == 0:
            k.memset(M0, 0.0)
        else:
            k.dma(M0[:, 0, :], V(self.d_mm0[l], None))
        ipseq = NT // nseq
        mxv = MXF.rr("p (s i) k f -> p s i k f", s=nseq)
        nbv = NBL.rr("p (s i) k f -> p s i k f", s=nseq)
        mbv = MBAR.rr("p (s i) f k -> p s i f k", s=nseq)
        scv = self.SC.rr("p (s i f) k -> p s i f k", s=nseq, f=2)
        for cs in range(cps):
            for d in range(2):
                cc = cs if d == 0 else cps - 1 - cs
                ii, half = cc // 2, cc % 2
                ds = slice(d * 4, d * 4 + 4)
                m0 = M0[:, :, ds]
                mb = mbv[:, :, ii, half, ds]
                k.tt(mb, m0, mxv[:, :, ii, ds, half], ALU.max)
                k.tt(scv[:, :, ii, half, ds], m0, mb, ALU.subtract)
                k.tt(m0, mb, nbv[:, :, ii, ds, half], ALU.subtract)
        self.stage("mg_g")
        k.act(self.SC, self.SC, AF.Exp)
        self.stage("mg_h")
        if g == 0:
            k.dma(V(self.d_nmm[l:l + 1, :], None), M0[0:1].rr("p s k -> p (s k)"))
        self.stage("mg_i")
        MT = k.sb("gMT" + tag, [128, NT, 8])
        k.ts(MT, MBAR[:, :, 0, :], self.HM[:, 0:1], ALU.mult)
        k.stt(MT, MBAR[:, :, 1, :], self.HM[:, 1:2], MT, ALU.mult, ALU.add)
        k.tt(self.OMG, AG, MT, ALU.subtract)
        k.act(self.OMG, self.OMG, AF.Exp)
        k.tt(self.CLAMP, NB, MT, ALU.subtract)
        k.act(self.CLAMP, self.CLAMP, AF.Exp)
        self.stage("mg_j")
        self.dump("mg_AG%d%d" % (l, g), AG, [128, NT, 8])
        self.stage("mg_k")
        self.dump("mg_MBAR%d%d" % (l, g), MBAR.rr("p i f k -> p (i f k)"), [128, NT * 16])
        self.dump("mg_SC%d%d" % (l, g), self.SC, [128, NCH, 8])
        self.dump("mg_OMG%d%d" % (l, g), self.OMG, [128, NT, 8])
        k.close_scope()

    def unit_mlstm(self, l, g, h):
        k = self.k
        nseq, tseq = self.nseq, self.tseq
        cps = tseq // 64
        tag = "%d%d%d" % (l, g, h)
        self.load_w(self.d_win[l], U_ML[h], 640)
        k.open_scope()
        GT = k.sb("mGT" + tag, [128, TG])
        VT = k.sb("mVT" + tag, [128, NT, 128])
        OT = k.sb("mOT" + tag, [128, NT, 128])
        QB = k.sb("mQB" + tag, [128, TG], BF16)
        KB = k.sb("mKB" + tag, [128, TG], BF16)
        k.open_scope()
        PRE = k.sb("mPRE" + tag, [128, 2, nseq, tseq + 2])
        self.stage("ml_a")
        k.memset(PRE[:, :, :, 0:1], 0.0)
        k.memset(PRE[:, :, :, tseq + 1:tseq + 2], 0.0)
        self.stage("ml_b")
        for j in range(2):
            self.proj_fm(j * 128, self.pad_evac(PRE, j))
        self.stage("ml_c")
        self.proj_fm(256, lambda nb, ps: k.act(GT[:, nb * 512:(nb + 1) * 512], ps, AF.Silu))
        self.stage("ml_d")

        def ev_vo(i, ps):
            k.copy(VT[:, i, :], ps[:, 0:128])
            k.act(OT[:, i, :], ps[:, 128:256], AF.Sigmoid)
        self.proj_tm(384, 256, ev_vo)
        self.stage("ml_proj")
        X = k.sb("mX" + tag, [128, TG])
        for j, dst in enumerate((QB, KB)):
            self.conv(X, PRE, j, self.pf("cb_w", h, j, 0), self.pf("cb_w", h, j, 1), self.pf("cb_w", h, j, 2), self.pf("cb_b", h, j))
            if j == 0:
                k.act(dst, X, AF.Silu)
            else:
                k.act(X, X, AF.Silu)
                k.ts(dst, X, 128.0 ** -0.5, ALU.mult)
        k.close_scope()
        self.stage("ml_conv")
        KTM = k.sb("mKTM" + tag, [128, NT, 128], BF16)
        pbb = self.bank().bitcast(BF16)
        for i in range(NT):
            k.tr(pbb[:, i * 128:(i + 1) * 128], KB[:, i * 128:(i + 1) * 128], self.IDB)
        k.copy(KTM.rr("p i c -> p (i c)"), pbb[:, 0:1024])
        self.stage("ml_ktm")
        MTd = [k.sb("mMT%d" % d + tag, [128, NT, 128], BF16) for d in range(2)]
        for q in range(2):
            pb = self.bank()
            for ii in range(4):
                i = q * 4 + ii
                ts_ = slice(i * 128, (i + 1) * 128)
                k.mm(pb[:, ii * 128:(ii + 1) * 128], KB[:, ts_], QB[:, ts_])
            pv = pb[:, 0:512].rr("p (a b) -> p a b", a=4)
            k.tt(MTd[0][:, q * 4:q * 4 + 4, :], pv, self.MASK[:, 1:2, :].bc([128, 4, 128]), ALU.mult)
            k.tt(MTd[1][:, q * 4:q * 4 + 4, :], pv, self.MASK[:, 5:6, :].bc([128, 4, 128]), ALU.mult)
        self.stage("ml_mt")
        WV = [k.sb("mWV%d" % d + tag, [128, NT, 130], BF16) for d in range(2)]
        HI = [k.sb("mHI%d" % d + tag, [128, NT, 130]) for d in range(2)]
        for d in range(2):
            om = self.OMG[:, :, d * 4 + h:d * 4 + h + 1]
            k.tt(WV[d][:, :, 0:128], VT, om.bc([128, NT, 128]), ALU.mult)
            k.copy(WV[d][:, :, 128:129], om)
            k.memset(WV[d][:, :, 129:130], 0.0)
            for i in range(NT):
                pb = self.bank()
                k.mm(pb[:, 0:130], MTd[d][:, i, :], WV[d][:, i, :])
                k.copy(HI[d][:, i, :], pb[:, 0:130], eng=("act" if i % 2 else "dve"))
        self.stage("ml_hi")
        HS = k.sb("mHS" + tag, [128, NT, 128])
        TOTS = [k.sb("mTOTS%d" % i + tag, [128, NT, 130]) for i in range(2)]
        Z = {}
        Zb = {}
        for s in range(nseq):
            for d in range(2):
                Z[(s, d)] = k.sb("mZ%d%d" % (s, d) + tag, [128, 130])
                Zb[(s, d)] = k.sb("mZb%d%d" % (s, d) + tag, [128, 130], BF16)
                if g == 0:
                    k.memset(Z[(s, d)], 0.0)
                else:
                    k.dma(Z[(s, d)], V(self.d_ml0[l, d, h], None))
        for cs in range(cps):
            for s in range(nseq):
                for d in range(2):
                    c = s * cps + (cs if d == 0 else cps - 1 - cs)
                    i, half = c // 2, c % 2
                    tr_ = slice(half * 64, half * 64 + 64)
                    z, zb = Z[(s, d)], Zb[(s, d)]
                    dh = d * 4 + h
                    k.ts(z, z, self.SC[:, c, dh:dh + 1], ALU.mult)
                    k.copy(zb, z, eng="act")
                    pbZ = self.bank()
                    k.mm(pbZ[:, 0:130], KTM[tr_, i, :], WV[d][tr_, i, :])
                    pbS = self.bank()
                    k.mm(pbS[tr_, 0:130], QB[:, c * 64:(c + 1) * 64], zb)
                    k.tt(z, z, pbZ[:, 0:130], ALU.add)
                    k.tt(TOTS[d][tr_, i, :], pbS[tr_, 0:130], HI[d][tr_, i, :], ALU.add)
                    self.stage("ml_c%d_%d" % (c, d))
        DNb = k.sb("mDNb" + tag, [128, 2, NT])
        for d in range(2):
            k.act(DNb[:, d, :], TOTS[d][:, :, 128], AF.Abs)
            k.tt(DNb[:, d, :], DNb[:, d, :], self.CLAMP[:, :, d * 4 + h], ALU.max)
        k.recip(DNb, DNb)
        for d in range(2):
            k.tt(TOTS[d][:, :, 0:128], TOTS[d][:, :, 0:128], DNb[:, d, :].us(2).bc([128, NT, 128]), ALU.mult, eng=("pool" if d else "dve"))
        k.tt(HS, TOTS[0][:, :, 0:128], TOTS[1][:, :, 0:128], ALU.add)
        self.stage("ml_chain")
        if g == 0:
            for s in range(nseq):
                for d in range(2):
                    k.dma(V(self.d_nmc[s, l, d, h], None), Z[(s, d)])
        self.stage("ml_nmc")
        if h == 0:
            self.dump("ml_HS%d%d" % (l, g), HS, [128, NT, 128])
            self.dump("ml_HI%d%d" % (l, g), HI[0], [128, NT, 130])
        self.stage("ml_dump")
        k.tt(HS, HS, OT, ALU.mult)
        HN = k.sb("mHN" + tag, [128, NT, 128])
        ST = k.sb("mST" + tag, [128, 3, NT])
        k.reduce(ST[:, 0, :], HS, ALU.add)
        k.act(HN, HS, AF.Square)
        k.reduce(ST[:, 1, :], HN, ALU.add)
        k.ts(ST[:, 0, :], ST[:, 0, :], 1.0 / 128, ALU.mult)
        k.tt(ST[:, 2, :], ST[:, 0, :], ST[:, 0, :], ALU.mult)
        k.stt(ST[:, 1, :], ST[:, 1, :], 1.0 / 128, ST[:, 2, :], ALU.mult, ALU.subtract)
        k.act(ST[:, 1, :], ST[:, 1, :], AF.Sqrt, bias=self.EPSB)
        k.recip(ST[:, 1, :], ST[:, 1, :])
        k.tt(HN, HS, ST[:, 0, :].us(2).bc([128, NT, 128]), ALU.subtract)
        k.tt(HN, HN, ST[:, 1, :].us(2).bc([128, NT, 128]), ALU.mult)
        OUTF = k.sb("mOUT" + tag, [128, TG])
        for q in range(2):
            pb = self.bank()
            for ii in range(4):
                k.tr(pb[:, ii * 128:(ii + 1) * 128], HN[:, q * 4 + ii, :], self.IDF)
            k.act(OUTF[:, q * 512:(q + 1) * 512], pb[:, 0:512], AF.Identity, scale=self.pf("ml_gn_g", h), bias=self.pf("ml_gn_b", h))
        k.tt(self.mixT[:, 4 + h, :], OUTF, GT, ALU.mult)
        if h == 0:
            self.dump("ml_out%d%d" % (l, g), self.mixT[:, 4, :], [128, TG])
        k.close_scope()

    def attn_prep(self, l, g, h):
        k = self.k
        nseq = self.nseq
        tag = "%d%d%d" % (l, g, h)
        npast = 0 if g == 0 else PAST // 128
        nkt = npast + NT
        c = {"h": h, "nkt": nkt, "npast": npast}
        c["GT"] = GT = k.sb("aGT" + tag, [128, TG])
        c["VBk"] = VBk = k.sb("aVB" + tag, [128, nkt, 128], BF16)
        c["QT"] = QT = k.sb("aQT" + tag, [128, TG], BF16)
        c["KTa"] = KTa = k.sb("aKT" + tag, [128, nkt * 128], BF16)
        self.load_w(self.d_win[l], U_AT[h], 512)
        k.open_scope()
        QKV = k.sb("aQKV" + tag, [128, NT, 384])
        self.proj_tm(0, 384, lambda i, ps: k.copy(QKV[:, i, :], ps, eng=("act" if i % 2 else "dve")))
        self.proj_fm(384, lambda nb, ps: k.act(GT[:, nb * 512:(nb + 1) * 512], ps, AF.Silu))
        if g == 0:
            for s in range(nseq):
                k.dma(V(self.d_nk[s, l, h].rearrange("(i p) c -> p i c", p=128), None), QKV[:, 2 * s:2 * s + 2, 128:256])
                k.dma(V(self.d_nv[s, l, h].rearrange("(i p) c -> p i c", p=128), None), QKV[:, 2 * s:2 * s + 2, 256:384])
        else:
            T = [k.sb("aT%d" % i + tag, [128, NT, 4, 16]) for i in range(4)]
            for off in (0, 128):
                xv = QKV[:, :, off:off + 128].rr("p i (a x t) -> p i a x t", a=4, x=2)
                x1, x2 = xv[:, :, :, 0, :], xv[:, :, :, 1, :]
                k.tt(T[0], x1, self.COS, ALU.mult)
                k.tt(T[1], x2, self.SIN, ALU.mult)
                k.tt(T[2], x2, self.COS, ALU.mult)
                k.tt(T[3], x1, self.SIN, ALU.mult)
                k.tt(x1, T[0], T[1], ALU.subtract)
                k.tt(x2, T[2], T[3], ALU.add)
        QKB = k.sb("aQKB" + tag, [128, NT, 256], BF16)
        k.copy(QKB, QKV[:, :, 0:256])
        k.copy(VBk[:, npast:nkt, :], QKV[:, :, 256:384], eng="pool")
        for which, dst, c0 in ((0, QT, 0), (1, KTa, npast * 128)):
            pbb = self.bank().bitcast(BF16)
            for i in range(NT):
                k.tr(pbb[:, i * 128:(i + 1) * 128], QKB[:, i, which * 128:(which + 1) * 128], self.IDB)
            k.copy(dst[:, c0:c0 + TG], pbb[:, 0:1024], eng=("act" if which else "dve"))
        if g == 1:
            CKB = k.sb("aCKB" + tag, [128, npast, 128], BF16)
            k.dma(CKB, V(self.d_ck[l, h].rearrange("(i p) c -> p i c", p=128), None), q="pool")
            k.dma(VBk[:, 0:npast, :], V(self.d_cv[l, h].rearrange("(i p) c -> p i c", p=128), None), q="pool")
            pbb = self.bank().bitcast(BF16)
            for i in range(npast):
                k.tr(pbb[:, i * 128:(i + 1) * 128], CKB[:, i, :], self.IDB)
            k.copy(KTa[:, 0:npast * 128], pbb[:, 0:npast * 128])
        k.close_scope()
        return c

    def unit_attn2(self, l, g, h0):
        k = self.k
        nseq = self.nseq
        scale = 64.0 ** -0.5
        k.open_scope()
        ctxs = [self.attn_prep(l, g, h0 + j) for j in range(2)]
        for j, c in enumerate(ctxs):
            tag = "%d%d%d" % (l, g, c["h"])
            nkt = c["nkt"]
            c["E"] = [k.sb("aE%d" % b + tag, [128, nkt * 128], BF16) for b in range(2)]
            c["ET"] = [k.sb("aET%d" % b + tag, [128, nkt, 128], BF16) for b in range(2)]
            c["SMq"] = [[k.sb("aSM%d%d" % (a, b) + tag, [128, 4]) for b in range(2)] for a in range(2)]
            c["MXp"] = [k.sb("aMX%d" % b + tag, [128, 4]) for b in range(2)]
            c["NBp"] = [k.sb("aNB%d" % b + tag, [128, 1]) for b in range(2)]
            c["RS"] = k.sb("aRS" + tag, [128, 4])
            c["O2"] = k.sb("aO2" + tag, [128, 128])
            c["OD"] = k.sb("aOD" + tag, [128, 128])
            c["JK"] = k.sb("aJK" + tag, [128, 128])
            c["OT"] = k.sb("aOT" + tag, [128, 128])
            c["abank"] = j * 3
            c["ocol"] = j * 256
        pbO = self.PB[7]
        pbT = self.PB[6]
        items = [(i, br) for i in range(NT) for br in range(2)]

        def keys_of(c, i):
            if g == 0:
                sq = i // (NT // nseq)
                return [2 * sq, 2 * sq + 1]
            return list(range(c["nkt"]))

        def stage_a(c, kidx):
            i, br = items[kidx]
            par = kidx % 2
            kts = keys_of(c, i)
            k0 = kts[0] * 128
            ncols = len(kts) * 128
            chunks = [(c0, min(512, ncols - c0)) for c0 in range(0, ncols, 512)]
            brs = slice(br * 64, br * 64 + 64)
            banks = [self.PB[c["abank"] + ci] for ci in range(len(chunks))]
            for ci, (c0, cn) in enumerate(chunks):
                k.mm(banks[ci][:, 0:cn], c["QT"][brs, i * 128:(i + 1) * 128], c["KTa"][brs, k0 + c0:k0 + c0 + cn])
            for ci, (c0, cn) in enumerate(chunks):
                k.reduce(c["MXp"][par][:, ci:ci + 1], banks[ci][:, 0:cn], ALU.max)
            k.reduce(c["NBp"][par], c["MXp"][par][:, 0:len(chunks)], ALU.max)
            k.ts(c["NBp"][par], c["NBp"][par], -scale, ALU.mult)
            for ci, (c0, cn) in enumerate(chunks):
                k.act(c["E"][par][:, c0:c0 + cn], banks[ci][:, 0:cn], AF.Exp, scale=scale, bias=c["NBp"][par],
                      accum=c["SMq"][i % 2][br][:, ci:ci + 1])

        def stage_b(c, kidx):
            i, br = items[kidx]
            par = kidx % 2
            kts = keys_of(c, i)
            nk = len(kts)
            oc = c["ocol"] + br * 128
            for q0 in range(0, nk, 8):
                qn = min(8, nk - q0)
                pbb = pbT.bitcast(BF16)
                for jj in range(qn):
                    k.tr(pbb[:, jj * 128:(jj + 1) * 128], c["E"][par][:, (q0 + jj) * 128:(q0 + jj + 1) * 128], self.IDB)
                k.copy(c["ET"][par][:, q0:q0 + qn, :].rr("p a b -> p (a b)"), pbb[:, 0:qn * 128], eng=("act" if qn == 8 else "dve"))
            for jj in range(nk):
                k.mm(pbO[:, oc:oc + 128], c["ET"][par][:, jj, :], c["VBk"][:, kts[jj], :], start=(jj == 0), stop=(jj == nk - 1))

        def tail(c, i):
            qp = i % 2
            RS, O2, OD, JK, OTt = c["RS"], c["O2"], c["OD"], c["JK"], c["OT"]
            oc = c["ocol"]
            nch = (len(keys_of(c, i)) * 128 + 511) // 512
            for br in range(2):
                k.reduce(RS[:, br:br + 1], c["SMq"][qp][br][:, 0:nch], ALU.add)
            k.recip(RS[:, 0:2], RS[:, 0:2])
            k.tt(RS[:, 1:2], RS[:, 1:2], self.LAM[:, 0:1], ALU.mult)
            k.act(O2, pbO[:, oc + 128:oc + 256], AF.Identity, scale=RS[:, 1:2])
            k.stt(OD, pbO[:, oc:oc + 128], RS[:, 0:1], O2, ALU.mult, ALU.subtract)
            k.act(JK, OD, AF.Square, accum=RS[:, 2:3])
            k.act(RS[:, 2:3], RS[:, 2:3], AF.Sqrt, scale=1.0 / 128, bias=self.EPSB)
            k.recip(RS[:, 2:3], RS[:, 2:3])
            k.ts(OD, OD, RS[:, 2:3], ALU.mult)
            k.tr(pbT[:, 0:128], OD, self.IDF)
            k.act(OTt, pbT[:, 0:128], AF.Identity, scale=self.SUBG)
            k.tt(self.mixT[:, 8 + c["h"], i * 128:(i + 1) * 128], OTt, c["GT"][:, i * 128:(i + 1) * 128], ALU.mult, eng="pool")

        for c in ctxs:
            stage_a(c, 0)
        for kidx in range(len(items)):
            if kidx + 1 < len(items):
                for c in ctxs:
                    stage_a(c, kidx + 1)
            for c in ctxs:
                stage_b(c, kidx)
                if items[kidx][1] == 1:
                    tail(c, items[kidx][0])
        if h0 == 0:
            self.dump("at_out%d%d" % (l, g), self.mixT[:, 8, :], [128, TG])
        k.close_scope()

    def unit_attn3(self, l, g, h):
        k = self.k
        nseq = self.nseq
        scale = 64.0 ** -0.5
        tag = "%d%d%d" % (l, g, h)
        npast = 0 if g == 0 else PAST // 128
        nkt = npast + NT
        nkl = 2 if g == 0 else nkt
        k.open_scope()
        GT = k.sb("aGT" + tag, [128, TG])
        VP = k.sb("aVP" + tag, [128, nkt, 130], BF16)
        QT = k.sb("aQT" + tag, [128, TG], BF16)
        KTa = k.sb("aKT" + tag, [128, nkt * 128], BF16)
        NB = k.sb("aNB" + tag, [128, 2])
        self.load_w(self.d_win[l], U_AT[h], 512)
        k.open_scope()
        QKV = k.sb("aQKV" + tag, [128, NT, 384])
        self.proj_tm(0, 384, lambda i, ps: k.copy(QKV[:, i, :], ps, eng=("act" if i % 2 else "dve")))
        self.proj_fm(384, lambda nb, ps: k.act(GT[:, nb * 512:(nb + 1) * 512], ps, AF.Silu))
        if g == 0:
            for s in range(nseq):
                k.dma(V(self.d_nk[s, l, h].rearrange("(i p) c -> p i c", p=128), None), QKV[:, 2 * s:2 * s + 2, 128:256])
                k.dma(V(self.d_nv[s, l, h].rearrange("(i p) c -> p i c", p=128), None), QKV[:, 2 * s:2 * s + 2, 256:384])
        else:
            T = [k.sb("aT%d" % i + tag, [128, NT, 4, 16]) for i in range(4)]
            for off in (0, 128):
                xv = QKV[:, :, off:off + 128].rr("p i (a x t) -> p i a x t", a=4, x=2)
                x1, x2 = xv[:, :, :, 0, :], xv[:, :, :, 1, :]
                k.tt(T[0], x1, self.COS, ALU.mult)
                k.tt(T[1], x2, self.SIN, ALU.mult)
                k.tt(T[2], x2, self.COS, ALU.mult)
                k.tt(T[3], x1, self.SIN, ALU.mult)
                k.tt(x1, T[0], T[1], ALU.subtract)
                k.tt(x2, T[2], T[3], ALU.add)
        QKB = k.sb("aQKB" + tag, [128, NT, 256], BF16)
        k.copy(QKB, QKV[:, :, 0:256])
        k.copy(VP[:, npast:nkt, 0:128], QKV[:, :, 256:384], eng="pool")
        k.memset(VP[:, :, 128:129], 1.0)
        k.memset(VP[:, :, 129:130], 0.0)
        for which, dst, c0 in ((0, QT, 0), (1, KTa, npast * 128)):
            pbb = self.bank().bitcast(BF16)
            for i in range(NT):
                k.tr(pbb[:, i * 128:(i + 1) * 128], QKB[:, i, which * 128:(which + 1) * 128], self.IDB)
            k.copy(dst[:, c0:c0 + TG], pbb[:, 0:1024], eng=("act" if which else "dve"))
        SQ = k.sb("aSQ" + tag, [128, NT, 256])
        N2 = k.sb("aN2" + tag, [128, NT, 4])
        M4 = k.sb("aM4" + tag, [128, 4])
        k.act(SQ, QKV[:, :, 0:256], AF.Square)
        k.reduce(N2, SQ.rr("p i (a d) -> p i a d", a=4), ALU.add)
        k.reduce(M4, N2.rr("p i a -> p a i"), ALU.max)
        if g == 1:
            CKB = k.sb("aCKB" + tag, [128, npast, 128], BF16)
            k.dma(CKB, V(self.d_ck[l, h].rearrange("(i p) c -> p i c", p=128), None), q="pool")
            k.dma(VP[:, 0:npast, 0:128], V(self.d_cv[l, h].rearrange("(i p) c -> p i c", p=128), None), q="pool")
            pbb = self.bank().bitcast(BF16)
            for i in range(npast):
                k.tr(pbb[:, i * 128:(i + 1) * 128], CKB[:, i, :], self.IDB)
            k.copy(KTa[:, 0:npast * 128], pbb[:, 0:npast * 128])
            CSQ = k.sb("aCSQ" + tag, [128, npast, 128])
            CN2 = k.sb("aCN2" + tag, [128, npast, 2])
            CM = k.sb("aCM" + tag, [128, 2])
            k.act(CSQ, CKB, AF.Square)
            k.reduce(CN2, CSQ.rr("p i (a d) -> p i a d", a=2), ALU.add)
            k.reduce(CM, CN2.rr("p i a -> p a i"), ALU.max)
            k.tt(M4[:, 2:4], M4[:, 2:4], CM, ALU.max)
        pbm = self.bank()
        k.tr(pbm[0:4, 0:128], M4, self.IDF)
        MC = k.sb("aMC" + tag, [4, 1])
        k.reduce(MC, pbm[0:4, 0:128], ALU.max)
        RH = k.sb("aRH" + tag, [4, 4])
        k.ts(RH, self.IDF[0:4, 0:4], MC[0:4, 0:1], ALU.mult)
        pbr = self.bank()
        k.mm(pbr[:, 0:4], self.ONES[0:4, :], RH)
        MR = k.sb("aMR" + tag, [128, 4])
        k.copy(MR, pbr[:, 0:4])
        k.tt(NB, MR[:, 0:2], MR[:, 2:4], ALU.mult)
        k.act(NB, NB, AF.Sqrt)
        k.ts(NB, NB, -scale, ALU.mult)
        k.close_scope()
        ETs = [k.sb("aET%d" % b + tag, [128, nkl, TG], BF16) for b in range(2)]
        O1S = k.sb("aO1" + tag, [128, NT, 130])
        RS = k.sb("aRS" + tag, [128, 4])
        O2 = k.sb("aO2" + tag, [128, 128])
        OD = k.sb("aOD" + tag, [128, 128])
        JK = k.sb("aJK" + tag, [128, 128])
        OTt = k.sb("aOT" + tag, [128, 128])
        qblk = 512 if g == 1 else TSEQ_P

        def qk_items(br):
            brs = slice(br * 64, br * 64 + 64)
            for qb in range(TG // qblk):
                qs = slice(qb * qblk, (qb + 1) * qblk)
                kt0 = 0 if g == 1 else 2 * qb
                for j in range(nkl):
                    yield (brs, qs, kt0, j)

        def qk_exp(br, it):
            brs, qs, kt0, j = it
            pb = self.PB[self.abank % 6]
            self.abank += 1
            k.mm(pb[:, 0:qblk], KTa[brs, (kt0 + j) * 128:(kt0 + j + 1) * 128], QT[brs, qs])
            k.act(ETs[br][:, j, qs], pb[:, 0:qblk], AF.Exp, scale=scale, bias=NB[:, br:br + 1])

        def pv(br, i):
            ET = ETs[br]
            kt0 = 0 if g == 1 else 2 * (i // 2)
            pbO = self.PB[6 + i % 2]
            for j in range(nkl):
                k.mm(pbO[:, 0:130], ET[:, j, i * 128:(i + 1) * 128], VP[:, kt0 + j, :], start=(j == 0), stop=(j == nkl - 1))
            if br == 0:
                k.copy(O1S[:, i, :], pbO[:, 0:130], eng=("act" if i % 2 else "dve"))
            else:
                k.copy(RS[:, 0:1], O1S[:, i, 128:129])
                k.copy(RS[:, 1:2], pbO[:, 128:129])
                k.recip(RS[:, 0:2], RS[:, 0:2])
                k.tt(RS[:, 1:2], RS[:, 1:2], self.LAM[:, 0:1], ALU.mult)
                k.act(O2, pbO[:, 0:128], AF.Identity, scale=RS[:, 1:2])
                k.stt(OD, O1S[:, i, 0:128], RS[:, 0:1], O2, ALU.mult, ALU.subtract)
                k.act(JK, OD, AF.Square, accum=RS[:, 2:3])
                k.act(RS[:, 2:3], RS[:, 2:3], AF.Sqrt, scale=1.0 / 128, bias=self.EPSB)
                k.recip(RS[:, 2:3], RS[:, 2:3])
                k.ts(OD, OD, RS[:, 2:3], ALU.mult)
                pbt = self.PB[self.abank % 6]
                self.abank += 1
                k.tr(pbt[:, 0:128], OD, self.IDF)
                k.act(OTt, pbt[:, 0:128], AF.Identity, scale=self.SUBG)
                k.tt(self.mixT[:, 8 + h, i * 128:(i + 1) * 128], OTt, GT[:, i * 128:(i + 1) * 128], ALU.mult, eng="pool")

        for it in qk_items(0):
            qk_exp(0, it)
        its1 = list(qk_items(1))
        per = max(1, len(its1) // NT)
        nxt = 0
        for n, it in enumerate(its1):
            qk_exp(1, it)
            if (n + 1) % per == 0 and nxt < NT:
                pv(0, nxt)
                nxt += 1
        while nxt < NT:
            pv(0, nxt)
            nxt += 1
        for i in range(NT):
            pv(1, i)
        if h == 0:
            self.dump("at_out%d%d" % (l, g), self.mixT[:, 8, :], [128, TG])
        k.close_scope()

    def phase_o(self, l, g):
        k = self.k
        k.open_scope()
        WO = k.sb("oWO", [128, KC, D], BF16)
        for s in range(4):
            src = self.d_wout[l].rearrange("(kc p) c -> p kc c", p=128)[:, :, s * 512:(s + 1) * 512]
            k.dma(WO[:, :, s * 512:(s + 1) * 512], V(src, None), q="pool")
        GBC = k.sb("oGBC", [128, D])
        LG = k.sb("oLG", [128, D])
        LB = k.sb("oLB", [128, D])
        k.dma(GBC, V(self.d_modg[l, g:g + 1, :].partition_broadcast(128).rearrange("p a d -> p (a d)"), self.modgbuf[l][g]))
        k.dma(LG, V(self.d_lng[l:l + 1, :].partition_broadcast(128).rearrange("p a d -> p (a d)"), None))
        k.dma(LB, V(self.d_lnb[l:l + 1, :].partition_broadcast(128).rearrange("p a d -> p (a d)"), None))
        XT = k.sb("oXT", [128, D])
        VV = k.sb("oVV", [128, D])
        JK = k.sb("oJK", [128, D], BF16)
        ST = k.sb("oST", [128, 4])
        for i in range(NT):
            k.dma(XT, self.xsrc(l, g, i))
            for s in range(4):
                pb = self.bank()
                for kc in range(KC):
                    k.mm(pb[:, 0:512], self.mixT[:, kc, i * 128:(i + 1) * 128], WO[:, kc, s * 512:(s + 1) * 512], start=(kc == 0), stop=(kc == KC - 1))
                k.tt(VV[:, s * 512:(s + 1) * 512], pb[:, 0:512], GBC[:, s * 512:(s + 1) * 512], ALU.mult)
            k.stt(VV, XT, ALPHA, VV, ALU.mult, ALU.add)
            k.act(JK, VV, AF.Identity, accum=ST[:, 0:1])
            k.act(JK, VV, AF.Square, accum=ST[:, 1:2])
            k.ts(ST[:, 0:1], ST[:, 0:1], 1.0 / D, ALU.mult)
            k.tt(ST[:, 2:3], ST[:, 0:1], ST[:, 0:1], ALU.mult)
            k.stt(ST[:, 1:2], ST[:, 1:2], 1.0 / D, ST[:, 2:3], ALU.mult, ALU.subtract)
            k.act(ST[:, 1:2], ST[:, 1:2], AF.Sqrt, bias=self.EPSB)
            k.recip(ST[:, 1:2], ST[:, 1:2])
            k.ts(VV, VV, ST[:, 0:1], ALU.subtract, ST[:, 1:2], ALU.mult)
            k.tt(VV, VV, LG, ALU.mult)
            k.tt(VV, VV, LB, ALU.add)
            if l == DEPTH - 1:
                dst = V(self.d_yout[g, i * 128:(i + 1) * 128, :], None)
            else:
                dst = V(self.d_x1[g, i * 128:(i + 1) * 128, :], self.x1bufs[g][i])
            k.dma(dst, VV)
        k.close_scope()


def _shared_inputs(inp):
    perm = _perm_cols()
    sh = {}
    sh["w_ada"] = np.ascontiguousarray(inp["w_ada"], dtype=np.float32)
    sh["b_ada"] = np.ascontiguousarray(inp["b_ada"], dtype=np.float32)
    sh["w_in"] = np.ascontiguousarray(inp["w_in"][:, :, perm], dtype=np.float32)
    sh["w_out"] = np.ascontiguousarray(inp["w_out"], dtype=np.float32)
    sh["pfm"] = np.stack([_pack_pfm(inp, l) for l in range(DEPTH)])
    sh["prow"] = np.stack([_pack_prow(inp, l) for l in range(DEPTH)])
    sh["ln_g"] = np.ascontiguousarray(inp["ln_g"], dtype=np.float32)
    sh["ln_b"] = np.ascontiguousarray(inp["ln_b"], dtype=np.float32)
    wup = np.zeros((DEPTH, 2, 128, 512), np.float32)
    wup[:, 0] = inp["rwkv_w_up"].reshape(DEPTH, 128, 512)
    wup[:, 1] = inp["rwkv_a_up"].reshape(DEPTH, 128, 512)
    sh["wup"] = wup
    sh["consts"] = _consts()
    return sh


def _core_inputs(inp, c, sh):
    sb = c % 2
    m = dict(sh)
    xin = np.empty((2, TG, D), np.float32)
    xin[0] = inp["x_prompt"][4 * c:4 * c + 4].reshape(TG, D)
    xin[1] = inp["x_sample"][sb]
    m["xin"] = xin
    cT = np.empty((128, KC, 2), np.float32)
    cT[:, :, 0] = inp["c_ctx"].reshape(KC, 128).T
    cT[:, :, 1] = inp["c"][sb].reshape(KC, 128).T
    m["cT"] = cT.reshape(128, 32)
    rw = inp["state_rwkv"][sb]
    rw0 = np.zeros((DEPTH, 2, 4, 128, 128), np.float32)
    for p in range(4):
        for hh in range(2):
            rw0[:, :, p, hh * 64:(hh + 1) * 64, hh * 64:(hh + 1) * 64] = np.swapaxes(rw[:, :, 2 * p + hh], -1, -2)
    m["rw0"] = rw0
    ml0 = np.zeros((DEPTH, 2, 4, 128, 130), np.float32)
    ml0[..., 0:128] = np.swapaxes(inp["state_mlstm_c"][sb], -1, -2)
    ml0[..., 128] = inp["state_mlstm_n"][sb]
    m["ml0"] = ml0
    m["mm0"] = np.ascontiguousarray(np.broadcast_to(inp["state_mlstm_m"][sb].reshape(DEPTH, 1, 8), (DEPTH, 128, 8)), dtype=np.float32)
    m["ck"] = np.ascontiguousarray(inp["cache_attn_k"][sb], dtype=np.float32)
    m["cv"] = np.ascontiguousarray(inp["cache_attn_v"][sb], dtype=np.float32)
    return m


def _assemble(results):
    B = 8 * NSEQ_P
    y_prompt = np.empty((B, TSEQ_P, D), np.float32)
    y_sample = np.empty((2, TG, D), np.float32)
    new_k = np.empty((B, DEPTH, 8, TSEQ_P, 128), np.float32)
    new_v = np.empty((B, DEPTH, 8, TSEQ_P, 128), np.float32)
    new_rw = np.empty((B, DEPTH, 2, 8, 64, 64), np.float32)
    new_c = np.empty((B, DEPTH, 2, 4, 128, 128), np.float32)
    new_n = np.empty((B, DEPTH, 2, 4, 128), np.float32)
    new_m = np.empty((B, DEPTH, 2, 4), np.float32)
    for c, r in enumerate(results):
        bs = slice(4 * c, 4 * c + 4)
        y_prompt[bs] = r["yout"][0].reshape(4, TSEQ_P, D)
        if c < 2:
            y_sample[c] = r["yout"][1]
        new_k[bs] = r["nk"]
        new_v[bs] = r["nv"]
        nrw = r["nrw"]
        for p in range(4):
            for hh in range(2):
                blk = nrw[:, :, :, p, hh * 64:(hh + 1) * 64, hh * 64:(hh + 1) * 64]
                new_rw[bs, :, :, 2 * p + hh] = np.swapaxes(blk, -1, -2)
        nmc = r["nmc"]
        new_c[bs] = np.swapaxes(nmc[..., 0:128], -1, -2)
        new_n[bs] = nmc[..., 128]
        new_m[bs] = np.transpose(r["nmm"].reshape(DEPTH, 4, 2, 4), (1, 0, 2, 3))
    return (y_prompt, y_sample, new_k, new_v, new_rw, new_c, new_n, new_m)


def kernel(**inputs):
    inp = {k: np.asarray(v) for k, v in inputs.items()}
    prog = Prog()
    nc = prog.build()
    sh = _shared_inputs(inp)
    in_maps = [_core_inputs(inp, c, sh) for c in range(8)]
    res = run_bass_kernel_spmd(nc, in_maps, core_ids=list(range(8)))
    return _assemble(res.results)
```

```python
import math
from contextlib import ExitStack

import numpy as np
import concourse.bass as bass
import concourse.mybir as mybir
from concourse.bass_utils import run_bass_kernel_spmd

F32 = mybir.dt.float32
BF16 = mybir.dt.bfloat16
AF = mybir.ActivationFunctionType
ALU = mybir.AluOpType
AX = mybir.AxisListType

D = 2048
KC = 16
DEPTH = 2
NSEQ_P = 4
TSEQ_P = 256
TG = 1024
NT = 8
NCH = 16
PAST = 512
P_IN = 8976
ALPHA = (2 * DEPTH) ** 0.25
LN_EPS = 1e-5
GN_EPS_A = 64e-5
GN_EPS = 1e-5
RMS_EPS = 1e-5
WDECAY = math.exp(-0.5)

U_LORA = 0
U_RW = [256 + 512 * p for p in range(4)]
U_ML = [2304 + 640 * h for h in range(4)]
U_MG = 2304 + 2560
U_AT = [4880 + 512 * h for h in range(8)]


def _perm_cols():
    perm = []
    perm += list(range(1536, 1792))
    for p in range(4):
        perm += list(range(p * 128, p * 128 + 128))
        perm += list(range(512 + p * 128, 512 + p * 128 + 128))
        perm += list(range(1024 + p * 128, 1024 + p * 128 + 128))
        perm += list(range(1792 + p * 128, 1792 + p * 128 + 128))
    b0 = 2304
    for h in range(4):
        perm += list(range(b0 + h * 128, b0 + h * 128 + 128))
        perm += list(range(b0 + 512 + h * 128, b0 + 512 + h * 128 + 128))
        perm += list(range(4368 + h * 128, 4368 + h * 128 + 128))
        perm += list(range(3328 + h * 128, 3328 + h * 128 + 128))
        perm += list(range(3840 + h * 128, 3840 + h * 128 + 128))
    perm += list(range(4352, 4368))
    for h in range(8):
        perm += list(range(4880 + h * 128, 4880 + h * 128 + 128))
        perm += list(range(5904 + h * 128, 5904 + h * 128 + 128))
        perm += list(range(6928 + h * 128, 6928 + h * 128 + 128))
        perm += list(range(7952 + h * 128, 7952 + h * 128 + 128))
    assert len(perm) == P_IN and len(set(perm)) == P_IN
    return np.array(perm)


C_IDENT = 0
C_MASK = 128
C_ONESBD = C_MASK + 8 * 128
C_ONES = C_ONESBD + 128
C_HM = C_ONES + 128
C_COS = C_HM + 2
C_SIN = C_COS + 512
NCONST = C_SIN + 512


def _consts():
    c = np.zeros((128, NCONST), np.float32)
    c[:, C_IDENT:C_IDENT + 128] = np.eye(128)
    r = np.arange(128)[:, None]
    q = np.arange(128)[None, :]
    same = (r // 64) == (q // 64)
    us = (same & (r < q)).astype(np.float32)
    ui = (same & (r <= q)).astype(np.float32)
    ls = (same & (r > q)).astype(np.float32)
    li = (same & (r >= q)).astype(np.float32)
    for i, m in enumerate([-us, ui, us, ui, -ls, li, ls, li]):
        c[:, C_MASK + i * 128:C_MASK + (i + 1) * 128] = m
    c[:, C_ONESBD:C_ONESBD + 128] = same.astype(np.float32)
    c[:, C_ONES:C_ONES + 128] = 1.0
    c[:64, C_HM] = 1.0
    c[64:, C_HM + 1] = 1.0
    half = 32
    inv = 1.0 / (10000.0 ** (np.arange(0, half, 2, dtype=np.float32) / half))
    t = (np.arange(8)[None, :] * 128 + np.arange(128)[:, None]).astype(np.float32)
    row = np.floor(t / 64.0)
    col = t - row * 64.0
    cos = np.zeros((128, 8, 4, 16), np.float32)
    sin = np.zeros((128, 8, 4, 16), np.float32)
    for br in range(2):
        for rc, pos in enumerate([row, col]):
            ang = pos[:, :, None] * inv[None, None, :]
            cos[:, :, br * 2 + rc, :] = np.cos(ang)
            sin[:, :, br * 2 + rc, :] = np.sin(ang)
    c[:, C_COS:C_COS + 512] = cos.reshape(128, 512)
    c[:, C_SIN:C_SIN + 512] = sin.reshape(128, 512)
    return c


PF = {}
_n = 0
for _p in range(4):
    for _j in range(3):
        for _tap in range(3):
            PF[("ca_w", _p, _j, _tap)] = _n; _n += 1
        PF[("ca_b", _p, _j)] = _n; _n += 1
    for _d in range(2):
        PF[("w0", _p, _d)] = _n; _n += 1
        PF[("a0", _p, _d)] = _n; _n += 1
    for _nm in ("k_k", "k_a", "omk_a", "r_k", "gn_g", "gn_b"):
        PF[("rw_" + _nm, _p)] = _n; _n += 1
for _h in range(4):
    for _j in range(2):
        for _tap in range(3):
            PF[("cb_w", _h, _j, _tap)] = _n; _n += 1
        PF[("cb_b", _h, _j)] = _n; _n += 1
    PF[("ml_gn_g", _h)] = _n; _n += 1
    PF[("ml_gn_b", _h)] = _n; _n += 1
PF[("subln",)] = _n; _n += 1
NPF = _n

PR_BI = 0
PR_BF = 8
PR_LQ1 = 16
PR_LK1 = 80
PR_LQ2 = 144
PR_LK2 = 208
NPR = 272


def _pack_pfm(inp, l):
    o = np.zeros((128, NPF), np.float32)
    for p in range(4):
        sl = slice(p * 128, p * 128 + 128)
        for j in range(3):
            for tap in range(3):
                o[:, PF[("ca_w", p, j, tap)]] = inp["conv_a_w"][l, tap, j * 512 + p * 128: j * 512 + p * 128 + 128]
            o[:, PF[("ca_b", p, j)]] = inp["conv_a_b"][l, j * 512 + p * 128: j * 512 + p * 128 + 128]
        for d in range(2):
            o[:, PF[("w0", p, d)]] = inp["rwkv_w0"][l, d, sl]
            o[:, PF[("a0", p, d)]] = inp["rwkv_a0"][l, d, sl]
        o[:, PF[("rw_k_k", p)]] = inp["rwkv_k_k"][l, sl]
        o[:, PF[("rw_k_a", p)]] = inp["rwkv_k_a"][l, sl]
        o[:, PF[("rw_r_k", p)]] = inp["rwkv_r_k"][l].reshape(512)[sl]
        o[:, PF[("rw_gn_g", p)]] = inp["rwkv_gn_g"][l, sl]
        o[:, PF[("rw_gn_b", p)]] = inp["rwkv_gn_b"][l, sl]
    for h in range(4):
        sl = slice(h * 128, h * 128 + 128)
        for j in range(2):
            for tap in range(3):
                o[:, PF[("cb_w", h, j, tap)]] = inp["conv_b_w"][l, tap, j * 512 + h * 128: j * 512 + h * 128 + 128]
            o[:, PF[("cb_b", h, j)]] = inp["conv_b_b"][l, j * 512 + h * 128: j * 512 + h * 128 + 128]
        o[:, PF[("ml_gn_g", h)]] = inp["mlstm_gn_g"][l, sl]
        o[:, PF[("ml_gn_b", h)]] = inp["mlstm_gn_b"][l, sl]
    o[:, PF[("subln",)]] = inp["diff_subln_g"][l]
    return o


def _pack_prow(inp, l):
    o = np.zeros((128, NPR), np.float32)
    o[:, PR_BI:PR_BI + 8] = inp["mlstm_b_i"][l].reshape(8)[None, :]
    o[:, PR_BF:PR_BF + 8] = inp["mlstm_b_f"][l].reshape(8)[None, :]
    o[:, PR_LQ1:PR_LQ1 + 64] = inp["diff_lq1"][l][None, :]
    o[:, PR_LK1:PR_LK1 + 64] = inp["diff_lk1"][l][None, :]
    o[:, PR_LQ2:PR_LQ2 + 64] = inp["diff_lq2"][l][None, :]
    o[:, PR_LK2:PR_LK2 + 64] = inp["diff_lk2"][l][None, :]
    return o


ENGS = ("pe", "act", "dve", "pool", "sp")
SAME_ENGINE_SYNC = True
SELF_RAW_ONLY = True


class Buf:
    __slots__ = ("name", "w", "r", "dsem", "excl")

    def __init__(self, name, init=None):
        self.name = name
        self.excl = False
        self.w = dict(init) if init else {}
        self.r = {}
        self.dsem = None


class Sched:
    def __init__(self, nc, es, n_dma_sems=90):
        self.nc = nc
        self.eng = {"pe": nc.tensor, "act": nc.scalar, "dve": nc.vector, "pool": nc.gpsimd, "sp": nc.sync}
        self.cnt = {}
        self.waited = {e: {} for e in ENGS}
        self.sem = {}
        for e in ENGS:
            self.sem[e] = es.enter_context(nc.semaphore("s_" + e))
            self.cnt[e] = 0
        self.free_dsems = [es.enter_context(nc.semaphore("d%d" % i)) for i in range(n_dma_sems)]
        self.n_dsem = 0
        self.nops = {e: 0 for e in ENGS}
        self.nwaits = {e: 0 for e in ENGS}
        self.fence = {}
        self.all_dma_events = {}
        self.recycled = []

    def newbuf(self, name, fenced=True):
        return Buf(name, self.fence if fenced else None)

    def close_scope(self, bufs):
        for b in bufs:
            for dct in (b.w, b.r):
                for k, v in dct.items():
                    if self.fence.get(k, 0) < v:
                        self.fence[k] = v
            if b.dsem is not None:
                self.recycled.append(b.dsem)

    def _dsem_for(self, b):
        if b.dsem is None:
            if self.recycled:
                key = self.recycled.pop()
            else:
                key = "D%d" % self.n_dsem
                self.sem[key] = self.free_dsems[self.n_dsem]
                self.n_dsem += 1
                self.cnt[key] = 0
            b.dsem = key
        return b.dsem

    def _emit_wait(self, ename, k, v):
        self.eng[ename].wait_ge(self.sem[k], v)
        self.nwaits[ename] += 1

    def _waits(self, eng, reads, writes):
        deps = {}
        for b in reads:
            for k, v in b.w.items():
                if deps.get(k, 0) < v:
                    deps[k] = v
        for b in writes:
            for k, v in b.w.items():
                if k == eng and SELF_RAW_ONLY:
                    continue
                if deps.get(k, 0) < v:
                    deps[k] = v
            for k, v in b.r.items():
                if k == eng and SELF_RAW_ONLY:
                    continue
                if deps.get(k, 0) < v:
                    deps[k] = v
        wd = self.waited[eng]
        for k, v in deps.items():
            if k == eng and (eng == "pe" or not SAME_ENGINE_SYNC):
                continue
            if wd.get(k, 0) >= v:
                continue
            wd[k] = v
            self._emit_wait(eng, k, v)

    def op(self, eng, fn, reads=(), writes=()):
        writes = [b for b in writes if b is not None] + [b for b in reads if b is not None and b.excl]
        reads = [b for b in reads if b is not None and not b.excl]
        self._waits(eng, reads, writes)
        self.cnt[eng] += 1
        n = self.cnt[eng]
        inst = fn(self.eng[eng])
        inst.then_inc(self.sem[eng], 1)
        self.nops[eng] += 1
        for b in reads:
            if b.r.get(eng, 0) < n:
                b.r[eng] = n
        for b in writes:
            b.w = {eng: n}
            b.r = {}

    def dma(self, qeng, fn, reads=(), writes=(), sembuf=None):
        reads = [b for b in reads if b is not None]
        writes = [b for b in writes if b is not None]
        self._waits(qeng, reads, writes)
        if sembuf is None:
            sembuf = writes[0] if writes else reads[0]
        key = self._dsem_for(sembuf)
        self.cnt[key] += 16
        n = self.cnt[key]
        inst = fn(self.eng[qeng])
        inst.then_inc(self.sem[key], 16)
        self.nops[qeng] += 1
        self.all_dma_events[key] = n
        for b in reads:
            if b.r.get(key, 0) < n:
                b.r[key] = n
        for b in writes:
            b.w = {key: n}
            b.r = {}

    def finish(self):
        for k, v in self.all_dma_events.items():
            if self.waited["sp"].get(k, 0) < v:
                self.waited["sp"][k] = v
                self._emit_wait("sp", k, v)
        for e in ("pe", "act", "dve", "pool"):
            if self.cnt[e] > 0 and self.waited["sp"].get(e, 0) < self.cnt[e]:
                self._emit_wait("sp", e, self.cnt[e])


class V:
    __slots__ = ("ap", "b")

    def __init__(self, ap, b):
        self.ap = ap
        self.b = b

    def __getitem__(self, idx):
        return V(self.ap[idx], self.b)

    def rr(self, pat, **kw):
        return V(self.ap.rearrange(pat, **kw), self.b)

    def bc(self, shape):
        return V(self.ap.broadcast_to(shape), self.b)

    def us(self, axis):
        return V(self.ap.unsqueeze(axis), self.b)

    def bitcast(self, dt):
        return V(self.ap.bitcast(dt), self.b)


class K:
    def __init__(self, nc, es):
        self.nc = nc
        self.S = Sched(nc, es)
        self.scopes = []

    def open_scope(self):
        es = ExitStack()
        es.__enter__()
        self.scopes.append((es, []))

    def close_scope(self):
        es, bufs = self.scopes.pop()
        self.S.close_scope(bufs)
        es.__exit__(None, None, None)

    _uid = 0

    def sb(self, name, shape, dt=F32):
        es, bufs = self.scopes[-1]
        K._uid += 1
        name = "%s_%d" % (name, K._uid)
        h = es.enter_context(self.nc.sbuf_tensor(name, list(shape), dt))
        b = self.S.newbuf(name)
        bufs.append(b)
        return V(h.ap(), b)

    def ps(self, name, shape, dt=F32):
        es, bufs = self.scopes[-1]
        h = es.enter_context(self.nc.psum_tensor(name, list(shape), dt))
        b = self.S.newbuf(name)
        bufs.append(b)
        return V(h.ap(), b)

    def dram(self, ap, tracked=False, name="dram"):
        return V(ap, self.S.newbuf(name, fenced=False) if tracked else None)

    def act(self, out, in_, func, bias=None, scale=None, accum=None, eng="act"):
        kw = {}
        reads = [in_.b]
        if bias is not None:
            if isinstance(bias, V):
                kw["bias"] = bias.ap
                reads.append(bias.b)
            else:
                kw["bias"] = float(bias)
        if scale is not None:
            if isinstance(scale, V):
                kw["scale"] = scale.ap
                reads.append(scale.b)
            else:
                kw["scale"] = float(scale)
        writes = [out.b]
        if accum is not None:
            kw["accum_out"] = accum.ap
            writes.append(accum.b)
        self.S.op("act", lambda e: e.activation(out=out.ap, in_=in_.ap, func=func, **kw), reads, writes)

    def tt(self, out, in0, in1, op, eng="dve"):
        self.S.op(eng, lambda e: e.tensor_tensor(out=out.ap, in0=in0.ap, in1=in1.ap, op=op), [in0.b, in1.b], [out.b])

    def ts(self, out, in0, s1, op0, s2=None, op1=None, accum=None, eng="dve"):
        reads = [in0.b]
        a1 = s1.ap if isinstance(s1, V) else float(s1)
        if isinstance(s1, V):
            reads.append(s1.b)
        kw = {}
        if s2 is not None:
            a2 = s2.ap if isinstance(s2, V) else float(s2)
            if isinstance(s2, V):
                reads.append(s2.b)
            kw["op1"] = op1
        else:
            a2 = None
        writes = [out.b]
        if accum is not None:
            kw["accum_out"] = accum.ap
            writes.append(accum.b)
            if op1 is not None:
                kw["op1"] = op1
        self.S.op(eng, lambda e: e.tensor_scalar(out=out.ap, in0=in0.ap, scalar1=a1, scalar2=a2, op0=op0, **kw), reads, writes)

    def stt(self, out, in0, scalar, in1, op0, op1):
        reads = [in0.b, in1.b]
        sc = scalar.ap if isinstance(scalar, V) else float(scalar)
        if isinstance(scalar, V):
            reads.append(scalar.b)
        self.S.op("dve", lambda e: e.scalar_tensor_tensor(out=out.ap, in0=in0.ap, scalar=sc, in1=in1.ap, op0=op0, op1=op1), reads, [out.b])

    def copy(self, out, in_, eng="dve"):
        if eng == "act":
            self.act(out, in_, AF.Copy)
        else:
            self.S.op(eng, lambda e: e.tensor_copy(out=out.ap, in_=in_.ap), [in_.b], [out.b])

    def memset(self, out, val, eng="pool"):
        self.S.op(eng, lambda e: e.memset(out.ap, float(val)), [], [out.b])

    def reduce(self, out, in_, op, axis=AX.X, eng="dve"):
        self.S.op(eng, lambda e: e.tensor_reduce(out=out.ap, in_=in_.ap, axis=axis, op=op), [in_.b], [out.b])

    def recip(self, out, in_):
        self.S.op("dve", lambda e: e.reciprocal(out=out.ap, in_=in_.ap), [in_.b], [out.b])

    def scan(self, out, d0, d1, initial, op0, op1):
        self.S.op("dve", lambda e: e.tensor_tensor_scan(out=out.ap, data0=d0.ap, data1=d1.ap, initial=float(initial), op0=op0, op1=op1), [d0.b, d1.b], [out.b])

    def mm(self, out, lhsT, rhs, start=True, stop=True):
        self.S.op("pe", lambda e: e.matmul(out.ap, lhsT=lhsT.ap, rhs=rhs.ap, start=start, stop=stop), [lhsT.b, rhs.b], [out.b])

    def tr(self, out, in_, ident):
        self.S.op("pe", lambda e: e.transpose(out=out.ap, in_=in_.ap, identity=ident.ap), [in_.b, ident.b], [out.b])

    def dma(self, out, in_, q="sp"):
        wr = [out.b]
        rd = [in_.b]
        out_is_dram = "DRAM" in str(out.ap.space).upper()
        sembuf = in_.b if (out_is_dram and in_.b is not None) else out.b
        self.S.dma(q, lambda e: e.dma_start(out=out.ap, in_=in_.ap), rd, wr, sembuf=sembuf)


class _Stop(Exception):
    pass


class Prog:
    stop_stage = None

    def stage(self, name):
        if self.stop_stage is not None and name == self.stop_stage:
            raise _Stop()

    def run_unit(self, fn, *a):
        depth = len(self.k.scopes)
        try:
            fn(*a)
        except _Stop:
            while len(self.k.scopes) > depth:
                self.k.close_scope()

    def __init__(self, dbg=False, layers=(0, 1), groups=(0, 1), parts=None):
        self.dbg = dbg
        self.layers = layers
        self.groups = groups
        self.parts = parts
        self.dumps = {}
        self.bank_i = 0
        self.tbank = 0
        self.abank = 0

    def want(self, part):
        return self.parts is None or part in self.parts

    def declare(self):
        nc = self.nc
        di = lambda n, s: nc.dram_tensor(n, list(s), F32, kind="ExternalInput").ap()
        do = lambda n, s: nc.dram_tensor(n, list(s), F32, kind="ExternalOutput").ap()
        dint = lambda n, s: nc.dram_tensor(n, list(s), F32, kind="Internal").ap()
        self.d_xin = di("xin", [2, TG, D])
        self.d_cT = di("cT", [128, 32])
        self.d_wada = di("w_ada", [2, D, 3 * D])
        self.d_bada = di("b_ada", [2, 3 * D])
        self.d_win = di("w_in", [2, D, P_IN])
        self.d_wout = di("w_out", [2, D, D])
        self.d_pfm = di("pfm", [2, 128, NPF])
        self.d_prow = di("prow", [2, 128, NPR])
        self.d_lng = di("ln_g", [2, D])
        self.d_lnb = di("ln_b", [2, D])
        self.d_wup = di("wup", [2, 2, 128, 512])
        self.d_rw0 = di("rw0", [2, 2, 4, 128, 128])
        self.d_ml0 = di("ml0", [2, 2, 4, 128, 130])
        self.d_mm0 = di("mm0", [2, 128, 8])
        self.d_ck = di("ck", [2, 8, PAST, 128])
        self.d_cv = di("cv", [2, 8, PAST, 128])
        self.d_consts = di("consts", [128, NCONST])
        self.d_yout = do("yout", [2, TG, D])
        self.d_nk = do("nk", [4, 2, 8, TSEQ_P, 128])
        self.d_nv = do("nv", [4, 2, 8, TSEQ_P, 128])
        self.d_nrw = do("nrw", [4, 2, 2, 4, 128, 128])
        self.d_nmc = do("nmc", [4, 2, 2, 4, 128, 130])
        self.d_nmm = do("nmm", [2, 32])
        self.d_x1 = dint("x1s", [2, TG, D])
        self.d_modg = dint("modg", [2, 2, D])

    def dump(self, name, v, shape):
        if not self.dbg:
            return
        ap = self.nc.dram_tensor("dbg_" + name, list(shape), F32, kind="ExternalOutput").ap()
        self.dumps[name] = list(shape)
        k = self.k
        if v.ap.dtype != F32:
            k.open_scope()
            t = k.sb("dbgt_" + name, shape, F32)
            k.copy(t, v)
            k.dma(V(ap, None), t)
            k.close_scope()
        else:
            k.dma(V(ap, None), v)

    def bank(self):
        b = self.PB[self.bank_i % 8]
        self.bank_i += 1
        return b

    def build(self):
        self.nc = nc = bass.Bass("TRN2", target_bir_lowering=False)
        self.declare()
        with ExitStack() as es:
            self.k = k = K(nc, es)
            k.open_scope()
            self.setup()
            for l in self.layers:
                self.layer_setup(l)
                for g in self.groups:
                    self.group(l, g)
            k.S.finish()
            k.close_scope()
        return nc

    def setup(self):
        k = self.k
        self.CON = k.sb("CON", [128, NCONST])
        k.dma(self.CON, V(self.d_consts, None))
        C = self.CON
        self.IDF = C[:, C_IDENT:C_IDENT + 128]
        self.MASK = C[:, C_MASK:C_MASK + 1024].rr("p (a b) -> p a b", a=8)
        self.ONESBD = C[:, C_ONESBD:C_ONESBD + 128]
        self.ONES = C[:, C_ONES:C_ONES + 128]
        self.HM = C[:, C_HM:C_HM + 2]
        self.COS = C[:, C_COS:C_COS + 512].rr("p (i a b) -> p i a b", i=8, a=4)
        self.SIN = C[:, C_SIN:C_SIN + 512].rr("p (i a b) -> p i a b", i=8, a=4)
        self.IDB = k.sb("IDB", [128, 128], BF16)
        k.copy(self.IDB, self.IDF)
        es, bufs = k.scopes[-1]
        h = es.enter_context(self.nc.psum_tensor("PS", [128, 8, 512], F32))
        self.PB = []
        for i in range(8):
            b = k.S.newbuf("bank%d" % i)
            b.excl = True
            bufs.append(b)
            self.PB.append(V(h.ap()[:, i, :], b))
        self.mixT = k.sb("mixT", [128, KC, TG], BF16)
        self.WS = k.sb("WS", [128, KC, 656], BF16)
        self.PFM = k.sb("PFM", [128, NPF])
        self.PROW = k.sb("PROW", [128, NPR])
        self.WUP = k.sb("WUP", [128, 2, 512], BF16)
        self.MODT = k.sb("MODT", [128, 2, 32])
        self.LAM = k.sb("LAM", [128, 2])
        self.SUBG = k.sb("SUBG", [128, 1])
        self.RMASK = k.sb("RMASK", [128, TG])
        k.memset(self.RMASK, 1.0)
        k.memset(self.RMASK.rr("p (c t) -> p c t", t=64)[:, :, 0:1], 0.0)
        self.EPSC = k.sb("EPSC", [128, 4])
        k.memset(self.EPSC[:, 0:1], 1e-12)
        k.memset(self.EPSC[:, 1:2], GN_EPS_A)
        k.memset(self.EPSC[:, 2:3], GN_EPS)
        k.memset(self.EPSC[:, 3:4], 1.0)
        self.EPS12 = self.EPSC[:, 0:1]
        self.EPSA = self.EPSC[:, 1:2]
        self.EPSB = self.EPSC[:, 2:3]
        self.ONE1 = self.EPSC[:, 3:4]
        self.x1bufs = [[k.S.newbuf("x1_%d_%d" % (g, i), fenced=False) for i in range(NT)] for g in range(2)]
        self.modgbuf = [[k.S.newbuf("modg%d%d" % (l, j), fenced=False) for j in range(2)] for l in range(2)]

    def pf(self, *key):
        c = PF[key]
        return self.PFM[:, c:c + 1]

    def layer_setup(self, l):
        k = self.k
        k.dma(self.PFM, V(self.d_pfm[l], None))
        k.dma(self.PROW, V(self.d_prow[l], None))
        for p in range(4):
            k.ts(self.pf("rw_omk_a", p), self.pf("rw_k_a", p), -1.0, ALU.mult, 1.0, ALU.add)
        k.dma(self.WUP, V(self.d_wup[l].rearrange("a p c -> p a c"), None), q="pool")
        lam_init = 0.8 - 0.6 * math.exp(-0.3 * l)
        k.open_scope()
        t = k.sb("lamt", [128, 64])
        s = k.sb("lams", [128, 2])
        k.tt(t, self.PROW[:, PR_LQ1:PR_LQ1 + 64], self.PROW[:, PR_LK1:PR_LK1 + 64], ALU.mult)
        k.reduce(s[:, 0:1], t, ALU.add)
        k.tt(t, self.PROW[:, PR_LQ2:PR_LQ2 + 64], self.PROW[:, PR_LK2:PR_LK2 + 64], ALU.mult)
        k.reduce(s[:, 1:2], t, ALU.add)
        k.act(s, s, AF.Exp)
        k.tt(self.LAM[:, 0:1], s[:, 0:1], s[:, 1:2], ALU.subtract)
        k.ts(self.LAM[:, 0:1], self.LAM[:, 0:1], lam_init, ALU.add)
        k.ts(self.SUBG, self.pf("subln"), 1.0 - lam_init, ALU.mult)
        k.close_scope()
        if self.want("ada"):
            self.ada(l)

    def prefetch(self):
        if self.wi < len(self.wlist):
            c0, ncols = self.wlist[self.wi]
            self.wi += 1
            self.load_w(self.d_win[self.l], c0, ncols)

    def load_w(self, src_rows, c0, ncols):
        k = self.k
        src = src_rows.rearrange("(kc p) c -> p kc c", p=128)[:, :, c0:c0 + ncols]
        k.dma(self.WS[:, :, 0:ncols], V(src, None), q="pool")

    def ada(self, l):
        k = self.k
        k.open_scope()
        ct = k.sb("ada_c", [128, 32])
        cb = k.sb("ada_cb", [128, 32], BF16)
        k.dma(ct, V(self.d_cT, None))
        k.act(cb, ct, AF.Silu)
        brow = k.sb("ada_brow", [1, 512])
        rowt = [k.sb("ada_row%d" % j, [1, 512]) for j in range(2)]
        pm = self.PB[7]
        for s in range(12):
            self.load_w(self.d_wada[l], s * 512, 512)
            k.dma(brow, V(self.d_bada[l:l + 1, s * 512:(s + 1) * 512], None))
            for j in range(2):
                pb = self.PB[j]
                for kc in range(KC):
                    k.mm(pb[0:1, 0:512], cb[:, kc * 2 + j:kc * 2 + j + 1], self.WS[:, kc, 0:512], start=(kc == 0), stop=(kc == KC - 1))
                k.tt(rowt[j], pb[0:1, 0:512], brow, ALU.add)
                if s < 8:
                    for q in range(4):
                        c = s * 4 + q
                        k.mm(pm[:, (j * 32 + c) * 2:(j * 32 + c) * 2 + 2], rowt[j][0:1, q * 128:(q + 1) * 128], self.ONES[0:1, 0:2])
                else:
                    k.dma(V(self.d_modg[l, j:j + 1, (s - 8) * 512:(s - 7) * 512], self.modgbuf[l][j]), rowt[j])
        k.copy(self.MODT.rr("p a b -> p (a b)"), pm[:, 0:128].rr("p (c t) -> p c t", t=2)[:, :, 0])
        k.ts(self.MODT[:, :, 16:32], self.MODT[:, :, 16:32], 1.0, ALU.add)
        self.dump("modT%d" % l, self.MODT, [128, 2, 32])
        k.close_scope()

    def group(self, l, g):
        k = self.k
        self.l, self.g = l, g
        self.nseq, self.tseq = (NSEQ_P, TSEQ_P) if g == 0 else (1, TG)
        tag = "%d%d" % (l, g)
        k.open_scope()
        self.uT = k.sb("uT" + tag, [128, KC, TG], BF16)
        self.LW = k.sb("LW" + tag, [128, TG], BF16)
        self.LA = k.sb("LA" + tag, [128, TG], BF16)
        self.OMG = k.sb("OMG" + tag, [128, NT, 8])
        self.CLAMP = k.sb("CLAMP" + tag, [128, NT, 8])
        self.SC = k.sb("SC" + tag, [128, NCH, 8])
        if self.want("u"):
            self.uphase(l, g)
        units = []
        if self.want("lora"):
            units.append((self.unit_lora, (l, g), U_LORA, 256))
        for p in range(4):
            if self.want("rw%d" % p):
                units.append((self.unit_rwkv, (l, g, p), U_RW[p], 512))
        if self.want("mg"):
            units.append((self.unit_mg, (l, g), U_MG, 16))
        for h in range(4):
            if self.want("ml%d" % h):
                units.append((self.unit_mlstm, (l, g, h), U_ML[h], 640))
        for h in range(8):
            if self.want("at%d" % h):
                units.append((self.unit_attn3, (l, g, h), U_AT[h], 512))
        self.wlist = [(u[2], u[3]) for u in units]
        self.wi = 0
        self.prefetch()
        for ui, (fn, args, _c0, _nc) in enumerate(units):
            self.run_unit(fn, *args)
            while self.wi < min(ui + 2, len(self.wlist)):
                self.prefetch()
        k.close_scope()
        if self.want("o"):
            self.phase_o(l, g)

    def xsrc(self, l, g, i):
        if l == 0:
            return V(self.d_xin[g, i * 128:(i + 1) * 128, :], None)
        return V(self.d_x1[g, i * 128:(i + 1) * 128, :], self.x1bufs[g][i])

    def uphase(self, l, g):
        k = self.k
        k.open_scope()
        xts = [k.sb("xt%d" % j, [128, D]) for j in range(2)]
        for i in range(NT):
            xt = xts[i % 2]
            k.dma(xt, self.xsrc(l, g, i))
            for q in range(4):
                pb = self.bank()
                for kk in range(4):
                    kc = q * 4 + kk
                    k.tr(pb[:, kk * 128:(kk + 1) * 128], xt[:, kc * 128:(kc + 1) * 128], self.IDF)
                for kk in range(4):
                    kc = q * 4 + kk
                    k.act(self.uT[:, kc, i * 128:(i + 1) * 128], pb[:, kk * 128:(kk + 1) * 128], AF.Identity,
                          scale=self.MODT[:, g, 16 + kc:17 + kc], bias=self.MODT[:, g, kc:kc + 1])
        k.close_scope()
        self.dump("uT%d%d" % (l, g), self.uT[:, :, 0:256], [128, KC, 256])

    def proj_fm(self, col, evac):
        k = self.k
        for nb in range(2):
            pb = self.bank()
            for kc in range(KC):
                k.mm(pb[:, 0:512], self.WS[:, kc, col:col + 128], self.uT[:, kc, nb * 512:(nb + 1) * 512], start=(kc == 0), stop=(kc == KC - 1))
            evac(nb, pb[:, 0:512])

    def proj_tm(self, col, ncols, evac):
        k = self.k
        for i in range(NT):
            pb = self.bank()
            for kc in range(KC):
                k.mm(pb[:, 0:ncols], self.uT[:, kc, i * 128:(i + 1) * 128], self.WS[:, kc, col:col + ncols], start=(kc == 0), stop=(kc == KC - 1))
            evac(i, pb[:, 0:ncols])

    def pad_evac(self, PRE, j):
        k = self.k
        nseq, tseq = self.nseq, self.tseq

        def ev(nb, ps):
            if nseq == 1:
                k.act(PRE[:, j, 0, 1 + nb * 512:1 + (nb + 1) * 512], ps, AF.Copy)
            else:
                k.act(PRE[:, j, 2 * nb:2 * nb + 2, 1:tseq + 1], ps.rr("p (s t) -> p s t", s=2), AF.Copy)
        return ev

    def conv(self, X, PRE, j, w0, w1, w2, b):
        k = self.k
        nseq, tseq = self.nseq, self.tseq
        xv = X.rr("p (s t) -> p s t", s=nseq)
        k.act(xv, PRE[:, j, :, 1:tseq + 1], AF.Identity, scale=w1, bias=b)
        k.stt(xv, PRE[:, j, :, 0:tseq], w0, xv, ALU.mult, ALU.add)
        k.stt(xv, PRE[:, j, :, 2:tseq + 2], w2, xv, ALU.mult, ALU.add)

    def unit_lora(self, l, g):
        k = self.k
        self.proj_fm(0, lambda nb, ps: k.act(self.LW[:, nb * 512:(nb + 1) * 512], ps, AF.Tanh))
        self.proj_fm(128, lambda nb, ps: k.act(self.LA[:, nb * 512:(nb + 1) * 512], ps, AF.Copy))
        self.prefetch()
        self.dump("LW%d%d" % (l, g), self.LW, [128, TG])

    def unit_rwkv(self, l, g, p):
        k = self.k
        nseq, tseq = self.nseq, self.tseq
        cps = tseq // 64
        tag = "%d%d%d" % (l, g, p)
        k.open_scope()
        GT = k.sb("rGT" + tag, [128, TG])
        BON = k.sb("rBON" + tag, [128, TG])
        KR = [k.sb("rKR%d" % d + tag, [128, NT, 2, 128], BF16) for d in range(2)]
        AH = [k.sb("rAH%d" % d + tag, [128, TG], BF16) for d in range(2)]
        KH = [k.sb("rKH%d" % d + tag, [128, TG], BF16) for d in range(2)]
        VB = k.sb("rVB" + tag, [128, TG], BF16)
        GL = [k.sb("rGL%d" % d + tag, [128, NCH]) for d in range(2)]
        YTM = k.sb("rY" + tag, [128, NT, 128])

        k.open_scope()
        R = k.sb("rR" + tag, [128, TG])
        Kk = k.sb("rK" + tag, [128, TG])
        V32 = k.sb("rV" + tag, [128, TG])
        k.open_scope()
        PRE = k.sb("rPRE" + tag, [128, 3, nseq, tseq + 2])
        k.memset(PRE[:, :, :, 0:1], 0.0)
        k.memset(PRE[:, :, :, tseq + 1:tseq + 2], 0.0)
        for j in range(3):
            self.proj_fm(j * 128, self.pad_evac(PRE, j))
        self.proj_fm(384, lambda nb, ps: k.act(GT[:, nb * 512:(nb + 1) * 512], ps, AF.Silu))
        self.prefetch()
        for j, X in enumerate((R, Kk, V32)):
            self.conv(X, PRE, j, self.pf("ca_w", p, j, 0), self.pf("ca_w", p, j, 1), self.pf("ca_w", p, j, 2), self.pf("ca_b", p, j))
        k.close_scope()
        self.stage("rw_conv")
        B = [k.sb("rB%d" % i + tag, [128, TG]) for i in range(9)]
        k.copy(VB, V32, eng="pool")
        KK, SQ, RS = B[0], B[1], B[2]
        k.ts(KK, Kk, self.pf("rw_k_k", p), ALU.mult)
        k.act(SQ, KK, AF.Square)
        for nb in range(2):
            pb = self.bank()
            k.mm(pb[:, 0:512], self.ONESBD, SQ[:, nb * 512:(nb + 1) * 512])
            k.act(RS[:, nb * 512:(nb + 1) * 512], pb[:, 0:512], AF.Sqrt, bias=self.EPS12)
        k.recip(RS, RS)
        k.tt(KK, KK, RS, ALU.mult)
        self.stage("rw_kk")
        KS = B[8]
        v3 = lambda X: X.rr("p (i t) -> p i t", t=128)
        c3 = lambda X: X.rr("p (c t) -> p c t", t=64)
        for d in range(2):
            SIG, AA, KT, AL, CS, T1, T2 = B[1], B[2], B[3], B[4], B[5], B[6], B[7]
            dr = slice(d * 64, d * 64 + 64)
            for nb in range(2):
                ns = slice(nb * 512, (nb + 1) * 512)
                pb = self.bank()
                k.mm(pb[:, 0:512], self.WUP[dr, 0, p * 128:(p + 1) * 128], self.LW[dr, ns])
                k.act(SIG[:, ns], pb[:, 0:512], AF.Sigmoid, bias=self.pf("w0", p, d))
                pb = self.bank()
                k.mm(pb[:, 0:512], self.WUP[dr, 1, p * 128:(p + 1) * 128], self.LA[dr, ns])
                k.act(AA[:, ns], pb[:, 0:512], AF.Sigmoid, bias=self.pf("a0", p, d))
            k.ts(T1, AA, self.pf("rw_k_a", p), ALU.mult, self.pf("rw_omk_a", p), ALU.add)
            k.tt(KT, T1, Kk, ALU.mult)
            if d == 0:
                k.copy(KS, KT, eng="pool")
            else:
                k.tt(KS, KS, KT, ALU.add, eng="pool")
            k.tt(AL, AA, KK, ALU.mult)
            k.scan(CS, self.RMASK, SIG, 0.0, ALU.mult, ALU.add)
            if d == 1:
                k.tt(c3(T1), c3(CS), c3(CS)[:, :, 63:64].bc([128, NCH, 64]), ALU.subtract)
                k.tt(CS, SIG, T1, ALU.subtract)
            k.tt(T2, CS, SIG, ALU.subtract)
            k.act(T2, T2, AF.Exp, scale=-WDECAY)
            k.tt(KR[d][:, :, 0, :], v3(KK), v3(T2), ALU.mult)
            EP = T1
            k.act(EP, CS, AF.Exp, scale=-WDECAY)
            k.tt(KR[d][:, :, 1, :], v3(R), v3(EP), ALU.mult)
            k.copy(GL[d], c3(EP)[:, :, 63 if d == 0 else 0])
            EM = AA
            k.act(EM, CS, AF.Exp, scale=WDECAY)
            k.tt(AH[d], AL, EM, ALU.mult)
            k.tt(KH[d], KT, EM, ALU.mult)
        k.ts(B[1], R, self.pf("rw_r_k", p), ALU.mult)
        k.tt(B[1], B[1], KS, ALU.mult)
        for nb in range(2):
            ns = slice(nb * 512, (nb + 1) * 512)
            pb = self.bank()
            k.mm(pb[:, 0:512], self.ONESBD, B[1][:, ns])
            k.tt(BON[:, ns], pb[:, 0:512], V32[:, ns], ALU.mult)
        if p == 0:
            self.dump("rw_R%d%d" % (l, g), R, [128, TG])
            self.dump("rw_KK%d%d" % (l, g), KK, [128, TG])
            self.dump("rw_BON%d%d" % (l, g), BON, [128, TG])
            self.dump("rw_GL0%d%d" % (l, g), GL[0], [128, NCH])
            self.dump("rw_GL1%d%d" % (l, g), GL[1], [128, NCH])
        k.close_scope()

        self.stage("rw_prep")
        k.open_scope()
        TMB = k.sb("rTMB" + tag, [128, NT, 5, 128], BF16)
        for i in range(NT):
            pbb = self.bank().bitcast(BF16)
            ts_ = slice(i * 128, (i + 1) * 128)
            for s, src in enumerate((VB, KH[0], KH[1], AH[0], AH[1])):
                k.tr(pbb[:, s * 128:(s + 1) * 128], src[:, ts_], self.IDB)
            k.copy(TMB[:, i].rr("p a b -> p (a b)"), pbb[:, 0:640], eng=("act" if i % 2 else "dve"))
        self.stage("rw_tmb")
        k.memset(YTM, 0.0)
        self.RB = [k.sb("rRB%d" % i + tag, [128, 128], BF16) for i in range(2)]
        self.UB = [k.sb("rUB%d" % i + tag, [128, 128], BF16) for i in range(2)]
        self.HT = [k.sb("rHT%d" % i + tag, [128, 128]) for i in range(2)]
        Hf = {}
        Hb = {}
        for s in range(nseq):
            for d in range(2):
                Hf[(s, d)] = k.sb("rHf%d%d" % (s, d) + tag, [128, 128])
                Hb[(s, d)] = k.sb("rHb%d%d" % (s, d) + tag, [128, 128], BF16)
        for d in range(2):
            k.open_scope()
            GMQ = k.sb("rGMQ%d" % d + tag, [128, NCH, 4, 128], BF16)
            P0 = k.sb("rP0%d" % d + tag, [128, NCH, 128], BF16)
            QA = k.sb("rQA%d" % d + tag, [128, NCH, 128], BF16)
            PA = k.sb("rPA%d" % d + tag, [128, NCH, 128], BF16)
            X = k.sb("rX%d" % d + tag, [128, NCH, 128], BF16)
            R1 = k.sb("rR1%d" % d + tag, [128, NT, 128])
            mP = 4 if d == 0 else 0
            for i in range(NT):
                ts_ = slice(i * 128, (i + 1) * 128)
                for hh in range(2):
                    j = i * 2 + hh
                    hr = slice(hh * 64, hh * 64 + 64)
                    pb = self.bank()
                    krv = KR[d][hr, i].rr("p a b -> p (a b)")
                    k.mm(pb[:, 0:256], AH[d][hr, ts_], krv)
                    k.mm(pb[:, 256:512], KH[d][hr, ts_], krv)
                    k.tt(GMQ[:, j], pb[:, 0:512].rr("p (a b) -> p a b", a=4), self.MASK[:, 4 * d:4 * d + 4, :], ALU.mult)
            for hh in range(2):
                hr = slice(hh * 64, hh * 64 + 64)
                for q in range(2):
                    pbP = self.bank()
                    for ii in range(4):
                        i = q * 4 + ii
                        k.mm(pbP[:, ii * 128:(ii + 1) * 128], KR[d][hr, i, 0, :], AH[d][hr, i * 128:(i + 1) * 128])
                    k.tt(P0[:, q * 8 + hh:q * 8 + 8:2, :], pbP[:, 0:512].rr("p (a b) -> p a b", a=4),
                         self.MASK[:, mP:mP + 1, :].bc([128, 4, 128]), ALU.mult)
            self.stage("rw_gram")
            k.tt(X, GMQ[:, :, 0, :], self.IDB.us(1).bc([128, NCH, 128]), ALU.add)
            Qc, Pc = GMQ[:, :, 0, :], P0
            Qn, Pn = QA, PA
            for lev in range(1, 6):
                for grp in range(4):
                    js = range(grp * 4, grp * 4 + 4)
                    gsl = slice(grp * 4, grp * 4 + 4)
                    pbP = self.bank()
                    for jj, j in enumerate(js):
                        k.mm(pbP[:, jj * 128:(jj + 1) * 128], Qc[:, j, :], Pc[:, j, :])
                    k.copy(Pn[:, gsl, :], pbP[:, 0:512].rr("p (a b) -> p a b", a=4), eng="act")
                    if lev < 5:
                        pbQ = self.bank()
                        for jj, j in enumerate(js):
                            k.mm(pbQ[:, jj * 128:(jj + 1) * 128], Pc[:, j, :], Qc[:, j, :])
                        k.copy(Qn[:, gsl, :], pbQ[:, 0:512].rr("p (a b) -> p a b", a=4), eng="dve")
                    pbX = self.bank()
                    for jj, j in enumerate(js):
                        k.mm(pbX[:, jj * 128:(jj + 1) * 128], Pn[:, j, :], X[:, j, :])
                    k.tt(X[:, gsl, :], X[:, gsl, :], pbX[:, 0:512].rr("p (a b) -> p a b", a=4), ALU.add)
                if lev == 1:
                    Qc, Pc, Qn, Pn = QA, PA, k.sb("rQB%d" % d + tag, [128, NCH, 128], BF16), P0
                else:
                    Qc, Pc, Qn, Pn = Qn, Pn, Qc, Pc
            self.stage("rw_inv")
            for i in range(NT):
                pb = self.bank()
                for hh in range(2):
                    j = i * 2 + hh
                    vs = TMB[:, i, 0, hh * 64:(hh + 1) * 64]
                    k.mm(pb[:, hh * 64:(hh + 1) * 64], GMQ[:, j, 2, :], vs)
                    k.mm(pb[:, 128 + hh * 64:128 + (hh + 1) * 64], GMQ[:, j, 3, :], vs)
                k.copy(R1[:, i, :], pb[:, 0:128], eng="act")
                k.tt(YTM[:, i, :], YTM[:, i, :], pb[:, 128:256], ALU.add)
            self.stage("rw_r1")
            for s in range(nseq):
                if g == 0:
                    k.memset(Hf[(s, d)], 0.0)
                else:
                    k.dma(Hf[(s, d)], V(self.d_rw0[l, d, p], None))
                k.copy(Hb[(s, d)], Hf[(s, d)], eng="act")
            for cs in range(cps):
                for s in range(nseq):
                    c = s * cps + (cs if d == 0 else cps - 1 - cs)
                    i, half = c // 2, c % 2
                    tr_ = slice(half * 64, half * 64 + 64)
                    hf, hb = Hf[(s, d)], Hb[(s, d)]
                    pbR = self.bank()
                    k.mm(pbR[tr_, 0:128], KR[d][:, i, 0, tr_], hb)
                    Rb = self.RB[(s + d) % 2]
                    k.tt(Rb[tr_, :], pbR[tr_, 0:128], R1[tr_, i, :], ALU.add)
                    self.stage("sc_R%d" % c)
                    pbU = self.bank()
                    for hh in range(2):
                        j = i * 2 + hh
                        k.mm(pbU[tr_, hh * 64:(hh + 1) * 64], X[tr_, j, tr_], Rb[tr_, hh * 64:(hh + 1) * 64])
                    Ub = self.UB[(s + d) % 2]
                    k.act(Ub[tr_, :], pbU[tr_, 0:128], AF.Copy, scale=-1.0)
                    self.stage("sc_U%d" % c)
                    pbY = self.bank()
                    k.mm(pbY[tr_, 0:128], KR[d][:, i, 1, tr_], hb)
                    pbH = self.bank()
                    k.mm(pbH[:, 0:128], TMB[tr_, i, 1 + d, :], TMB[tr_, i, 0, :], start=True, stop=False)
                    k.mm(pbH[:, 0:128], TMB[tr_, i, 3 + d, :], Ub[tr_, :], start=False, stop=True)
                    pbY2 = self.bank()
                    for hh in range(2):
                        j = i * 2 + hh
                        k.mm(pbY2[tr_, hh * 64:(hh + 1) * 64], GMQ[tr_, j, 1, tr_], Ub[tr_, hh * 64:(hh + 1) * 64])
                    HT = self.HT[(s + d) % 2]
                    k.tt(HT, pbH[:, 0:128], self.ONESBD, ALU.mult)
                    k.tt(HT, HT, hf, ALU.add)
                    k.ts(hf, HT, GL[d][:, c:c + 1], ALU.mult)
                    k.copy(hb, hf, eng="act")
                    k.tt(YTM[tr_, i, :], YTM[tr_, i, :], pbY[tr_, 0:128], ALU.add)
                    k.tt(YTM[tr_, i, :], YTM[tr_, i, :], pbY2[tr_, 0:128], ALU.add)
                    self.stage("sc_Y%d" % c)
                    self.stage("sc_H%d" % c)
            self.stage("sc_end")
            if g == 0:
                for s in range(nseq):
                    k.dma(V(self.d_nrw[s, l, d, p], None), Hf[(s, d)])
            self.stage("sc_out%d" % d)
            if p == 0 and d == 0:
                self.dump("rw_X%d%d" % (l, g), X[:, 0:2, :], [128, 2, 128])
                self.dump("rw_R1%d%d" % (l, g), R1, [128, NT, 128])
            k.close_scope()
        if p == 0:
            self.dump("rw_Y%d%d" % (l, g), YTM, [128, NT, 128])
        self.stage("rw_scan")
        YN = k.sb("rYN" + tag, [128, NT, 128])
        ST = k.sb("rST" + tag, [128, 4, 16])
        yv = YTM.rr("p i (h v) -> p (i h) v", h=2)
        ynv = YN.rr("p i (h v) -> p (i h) v", h=2)
        k.reduce(ST[:, 0, :], yv, ALU.add)
        k.act(YN, YTM, AF.Square)
        k.reduce(ST[:, 1, :], ynv, ALU.add)
        k.ts(ST[:, 0, :], ST[:, 0, :], 1.0 / 64, ALU.mult)
        k.tt(ST[:, 2, :], ST[:, 0, :], ST[:, 0, :], ALU.mult)
        k.stt(ST[:, 1, :], ST[:, 1, :], 1.0 / 64, ST[:, 2, :], ALU.mult, ALU.subtract)
        k.act(ST[:, 1, :], ST[:, 1, :], AF.Sqrt, bias=self.EPSA)
        k.recip(ST[:, 1, :], ST[:, 1, :])
        k.tt(ynv, yv, ST[:, 0, :].us(2).bc([128, 16, 64]), ALU.subtract)
        k.tt(ynv, ynv, ST[:, 1, :].us(2).bc([128, 16, 64]), ALU.mult)
        self.stage("rw_ln")
        OUTF = k.sb("rOUT" + tag, [128, TG])
        for q in range(2):
            pb = self.bank()
            for ii in range(4):
                i = q * 4 + ii
                k.tr(pb[:, ii * 128:(ii + 1) * 128], YN[:, i, :], self.IDF)
            k.act(OUTF[:, q * 512:(q + 1) * 512], pb[:, 0:512], AF.Identity, scale=self.pf("rw_gn_g", p), bias=self.pf("rw_gn_b", p))
        k.tt(OUTF, OUTF, BON, ALU.add)
        k.tt(self.mixT[:, p, :], OUTF, GT, ALU.mult)
        if p == 0:
            self.dump("rw_out%d%d" % (l, g), self.mixT[:, 0, :], [128, TG])
        k.close_scope()
        k.close_scope()

    def unit_mg(self, l, g):
        k = self.k
        nseq, tseq = self.nseq, self.tseq
        cps = tseq // 64
        tag = "%d%d" % (l, g)
        k.open_scope()
        GI = k.sb("gGI" + tag, [128, NT, 16])
        self.proj_tm(0, 16, lambda i, ps: k.act(GI[:, i, :], ps, AF.Copy))
        self.prefetch()
        gi = k.sb("ggi" + tag, [128, NT, 8])
        LFN = k.sb("gLFN" + tag, [128, NT, 8])
        NB = k.sb("gNB" + tag, [128, NT, 8])
        AG = k.sb("gAG" + tag, [128, NT, 8])
        k.tt(gi, GI[:, :, 0:8], self.PROW[:, PR_BI:PR_BI + 8].us(1).bc([128, NT, 8]), ALU.add)
        k.tt(LFN, GI[:, :, 8:16], self.PROW[:, PR_BF:PR_BF + 8].us(1).bc([128, NT, 8]), ALU.add)
        k.act(LFN, LFN, AF.Exp, scale=-1.0)
        k.act(LFN, LFN, AF.Ln, bias=self.ONE1)
        self.stage("mg_a")
        pb = self.bank()
        for d in range(2):
            k.mm(pb[:, d * 32:(d + 1) * 32], self.MASK[:, 1 if d == 0 else 5, :], LFN[:, :, d * 4:(d + 1) * 4])
        for d in range(2):
            k.copy(NB[:, :, d * 4:(d + 1) * 4], pb[:, d * 32:(d + 1) * 32].rr("p (i h) -> p i h", h=4))
        k.tt(AG, gi, NB, ALU.add)
        self.stage("mg_b")
        pbt = self.bank()
        k.tr(pbt[0:64, 0:128], AG.rr("p i k -> p (i k)"), self.IDF)
        self.stage("mg_c")
        MXT = k.sb("gMXT" + tag, [64, 2])
        k.reduce(MXT, pbt[0:64, 0:128].rr("p (f t) -> p f t", f=2), ALU.max)
        RH = k.sb("gRH" + tag, [64, 64, 2])
        k.tt(RH, self.IDF[0:64, 0:64].us(2).bc([64, 64, 2]), MXT.us(1).bc([64, 64, 2]), ALU.mult)
        self.stage("mg_d")
        pbm = self.bank()
        k.mm(pbm[:, 0:128], self.ONES[0:64, :], RH.rr("p a b -> p (a b)"))
        MXF = k.sb("gMXF" + tag, [128, NT, 8, 2])
        k.copy(MXF.rr("p i k f -> p (i k f)"), pbm[:, 0:128])
        self.stage("mg_e")
        R2 = k.sb("gR2" + tag, [128, 64, 2])
        k.tt(R2, LFN.rr("p i k -> p (i k)").us(2).bc([128, 64, 2]), self.HM.us(1).bc([128, 64, 2]), ALU.mult)
        pbl = self.bank()
        k.mm(pbl[:, 0:128], self.ONES, R2.rr("p a b -> p (a b)"))
        NBL = k.sb("gNBL" + tag, [128, NT, 8, 2])
        k.copy(NBL.rr("p i k f -> p (i k f)"), pbl[:, 0:128])
        self.stage("mg_f")
        M0 = k.sb("gM0" + tag, [128, nseq, 8])
        MBAR = k.sb("gMBAR" + tag, [128, NT, 2, 8])
        if g == 0:
            k.memset(M0, 0.0)
        else:
            k.dma(M0[:, 0, :], V(self.d_mm0[l], None))
        ipseq = NT // nseq
        mxv = MXF.rr("p (s i) k f -> p s i k f", s=nseq)
        nbv = NBL.rr("p (s i) k f -> p s i k f", s=nseq)
        mbv = MBAR.rr("p (s i) f k -> p s i f k", s=nseq)
        scv = self.SC.rr("p (s i f) k -> p s i f k", s=nseq, f=2)
        for cs in range(cps):
            for d in range(2):
                cc = cs if d == 0 else cps - 1 - cs
                ii, half = cc // 2, cc % 2
                ds = slice(d * 4, d * 4 + 4)
                m0 = M0[:, :, ds]
                mb = mbv[:, :, ii, half, ds]
                k.tt(mb, m0, mxv[:, :, ii, ds, half], ALU.max)
                k.tt(scv[:, :, ii, half, ds], m0, mb, ALU.subtract)
                k.tt(m0, mb, nbv[:, :, ii, ds, half], ALU.subtract)
        self.stage("mg_g")
        k.act(self.SC, self.SC, AF.Exp)
        self.stage("mg_h")
        if g == 0:
            k.dma(V(self.d_nmm[l:l + 1, :], None), M0[0:1].rr("p s k -> p (s k)"))
        self.stage("mg_i")
        MT = k.sb("gMT" + tag, [128, NT, 8])
        k.ts(MT, MBAR[:, :, 0, :], self.HM[:, 0:1], ALU.mult)
        k.stt(MT, MBAR[:, :, 1, :], self.HM[:, 1:2], MT, ALU.mult, ALU.add)
        k.tt(self.OMG, AG, MT, ALU.subtract)
        k.act(self.OMG, self.OMG, AF.Exp)
        k.tt(self.CLAMP, NB, MT, ALU.subtract)
        k.act(self.CLAMP, self.CLAMP, AF.Exp)
        self.stage("mg_j")
        self.dump("mg_AG%d%d" % (l, g), AG, [128, NT, 8])
        self.stage("mg_k")
        self.dump("mg_MBAR%d%d" % (l, g), MBAR.rr("p i f k -> p (i f k)"), [128, NT * 16])
        self.dump("mg_SC%d%d" % (l, g), self.SC, [128, NCH, 8])
        self.dump("mg_OMG%d%d" % (l, g), self.OMG, [128, NT, 8])
        k.close_scope()

    def unit_mlstm(self, l, g, h):
        k = self.k
        nseq, tseq = self.nseq, self.tseq
        cps = tseq // 64
        tag = "%d%d%d" % (l, g, h)
        k.open_scope()
        GT = k.sb("mGT" + tag, [128, TG])
        VT = k.sb("mVT" + tag, [128, NT, 128])
        OT = k.sb("mOT" + tag, [128, NT, 128])
        QB = k.sb("mQB" + tag, [128, TG], BF16)
        KB = k.sb("mKB" + tag, [128, TG], BF16)
        k.open_scope()
        PRE = k.sb("mPRE" + tag, [128, 2, nseq, tseq + 2])
        self.stage("ml_a")
        k.memset(PRE[:, :, :, 0:1], 0.0)
        k.memset(PRE[:, :, :, tseq + 1:tseq + 2], 0.0)
        self.stage("ml_b")
        for j in range(2):
            self.proj_fm(j * 128, self.pad_evac(PRE, j))
        self.stage("ml_c")
        self.proj_fm(256, lambda nb, ps: k.act(GT[:, nb * 512:(nb + 1) * 512], ps, AF.Silu))
        self.stage("ml_d")

        def ev_vo(i, ps):
            k.copy(VT[:, i, :], ps[:, 0:128])
            k.act(OT[:, i, :], ps[:, 128:256], AF.Sigmoid)
        self.proj_tm(384, 256, ev_vo)
        self.prefetch()
        self.stage("ml_proj")
        X = k.sb("mX" + tag, [128, TG])
        for j, dst in enumerate((QB, KB)):
            self.conv(X, PRE, j, self.pf("cb_w", h, j, 0), self.pf("cb_w", h, j, 1), self.pf("cb_w", h, j, 2), self.pf("cb_b", h, j))
            if j == 0:
                k.act(dst, X, AF.Silu)
            else:
                k.act(X, X, AF.Silu)
                k.ts(dst, X, 128.0 ** -0.5, ALU.mult)
        k.close_scope()
        self.stage("ml_conv")
        KTM = k.sb("mKTM" + tag, [128, NT, 128], BF16)
        pbb = self.bank().bitcast(BF16)
        for i in range(NT):
            k.tr(pbb[:, i * 128:(i + 1) * 128], KB[:, i * 128:(i + 1) * 128], self.IDB)
        k.copy(KTM.rr("p i c -> p (i c)"), pbb[:, 0:1024])
        self.stage("ml_ktm")
        MTd = [k.sb("mMT%d" % d + tag, [128, NT, 128], BF16) for d in range(2)]
        for q in range(2):
            pb = self.bank()
            for ii in range(4):
                i = q * 4 + ii
                ts_ = slice(i * 128, (i + 1) * 128)
                k.mm(pb[:, ii * 128:(ii + 1) * 128], KB[:, ts_], QB[:, ts_])
            pv = pb[:, 0:512].rr("p (a b) -> p a b", a=4)
            k.tt(MTd[0][:, q * 4:q * 4 + 4, :], pv, self.MASK[:, 1:2, :].bc([128, 4, 128]), ALU.mult)
            k.tt(MTd[1][:, q * 4:q * 4 + 4, :], pv, self.MASK[:, 5:6, :].bc([128, 4, 128]), ALU.mult)
        self.stage("ml_mt")
        WV = [k.sb("mWV%d" % d + tag, [128, NT, 130], BF16) for d in range(2)]
        HI = [k.sb("mHI%d" % d + tag, [128, NT, 130]) for d in range(2)]
        for d in range(2):
            om = self.OMG[:, :, d * 4 + h:d * 4 + h + 1]
            k.tt(WV[d][:, :, 0:128], VT, om.bc([128, NT, 128]), ALU.mult)
            k.copy(WV[d][:, :, 128:129], om)
            k.memset(WV[d][:, :, 129:130], 0.0)
            for i in range(NT):
                pb = self.bank()
                k.mm(pb[:, 0:130], MTd[d][:, i, :], WV[d][:, i, :])
                k.copy(HI[d][:, i, :], pb[:, 0:130], eng=("act" if i % 2 else "dve"))
        self.stage("ml_hi")
        HS = k.sb("mHS" + tag, [128, NT, 128])
        TOTS = [k.sb("mTOTS%d" % i + tag, [128, NT, 130]) for i in range(2)]
        Z = {}
        Zb = {}
        for s in range(nseq):
            for d in range(2):
                Z[(s, d)] = k.sb("mZ%d%d" % (s, d) + tag, [128, 130])
                Zb[(s, d)] = k.sb("mZb%d%d" % (s, d) + tag, [128, 130], BF16)
                if g == 0:
                    k.memset(Z[(s, d)], 0.0)
                else:
                    k.dma(Z[(s, d)], V(self.d_ml0[l, d, h], None))
        for cs in range(cps):
            for s in range(nseq):
                for d in range(2):
                    c = s * cps + (cs if d == 0 else cps - 1 - cs)
                    i, half = c // 2, c % 2
                    tr_ = slice(half * 64, half * 64 + 64)
                    z, zb = Z[(s, d)], Zb[(s, d)]
                    dh = d * 4 + h
                    k.ts(z, z, self.SC[:, c, dh:dh + 1], ALU.mult)
                    k.copy(zb, z, eng="act")
                    pbZ = self.bank()
                    k.mm(pbZ[:, 0:130], KTM[tr_, i, :], WV[d][tr_, i, :])
                    pbS = self.bank()
                    k.mm(pbS[tr_, 0:130], QB[:, c * 64:(c + 1) * 64], zb)
                    k.tt(z, z, pbZ[:, 0:130], ALU.add)
                    k.tt(TOTS[d][tr_, i, :], pbS[tr_, 0:130], HI[d][tr_, i, :], ALU.add)
                    self.stage("ml_c%d_%d" % (c, d))
        DNb = k.sb("mDNb" + tag, [128, 2, NT])
        for d in range(2):
            k.act(DNb[:, d, :], TOTS[d][:, :, 128], AF.Abs)
            k.tt(DNb[:, d, :], DNb[:, d, :], self.CLAMP[:, :, d * 4 + h], ALU.max)
        k.recip(DNb, DNb)
        for d in range(2):
            k.tt(TOTS[d][:, :, 0:128], TOTS[d][:, :, 0:128], DNb[:, d, :].us(2).bc([128, NT, 128]), ALU.mult, eng=("pool" if d else "dve"))
        k.tt(HS, TOTS[0][:, :, 0:128], TOTS[1][:, :, 0:128], ALU.add)
        self.stage("ml_chain")
        if g == 0:
            for s in range(nseq):
                for d in range(2):
                    k.dma(V(self.d_nmc[s, l, d, h], None), Z[(s, d)])
        self.stage("ml_nmc")
        if h == 0:
            self.dump("ml_HS%d%d" % (l, g), HS, [128, NT, 128])
            self.dump("ml_HI%d%d" % (l, g), HI[0], [128, NT, 130])
        self.stage("ml_dump")
        k.tt(HS, HS, OT, ALU.mult)
        HN = k.sb("mHN" + tag, [128, NT, 128])
        ST = k.sb("mST" + tag, [128, 3, NT])
        k.reduce(ST[:, 0, :], HS, ALU.add)
        k.act(HN, HS, AF.Square)
        k.reduce(ST[:, 1, :], HN, ALU.add)
        k.ts(ST[:, 0, :], ST[:, 0, :], 1.0 / 128, ALU.mult)
        k.tt(ST[:, 2, :], ST[:, 0, :], ST[:, 0, :], ALU.mult)
        k.stt(ST[:, 1, :], ST[:, 1, :], 1.0 / 128, ST[:, 2, :], ALU.mult, ALU.subtract)
        k.act(ST[:, 1, :], ST[:, 1, :], AF.Sqrt, bias=self.EPSB)
        k.recip(ST[:, 1, :], ST[:, 1, :])
        k.tt(HN, HS, ST[:, 0, :].us(2).bc([128, NT, 128]), ALU.subtract)
        k.tt(HN, HN, ST[:, 1, :].us(2).bc([128, NT, 128]), ALU.mult)
        OUTF = k.sb("mOUT" + tag, [128, TG])
        for q in range(2):
            pb = self.bank()
            for ii in range(4):
                k.tr(pb[:, ii * 128:(ii + 1) * 128], HN[:, q * 4 + ii, :], self.IDF)
            k.act(OUTF[:, q * 512:(q + 1) * 512], pb[:, 0:512], AF.Identity, scale=self.pf("ml_gn_g", h), bias=self.pf("ml_gn_b", h))
        k.tt(self.mixT[:, 4 + h, :], OUTF, GT, ALU.mult)
        if h == 0:
            self.dump("ml_out%d%d" % (l, g), self.mixT[:, 4, :], [128, TG])
        k.close_scope()

    def attn_prep(self, l, g, h):
        k = self.k
        nseq = self.nseq
        tag = "%d%d%d" % (l, g, h)
        npast = 0 if g == 0 else PAST // 128
        nkt = npast + NT
        c = {"h": h, "nkt": nkt, "npast": npast}
        c["GT"] = GT = k.sb("aGT" + tag, [128, TG])
        c["VBk"] = VBk = k.sb("aVB" + tag, [128, nkt, 128], BF16)
        c["QT"] = QT = k.sb("aQT" + tag, [128, TG], BF16)
        c["KTa"] = KTa = k.sb("aKT" + tag, [128, nkt * 128], BF16)
        self.load_w(self.d_win[l], U_AT[h], 512)
        k.open_scope()
        QKV = k.sb("aQKV" + tag, [128, NT, 384])
        self.proj_tm(0, 384, lambda i, ps: k.copy(QKV[:, i, :], ps, eng=("act" if i % 2 else "dve")))
        self.proj_fm(384, lambda nb, ps: k.act(GT[:, nb * 512:(nb + 1) * 512], ps, AF.Silu))
        if g == 0:
            for s in range(nseq):
                k.dma(V(self.d_nk[s, l, h].rearrange("(i p) c -> p i c", p=128), None), QKV[:, 2 * s:2 * s + 2, 128:256])
                k.dma(V(self.d_nv[s, l, h].rearrange("(i p) c -> p i c", p=128), None), QKV[:, 2 * s:2 * s + 2, 256:384])
        else:
            T = [k.sb("aT%d" % i + tag, [128, NT, 4, 16]) for i in range(4)]
            for off in (0, 128):
                xv = QKV[:, :, off:off + 128].rr("p i (a x t) -> p i a x t", a=4, x=2)
                x1, x2 = xv[:, :, :, 0, :], xv[:, :, :, 1, :]
                k.tt(T[0], x1, self.COS, ALU.mult)
                k.tt(T[1], x2, self.SIN, ALU.mult)
                k.tt(T[2], x2, self.COS, ALU.mult)
                k.tt(T[3], x1, self.SIN, ALU.mult)
                k.tt(x1, T[0], T[1], ALU.subtract)
                k.tt(x2, T[2], T[3], ALU.add)
        QKB = k.sb("aQKB" + tag, [128, NT, 256], BF16)
        k.copy(QKB, QKV[:, :, 0:256])
        k.copy(VBk[:, npast:nkt, :], QKV[:, :, 256:384], eng="pool")
        for which, dst, c0 in ((0, QT, 0), (1, KTa, npast * 128)):
            pbb = self.bank().bitcast(BF16)
            for i in range(NT):
                k.tr(pbb[:, i * 128:(i + 1) * 128], QKB[:, i, which * 128:(which + 1) * 128], self.IDB)
            k.copy(dst[:, c0:c0 + TG], pbb[:, 0:1024], eng=("act" if which else "dve"))
        if g == 1:
            CKB = k.sb("aCKB" + tag, [128, npast, 128], BF16)
            k.dma(CKB, V(self.d_ck[l, h].rearrange("(i p) c -> p i c", p=128), None), q="pool")
            k.dma(VBk[:, 0:npast, :], V(self.d_cv[l, h].rearrange("(i p) c -> p i c", p=128), None), q="pool")
            pbb = self.bank().bitcast(BF16)
            for i in range(npast):
                k.tr(pbb[:, i * 128:(i + 1) * 128], CKB[:, i, :], self.IDB)
            k.copy(KTa[:, 0:npast * 128], pbb[:, 0:npast * 128])
        k.close_scope()
        return c

    def unit_attn2(self, l, g, h0):
        k = self.k
        nseq = self.nseq
        scale = 64.0 ** -0.5
        k.open_scope()
        ctxs = [self.attn_prep(l, g, h0 + j) for j in range(2)]
        for j, c in enumerate(ctxs):
            tag = "%d%d%d" % (l, g, c["h"])
            nkt = c["nkt"]
            c["E"] = [k.sb("aE%d" % b + tag, [128, nkt * 128], BF16) for b in range(2)]
            c["ET"] = [k.sb("aET%d" % b + tag, [128, nkt, 128], BF16) for b in range(2)]
            c["SMq"] = [[k.sb("aSM%d%d" % (a, b) + tag, [128, 4]) for b in range(2)] for a in range(2)]
            c["MXp"] = [k.sb("aMX%d" % b + tag, [128, 4]) for b in range(2)]
            c["NBp"] = [k.sb("aNB%d" % b + tag, [128, 1]) for b in range(2)]
            c["RS"] = k.sb("aRS" + tag, [128, 4])
            c["O2"] = k.sb("aO2" + tag, [128, 128])
            c["OD"] = k.sb("aOD" + tag, [128, 128])
            c["JK"] = k.sb("aJK" + tag, [128, 128])
            c["OT"] = k.sb("aOT" + tag, [128, 128])
            c["abank"] = j * 3
            c["ocol"] = j * 256
        pbO = self.PB[7]
        pbT = self.PB[6]
        items = [(i, br) for i in range(NT) for br in range(2)]

        def keys_of(c, i):
            if g == 0:
                sq = i // (NT // nseq)
                return [2 * sq, 2 * sq + 1]
            return list(range(c["nkt"]))

        def stage_a(c, kidx):
            i, br = items[kidx]
            par = kidx % 2
            kts = keys_of(c, i)
            k0 = kts[0] * 128
            ncols = len(kts) * 128
            chunks = [(c0, min(512, ncols - c0)) for c0 in range(0, ncols, 512)]
            brs = slice(br * 64, br * 64 + 64)
            banks = [self.PB[c["abank"] + ci] for ci in range(len(chunks))]
            for ci, (c0, cn) in enumerate(chunks):
                k.mm(banks[ci][:, 0:cn], c["QT"][brs, i * 128:(i + 1) * 128], c["KTa"][brs, k0 + c0:k0 + c0 + cn])
            for ci, (c0, cn) in enumerate(chunks):
                k.reduce(c["MXp"][par][:, ci:ci + 1], banks[ci][:, 0:cn], ALU.max)
            k.reduce(c["NBp"][par], c["MXp"][par][:, 0:len(chunks)], ALU.max)
            k.ts(c["NBp"][par], c["NBp"][par], -scale, ALU.mult)
            for ci, (c0, cn) in enumerate(chunks):
                k.act(c["E"][par][:, c0:c0 + cn], banks[ci][:, 0:cn], AF.Exp, scale=scale, bias=c["NBp"][par],
                      accum=c["SMq"][i % 2][br][:, ci:ci + 1])

        def stage_b(c, kidx):
            i, br = items[kidx]
            par = kidx % 2
            kts = keys_of(c, i)
            nk = len(kts)
            oc = c["ocol"] + br * 128
            for q0 in range(0, nk, 8):
                qn = min(8, nk - q0)
                pbb = pbT.bitcast(BF16)
                for jj in range(qn):
                    k.tr(pbb[:, jj * 128:(jj + 1) * 128], c["E"][par][:, (q0 + jj) * 128:(q0 + jj + 1) * 128], self.IDB)
                k.copy(c["ET"][par][:, q0:q0 + qn, :].rr("p a b -> p (a b)"), pbb[:, 0:qn * 128], eng=("act" if qn == 8 else "dve"))
            for jj in range(nk):
                k.mm(pbO[:, oc:oc + 128], c["ET"][par][:, jj, :], c["VBk"][:, kts[jj], :], start=(jj == 0), stop=(jj == nk - 1))

        def tail(c, i):
            qp = i % 2
            RS, O2, OD, JK, OTt = c["RS"], c["O2"], c["OD"], c["JK"], c["OT"]
            oc = c["ocol"]
            nch = (len(keys_of(c, i)) * 128 + 511) // 512
            for br in range(2):
                k.reduce(RS[:, br:br + 1], c["SMq"][qp][br][:, 0:nch], ALU.add)
            k.recip(RS[:, 0:2], RS[:, 0:2])
            k.tt(RS[:, 1:2], RS[:, 1:2], self.LAM[:, 0:1], ALU.mult)
            k.act(O2, pbO[:, oc + 128:oc + 256], AF.Identity, scale=RS[:, 1:2])
            k.stt(OD, pbO[:, oc:oc + 128], RS[:, 0:1], O2, ALU.mult, ALU.subtract)
            k.act(JK, OD, AF.Square, accum=RS[:, 2:3])
            k.act(RS[:, 2:3], RS[:, 2:3], AF.Sqrt, scale=1.0 / 128, bias=self.EPSB)
            k.recip(RS[:, 2:3], RS[:, 2:3])
            k.ts(OD, OD, RS[:, 2:3], ALU.mult)
            k.tr(pbT[:, 0:128], OD, self.IDF)
            k.act(OTt, pbT[:, 0:128], AF.Identity, scale=self.SUBG)
            k.tt(self.mixT[:, 8 + c["h"], i * 128:(i + 1) * 128], OTt, c["GT"][:, i * 128:(i + 1) * 128], ALU.mult, eng="pool")

        for c in ctxs:
            stage_a(c, 0)
        for kidx in range(len(items)):
            if kidx + 1 < len(items):
                for c in ctxs:
                    stage_a(c, kidx + 1)
            for c in ctxs:
                stage_b(c, kidx)
                if items[kidx][1] == 1:
                    tail(c, items[kidx][0])
        if h0 == 0:
            self.dump("at_out%d%d" % (l, g), self.mixT[:, 8, :], [128, TG])
        k.close_scope()

    def unit_attn3(self, l, g, h):
        k = self.k
        nseq = self.nseq
        scale = 64.0 ** -0.5
        tag = "%d%d%d" % (l, g, h)
        npast = 0 if g == 0 else PAST // 128
        nkt = npast + NT
        nkl = 2 if g == 0 else nkt
        k.open_scope()
        GT = k.sb("aGT" + tag, [128, TG])
        VP = k.sb("aVP" + tag, [128, nkt, 130], BF16)
        QT = k.sb("aQT" + tag, [128, TG], BF16)
        KTa = k.sb("aKT" + tag, [128, nkt * 128], BF16)
        NB = k.sb("aNB" + tag, [128, 2])
        k.open_scope()
        QKV = k.sb("aQKV" + tag, [128, NT, 384])
        self.proj_tm(0, 384, lambda i, ps: k.copy(QKV[:, i, :], ps, eng=("act" if i % 2 else "dve")))
        self.proj_fm(384, lambda nb, ps: k.act(GT[:, nb * 512:(nb + 1) * 512], ps, AF.Silu))
        self.prefetch()
        if g == 0:
            for s in range(nseq):
                k.dma(V(self.d_nk[s, l, h].rearrange("(i p) c -> p i c", p=128), None), QKV[:, 2 * s:2 * s + 2, 128:256])
                k.dma(V(self.d_nv[s, l, h].rearrange("(i p) c -> p i c", p=128), None), QKV[:, 2 * s:2 * s + 2, 256:384])
        else:
            T = [k.sb("aT%d" % i + tag, [128, NT, 4, 16]) for i in range(4)]
            for off in (0, 128):
                xv = QKV[:, :, off:off + 128].rr("p i (a x t) -> p i a x t", a=4, x=2)
                x1, x2 = xv[:, :, :, 0, :], xv[:, :, :, 1, :]
                k.tt(T[0], x1, self.COS, ALU.mult)
                k.tt(T[1], x2, self.SIN, ALU.mult)
                k.tt(T[2], x2, self.COS, ALU.mult)
                k.tt(T[3], x1, self.SIN, ALU.mult)
                k.tt(x1, T[0], T[1], ALU.subtract)
                k.tt(x2, T[2], T[3], ALU.add)
        QKB = k.sb("aQKB" + tag, [128, NT, 256], BF16)
        k.copy(QKB, QKV[:, :, 0:256])
        k.copy(VP[:, npast:nkt, 0:128], QKV[:, :, 256:384], eng="pool")
        k.memset(VP[:, :, 128:129], 1.0)
        k.memset(VP[:, :, 129:130], 0.0)
        for which, dst, c0 in ((0, QT, 0), (1, KTa, npast * 128)):
            pbb = self.bank().bitcast(BF16)
            for i in range(NT):
                k.tr(pbb[:, i * 128:(i + 1) * 128], QKB[:, i, which * 128:(which + 1) * 128], self.IDB)
            k.copy(dst[:, c0:c0 + TG], pbb[:, 0:1024], eng=("act" if which else "dve"))
        SQ = k.sb("aSQ" + tag, [128, NT, 256])
        N2 = k.sb("aN2" + tag, [128, NT, 4])
        M4 = k.sb("aM4" + tag, [128, 4])
        k.act(SQ, QKV[:, :, 0:256], AF.Square)
        k.reduce(N2, SQ.rr("p i (a d) -> p i a d", a=4), ALU.add)
        k.reduce(M4, N2.rr("p i a -> p a i"), ALU.max)
        if g == 1:
            CKB = k.sb("aCKB" + tag, [128, npast, 128], BF16)
            k.dma(CKB, V(self.d_ck[l, h].rearrange("(i p) c -> p i c", p=128), None), q="pool")
            k.dma(VP[:, 0:npast, 0:128], V(self.d_cv[l, h].rearrange("(i p) c -> p i c", p=128), None), q="pool")
            pbb = self.bank().bitcast(BF16)
            for i in range(npast):
                k.tr(pbb[:, i * 128:(i + 1) * 128], CKB[:, i, :], self.IDB)
            k.copy(KTa[:, 0:npast * 128], pbb[:, 0:npast * 128])
            CSQ = k.sb("aCSQ" + tag, [128, npast, 128])
            CN2 = k.sb("aCN2" + tag, [128, npast, 2])
            CM = k.sb("aCM" + tag, [128, 2])
            k.act(CSQ, CKB, AF.Square)
            k.reduce(CN2, CSQ.rr("p i (a d) -> p i a d", a=2), ALU.add)
            k.reduce(CM, CN2.rr("p i a -> p a i"), ALU.max)
            k.tt(M4[:, 2:4], M4[:, 2:4], CM, ALU.max)
        pbm = self.bank()
        k.tr(pbm[0:4, 0:128], M4, self.IDF)
        MC = k.sb("aMC" + tag, [4, 1])
        k.reduce(MC, pbm[0:4, 0:128], ALU.max)
        RH = k.sb("aRH" + tag, [4, 4])
        k.ts(RH, self.IDF[0:4, 0:4], MC[0:4, 0:1], ALU.mult)
        pbr = self.bank()
        k.mm(pbr[:, 0:4], self.ONES[0:4, :], RH)
        MR = k.sb("aMR" + tag, [128, 4])
        k.copy(MR, pbr[:, 0:4])
        k.tt(NB, MR[:, 0:2], MR[:, 2:4], ALU.mult)
        k.act(NB, NB, AF.Sqrt)
        k.ts(NB, NB, -scale, ALU.mult)
        k.close_scope()
        ETs = [k.sb("aET%d" % b + tag, [128, nkl, TG], BF16) for b in range(2)]
        O1S = k.sb("aO1" + tag, [128, NT, 130])
        RS = k.sb("aRS" + tag, [128, 4])
        O2 = k.sb("aO2" + tag, [128, 128])
        OD = k.sb("aOD" + tag, [128, 128])
        JK = k.sb("aJK" + tag, [128, 128])
        OTt = k.sb("aOT" + tag, [128, 128])
        qblk = 512 if g == 1 else TSEQ_P

        def qk_items(br):
            brs = slice(br * 64, br * 64 + 64)
            for qb in range(TG // qblk):
                qs = slice(qb * qblk, (qb + 1) * qblk)
                kt0 = 0 if g == 1 else 2 * qb
                for j in range(nkl):
                    yield (brs, qs, kt0, j)

        def qk_exp(br, it):
            brs, qs, kt0, j = it
            pb = self.PB[self.abank % 6]
            self.abank += 1
            k.mm(pb[:, 0:qblk], KTa[brs, (kt0 + j) * 128:(kt0 + j + 1) * 128], QT[brs, qs])
            k.act(ETs[br][:, j, qs], pb[:, 0:qblk], AF.Exp, scale=scale, bias=NB[:, br:br + 1])

        def pv(br, i):
            ET = ETs[br]
            kt0 = 0 if g == 1 else 2 * (i // 2)
            pbO = self.PB[6 + i % 2]
            for j in range(nkl):
                k.mm(pbO[:, 0:130], ET[:, j, i * 128:(i + 1) * 128], VP[:, kt0 + j, :], start=(j == 0), stop=(j == nkl - 1))
            if br == 0:
                k.copy(O1S[:, i, :], pbO[:, 0:130], eng=("act" if i % 2 else "dve"))
            else:
                k.copy(RS[:, 0:1], O1S[:, i, 128:129])
                k.copy(RS[:, 1:2], pbO[:, 128:129])
                k.recip(RS[:, 0:2], RS[:, 0:2])
                k.tt(RS[:, 1:2], RS[:, 1:2], self.LAM[:, 0:1], ALU.mult)
                k.act(O2, pbO[:, 0:128], AF.Identity, scale=RS[:, 1:2])
                k.stt(OD, O1S[:, i, 0:128], RS[:, 0:1], O2, ALU.mult, ALU.subtract)
                k.act(JK, OD, AF.Square, accum=RS[:, 2:3])
                k.act(RS[:, 2:3], RS[:, 2:3], AF.Sqrt, scale=1.0 / 128, bias=self.EPSB)
                k.recip(RS[:, 2:3], RS[:, 2:3])
                k.ts(OD, OD, RS[:, 2:3], ALU.mult)
                pbt = self.PB[self.abank % 6]
                self.abank += 1
                k.tr(pbt[:, 0:128], OD, self.IDF)
                k.act(OTt, pbt[:, 0:128], AF.Identity, scale=self.SUBG)
                k.tt(self.mixT[:, 8 + h, i * 128:(i + 1) * 128], OTt, GT[:, i * 128:(i + 1) * 128], ALU.mult, eng="pool")

        for it in qk_items(0):
            qk_exp(0, it)
        its1 = list(qk_items(1))
        per = max(1, len(its1) // NT)
        nxt = 0
        for n, it in enumerate(its1):
            qk_exp(1, it)
            if (n + 1) % per == 0 and nxt < NT:
                pv(0, nxt)
                nxt += 1
        while nxt < NT:
            pv(0, nxt)
            nxt += 1
        for i in range(NT):
            pv(1, i)
        if h == 0:
            self.dump("at_out%d%d" % (l, g), self.mixT[:, 8, :], [128, TG])
        k.close_scope()

    def phase_o(self, l, g):
        k = self.k
        k.open_scope()
        WO = k.sb("oWO", [128, KC, D], BF16)
        for s in range(4):
            src = self.d_wout[l].rearrange("(kc p) c -> p kc c", p=128)[:, :, s * 512:(s + 1) * 512]
            k.dma(WO[:, :, s * 512:(s + 1) * 512], V(src, None), q="pool")
        GBC = k.sb("oGBC", [128, D])
        LG = k.sb("oLG", [128, D])
        LB = k.sb("oLB", [128, D])
        k.dma(GBC, V(self.d_modg[l, g:g + 1, :].partition_broadcast(128).rearrange("p a d -> p (a d)"), self.modgbuf[l][g]))
        k.dma(LG, V(self.d_lng[l:l + 1, :].partition_broadcast(128).rearrange("p a d -> p (a d)"), None))
        k.dma(LB, V(self.d_lnb[l:l + 1, :].partition_broadcast(128).rearrange("p a d -> p (a d)"), None))
        XT = k.sb("oXT", [128, D])
        VV = k.sb("oVV", [128, D])
        JK = k.sb("oJK", [128, D], BF16)
        ST = k.sb("oST", [128, 4])
        for i in range(NT):
            k.dma(XT, self.xsrc(l, g, i))
            for s in range(4):
                pb = self.bank()
                for kc in range(KC):
                    k.mm(pb[:, 0:512], self.mixT[:, kc, i * 128:(i + 1) * 128], WO[:, kc, s * 512:(s + 1) * 512], start=(kc == 0), stop=(kc == KC - 1))
                k.tt(VV[:, s * 512:(s + 1) * 512], pb[:, 0:512], GBC[:, s * 512:(s + 1) * 512], ALU.mult)
            k.stt(VV, XT, ALPHA, VV, ALU.mult, ALU.add)
            k.act(JK, VV, AF.Identity, accum=ST[:, 0:1])
            k.act(JK, VV, AF.Square, accum=ST[:, 1:2])
            k.ts(ST[:, 0:1], ST[:, 0:1], 1.0 / D, ALU.mult)
            k.tt(ST[:, 2:3], ST[:, 0:1], ST[:, 0:1], ALU.mult)
            k.stt(ST[:, 1:2], ST[:, 1:2], 1.0 / D, ST[:, 2:3], ALU.mult, ALU.subtract)
            k.act(ST[:, 1:2], ST[:, 1:2], AF.Sqrt, bias=self.EPSB)
            k.recip(ST[:, 1:2], ST[:, 1:2])
            k.ts(VV, VV, ST[:, 0:1], ALU.subtract, ST[:, 1:2], ALU.mult)
            k.tt(VV, VV, LG, ALU.mult)
            k.tt(VV, VV, LB, ALU.add)
            if l == DEPTH - 1:
                dst = V(self.d_yout[g, i * 128:(i + 1) * 128, :], None)
            else:
                dst = V(self.d_x1[g, i * 128:(i + 1) * 128, :], self.x1bufs[g][i])
            k.dma(dst, VV)
        k.close_scope()


def _shared_inputs(inp):
    perm = _perm_cols()
    sh = {}
    sh["w_ada"] = np.ascontiguousarray(inp["w_ada"], dtype=np.float32)
    sh["b_ada"] = np.ascontiguousarray(inp["b_ada"], dtype=np.float32)
    sh["w_in"] = np.ascontiguousarray(inp["w_in"][:, :, perm], dtype=np.float32)
    sh["w_out"] = np.ascontiguousarray(inp["w_out"], dtype=np.float32)
    sh["pfm"] = np.stack([_pack_pfm(inp, l) for l in range(DEPTH)])
    sh["prow"] = np.stack([_pack_prow(inp, l) for l in range(DEPTH)])
    sh["ln_g"] = np.ascontiguousarray(inp["ln_g"], dtype=np.float32)
    sh["ln_b"] = np.ascontiguousarray(inp["ln_b"], dtype=np.float32)
    wup = np.zeros((DEPTH, 2, 128, 512), np.float32)
    wup[:, 0] = inp["rwkv_w_up"].reshape(DEPTH, 128, 512)
    wup[:, 1] = inp["rwkv_a_up"].reshape(DEPTH, 128, 512)
    sh["wup"] = wup
    sh["consts"] = _consts()
    return sh


def _core_inputs(inp, c, sh):
    sb = c % 2
    m = dict(sh)
    xin = np.empty((2, TG, D), np.float32)
    xin[0] = inp["x_prompt"][4 * c:4 * c + 4].reshape(TG, D)
    xin[1] = inp["x_sample"][sb]
    m["xin"] = xin
    cT = np.empty((128, KC, 2), np.float32)
    cT[:, :, 0] = inp["c_ctx"].reshape(KC, 128).T
    cT[:, :, 1] = inp["c"][sb].reshape(KC, 128).T
    m["cT"] = cT.reshape(128, 32)
    rw = inp["state_rwkv"][sb]
    rw0 = np.zeros((DEPTH, 2, 4, 128, 128), np.float32)
    for p in range(4):
        for hh in range(2):
            rw0[:, :, p, hh * 64:(hh + 1) * 64, hh * 64:(hh + 1) * 64] = np.swapaxes(rw[:, :, 2 * p + hh], -1, -2)
    m["rw0"] = rw0
    ml0 = np.zeros((DEPTH, 2, 4, 128, 130), np.float32)
    ml0[..., 0:128] = np.swapaxes(inp["state_mlstm_c"][sb], -1, -2)
    ml0[..., 128] = inp["state_mlstm_n"][sb]
    m["ml0"] = ml0
    m["mm0"] = np.ascontiguousarray(np.broadcast_to(inp["state_mlstm_m"][sb].reshape(DEPTH, 1, 8), (DEPTH, 128, 8)), dtype=np.float32)
    m["ck"] = np.ascontiguousarray(inp["cache_attn_k"][sb], dtype=np.float32)
    m["cv"] = np.ascontiguousarray(inp["cache_attn_v"][sb], dtype=np.float32)
    return m


def _assemble(results):
    B = 8 * NSEQ_P
    y_prompt = np.empty((B, TSEQ_P, D), np.float32)
    y_sample = np.empty((2, TG, D), np.float32)
    new_k = np.empty((B, DEPTH, 8, TSEQ_P, 128), np.float32)
    new_v = np.empty((B, DEPTH, 8, TSEQ_P, 128), np.float32)
    new_rw = np.empty((B, DEPTH, 2, 8, 64, 64), np.float32)
    new_c = np.empty((B, DEPTH, 2, 4, 128, 128), np.float32)
    new_n = np.empty((B, DEPTH, 2, 4, 128), np.float32)
    new_m = np.empty((B, DEPTH, 2, 4), np.float32)
    for c, r in enumerate(results):
        bs = slice(4 * c, 4 * c + 4)
        y_prompt[bs] = r["yout"][0].reshape(4, TSEQ_P, D)
        if c < 2:
            y_sample[c] = r["yout"][1]
        new_k[bs] = r["nk"]
        new_v[bs] = r["nv"]
        nrw = r["nrw"]
        for p in range(4):
            for hh in range(2):
                blk = nrw[:, :, :, p, hh * 64:(hh + 1) * 64, hh * 64:(hh + 1) * 64]
                new_rw[bs, :, :, 2 * p + hh] = np.swapaxes(blk, -1, -2)
        nmc = r["nmc"]
        new_c[bs] = np.swapaxes(nmc[..., 0:128], -1, -2)
        new_n[bs] = nmc[..., 128]
        new_m[bs] = np.transpose(r["nmm"].reshape(DEPTH, 4, 2, 4), (1, 0, 2, 3))
    return (y_prompt, y_sample, new_k, new_v, new_rw, new_c, new_n, new_m)


def kernel(**inputs):
    inp = {k: np.asarray(v) for k, v in inputs.items()}
    prog = Prog()
    nc = prog.build()
    sh = _shared_inputs(inp)
    in_maps = [_core_inputs(inp, c, sh) for c in range(8)]
    res = run_bass_kernel_spmd(nc, in_maps, core_ids=list(range(8)))
    return _assemble(res.results)
```

```python
import math
from contextlib import ExitStack

import numpy as np
import concourse.bass as bass
import concourse.mybir as mybir
from concourse.bass_utils import run_bass_kernel_spmd

F32 = mybir.dt.float32
BF16 = mybir.dt.bfloat16
AF = mybir.ActivationFunctionType
ALU = mybir.AluOpType
AX = mybir.AxisListType

D = 2048
KC = 16
DEPTH = 2
NSEQ_P = 4
TSEQ_P = 256
TG = 1024
NT = 8
NCH = 16
PAST = 512
P_IN = 8976
ALPHA = (2 * DEPTH) ** 0.25
LN_EPS = 1e-5
GN_EPS_A = 64e-5
GN_EPS = 1e-5
RMS_EPS = 1e-5
WDECAY = math.exp(-0.5)

U_LORA = 0
U_RW = [256 + 512 * p for p in range(4)]
U_ML = [2304 + 640 * h for h in range(4)]
U_MG = 2304 + 2560
U_AT = [4880 + 512 * h for h in range(8)]


def _perm_cols():
    perm = []
    perm += list(range(1536, 1792))
    for p in range(4):
        perm += list(range(p * 128, p * 128 + 128))
        perm += list(range(512 + p * 128, 512 + p * 128 + 128))
        perm += list(range(1024 + p * 128, 1024 + p * 128 + 128))
        perm += list(range(1792 + p * 128, 1792 + p * 128 + 128))
    b0 = 2304
    for h in range(4):
        perm += list(range(b0 + h * 128, b0 + h * 128 + 128))
        perm += list(range(b0 + 512 + h * 128, b0 + 512 + h * 128 + 128))
        perm += list(range(4368 + h * 128, 4368 + h * 128 + 128))
        perm += list(range(3328 + h * 128, 3328 + h * 128 + 128))
        perm += list(range(3840 + h * 128, 3840 + h * 128 + 128))
    perm += list(range(4352, 4368))
    for h in range(8):
        perm += list(range(4880 + h * 128, 4880 + h * 128 + 128))
        perm += list(range(5904 + h * 128, 5904 + h * 128 + 128))
        perm += list(range(6928 + h * 128, 6928 + h * 128 + 128))
        perm += list(range(7952 + h * 128, 7952 + h * 128 + 128))
    assert len(perm) == P_IN and len(set(perm)) == P_IN
    return np.array(perm)


C_IDENT = 0
C_MASK = 128
C_ONESBD = C_MASK + 8 * 128
C_ONES = C_ONESBD + 128
C_HM = C_ONES + 128
C_COS = C_HM + 2
C_SIN = C_COS + 512
NCONST = C_SIN + 512


def _consts():
    c = np.zeros((128, NCONST), np.float32)
    c[:, C_IDENT:C_IDENT + 128] = np.eye(128)
    r = np.arange(128)[:, None]
    q = np.arange(128)[None, :]
    same = (r // 64) == (q // 64)
    us = (same & (r < q)).astype(np.float32)
    ui = (same & (r <= q)).astype(np.float32)
    ls = (same & (r > q)).astype(np.float32)
    li = (same & (r >= q)).astype(np.float32)
    for i, m in enumerate([-us, ui, us, ui, -ls, li, ls, li]):
        c[:, C_MASK + i * 128:C_MASK + (i + 1) * 128] = m
    c[:, C_ONESBD:C_ONESBD + 128] = same.astype(np.float32)
    c[:, C_ONES:C_ONES + 128] = 1.0
    c[:64, C_HM] = 1.0
    c[64:, C_HM + 1] = 1.0
    half = 32
    inv = 1.0 / (10000.0 ** (np.arange(0, half, 2, dtype=np.float32) / half))
    t = (np.arange(8)[None, :] * 128 + np.arange(128)[:, None]).astype(np.float32)
    row = np.floor(t / 64.0)
    col = t - row * 64.0
    cos = np.zeros((128, 8, 4, 16), np.float32)
    sin = np.zeros((128, 8, 4, 16), np.float32)
    for br in range(2):
        for rc, pos in enumerate([row, col]):
            ang = pos[:, :, None] * inv[None, None, :]
            cos[:, :, br * 2 + rc, :] = np.cos(ang)
            sin[:, :, br * 2 + rc, :] = np.sin(ang)
    c[:, C_COS:C_COS + 512] = cos.reshape(128, 512)
    c[:, C_SIN:C_SIN + 512] = sin.reshape(128, 512)
    return c


PF = {}
_n = 0
for _p in range(4):
    for _j in range(3):
        for _tap in range(3):
            PF[("ca_w", _p, _j, _tap)] = _n; _n += 1
        PF[("ca_b", _p, _j)] = _n; _n += 1
    for _d in range(2):
        PF[("w0", _p, _d)] = _n; _n += 1
        PF[("a0", _p, _d)] = _n; _n += 1
    for _nm in ("k_k", "k_a", "omk_a", "r_k", "gn_g", "gn_b"):
        PF[("rw_" + _nm, _p)] = _n; _n += 1
for _h in range(4):
    for _j in range(2):
        for _tap in range(3):
            PF[("cb_w", _h, _j, _tap)] = _n; _n += 1
        PF[("cb_b", _h, _j)] = _n; _n += 1
    PF[("ml_gn_g", _h)] = _n; _n += 1
    PF[("ml_gn_b", _h)] = _n; _n += 1
PF[("subln",)] = _n; _n += 1
NPF = _n

PR_BI = 0
PR_BF = 8
PR_LQ1 = 16
PR_LK1 = 80
PR_LQ2 = 144
PR_LK2 = 208
NPR = 272


def _pack_pfm(inp, l):
    o = np.zeros((128, NPF), np.float32)
    for p in range(4):
        sl = slice(p * 128, p * 128 + 128)
        for j in range(3):
            for tap in range(3):
                o[:, PF[("ca_w", p, j, tap)]] = inp["conv_a_w"][l, tap, j * 512 + p * 128: j * 512 + p * 128 + 128]
            o[:, PF[("ca_b", p, j)]] = inp["conv_a_b"][l, j * 512 + p * 128: j * 512 + p * 128 + 128]
        for d in range(2):
            o[:, PF[("w0", p, d)]] = inp["rwkv_w0"][l, d, sl]
            o[:, PF[("a0", p, d)]] = inp["rwkv_a0"][l, d, sl]
        o[:, PF[("rw_k_k", p)]] = inp["rwkv_k_k"][l, sl]
        o[:, PF[("rw_k_a", p)]] = inp["rwkv_k_a"][l, sl]
        o[:, PF[("rw_r_k", p)]] = inp["rwkv_r_k"][l].reshape(512)[sl]
        o[:, PF[("rw_gn_g", p)]] = inp["rwkv_gn_g"][l, sl]
        o[:, PF[("rw_gn_b", p)]] = inp["rwkv_gn_b"][l, sl]
    for h in range(4):
        sl = slice(h * 128, h * 128 + 128)
        for j in range(2):
            for tap in range(3):
                o[:, PF[("cb_w", h, j, tap)]] = inp["conv_b_w"][l, tap, j * 512 + h * 128: j * 512 + h * 128 + 128]
            o[:, PF[("cb_b", h, j)]] = inp["conv_b_b"][l, j * 512 + h * 128: j * 512 + h * 128 + 128]
        o[:, PF[("ml_gn_g", h)]] = inp["mlstm_gn_g"][l, sl]
        o[:, PF[("ml_gn_b", h)]] = inp["mlstm_gn_b"][l, sl]
    o[:, PF[("subln",)]] = inp["diff_subln_g"][l]
    return o


def _pack_prow(inp, l):
    o = np.zeros((128, NPR), np.float32)
    o[:, PR_BI:PR_BI + 8] = inp["mlstm_b_i"][l].reshape(8)[None, :]
    o[:, PR_BF:PR_BF + 8] = inp["mlstm_b_f"][l].reshape(8)[None, :]
    o[:, PR_LQ1:PR_LQ1 + 64] = inp["diff_lq1"][l][None, :]
    o[:, PR_LK1:PR_LK1 + 64] = inp["diff_lk1"][l][None, :]
    o[:, PR_LQ2:PR_LQ2 + 64] = inp["diff_lq2"][l][None, :]
    o[:, PR_LK2:PR_LK2 + 64] = inp["diff_lk2"][l][None, :]
    return o


ENGS = ("pe", "act", "dve", "pool", "sp")
SAME_ENGINE_SYNC = True
SELF_RAW_ONLY = True


class Buf:
    __slots__ = ("name", "w", "r", "dsem", "excl")

    def __init__(self, name, init=None):
        self.name = name
        self.excl = False
        self.w = dict(init) if init else {}
        self.r = {}
        self.dsem = None


class Sched:
    def __init__(self, nc, es, n_dma_sems=90):
        self.nc = nc
        self.eng = {"pe": nc.tensor, "act": nc.scalar, "dve": nc.vector, "pool": nc.gpsimd, "sp": nc.sync}
        self.cnt = {}
        self.waited = {e: {} for e in ENGS}
        self.sem = {}
        for e in ENGS:
            self.sem[e] = es.enter_context(nc.semaphore("s_" + e))
            self.cnt[e] = 0
        self.free_dsems = [es.enter_context(nc.semaphore("d%d" % i)) for i in range(n_dma_sems)]
        self.n_dsem = 0
        self.nops = {e: 0 for e in ENGS}
        self.nwaits = {e: 0 for e in ENGS}
        self.fence = {}
        self.all_dma_events = {}
        self.recycled = []

    def newbuf(self, name, fenced=True):
        return Buf(name, self.fence if fenced else None)

    def close_scope(self, bufs):
        for b in bufs:
            for dct in (b.w, b.r):
                for k, v in dct.items():
                    if self.fence.get(k, 0) < v:
                        self.fence[k] = v
            if b.dsem is not None:
                self.recycled.append(b.dsem)

    def _dsem_for(self, b):
        if b.dsem is None:
            if self.recycled:
                key = self.recycled.pop()
            else:
                key = "D%d" % self.n_dsem
                self.sem[key] = self.free_dsems[self.n_dsem]
                self.n_dsem += 1
                self.cnt[key] = 0
            b.dsem = key
        return b.dsem

    def _emit_wait(self, ename, k, v):
        self.eng[ename].wait_ge(self.sem[k], v)
        self.nwaits[ename] += 1

    def _waits(self, eng, reads, writes):
        deps = {}
        for b in reads:
            for k, v in b.w.items():
                if deps.get(k, 0) < v:
                    deps[k] = v
        for b in writes:
            for k, v in b.w.items():
                if k == eng and SELF_RAW_ONLY:
                    continue
                if deps.get(k, 0) < v:
                    deps[k] = v
            for k, v in b.r.items():
                if k == eng and SELF_RAW_ONLY:
                    continue
                if deps.get(k, 0) < v:
                    deps[k] = v
        wd = self.waited[eng]
        for k, v in deps.items():
            if k == eng and (eng == "pe" or not SAME_ENGINE_SYNC):
                continue
            if wd.get(k, 0) >= v:
                continue
            wd[k] = v
            self._emit_wait(eng, k, v)

    def op(self, eng, fn, reads=(), writes=()):
        writes = [b for b in writes if b is not None] + [b for b in reads if b is not None and b.excl]
        reads = [b for b in reads if b is not None and not b.excl]
        self._waits(eng, reads, writes)
        self.cnt[eng] += 1
        n = self.cnt[eng]
        inst = fn(self.eng[eng])
        inst.then_inc(self.sem[eng], 1)
        self.nops[eng] += 1
        for b in reads:
            if b.r.get(eng, 0) < n:
                b.r[eng] = n
        for b in writes:
            b.w = {eng: n}
            b.r = {}

    def dma(self, qeng, fn, reads=(), writes=(), sembuf=None):
        reads = [b for b in reads if b is not None]
        writes = [b for b in writes if b is not None]
        self._waits(qeng, reads, writes)
        if sembuf is None:
            sembuf = writes[0] if writes else reads[0]
        key = self._dsem_for(sembuf)
        self.cnt[key] += 16
        n = self.cnt[key]
        inst = fn(self.eng[qeng])
        inst.then_inc(self.sem[key], 16)
        self.nops[qeng] += 1
        self.all_dma_events[key] = n
        for b in reads:
            if b.r.get(key, 0) < n:
                b.r[key] = n
        for b in writes:
            b.w = {key: n}
            b.r = {}

    def finish(self):
        for k, v in self.all_dma_events.items():
            if self.waited["sp"].get(k, 0) < v:
                self.waited["sp"][k] = v
                self._emit_wait("sp", k, v)
        for e in ("pe", "act", "dve", "pool"):
            if self.cnt[e] > 0 and self.waited["sp"].get(e, 0) < self.cnt[e]:
                self._emit_wait("sp", e, self.cnt[e])


class V:
    __slots__ = ("ap", "b")

    def __init__(self, ap, b):
        self.ap = ap
        self.b = b

    def __getitem__(self, idx):
        return V(self.ap[idx], self.b)

    def rr(self, pat, **kw):
        return V(self.ap.rearrange(pat, **kw), self.b)

    def bc(self, shape):
        return V(self.ap.broadcast_to(shape), self.b)

    def us(self, axis):
        return V(self.ap.unsqueeze(axis), self.b)

    def bitcast(self, dt):
        return V(self.ap.bitcast(dt), self.b)


class K:
    def __init__(self, nc, es):
        self.nc = nc
        self.S = Sched(nc, es)
        self.scopes = []

    def open_scope(self):
        es = ExitStack()
        es.__enter__()
        self.scopes.append((es, []))

    def close_scope(self):
        es, bufs = self.scopes.pop()
        self.S.close_scope(bufs)
        es.__exit__(None, None, None)

    _uid = 0

    def sb(self, name, shape, dt=F32):
        es, bufs = self.scopes[-1]
        K._uid += 1
        name = "%s_%d" % (name, K._uid)
        h = es.enter_context(self.nc.sbuf_tensor(name, list(shape), dt))
        b = self.S.newbuf(name)
        bufs.append(b)
        return V(h.ap(), b)

    def ps(self, name, shape, dt=F32):
        es, bufs = self.scopes[-1]
        h = es.enter_context(self.nc.psum_tensor(name, list(shape), dt))
        b = self.S.newbuf(name)
        bufs.append(b)
        return V(h.ap(), b)

    def dram(self, ap, tracked=False, name="dram"):
        return V(ap, self.S.newbuf(name, fenced=False) if tracked else None)

    def act(self, out, in_, func, bias=None, scale=None, accum=None, eng="act"):
        kw = {}
        reads = [in_.b]
        if bias is not None:
            if isinstance(bias, V):
                kw["bias"] = bias.ap
                reads.append(bias.b)
            else:
                kw["bias"] = float(bias)
        if scale is not None:
            if isinstance(scale, V):
                kw["scale"] = scale.ap
                reads.append(scale.b)
            else:
                kw["scale"] = float(scale)
        writes = [out.b]
        if accum is not None:
            kw["accum_out"] = accum.ap
            writes.append(accum.b)
        self.S.op("act", lambda e: e.activation(out=out.ap, in_=in_.ap, func=func, **kw), reads, writes)

    def tt(self, out, in0, in1, op, eng="dve"):
        self.S.op(eng, lambda e: e.tensor_tensor(out=out.ap, in0=in0.ap, in1=in1.ap, op=op), [in0.b, in1.b], [out.b])

    def ts(self, out, in0, s1, op0, s2=None, op1=None, accum=None, eng="dve"):
        reads = [in0.b]
        a1 = s1.ap if isinstance(s1, V) else float(s1)
        if isinstance(s1, V):
            reads.append(s1.b)
        kw = {}
        if s2 is not None:
            a2 = s2.ap if isinstance(s2, V) else float(s2)
            if isinstance(s2, V):
                reads.append(s2.b)
            kw["op1"] = op1
        else:
            a2 = None
        writes = [out.b]
        if accum is not None:
            kw["accum_out"] = accum.ap
            writes.append(accum.b)
            if op1 is not None:
                kw["op1"] = op1
        self.S.op(eng, lambda e: e.tensor_scalar(out=out.ap, in0=in0.ap, scalar1=a1, scalar2=a2, op0=op0, **kw), reads, writes)

    def stt(self, out, in0, scalar, in1, op0, op1):
        reads = [in0.b, in1.b]
        sc = scalar.ap if isinstance(scalar, V) else float(scalar)
        if isinstance(scalar, V):
            reads.append(scalar.b)
        self.S.op("dve", lambda e: e.scalar_tensor_tensor(out=out.ap, in0=in0.ap, scalar=sc, in1=in1.ap, op0=op0, op1=op1), reads, [out.b])

    def copy(self, out, in_, eng="dve"):
        if eng == "act":
            self.act(out, in_, AF.Copy)
        else:
            self.S.op(eng, lambda e: e.tensor_copy(out=out.ap, in_=in_.ap), [in_.b], [out.b])

    def memset(self, out, val, eng="pool"):
        self.S.op(eng, lambda e: e.memset(out.ap, float(val)), [], [out.b])

    def reduce(self, out, in_, op, axis=AX.X, eng="dve"):
        self.S.op(eng, lambda e: e.tensor_reduce(out=out.ap, in_=in_.ap, axis=axis, op=op), [in_.b], [out.b])

    def recip(self, out, in_):
        self.S.op("dve", lambda e: e.reciprocal(out=out.ap, in_=in_.ap), [in_.b], [out.b])

    def scan(self, out, d0, d1, initial, op0, op1):
        self.S.op("dve", lambda e: e.tensor_tensor_scan(out=out.ap, data0=d0.ap, data1=d1.ap, initial=float(initial), op0=op0, op1=op1), [d0.b, d1.b], [out.b])

    def mm(self, out, lhsT, rhs, start=True, stop=True):
        self.S.op("pe", lambda e: e.matmul(out.ap, lhsT=lhsT.ap, rhs=rhs.ap, start=start, stop=stop), [lhsT.b, rhs.b], [out.b])

    def tr(self, out, in_, ident):
        self.S.op("pe", lambda e: e.transpose(out=out.ap, in_=in_.ap, identity=ident.ap), [in_.b, ident.b], [out.b])

    def dma(self, out, in_, q="sp"):
        wr = [out.b]
        rd = [in_.b]
        out_is_dram = "DRAM" in str(out.ap.space).upper()
        sembuf = in_.b if (out_is_dram and in_.b is not None) else out.b
        self.S.dma(q, lambda e: e.dma_start(out=out.ap, in_=in_.ap), rd, wr, sembuf=sembuf)


class _Stop(Exception):
    pass


class Prog:
    stop_stage = None

    def stage(self, name):
        if self.stop_stage is not None and name == self.stop_stage:
            raise _Stop()

    def run_unit(self, fn, *a):
        depth = len(self.k.scopes)
        try:
            fn(*a)
        except _Stop:
            while len(self.k.scopes) > depth:
                self.k.close_scope()

    def __init__(self, dbg=False, layers=(0, 1), groups=(0, 1), parts=None):
        self.dbg = dbg
        self.layers = layers
        self.groups = groups
        self.parts = parts
        self.dumps = {}
        self.bank_i = 0
        self.tbank = 0
        self.abank = 0

    def want(self, part):
        return self.parts is None or part in self.parts

    def declare(self):
        nc = self.nc
        di = lambda n, s: nc.dram_tensor(n, list(s), F32, kind="ExternalInput").ap()
        do = lambda n, s: nc.dram_tensor(n, list(s), F32, kind="ExternalOutput").ap()
        dint = lambda n, s: nc.dram_tensor(n, list(s), F32, kind="Internal").ap()
        self.d_xin = di("xin", [2, TG, D])
        self.d_cT = di("cT", [128, 32])
        self.d_wada = di("w_ada", [2, D, 3 * D])
        self.d_bada = di("b_ada", [2, 3 * D])
        self.d_win = di("w_in", [2, D, P_IN])
        self.d_wout = di("w_out", [2, D, D])
        self.d_pfm = di("pfm", [2, 128, NPF])
        self.d_prow = di("prow", [2, 128, NPR])
        self.d_lng = di("ln_g", [2, D])
        self.d_lnb = di("ln_b", [2, D])
        self.d_wup = di("wup", [2, 2, 128, 512])
        self.d_rw0 = di("rw0", [2, 2, 4, 128, 128])
        self.d_ml0 = di("ml0", [2, 2, 4, 128, 130])
        self.d_mm0 = di("mm0", [2, 128, 8])
        self.d_ck = di("ck", [2, 8, PAST, 128])
        self.d_cv = di("cv", [2, 8, PAST, 128])
        self.d_consts = di("consts", [128, NCONST])
        self.d_yout = do("yout", [2, TG, D])
        self.d_nk = do("nk", [4, 2, 8, TSEQ_P, 128])
        self.d_nv = do("nv", [4, 2, 8, TSEQ_P, 128])
        self.d_nrw = do("nrw", [4, 2, 2, 4, 128, 128])
        self.d_nmc = do("nmc", [4, 2, 2, 4, 128, 130])
        self.d_nmm = do("nmm", [2, 32])
        self.d_x1 = dint("x1s", [2, TG, D])
        self.d_modg = dint("modg", [2, 2, D])

    def dump(self, name, v, shape):
        if not self.dbg:
            return
        ap = self.nc.dram_tensor("dbg_" + name, list(shape), F32, kind="ExternalOutput").ap()
        self.dumps[name] = list(shape)
        k = self.k
        if v.ap.dtype != F32:
            k.open_scope()
            t = k.sb("dbgt_" + name, shape, F32)
            k.copy(t, v)
            k.dma(V(ap, None), t)
            k.close_scope()
        else:
            k.dma(V(ap, None), v)

    def bank(self):
        b = self.PB[self.bank_i % 8]
        self.bank_i += 1
        return b

    def build(self):
        self.nc = nc = bass.Bass("TRN2", target_bir_lowering=False)
        self.declare()
        with ExitStack() as es:
            self.k = k = K(nc, es)
            k.open_scope()
            self.setup()
            for l in self.layers:
                self.layer_setup(l)
                for g in self.groups:
                    self.group(l, g)
            k.S.finish()
            k.close_scope()
        return nc

    def setup(self):
        k = self.k
        self.CON = k.sb("CON", [128, NCONST])
        k.dma(self.CON, V(self.d_consts, None))
        C = self.CON
        self.IDF = C[:, C_IDENT:C_IDENT + 128]
        self.MASK = C[:, C_MASK:C_MASK + 1024].rr("p (a b) -> p a b", a=8)
        self.ONESBD = C[:, C_ONESBD:C_ONESBD + 128]
        self.ONES = C[:, C_ONES:C_ONES + 128]
        self.HM = C[:, C_HM:C_HM + 2]
        self.COS = C[:, C_COS:C_COS + 512].rr("p (i a b) -> p i a b", i=8, a=4)
        self.SIN = C[:, C_SIN:C_SIN + 512].rr("p (i a b) -> p i a b", i=8, a=4)
        self.IDB = k.sb("IDB", [128, 128], BF16)
        k.copy(self.IDB, self.IDF)
        es, bufs = k.scopes[-1]
        h = es.enter_context(self.nc.psum_tensor("PS", [128, 8, 512], F32))
        self.PB = []
        for i in range(8):
            b = k.S.newbuf("bank%d" % i)
            b.excl = True
            bufs.append(b)
            self.PB.append(V(h.ap()[:, i, :], b))
        self.mixT = k.sb("mixT", [128, KC, TG], BF16)
        self.WS = k.sb("WS", [128, KC, 656], BF16)
        self.PFM = k.sb("PFM", [128, NPF])
        self.PROW = k.sb("PROW", [128, NPR])
        self.WUP = k.sb("WUP", [128, 2, 512], BF16)
        self.MODT = k.sb("MODT", [128, 2, 32])
        self.LAM = k.sb("LAM", [128, 2])
        self.SUBG = k.sb("SUBG", [128, 1])
        self.RMASK = k.sb("RMASK", [128, TG])
        k.memset(self.RMASK, 1.0)
        k.memset(self.RMASK.rr("p (c t) -> p c t", t=64)[:, :, 0:1], 0.0)
        self.EPSC = k.sb("EPSC", [128, 4])
        k.memset(self.EPSC[:, 0:1], 1e-12)
        k.memset(self.EPSC[:, 1:2], GN_EPS_A)
        k.memset(self.EPSC[:, 2:3], GN_EPS)
        k.memset(self.EPSC[:, 3:4], 1.0)
        self.EPS12 = self.EPSC[:, 0:1]
        self.EPSA = self.EPSC[:, 1:2]
        self.EPSB = self.EPSC[:, 2:3]
        self.ONE1 = self.EPSC[:, 3:4]
        self.x1bufs = [[k.S.newbuf("x1_%d_%d" % (g, i), fenced=False) for i in range(NT)] for g in range(2)]
        self.modgbuf = [[k.S.newbuf("modg%d%d" % (l, j), fenced=False) for j in range(2)] for l in range(2)]

    def pf(self, *key):
        c = PF[key]
        return self.PFM[:, c:c + 1]

    def layer_setup(self, l):
        k = self.k
        k.dma(self.PFM, V(self.d_pfm[l], None))
        k.dma(self.PROW, V(self.d_prow[l], None))
        for p in range(4):
            k.ts(self.pf("rw_omk_a", p), self.pf("rw_k_a", p), -1.0, ALU.mult, 1.0, ALU.add)
        k.dma(self.WUP, V(self.d_wup[l].rearrange("a p c -> p a c"), None), q="pool")
        lam_init = 0.8 - 0.6 * math.exp(-0.3 * l)
        k.open_scope()
        t = k.sb("lamt", [128, 64])
        s = k.sb("lams", [128, 2])
        k.tt(t, self.PROW[:, PR_LQ1:PR_LQ1 + 64], self.PROW[:, PR_LK1:PR_LK1 + 64], ALU.mult)
        k.reduce(s[:, 0:1], t, ALU.add)
        k.tt(t, self.PROW[:, PR_LQ2:PR_LQ2 + 64], self.PROW[:, PR_LK2:PR_LK2 + 64], ALU.mult)
        k.reduce(s[:, 1:2], t, ALU.add)
        k.act(s, s, AF.Exp)
        k.tt(self.LAM[:, 0:1], s[:, 0:1], s[:, 1:2], ALU.subtract)
        k.ts(self.LAM[:, 0:1], self.LAM[:, 0:1], lam_init, ALU.add)
        k.ts(self.SUBG, self.pf("subln"), 1.0 - lam_init, ALU.mult)
        k.close_scope()
        if self.want("ada"):
            self.ada(l)

    def prefetch(self):
        if self.wi < len(self.wlist):
            c0, ncols = self.wlist[self.wi]
            self.wi += 1
            self.load_w(self.d_win[self.l], c0, ncols)

    def load_w(self, src_rows, c0, ncols):
        k = self.k
        src = src_rows.rearrange("(kc p) c -> p kc c", p=128)[:, :, c0:c0 + ncols]
        k.dma(self.WS[:, :, 0:ncols], V(src, None), q="pool")

    def ada(self, l):
        k = self.k
        k.open_scope()
        ct = k.sb("ada_c", [128, 32])
        cb = k.sb("ada_cb", [128, 32], BF16)
        k.dma(ct, V(self.d_cT, None))
        k.act(cb, ct, AF.Silu)
        brow = k.sb("ada_brow", [1, 512])
        rowt = [k.sb("ada_row%d" % j, [1, 512]) for j in range(2)]
        pm = self.PB[7]
        WA = [self.WS, k.sb("ada_w2", [128, KC, 520], BF16)]
        for s in range(12):
            wa = WA[s % 2]
            srcw = self.d_wada[l].rearrange("(kc p) c -> p kc c", p=128)[:, :, s * 512:(s + 1) * 512]
            k.dma(wa[:, :, 0:512], V(srcw, None), q="pool")
            k.dma(brow, V(self.d_bada[l:l + 1, s * 512:(s + 1) * 512], None))
            for j in range(2):
                pb = self.PB[j + 2 * (s % 2)]
                for kc in range(KC):
                    k.mm(pb[0:1, 0:512], cb[:, kc * 2 + j:kc * 2 + j + 1], wa[:, kc, 0:512], start=(kc == 0), stop=(kc == KC - 1))
                k.tt(rowt[j], pb[0:1, 0:512], brow, ALU.add)
                if s < 8:
                    for q in range(4):
                        c = s * 4 + q
                        k.mm(pm[:, (j * 32 + c) * 2:(j * 32 + c) * 2 + 2], rowt[j][0:1, q * 128:(q + 1) * 128], self.ONES[0:1, 0:2])
                else:
                    k.dma(V(self.d_modg[l, j:j + 1, (s - 8) * 512:(s - 7) * 512], self.modgbuf[l][j]), rowt[j])
        k.copy(self.MODT.rr("p a b -> p (a b)"), pm[:, 0:128].rr("p (c t) -> p c t", t=2)[:, :, 0])
        k.ts(self.MODT[:, :, 16:32], self.MODT[:, :, 16:32], 1.0, ALU.add)
        self.dump("modT%d" % l, self.MODT, [128, 2, 32])
        k.close_scope()

    def group(self, l, g):
        k = self.k
        self.l, self.g = l, g
        self.nseq, self.tseq = (NSEQ_P, TSEQ_P) if g == 0 else (1, TG)
        tag = "%d%d" % (l, g)
        k.open_scope()
        self.uT = k.sb("uT" + tag, [128, KC, TG], BF16)
        self.LW = k.sb("LW" + tag, [128, TG], BF16)
        self.LA = k.sb("LA" + tag, [128, TG], BF16)
        self.OMG = k.sb("OMG" + tag, [128, NT, 8])
        self.CLAMP = k.sb("CLAMP" + tag, [128, NT, 8])
        self.SC = k.sb("SC" + tag, [128, NCH, 8])
        if self.want("u"):
            self.uphase(l, g)
        units = []
        if self.want("lora"):
            units.append((self.unit_lora, (l, g), U_LORA, 256))
        for p in range(4):
            if self.want("rw%d" % p):
                units.append((self.unit_rwkv, (l, g, p), U_RW[p], 512))
        if self.want("mg"):
            units.append((self.unit_mg, (l, g), U_MG, 16))
        for h in range(4):
            if self.want("ml%d" % h):
                units.append((self.unit_mlstm, (l, g, h), U_ML[h], 640))
        for h in range(8):
            if self.want("at%d" % h):
                units.append((self.unit_attn3, (l, g, h), U_AT[h], 512))
        self.wlist = [(u[2], u[3]) for u in units]
        self.wi = 0
        self.prefetch()
        for ui, (fn, args, _c0, _nc) in enumerate(units):
            self.run_unit(fn, *args)
            while self.wi < min(ui + 2, len(self.wlist)):
                self.prefetch()
        k.close_scope()
        if self.want("o"):
            self.phase_o(l, g)

    def xsrc(self, l, g, i):
        if l == 0:
            return V(self.d_xin[g, i * 128:(i + 1) * 128, :], None)
        return V(self.d_x1[g, i * 128:(i + 1) * 128, :], self.x1bufs[g][i])

    def uphase(self, l, g):
        k = self.k
        k.open_scope()
        xts = [k.sb("xt%d" % j, [128, D]) for j in range(2)]
        for i in range(NT):
            xt = xts[i % 2]
            k.dma(xt, self.xsrc(l, g, i))
            for q in range(4):
                pb = self.bank()
                for kk in range(4):
                    kc = q * 4 + kk
                    k.tr(pb[:, kk * 128:(kk + 1) * 128], xt[:, kc * 128:(kc + 1) * 128], self.IDF)
                for kk in range(4):
                    kc = q * 4 + kk
                    k.act(self.uT[:, kc, i * 128:(i + 1) * 128], pb[:, kk * 128:(kk + 1) * 128], AF.Identity,
                          scale=self.MODT[:, g, 16 + kc:17 + kc], bias=self.MODT[:, g, kc:kc + 1])
        k.close_scope()
        self.dump("uT%d%d" % (l, g), self.uT[:, :, 0:256], [128, KC, 256])

    def proj_fm(self, col, evac):
        k = self.k
        for nb in range(2):
            pb = self.bank()
            for kc in range(KC):
                k.mm(pb[:, 0:512], self.WS[:, kc, col:col + 128], self.uT[:, kc, nb * 512:(nb + 1) * 512], start=(kc == 0), stop=(kc == KC - 1))
            evac(nb, pb[:, 0:512])

    def proj_tm(self, col, ncols, evac):
        k = self.k
        for i in range(NT):
            pb = self.bank()
            for kc in range(KC):
                k.mm(pb[:, 0:ncols], self.uT[:, kc, i * 128:(i + 1) * 128], self.WS[:, kc, col:col + ncols], start=(kc == 0), stop=(kc == KC - 1))
            evac(i, pb[:, 0:ncols])

    def pad_evac(self, PRE, j):
        k = self.k
        nseq, tseq = self.nseq, self.tseq

        def ev(nb, ps):
            if nseq == 1:
                k.act(PRE[:, j, 0, 1 + nb * 512:1 + (nb + 1) * 512], ps, AF.Copy)
            else:
                k.act(PRE[:, j, 2 * nb:2 * nb + 2, 1:tseq + 1], ps.rr("p (s t) -> p s t", s=2), AF.Copy)
        return ev

    def conv(self, X, PRE, j, w0, w1, w2, b):
        k = self.k
        nseq, tseq = self.nseq, self.tseq
        xv = X.rr("p (s t) -> p s t", s=nseq)
        k.act(xv, PRE[:, j, :, 1:tseq + 1], AF.Identity, scale=w1, bias=b)
        k.stt(xv, PRE[:, j, :, 0:tseq], w0, xv, ALU.mult, ALU.add)
        k.stt(xv, PRE[:, j, :, 2:tseq + 2], w2, xv, ALU.mult, ALU.add)

    def unit_lora(self, l, g):
        k = self.k
        self.proj_fm(0, lambda nb, ps: k.act(self.LW[:, nb * 512:(nb + 1) * 512], ps, AF.Tanh))
        self.proj_fm(128, lambda nb, ps: k.act(self.LA[:, nb * 512:(nb + 1) * 512], ps, AF.Copy))
        self.prefetch()
        self.dump("LW%d%d" % (l, g), self.LW, [128, TG])

    def unit_rwkv(self, l, g, p):
        k = self.k
        nseq, tseq = self.nseq, self.tseq
        cps = tseq // 64
        tag = "%d%d%d" % (l, g, p)
        k.open_scope()
        GT = k.sb("rGT" + tag, [128, TG])
        BON = k.sb("rBON" + tag, [128, TG])
        KR = [k.sb("rKR%d" % d + tag, [128, NT, 2, 128], BF16) for d in range(2)]
        AH = [k.sb("rAH%d" % d + tag, [128, TG], BF16) for d in range(2)]
        KH = [k.sb("rKH%d" % d + tag, [128, TG], BF16) for d in range(2)]
        VB = k.sb("rVB" + tag, [128, TG], BF16)
        GL = [k.sb("rGL%d" % d + tag, [128, NCH]) for d in range(2)]
        YTM = k.sb("rY" + tag, [128, NT, 128])

        k.open_scope()
        R = k.sb("rR" + tag, [128, TG])
        Kk = k.sb("rK" + tag, [128, TG])
        V32 = k.sb("rV" + tag, [128, TG])
        k.open_scope()
        PRE = k.sb("rPRE" + tag, [128, 3, nseq, tseq + 2])
        k.memset(PRE[:, :, :, 0:1], 0.0)
        k.memset(PRE[:, :, :, tseq + 1:tseq + 2], 0.0)
        for j in range(3):
            self.proj_fm(j * 128, self.pad_evac(PRE, j))
        self.proj_fm(384, lambda nb, ps: k.act(GT[:, nb * 512:(nb + 1) * 512], ps, AF.Silu))
        self.prefetch()
        for j, X in enumerate((R, Kk, V32)):
            self.conv(X, PRE, j, self.pf("ca_w", p, j, 0), self.pf("ca_w", p, j, 1), self.pf("ca_w", p, j, 2), self.pf("ca_b", p, j))
        k.close_scope()
        self.stage("rw_conv")
        B = [k.sb("rB%d" % i + tag, [128, TG]) for i in range(9)]
        k.copy(VB, V32, eng="pool")
        KK, SQ, RS = B[0], B[1], B[2]
        k.ts(KK, Kk, self.pf("rw_k_k", p), ALU.mult)
        k.act(SQ, KK, AF.Square)
        for nb in range(2):
            pb = self.bank()
            k.mm(pb[:, 0:512], self.ONESBD, SQ[:, nb * 512:(nb + 1) * 512])
            k.act(RS[:, nb * 512:(nb + 1) * 512], pb[:, 0:512], AF.Sqrt, bias=self.EPS12)
        k.recip(RS, RS)
        k.tt(KK, KK, RS, ALU.mult)
        self.stage("rw_kk")
        KS = B[8]
        v3 = lambda X: X.rr("p (i t) -> p i t", t=128)
        c3 = lambda X: X.rr("p (c t) -> p c t", t=64)
        for d in range(2):
            SIG, AA, KT, AL, CS, T1, T2 = B[1], B[2], B[3], B[4], B[5], B[6], B[7]
            dr = slice(d * 64, d * 64 + 64)
            for nb in range(2):
                ns = slice(nb * 512, (nb + 1) * 512)
                pb = self.bank()
                k.mm(pb[:, 0:512], self.WUP[dr, 0, p * 128:(p + 1) * 128], self.LW[dr, ns])
                k.act(SIG[:, ns], pb[:, 0:512], AF.Sigmoid, bias=self.pf("w0", p, d))
                pb = self.bank()
                k.mm(pb[:, 0:512], self.WUP[dr, 1, p * 128:(p + 1) * 128], self.LA[dr, ns])
                k.act(AA[:, ns], pb[:, 0:512], AF.Sigmoid, bias=self.pf("a0", p, d))
            k.ts(T1, AA, self.pf("rw_k_a", p), ALU.mult, self.pf("rw_omk_a", p), ALU.add)
            k.tt(KT, T1, Kk, ALU.mult)
            if d == 0:
                k.copy(KS, KT, eng="pool")
            else:
                k.tt(KS, KS, KT, ALU.add, eng="pool")
            k.tt(AL, AA, KK, ALU.mult)
            k.scan(CS, self.RMASK, SIG, 0.0, ALU.mult, ALU.add)
            if d == 1:
                k.tt(c3(T1), c3(CS), c3(CS)[:, :, 63:64].bc([128, NCH, 64]), ALU.subtract)
                k.tt(CS, SIG, T1, ALU.subtract)
            k.tt(T2, CS, SIG, ALU.subtract)
            k.act(T2, T2, AF.Exp, scale=-WDECAY)
            k.tt(KR[d][:, :, 0, :], v3(KK), v3(T2), ALU.mult)
            EP = T1
            k.act(EP, CS, AF.Exp, scale=-WDECAY)
            k.tt(KR[d][:, :, 1, :], v3(R), v3(EP), ALU.mult)
            k.copy(GL[d], c3(EP)[:, :, 63 if d == 0 else 0])
            EM = AA
            k.act(EM, CS, AF.Exp, scale=WDECAY)
            k.tt(AH[d], AL, EM, ALU.mult)
            k.tt(KH[d], KT, EM, ALU.mult)
        k.ts(B[1], R, self.pf("rw_r_k", p), ALU.mult)
        k.tt(B[1], B[1], KS, ALU.mult)
        for nb in range(2):
            ns = slice(nb * 512, (nb + 1) * 512)
            pb = self.bank()
            k.mm(pb[:, 0:512], self.ONESBD, B[1][:, ns])
            k.tt(BON[:, ns], pb[:, 0:512], V32[:, ns], ALU.mult)
        if p == 0:
            self.dump("rw_R%d%d" % (l, g), R, [128, TG])
            self.dump("rw_KK%d%d" % (l, g), KK, [128, TG])
            self.dump("rw_BON%d%d" % (l, g), BON, [128, TG])
            self.dump("rw_GL0%d%d" % (l, g), GL[0], [128, NCH])
            self.dump("rw_GL1%d%d" % (l, g), GL[1], [128, NCH])
        k.close_scope()

        self.stage("rw_prep")
        k.open_scope()
        TMB = k.sb("rTMB" + tag, [128, NT, 5, 128], BF16)
        for i in range(NT):
            pbb = self.bank().bitcast(BF16)
            ts_ = slice(i * 128, (i + 1) * 128)
            for s, src in enumerate((VB, KH[0], KH[1], AH[0], AH[1])):
                k.tr(pbb[:, s * 128:(s + 1) * 128], src[:, ts_], self.IDB)
            k.copy(TMB[:, i].rr("p a b -> p (a b)"), pbb[:, 0:640], eng=("act" if i % 2 else "dve"))
        self.stage("rw_tmb")
        k.memset(YTM, 0.0)
        self.RB = [k.sb("rRB%d" % i + tag, [128, 128], BF16) for i in range(2)]
        self.UB = [k.sb("rUB%d" % i + tag, [128, 128], BF16) for i in range(2)]
        self.HT = [k.sb("rHT%d" % i + tag, [128, 128]) for i in range(2)]
        Hf = {}
        Hb = {}
        for s in range(nseq):
            for d in range(2):
                Hf[(s, d)] = k.sb("rHf%d%d" % (s, d) + tag, [128, 128])
                Hb[(s, d)] = k.sb("rHb%d%d" % (s, d) + tag, [128, 128], BF16)
        for d in range(2):
            k.open_scope()
            GMQ = k.sb("rGMQ%d" % d + tag, [128, NCH, 4, 128], BF16)
            P0 = k.sb("rP0%d" % d + tag, [128, NCH, 128], BF16)
            QA = k.sb("rQA%d" % d + tag, [128, NCH, 128], BF16)
            PA = k.sb("rPA%d" % d + tag, [128, NCH, 128], BF16)
            X = k.sb("rX%d" % d + tag, [128, NCH, 128], BF16)
            R1 = k.sb("rR1%d" % d + tag, [128, NT, 128])
            mP = 4 if d == 0 else 0
            for i in range(NT):
                ts_ = slice(i * 128, (i + 1) * 128)
                for hh in range(2):
                    j = i * 2 + hh
                    hr = slice(hh * 64, hh * 64 + 64)
                    pb = self.bank()
                    krv = KR[d][hr, i].rr("p a b -> p (a b)")
                    k.mm(pb[:, 0:256], AH[d][hr, ts_], krv)
                    k.mm(pb[:, 256:512], KH[d][hr, ts_], krv)
                    k.tt(GMQ[:, j], pb[:, 0:512].rr("p (a b) -> p a b", a=4), self.MASK[:, 4 * d:4 * d + 4, :], ALU.mult)
            for hh in range(2):
                hr = slice(hh * 64, hh * 64 + 64)
                for q in range(2):
                    pbP = self.bank()
                    for ii in range(4):
                        i = q * 4 + ii
                        k.mm(pbP[:, ii * 128:(ii + 1) * 128], KR[d][hr, i, 0, :], AH[d][hr, i * 128:(i + 1) * 128])
                    k.tt(P0[:, q * 8 + hh:q * 8 + 8:2, :], pbP[:, 0:512].rr("p (a b) -> p a b", a=4),
                         self.MASK[:, mP:mP + 1, :].bc([128, 4, 128]), ALU.mult)
            self.stage("rw_gram")
            k.tt(X, GMQ[:, :, 0, :], self.IDB.us(1).bc([128, NCH, 128]), ALU.add)
            Qc, Pc = GMQ[:, :, 0, :], P0
            Qn, Pn = QA, PA
            for lev in range(1, 6):
                for grp in range(4):
                    js = range(grp * 4, grp * 4 + 4)
                    gsl = slice(grp * 4, grp * 4 + 4)
                    pbP = self.bank()
                    for jj, j in enumerate(js):
                        k.mm(pbP[:, jj * 128:(jj + 1) * 128], Qc[:, j, :], Pc[:, j, :])
                    k.copy(Pn[:, gsl, :], pbP[:, 0:512].rr("p (a b) -> p a b", a=4), eng="act")
                    if lev < 5:
                        pbQ = self.bank()
                        for jj, j in enumerate(js):
                            k.mm(pbQ[:, jj * 128:(jj + 1) * 128], Pc[:, j, :], Qc[:, j, :])
                        k.copy(Qn[:, gsl, :], pbQ[:, 0:512].rr("p (a b) -> p a b", a=4), eng="dve")
                    pbX = self.bank()
                    for jj, j in enumerate(js):
                        k.mm(pbX[:, jj * 128:(jj + 1) * 128], Pn[:, j, :], X[:, j, :])
                    k.tt(X[:, gsl, :], X[:, gsl, :], pbX[:, 0:512].rr("p (a b) -> p a b", a=4), ALU.add)
                if lev == 1:
                    Qc, Pc, Qn, Pn = QA, PA, k.sb("rQB%d" % d + tag, [128, NCH, 128], BF16), P0
                else:
                    Qc, Pc, Qn, Pn = Qn, Pn, Qc, Pc
            self.stage("rw_inv")
            for i in range(NT):
                pb = self.bank()
                for hh in range(2):
                    j = i * 2 + hh
                    vs = TMB[:, i, 0, hh * 64:(hh + 1) * 64]
                    k.mm(pb[:, hh * 64:(hh + 1) * 64], GMQ[:, j, 2, :], vs)
                    k.mm(pb[:, 128 + hh * 64:128 + (hh + 1) * 64], GMQ[:, j, 3, :], vs)
                k.copy(R1[:, i, :], pb[:, 0:128], eng="act")
                k.tt(YTM[:, i, :], YTM[:, i, :], pb[:, 128:256], ALU.add)
            self.stage("rw_r1")
            for s in range(nseq):
                if g == 0:
                    k.memset(Hf[(s, d)], 0.0)
                else:
                    k.dma(Hf[(s, d)], V(self.d_rw0[l, d, p], None))
                k.copy(Hb[(s, d)], Hf[(s, d)], eng="act")
            for cs in range(cps):
                for s in range(nseq):
                    c = s * cps + (cs if d == 0 else cps - 1 - cs)
                    i, half = c // 2, c % 2
                    tr_ = slice(half * 64, half * 64 + 64)
                    hf, hb = Hf[(s, d)], Hb[(s, d)]
                    pbR = self.bank()
                    k.mm(pbR[tr_, 0:128], KR[d][:, i, 0, tr_], hb)
                    Rb = self.RB[(s + d) % 2]
                    k.tt(Rb[tr_, :], pbR[tr_, 0:128], R1[tr_, i, :], ALU.add)
                    self.stage("sc_R%d" % c)
                    pbU = self.bank()
                    for hh in range(2):
                        j = i * 2 + hh
                        k.mm(pbU[tr_, hh * 64:(hh + 1) * 64], X[tr_, j, tr_], Rb[tr_, hh * 64:(hh + 1) * 64])
                    Ub = self.UB[(s + d) % 2]
                    k.act(Ub[tr_, :], pbU[tr_, 0:128], AF.Copy, scale=-1.0)
                    self.stage("sc_U%d" % c)
                    pbY = self.bank()
                    k.mm(pbY[tr_, 0:128], KR[d][:, i, 1, tr_], hb)
                    pbH = self.bank()
                    k.mm(pbH[:, 0:128], TMB[tr_, i, 1 + d, :], TMB[tr_, i, 0, :], start=True, stop=False)
                    k.mm(pbH[:, 0:128], TMB[tr_, i, 3 + d, :], Ub[tr_, :], start=False, stop=True)
                    pbY2 = self.bank()
                    for hh in range(2):
                        j = i * 2 + hh
                        k.mm(pbY2[tr_, hh * 64:(hh + 1) * 64], GMQ[tr_, j, 1, tr_], Ub[tr_, hh * 64:(hh + 1) * 64])
                    HT = self.HT[(s + d) % 2]
                    k.tt(HT, pbH[:, 0:128], self.ONESBD, ALU.mult)
                    k.tt(HT, HT, hf, ALU.add)
                    k.ts(hf, HT, GL[d][:, c:c + 1], ALU.mult)
                    k.copy(hb, hf, eng="act")
                    k.tt(YTM[tr_, i, :], YTM[tr_, i, :], pbY[tr_, 0:128], ALU.add)
                    k.tt(YTM[tr_, i, :], YTM[tr_, i, :], pbY2[tr_, 0:128], ALU.add)
                    self.stage("sc_Y%d" % c)
                    self.stage("sc_H%d" % c)
            self.stage("sc_end")
            if g == 0:
                for s in range(nseq):
                    k.dma(V(self.d_nrw[s, l, d, p], None), Hf[(s, d)])
            self.stage("sc_out%d" % d)
            if p == 0 and d == 0:
                self.dump("rw_X%d%d" % (l, g), X[:, 0:2, :], [128, 2, 128])
                self.dump("rw_R1%d%d" % (l, g), R1, [128, NT, 128])
            k.close_scope()
        if p == 0:
            self.dump("rw_Y%d%d" % (l, g), YTM, [128, NT, 128])
        self.stage("rw_scan")
        YN = k.sb("rYN" + tag, [128, NT, 128])
        ST = k.sb("rST" + tag, [128, 4, 16])
        yv = YTM.rr("p i (h v) -> p (i h) v", h=2)
        ynv = YN.rr("p i (h v) -> p (i h) v", h=2)
        k.reduce(ST[:, 0, :], yv, ALU.add)
        k.act(YN, YTM, AF.Square)
        k.reduce(ST[:, 1, :], ynv, ALU.add)
        k.ts(ST[:, 0, :], ST[:, 0, :], 1.0 / 64, ALU.mult)
        k.tt(ST[:, 2, :], ST[:, 0, :], ST[:, 0, :], ALU.mult)
        k.stt(ST[:, 1, :], ST[:, 1, :], 1.0 / 64, ST[:, 2, :], ALU.mult, ALU.subtract)
        k.act(ST[:, 1, :], ST[:, 1, :], AF.Sqrt, bias=self.EPSA)
        k.recip(ST[:, 1, :], ST[:, 1, :])
        k.tt(ynv, yv, ST[:, 0, :].us(2).bc([128, 16, 64]), ALU.subtract)
        k.tt(ynv, ynv, ST[:, 1, :].us(2).bc([128, 16, 64]), ALU.mult)
        self.stage("rw_ln")
        OUTF = k.sb("rOUT" + tag, [128, TG])
        for q in range(2):
            pb = self.bank()
            for ii in range(4):
                i = q * 4 + ii
                k.tr(pb[:, ii * 128:(ii + 1) * 128], YN[:, i, :], self.IDF)
            k.act(OUTF[:, q * 512:(q + 1) * 512], pb[:, 0:512], AF.Identity, scale=self.pf("rw_gn_g", p), bias=self.pf("rw_gn_b", p))
        k.tt(OUTF, OUTF, BON, ALU.add)
        k.tt(self.mixT[:, p, :], OUTF, GT, ALU.mult)
        if p == 0:
            self.dump("rw_out%d%d" % (l, g), self.mixT[:, 0, :], [128, TG])
        k.close_scope()
        k.close_scope()

    def unit_mg(self, l, g):
        k = self.k
        nseq, tseq = self.nseq, self.tseq
        cps = tseq // 64
        tag = "%d%d" % (l, g)
        k.open_scope()
        GI = k.sb("gGI" + tag, [128, NT, 16])
        self.proj_tm(0, 16, lambda i, ps: k.act(GI[:, i, :], ps, AF.Copy))
        self.prefetch()
        gi = k.sb("ggi" + tag, [128, NT, 8])
        LFN = k.sb("gLFN" + tag, [128, NT, 8])
        NB = k.sb("gNB" + tag, [128, NT, 8])
        AG = k.sb("gAG" + tag, [128, NT, 8])
        k.tt(gi, GI[:, :, 0:8], self.PROW[:, PR_BI:PR_BI + 8].us(1).bc([128, NT, 8]), ALU.add)
        k.tt(LFN, GI[:, :, 8:16], self.PROW[:, PR_BF:PR_BF + 8].us(1).bc([128, NT, 8]), ALU.add)
        k.act(LFN, LFN, AF.Exp, scale=-1.0)
        k.act(LFN, LFN, AF.Ln, bias=self.ONE1)
        self.stage("mg_a")
        pb = self.bank()
        for d in range(2):
            k.mm(pb[:, d * 32:(d + 1) * 32], self.MASK[:, 1 if d == 0 else 5, :], LFN[:, :, d * 4:(d + 1) * 4])
        for d in range(2):
            k.copy(NB[:, :, d * 4:(d + 1) * 4], pb[:, d * 32:(d + 1) * 32].rr("p (i h) -> p i h", h=4))
        k.tt(AG, gi, NB, ALU.add)
        self.stage("mg_b")
        pbt = self.bank()
        k.tr(pbt[0:64, 0:128], AG.rr("p i k -> p (i k)"), self.IDF)
        self.stage("mg_c")
        MXT = k.sb("gMXT" + tag, [64, 2])
        k.reduce(MXT, pbt[0:64, 0:128].rr("p (f t) -> p f t", f=2), ALU.max)
        RH = k.sb("gRH" + tag, [64, 64, 2])
        k.tt(RH, self.IDF[0:64, 0:64].us(2).bc([64, 64, 2]), MXT.us(1).bc([64, 64, 2]), ALU.mult)
        self.stage("mg_d")
        pbm = self.bank()
        k.mm(pbm[:, 0:128], self.ONES[0:64, :], RH.rr("p a b -> p (a b)"))
        MXF = k.sb("gMXF" + tag, [128, NT, 8, 2])
        k.copy(MXF.rr("p i k f -> p (i k f)"), pbm[:, 0:128])
        self.stage("mg_e")
        R2 = k.sb("gR2" + tag, [128, 64, 2])
        k.tt(R2, LFN.rr("p i k -> p (i k)").us(2).bc([128, 64, 2]), self.HM.us(1).bc([128, 64, 2]), ALU.mult)
        pbl = self.bank()
        k.mm(pbl[:, 0:128], self.ONES, R2.rr("p a b -> p (a b)"))
        NBL = k.sb("gNBL" + tag, [128, NT, 8, 2])
        k.copy(NBL.rr("p i k f -> p (i k f)"), pbl[:, 0:128])
        self.stage("mg_f")
        M0 = k.sb("gM0" + tag, [128, nseq, 8])
        MBAR = k.sb("gMBAR" + tag, [128, NT, 2, 8])
        if g == 0:
            k.memset(M0, 0.0)
        else:
            k.dma(M0[:, 0, :], V(self.d_mm0[l], None))
        ipseq = NT // nseq
        mxv = MXF.rr("p (s i) k f -> p s i k f", s=nseq)
        nbv = NBL.rr("p (s i) k f -> p s i k f", s=nseq)
        mbv = MBAR.rr("p (s i) f k -> p s i f k", s=nseq)
        scv = self.SC.rr("p (s i f) k -> p s i f k", s=nseq, f=2)
        for cs in range(cps):
            for d in range(2):
                cc = cs if d == 0 else cps - 1 - cs
                ii, half = cc // 2, cc % 2
                ds = slice(d * 4, d * 4 + 4)
                m0 = M0[:, :, ds]
                mb = mbv[:, :, ii, half, ds]
                k.tt(mb, m0, mxv[:, :, ii, ds, half], ALU.max)
                k.tt(scv[:, :, ii, half, ds], m0, mb, ALU.subtract)
                k.tt(m0, mb, nbv[:, :, ii, ds, half], ALU.subtract)
        self.stage("mg_g")
        k.act(self.SC, self.SC, AF.Exp)
        self.stage("mg_h")
        if g == 0:
            k.dma(V(self.d_nmm[l:l + 1, :], None), M0[0:1].rr("p s k -> p (s k)"))
        self.stage("mg_i")
        MT = k.sb("gMT" + tag, [128, NT, 8])
        k.ts(MT, MBAR[:, :, 0, :], self.HM[:, 0:1], ALU.mult)
        k.stt(MT, MBAR[:, :, 1, :], self.HM[:, 1:2], MT, ALU.mult, ALU.add)
        k.tt(self.OMG, AG, MT, ALU.subtract)
        k.act(self.OMG, self.OMG, AF.Exp)
        k.tt(self.CLAMP, NB, MT, ALU.subtract)
        k.act(self.CLAMP, self.CLAMP, AF.Exp)
        self.stage("mg_j")
        self.dump("mg_AG%d%d" % (l, g), AG, [128, NT, 8])
        self.stage("mg_k")
        self.dump("mg_MBAR%d%d" % (l, g), MBAR.rr("p i f k -> p (i f k)"), [128, NT * 16])
        self.dump("mg_SC%d%d" % (l, g), self.SC, [128, NCH, 8])
        self.dump("mg_OMG%d%d" % (l, g), self.OMG, [128, NT, 8])
        k.close_scope()

    def unit_mlstm(self, l, g, h):
        k = self.k
        nseq, tseq = self.nseq, self.tseq
        cps = tseq // 64
        tag = "%d%d%d" % (l, g, h)
        k.open_scope()
        GT = k.sb("mGT" + tag, [128, TG])
        VT = k.sb("mVT" + tag, [128, NT, 128])
        OT = k.sb("mOT" + tag, [128, NT, 128])
        QB = k.sb("mQB" + tag, [128, TG], BF16)
        KB = k.sb("mKB" + tag, [128, TG], BF16)
        k.open_scope()
        PRE = k.sb("mPRE" + tag, [128, 2, nseq, tseq + 2])
        self.stage("ml_a")
        k.memset(PRE[:, :, :, 0:1], 0.0)
        k.memset(PRE[:, :, :, tseq + 1:tseq + 2], 0.0)
        self.stage("ml_b")
        for j in range(2):
            self.proj_fm(j * 128, self.pad_evac(PRE, j))
        self.stage("ml_c")
        self.proj_fm(256, lambda nb, ps: k.act(GT[:, nb * 512:(nb + 1) * 512], ps, AF.Silu))
        self.stage("ml_d")

        def ev_vo(i, ps):
            k.copy(VT[:, i, :], ps[:, 0:128])
            k.act(OT[:, i, :], ps[:, 128:256], AF.Sigmoid)
        self.proj_tm(384, 256, ev_vo)
        self.prefetch()
        self.stage("ml_proj")
        X = k.sb("mX" + tag, [128, TG])
        for j, dst in enumerate((QB, KB)):
            self.conv(X, PRE, j, self.pf("cb_w", h, j, 0), self.pf("cb_w", h, j, 1), self.pf("cb_w", h, j, 2), self.pf("cb_b", h, j))
            if j == 0:
                k.act(dst, X, AF.Silu)
            else:
                k.act(X, X, AF.Silu)
                k.ts(dst, X, 128.0 ** -0.5, ALU.mult)
        k.close_scope()
        self.stage("ml_conv")
        KTM = k.sb("mKTM" + tag, [128, NT, 128], BF16)
        pbb = self.bank().bitcast(BF16)
        for i in range(NT):
            k.tr(pbb[:, i * 128:(i + 1) * 128], KB[:, i * 128:(i + 1) * 128], self.IDB)
        k.copy(KTM.rr("p i c -> p (i c)"), pbb[:, 0:1024])
        self.stage("ml_ktm")
        MTd = [k.sb("mMT%d" % d + tag, [128, NT, 128], BF16) for d in range(2)]
        for q in range(2):
            pb = self.bank()
            for ii in range(4):
                i = q * 4 + ii
                ts_ = slice(i * 128, (i + 1) * 128)
                k.mm(pb[:, ii * 128:(ii + 1) * 128], KB[:, ts_], QB[:, ts_])
            pv = pb[:, 0:512].rr("p (a b) -> p a b", a=4)
            k.tt(MTd[0][:, q * 4:q * 4 + 4, :], pv, self.MASK[:, 1:2, :].bc([128, 4, 128]), ALU.mult)
            k.tt(MTd[1][:, q * 4:q * 4 + 4, :], pv, self.MASK[:, 5:6, :].bc([128, 4, 128]), ALU.mult)
        self.stage("ml_mt")
        WV = [k.sb("mWV%d" % d + tag, [128, NT, 130], BF16) for d in range(2)]
        HI = [k.sb("mHI%d" % d + tag, [128, NT, 130]) for d in range(2)]
        for d in range(2):
            om = self.OMG[:, :, d * 4 + h:d * 4 + h + 1]
            k.tt(WV[d][:, :, 0:128], VT, om.bc([128, NT, 128]), ALU.mult)
            k.copy(WV[d][:, :, 128:129], om)
            k.memset(WV[d][:, :, 129:130], 0.0)
            for i in range(NT):
                pb = self.bank()
                k.mm(pb[:, 0:130], MTd[d][:, i, :], WV[d][:, i, :])
                k.copy(HI[d][:, i, :], pb[:, 0:130], eng=("act" if i % 2 else "dve"))
        self.stage("ml_hi")
        HS = k.sb("mHS" + tag, [128, NT, 128])
        TOTS = [k.sb("mTOTS%d" % i + tag, [128, NT, 130]) for i in range(2)]
        Z = {}
        Zb = {}
        for s in range(nseq):
            for d in range(2):
                Z[(s, d)] = k.sb("mZ%d%d" % (s, d) + tag, [128, 130])
                Zb[(s, d)] = k.sb("mZb%d%d" % (s, d) + tag, [128, 130], BF16)
                if g == 0:
                    k.memset(Z[(s, d)], 0.0)
                else:
                    k.dma(Z[(s, d)], V(self.d_ml0[l, d, h], None))
        for cs in range(cps):
            for s in range(nseq):
                for d in range(2):
                    c = s * cps + (cs if d == 0 else cps - 1 - cs)
                    i, half = c // 2, c % 2
                    tr_ = slice(half * 64, half * 64 + 64)
                    z, zb = Z[(s, d)], Zb[(s, d)]
                    dh = d * 4 + h
                    k.ts(z, z, self.SC[:, c, dh:dh + 1], ALU.mult)
                    k.copy(zb, z, eng="act")
                    pbZ = self.bank()
                    k.mm(pbZ[:, 0:130], KTM[tr_, i, :], WV[d][tr_, i, :])
                    pbS = self.bank()
                    k.mm(pbS[tr_, 0:130], QB[:, c * 64:(c + 1) * 64], zb)
                    k.tt(z, z, pbZ[:, 0:130], ALU.add)
                    k.tt(TOTS[d][tr_, i, :], pbS[tr_, 0:130], HI[d][tr_, i, :], ALU.add)
                    self.stage("ml_c%d_%d" % (c, d))
        DNb = k.sb("mDNb" + tag, [128, 2, NT])
        for d in range(2):
            k.act(DNb[:, d, :], TOTS[d][:, :, 128], AF.Abs)
            k.tt(DNb[:, d, :], DNb[:, d, :], self.CLAMP[:, :, d * 4 + h], ALU.max)
        k.recip(DNb, DNb)
        for d in range(2):
            k.tt(TOTS[d][:, :, 0:128], TOTS[d][:, :, 0:128], DNb[:, d, :].us(2).bc([128, NT, 128]), ALU.mult, eng=("pool" if d else "dve"))
        k.tt(HS, TOTS[0][:, :, 0:128], TOTS[1][:, :, 0:128], ALU.add)
        self.stage("ml_chain")
        if g == 0:
            for s in range(nseq):
                for d in range(2):
                    k.dma(V(self.d_nmc[s, l, d, h], None), Z[(s, d)])
        self.stage("ml_nmc")
        if h == 0:
            self.dump("ml_HS%d%d" % (l, g), HS, [128, NT, 128])
            self.dump("ml_HI%d%d" % (l, g), HI[0], [128, NT, 130])
        self.stage("ml_dump")
        k.tt(HS, HS, OT, ALU.mult)
        HN = k.sb("mHN" + tag, [128, NT, 128])
        ST = k.sb("mST" + tag, [128, 3, NT])
        k.reduce(ST[:, 0, :], HS, ALU.add)
        k.act(HN, HS, AF.Square)
        k.reduce(ST[:, 1, :], HN, ALU.add)
        k.ts(ST[:, 0, :], ST[:, 0, :], 1.0 / 128, ALU.mult)
        k.tt(ST[:, 2, :], ST[:, 0, :], ST[:, 0, :], ALU.mult)
        k.stt(ST[:, 1, :], ST[:, 1, :], 1.0 / 128, ST[:, 2, :], ALU.mult, ALU.subtract)
        k.act(ST[:, 1, :], ST[:, 1, :], AF.Sqrt, bias=self.EPSB)
        k.recip(ST[:, 1, :], ST[:, 1, :])
        k.tt(HN, HS, ST[:, 0, :].us(2).bc([128, NT, 128]), ALU.subtract)
        k.tt(HN, HN, ST[:, 1, :].us(2).bc([128, NT, 128]), ALU.mult)
        OUTF = k.sb("mOUT" + tag, [128, TG])
        for q in range(2):
            pb = self.bank()
            for ii in range(4):
                k.tr(pb[:, ii * 128:(ii + 1) * 128], HN[:, q * 4 + ii, :], self.IDF)
            k.act(OUTF[:, q * 512:(q + 1) * 512], pb[:, 0:512], AF.Identity, scale=self.pf("ml_gn_g", h), bias=self.pf("ml_gn_b", h))
        k.tt(self.mixT[:, 4 + h, :], OUTF, GT, ALU.mult)
        if h == 0:
            self.dump("ml_out%d%d" % (l, g), self.mixT[:, 4, :], [128, TG])
        k.close_scope()

    def attn_prep(self, l, g, h):
        k = self.k
        nseq = self.nseq
        tag = "%d%d%d" % (l, g, h)
        npast = 0 if g == 0 else PAST // 128
        nkt = npast + NT
        c = {"h": h, "nkt": nkt, "npast": npast}
        c["GT"] = GT = k.sb("aGT" + tag, [128, TG])
        c["VBk"] = VBk = k.sb("aVB" + tag, [128, nkt, 128], BF16)
        c["QT"] = QT = k.sb("aQT" + tag, [128, TG], BF16)
        c["KTa"] = KTa = k.sb("aKT" + tag, [128, nkt * 128], BF16)
        self.load_w(self.d_win[l], U_AT[h], 512)
        k.open_scope()
        QKV = k.sb("aQKV" + tag, [128, NT, 384])
        self.proj_tm(0, 384, lambda i, ps: k.copy(QKV[:, i, :], ps, eng=("act" if i % 2 else "dve")))
        self.proj_fm(384, lambda nb, ps: k.act(GT[:, nb * 512:(nb + 1) * 512], ps, AF.Silu))
        if g == 0:
            for s in range(nseq):
                k.dma(V(self.d_nk[s, l, h].rearrange("(i p) c -> p i c", p=128), None), QKV[:, 2 * s:2 * s + 2, 128:256])
                k.dma(V(self.d_nv[s, l, h].rearrange("(i p) c -> p i c", p=128), None), QKV[:, 2 * s:2 * s + 2, 256:384])
        else:
            T = [k.sb("aT%d" % i + tag, [128, NT, 4, 16]) for i in range(4)]
            for off in (0, 128):
                xv = QKV[:, :, off:off + 128].rr("p i (a x t) -> p i a x t", a=4, x=2)
                x1, x2 = xv[:, :, :, 0, :], xv[:, :, :, 1, :]
                k.tt(T[0], x1, self.COS, ALU.mult)
                k.tt(T[1], x2, self.SIN, ALU.mult)
                k.tt(T[2], x2, self.COS, ALU.mult)
                k.tt(T[3], x1, self.SIN, ALU.mult)
                k.tt(x1, T[0], T[1], ALU.subtract)
                k.tt(x2, T[2], T[3], ALU.add)
        QKB = k.sb("aQKB" + tag, [128, NT, 256], BF16)
        k.copy(QKB, QKV[:, :, 0:256])
        k.copy(VBk[:, npast:nkt, :], QKV[:, :, 256:384], eng="pool")
        for which, dst, c0 in ((0, QT, 0), (1, KTa, npast * 128)):
            pbb = self.bank().bitcast(BF16)
            for i in range(NT):
                k.tr(pbb[:, i * 128:(i + 1) * 128], QKB[:, i, which * 128:(which + 1) * 128], self.IDB)
            k.copy(dst[:, c0:c0 + TG], pbb[:, 0:1024], eng=("act" if which else "dve"))
        if g == 1:
            CKB = k.sb("aCKB" + tag, [128, npast, 128], BF16)
            k.dma(CKB, V(self.d_ck[l, h].rearrange("(i p) c -> p i c", p=128), None), q="pool")
            k.dma(VBk[:, 0:npast, :], V(self.d_cv[l, h].rearrange("(i p) c -> p i c", p=128), None), q="pool")
            pbb = self.bank().bitcast(BF16)
            for i in range(npast):
                k.tr(pbb[:, i * 128:(i + 1) * 128], CKB[:, i, :], self.IDB)
            k.copy(KTa[:, 0:npast * 128], pbb[:, 0:npast * 128])
        k.close_scope()
        return c

    def unit_attn2(self, l, g, h0):
        k = self.k
        nseq = self.nseq
        scale = 64.0 ** -0.5
        k.open_scope()
        ctxs = [self.attn_prep(l, g, h0 + j) for j in range(2)]
        for j, c in enumerate(ctxs):
            tag = "%d%d%d" % (l, g, c["h"])
            nkt = c["nkt"]
            c["E"] = [k.sb("aE%d" % b + tag, [128, nkt * 128], BF16) for b in range(2)]
            c["ET"] = [k.sb("aET%d" % b + tag, [128, nkt, 128], BF16) for b in range(2)]
            c["SMq"] = [[k.sb("aSM%d%d" % (a, b) + tag, [128, 4]) for b in range(2)] for a in range(2)]
            c["MXp"] = [k.sb("aMX%d" % b + tag, [128, 4]) for b in range(2)]
            c["NBp"] = [k.sb("aNB%d" % b + tag, [128, 1]) for b in range(2)]
            c["RS"] = k.sb("aRS" + tag, [128, 4])
            c["O2"] = k.sb("aO2" + tag, [128, 128])
            c["OD"] = k.sb("aOD" + tag, [128, 128])
            c["JK"] = k.sb("aJK" + tag, [128, 128])
            c["OT"] = k.sb("aOT" + tag, [128, 128])
            c["abank"] = j * 3
            c["ocol"] = j * 256
        pbO = self.PB[7]
        pbT = self.PB[6]
        items = [(i, br) for i in range(NT) for br in range(2)]

        def keys_of(c, i):
            if g == 0:
                sq = i // (NT // nseq)
                return [2 * sq, 2 * sq + 1]
            return list(range(c["nkt"]))

        def stage_a(c, kidx):
            i, br = items[kidx]
            par = kidx % 2
            kts = keys_of(c, i)
            k0 = kts[0] * 128
            ncols = len(kts) * 128
            chunks = [(c0, min(512, ncols - c0)) for c0 in range(0, ncols, 512)]
            brs = slice(br * 64, br * 64 + 64)
            banks = [self.PB[c["abank"] + ci] for ci in range(len(chunks))]
            for ci, (c0, cn) in enumerate(chunks):
                k.mm(banks[ci][:, 0:cn], c["QT"][brs, i * 128:(i + 1) * 128], c["KTa"][brs, k0 + c0:k0 + c0 + cn])
            for ci, (c0, cn) in enumerate(chunks):
                k.reduce(c["MXp"][par][:, ci:ci + 1], banks[ci][:, 0:cn], ALU.max)
            k.reduce(c["NBp"][par], c["MXp"][par][:, 0:len(chunks)], ALU.max)
            k.ts(c["NBp"][par], c["NBp"][par], -scale, ALU.mult)
            for ci, (c0, cn) in enumerate(chunks):
                k.act(c["E"][par][:, c0:c0 + cn], banks[ci][:, 0:cn], AF.Exp, scale=scale, bias=c["NBp"][par],
                      accum=c["SMq"][i % 2][br][:, ci:ci + 1])

        def stage_b(c, kidx):
            i, br = items[kidx]
            par = kidx % 2
            kts = keys_of(c, i)
            nk = len(kts)
            oc = c["ocol"] + br * 128
            for q0 in range(0, nk, 8):
                qn = min(8, nk - q0)
                pbb = pbT.bitcast(BF16)
                for jj in range(qn):
                    k.tr(pbb[:, jj * 128:(jj + 1) * 128], c["E"][par][:, (q0 + jj) * 128:(q0 + jj + 1) * 128], self.IDB)
                k.copy(c["ET"][par][:, q0:q0 + qn, :].rr("p a b -> p (a b)"), pbb[:, 0:qn * 128], eng=("act" if qn == 8 else "dve"))
            for jj in range(nk):
                k.mm(pbO[:, oc:oc + 128], c["ET"][par][:, jj, :], c["VBk"][:, kts[jj], :], start=(jj == 0), stop=(jj == nk - 1))

        def tail(c, i):
            qp = i % 2
            RS, O2, OD, JK, OTt = c["RS"], c["O2"], c["OD"], c["JK"], c["OT"]
            oc = c["ocol"]
            nch = (len(keys_of(c, i)) * 128 + 511) // 512
            for br in range(2):
                k.reduce(RS[:, br:br + 1], c["SMq"][qp][br][:, 0:nch], ALU.add)
            k.recip(RS[:, 0:2], RS[:, 0:2])
            k.tt(RS[:, 1:2], RS[:, 1:2], self.LAM[:, 0:1], ALU.mult)
            k.act(O2, pbO[:, oc + 128:oc + 256], AF.Identity, scale=RS[:, 1:2])
            k.stt(OD, pbO[:, oc:oc + 128], RS[:, 0:1], O2, ALU.mult, ALU.subtract)
            k.act(JK, OD, AF.Square, accum=RS[:, 2:3])
            k.act(RS[:, 2:3], RS[:, 2:3], AF.Sqrt, scale=1.0 / 128, bias=self.EPSB)
            k.recip(RS[:, 2:3], RS[:, 2:3])
            k.ts(OD, OD, RS[:, 2:3], ALU.mult)
            k.tr(pbT[:, 0:128], OD, self.IDF)
            k.act(OTt, pbT[:, 0:128], AF.Identity, scale=self.SUBG)
            k.tt(self.mixT[:, 8 + c["h"], i * 128:(i + 1) * 128], OTt, c["GT"][:, i * 128:(i + 1) * 128], ALU.mult, eng="pool")

        for c in ctxs:
            stage_a(c, 0)
        for kidx in range(len(items)):
            if kidx + 1 < len(items):
                for c in ctxs:
                    stage_a(c, kidx + 1)
            for c in ctxs:
                stage_b(c, kidx)
                if items[kidx][1] == 1:
                    tail(c, items[kidx][0])
        if h0 == 0:
            self.dump("at_out%d%d" % (l, g), self.mixT[:, 8, :], [128, TG])
        k.close_scope()

    def unit_attn3(self, l, g, h):
        k = self.k
        nseq = self.nseq
        scale = 64.0 ** -0.5
        tag = "%d%d%d" % (l, g, h)
        npast = 0 if g == 0 else PAST // 128
        nkt = npast + NT
        nkl = 2 if g == 0 else nkt
        k.open_scope()
        GT = k.sb("aGT" + tag, [128, TG])
        VP = k.sb("aVP" + tag, [128, nkt, 130], BF16)
        QT = k.sb("aQT" + tag, [128, TG], BF16)
        KTa = k.sb("aKT" + tag, [128, nkt * 128], BF16)
        NB = k.sb("aNB" + tag, [128, 2])
        k.open_scope()
        QKV = k.sb("aQKV" + tag, [128, NT, 384])
        self.proj_tm(0, 384, lambda i, ps: k.copy(QKV[:, i, :], ps, eng=("act" if i % 2 else "dve")))
        self.proj_fm(384, lambda nb, ps: k.act(GT[:, nb * 512:(nb + 1) * 512], ps, AF.Silu))
        self.prefetch()
        if g == 0:
            for s in range(nseq):
                k.dma(V(self.d_nk[s, l, h].rearrange("(i p) c -> p i c", p=128), None), QKV[:, 2 * s:2 * s + 2, 128:256])
                k.dma(V(self.d_nv[s, l, h].rearrange("(i p) c -> p i c", p=128), None), QKV[:, 2 * s:2 * s + 2, 256:384])
        else:
            T = [k.sb("aT%d" % i + tag, [128, NT, 4, 16]) for i in range(4)]
            for off in (0, 128):
                xv = QKV[:, :, off:off + 128].rr("p i (a x t) -> p i a x t", a=4, x=2)
                x1, x2 = xv[:, :, :, 0, :], xv[:, :, :, 1, :]
                k.tt(T[0], x1, self.COS, ALU.mult)
                k.tt(T[1], x2, self.SIN, ALU.mult)
                k.tt(T[2], x2, self.COS, ALU.mult)
                k.tt(T[3], x1, self.SIN, ALU.mult)
                k.tt(x1, T[0], T[1], ALU.subtract)
                k.tt(x2, T[2], T[3], ALU.add)
        QKB = k.sb("aQKB" + tag, [128, NT, 256], BF16)
        k.copy(QKB, QKV[:, :, 0:256])
        k.copy(VP[:, npast:nkt, 0:128], QKV[:, :, 256:384], eng="pool")
        k.memset(VP[:, :, 128:129], 1.0)
        k.memset(VP[:, :, 129:130], 0.0)
        for which, dst, c0 in ((0, QT, 0), (1, KTa, npast * 128)):
            pbb = self.bank().bitcast(BF16)
            for i in range(NT):
                k.tr(pbb[:, i * 128:(i + 1) * 128], QKB[:, i, which * 128:(which + 1) * 128], self.IDB)
            k.copy(dst[:, c0:c0 + TG], pbb[:, 0:1024], eng=("act" if which else "dve"))
        SQ = k.sb("aSQ" + tag, [128, NT, 256])
        N2 = k.sb("aN2" + tag, [128, NT, 4])
        M4 = k.sb("aM4" + tag, [128, 4])
        k.act(SQ, QKV[:, :, 0:256], AF.Square)
        k.reduce(N2, SQ.rr("p i (a d) -> p i a d", a=4), ALU.add)
        k.reduce(M4, N2.rr("p i a -> p a i"), ALU.max)
        if g == 1:
            CKB = k.sb("aCKB" + tag, [128, npast, 128], BF16)
            k.dma(CKB, V(self.d_ck[l, h].rearrange("(i p) c -> p i c", p=128), None), q="pool")
            k.dma(VP[:, 0:npast, 0:128], V(self.d_cv[l, h].rearrange("(i p) c -> p i c", p=128), None), q="pool")
            pbb = self.bank().bitcast(BF16)
            for i in range(npast):
                k.tr(pbb[:, i * 128:(i + 1) * 128], CKB[:, i, :], self.IDB)
            k.copy(KTa[:, 0:npast * 128], pbb[:, 0:npast * 128])
            CSQ = k.sb("aCSQ" + tag, [128, npast, 128])
            CN2 = k.sb("aCN2" + tag, [128, npast, 2])
            CM = k.sb("aCM" + tag, [128, 2])
            k.act(CSQ, CKB, AF.Square)
            k.reduce(CN2, CSQ.rr("p i (a d) -> p i a d", a=2), ALU.add)
            k.reduce(CM, CN2.rr("p i a -> p a i"), ALU.max)
            k.tt(M4[:, 2:4], M4[:, 2:4], CM, ALU.max)
        pbm = self.bank()
        k.tr(pbm[0:4, 0:128], M4, self.IDF)
        MC = k.sb("aMC" + tag, [4, 1])
        k.reduce(MC, pbm[0:4, 0:128], ALU.max)
        RH = k.sb("aRH" + tag, [4, 4])
        k.ts(RH, self.IDF[0:4, 0:4], MC[0:4, 0:1], ALU.mult)
        pbr = self.bank()
        k.mm(pbr[:, 0:4], self.ONES[0:4, :], RH)
        MR = k.sb("aMR" + tag, [128, 4])
        k.copy(MR, pbr[:, 0:4])
        k.tt(NB, MR[:, 0:2], MR[:, 2:4], ALU.mult)
        k.act(NB, NB, AF.Sqrt)
        k.ts(NB, NB, -scale, ALU.mult)
        k.close_scope()
        ETs = [k.sb("aET%d" % b + tag, [128, nkl, TG], BF16) for b in range(2)]
        O1S = k.sb("aO1" + tag, [128, NT, 130])
        RS = k.sb("aRS" + tag, [128, 4])
        O2 = k.sb("aO2" + tag, [128, 128])
        OD = k.sb("aOD" + tag, [128, 128])
        JK = k.sb("aJK" + tag, [128, 128])
        OTt = k.sb("aOT" + tag, [128, 128])
        qblk = 512 if g == 1 else TSEQ_P

        def qk_items(br):
            brs = slice(br * 64, br * 64 + 64)
            for qb in range(TG // qblk):
                qs = slice(qb * qblk, (qb + 1) * qblk)
                kt0 = 0 if g == 1 else 2 * qb
                for j in range(nkl):
                    yield (brs, qs, kt0, j)

        def qk_exp(br, it):
            brs, qs, kt0, j = it
            pb = self.PB[self.abank % 6]
            self.abank += 1
            k.mm(pb[:, 0:qblk], KTa[brs, (kt0 + j) * 128:(kt0 + j + 1) * 128], QT[brs, qs])
            k.act(ETs[br][:, j, qs], pb[:, 0:qblk], AF.Exp, scale=scale, bias=NB[:, br:br + 1])

        def pv(br, i):
            ET = ETs[br]
            kt0 = 0 if g == 1 else 2 * (i // 2)
            pbO = self.PB[6 + i % 2]
            for j in range(nkl):
                k.mm(pbO[:, 0:130], ET[:, j, i * 128:(i + 1) * 128], VP[:, kt0 + j, :], start=(j == 0), stop=(j == nkl - 1))
            if br == 0:
                k.copy(O1S[:, i, :], pbO[:, 0:130], eng=("act" if i % 2 else "dve"))
            else:
                k.copy(RS[:, 0:1], O1S[:, i, 128:129])
                k.copy(RS[:, 1:2], pbO[:, 128:129])
                k.recip(RS[:, 0:2], RS[:, 0:2])
                k.tt(RS[:, 1:2], RS[:, 1:2], self.LAM[:, 0:1], ALU.mult)
                k.act(O2, pbO[:, 0:128], AF.Identity, scale=RS[:, 1:2])
                k.stt(OD, O1S[:, i, 0:128], RS[:, 0:1], O2, ALU.mult, ALU.subtract)
                k.act(JK, OD, AF.Square, accum=RS[:, 2:3])
                k.act(RS[:, 2:3], RS[:, 2:3], AF.Sqrt, scale=1.0 / 128, bias=self.EPSB)
                k.recip(RS[:, 2:3], RS[:, 2:3])
                k.ts(OD, OD, RS[:, 2:3], ALU.mult)
                pbt = self.PB[self.abank % 6]
                self.abank += 1
                k.tr(pbt[:, 0:128], OD, self.IDF)
                k.act(OTt, pbt[:, 0:128], AF.Identity, scale=self.SUBG)
                k.tt(self.mixT[:, 8 + h, i * 128:(i + 1) * 128], OTt, GT[:, i * 128:(i + 1) * 128], ALU.mult, eng="pool")

        for it in qk_items(0):
            qk_exp(0, it)
        its1 = list(qk_items(1))
        per = max(1, len(its1) // NT)
        nxt = 0
        for n, it in enumerate(its1):
            qk_exp(1, it)
            if (n + 1) % per == 0 and nxt < NT:
                pv(0, nxt)
                nxt += 1
        while nxt < NT:
            pv(0, nxt)
            nxt += 1
        for i in range(NT):
            pv(1, i)
        if h == 0:
            self.dump("at_out%d%d" % (l, g), self.mixT[:, 8, :], [128, TG])
        k.close_scope()

    def phase_o(self, l, g):
        k = self.k
        k.open_scope()
        WOs = [k.sb("oWO%d" % s, [128, KC, 520], BF16) for s in range(4)]
        for s in range(4):
            src = self.d_wout[l].rearrange("(kc p) c -> p kc c", p=128)[:, :, s * 512:(s + 1) * 512]
            k.dma(WOs[s][:, :, 0:512], V(src, None), q="pool")
        GBC = k.sb("oGBC", [128, D])
        LG = k.sb("oLG", [128, D])
        LB = k.sb("oLB", [128, D])
        k.dma(GBC, V(self.d_modg[l, g:g + 1, :].partition_broadcast(128).rearrange("p a d -> p (a d)"), self.modgbuf[l][g]))
        k.dma(LG, V(self.d_lng[l:l + 1, :].partition_broadcast(128).rearrange("p a d -> p (a d)"), None))
        k.dma(LB, V(self.d_lnb[l:l + 1, :].partition_broadcast(128).rearrange("p a d -> p (a d)"), None))
        XT = k.sb("oXT", [128, D])
        VV = k.sb("oVV", [128, D])
        JK = k.sb("oJK", [128, D], BF16)
        ST = k.sb("oST", [128, 4])
        for i in range(NT):
            k.dma(XT, self.xsrc(l, g, i))
            for s in range(4):
                pb = self.bank()
                for kc in range(KC):
                    k.mm(pb[:, 0:512], self.mixT[:, kc, i * 128:(i + 1) * 128], WOs[s][:, kc, 0:512], start=(kc == 0), stop=(kc == KC - 1))
                k.tt(VV[:, s * 512:(s + 1) * 512], pb[:, 0:512], GBC[:, s * 512:(s + 1) * 512], ALU.mult)
            k.stt(VV, XT, ALPHA, VV, ALU.mult, ALU.add)
            k.act(JK, VV, AF.Identity, accum=ST[:, 0:1])
            k.act(JK, VV, AF.Square, accum=ST[:, 1:2])
            k.ts(ST[:, 0:1], ST[:, 0:1], 1.0 / D, ALU.mult)
            k.tt(ST[:, 2:3], ST[:, 0:1], ST[:, 0:1], ALU.mult)
            k.stt(ST[:, 1:2], ST[:, 1:2], 1.0 / D, ST[:, 2:3], ALU.mult, ALU.subtract)
            k.act(ST[:, 1:2], ST[:, 1:2], AF.Sqrt, bias=self.EPSB)
            k.recip(ST[:, 1:2], ST[:, 1:2])
            k.ts(VV, VV, ST[:, 0:1], ALU.subtract, ST[:, 1:2], ALU.mult)
            k.tt(VV, VV, LG, ALU.mult)
            k.tt(VV, VV, LB, ALU.add)
            if l == DEPTH - 1:
                dst = V(self.d_yout[g, i * 128:(i + 1) * 128, :], None)
            else:
                dst = V(self.d_x1[g, i * 128:(i + 1) * 128, :], self.x1bufs[g][i])
            k.dma(dst, VV)
        k.close_scope()


def _shared_inputs(inp):
    perm = _perm_cols()
    sh = {}
    sh["w_ada"] = np.ascontiguousarray(inp["w_ada"], dtype=np.float32)
    sh["b_ada"] = np.ascontiguousarray(inp["b_ada"], dtype=np.float32)
    sh["w_in"] = np.ascontiguousarray(inp["w_in"][:, :, perm], dtype=np.float32)
    sh["w_out"] = np.ascontiguousarray(inp["w_out"], dtype=np.float32)
    sh["pfm"] = np.stack([_pack_pfm(inp, l) for l in range(DEPTH)])
    sh["prow"] = np.stack([_pack_prow(inp, l) for l in range(DEPTH)])
    sh["ln_g"] = np.ascontiguousarray(inp["ln_g"], dtype=np.float32)
    sh["ln_b"] = np.ascontiguousarray(inp["ln_b"], dtype=np.float32)
    wup = np.zeros((DEPTH, 2, 128, 512), np.float32)
    wup[:, 0] = inp["rwkv_w_up"].reshape(DEPTH, 128, 512)
    wup[:, 1] = inp["rwkv_a_up"].reshape(DEPTH, 128, 512)
    sh["wup"] = wup
    sh["consts"] = _consts()
    return sh


def _core_inputs(inp, c, sh):
    sb = c % 2
    m = dict(sh)
    xin = np.empty((2, TG, D), np.float32)
    xin[0] = inp["x_prompt"][4 * c:4 * c + 4].reshape(TG, D)
    xin[1] = inp["x_sample"][sb]
    m["xin"] = xin
    cT = np.empty((128, KC, 2), np.float32)
    cT[:, :, 0] = inp["c_ctx"].reshape(KC, 128).T
    cT[:, :, 1] = inp["c"][sb].reshape(KC, 128).T
    m["cT"] = cT.reshape(128, 32)
    rw = inp["state_rwkv"][sb]
    rw0 = np.zeros((DEPTH, 2, 4, 128, 128), np.float32)
    for p in range(4):
        for hh in range(2):
            rw0[:, :, p, hh * 64:(hh + 1) * 64, hh * 64:(hh + 1) * 64] = np.swapaxes(rw[:, :, 2 * p + hh], -1, -2)
    m["rw0"] = rw0
    ml0 = np.zeros((DEPTH, 2, 4, 128, 130), np.float32)
    ml0[..., 0:128] = np.swapaxes(inp["state_mlstm_c"][sb], -1, -2)
    ml0[..., 128] = inp["state_mlstm_n"][sb]
    m["ml0"] = ml0
    m["mm0"] = np.ascontiguousarray(np.broadcast_to(inp["state_mlstm_m"][sb].reshape(DEPTH, 1, 8), (DEPTH, 128, 8)), dtype=np.float32)
    m["ck"] = np.ascontiguousarray(inp["cache_attn_k"][sb], dtype=np.float32)
    m["cv"] = np.ascontiguousarray(inp["cache_attn_v"][sb], dtype=np.float32)
    return m


def _assemble(results):
    B = 8 * NSEQ_P
    y_prompt = np.empty((B, TSEQ_P, D), np.float32)
    y_sample = np.empty((2, TG, D), np.float32)
    new_k = np.empty((B, DEPTH, 8, TSEQ_P, 128), np.float32)
    new_v = np.empty((B, DEPTH, 8, TSEQ_P, 128), np.float32)
    new_rw = np.empty((B, DEPTH, 2, 8, 64, 64), np.float32)
    new_c = np.empty((B, DEPTH, 2, 4, 128, 128), np.float32)
    new_n = np.empty((B, DEPTH, 2, 4, 128), np.float32)
    new_m = np.empty((B, DEPTH, 2, 4), np.float32)
    for c, r in enumerate(results):
        bs = slice(4 * c, 4 * c + 4)
        y_prompt[bs] = r["yout"][0].reshape(4, TSEQ_P, D)
        if c < 2:
            y_sample[c] = r["yout"][1]
        new_k[bs] = r["nk"]
        new_v[bs] = r["nv"]
        nrw = r["nrw"]
        for p in range(4):
            for hh in range(2):
                blk = nrw[:, :, :, p, hh * 64:(hh + 1) * 64, hh * 64:(hh + 1) * 64]
                new_rw[bs, :, :, 2 * p + hh] = np.swapaxes(blk, -1, -2)
        nmc = r["nmc"]
        new_c[bs] = np.swapaxes(nmc[..., 0:128], -1, -2)
        new_n[bs] = nmc[..., 128]
        new_m[bs] = np.transpose(r["nmm"].reshape(DEPTH, 4, 2, 4), (1, 0, 2, 3))
    return (y_prompt, y_sample, new_k, new_v, new_rw, new_c, new_n, new_m)


def kernel(**inputs):
    inp = {k: np.asarray(v) for k, v in inputs.items()}
    prog = Prog()
    nc = prog.build()
    sh = _shared_inputs(inp)
    in_maps = [_core_inputs(inp, c, sh) for c in range(8)]
    res = run_bass_kernel_spmd(nc, in_maps, core_ids=list(range(8)))
    return _assemble(res.results)
```

```python
import math
from contextlib import ExitStack

import numpy as np
import concourse.bass as bass
import concourse.mybir as mybir
from concourse.bass_utils import run_bass_kernel_spmd

F32 = mybir.dt.float32
BF16 = mybir.dt.bfloat16
AF = mybir.ActivationFunctionType
ALU = mybir.AluOpType
AX = mybir.AxisListType

D = 2048
KC = 16
DEPTH = 2
NSEQ_P = 4
TSEQ_P = 256
TG = 1024
NT = 8
NCH = 16
PAST = 512
P_IN = 8976
ALPHA = (2 * DEPTH) ** 0.25
LN_EPS = 1e-5
GN_EPS_A = 64e-5
GN_EPS = 1e-5
RMS_EPS = 1e-5
WDECAY = math.exp(-0.5)

U_LORA = 0
U_RW = [256 + 512 * p for p in range(4)]
U_ML = [2304 + 640 * h for h in range(4)]
U_MG = 2304 + 2560
U_AT = [4880 + 512 * h for h in range(8)]


def _perm_cols():
    perm = []
    perm += list(range(1536, 1792))
    for p in range(4):
        perm += list(range(p * 128, p * 128 + 128))
        perm += list(range(512 + p * 128, 512 + p * 128 + 128))
        perm += list(range(1024 + p * 128, 1024 + p * 128 + 128))
        perm += list(range(1792 + p * 128, 1792 + p * 128 + 128))
    b0 = 2304
    for h in range(4):
        perm += list(range(b0 + h * 128, b0 + h * 128 + 128))
        perm += list(range(b0 + 512 + h * 128, b0 + 512 + h * 128 + 128))
        perm += list(range(4368 + h * 128, 4368 + h * 128 + 128))
        perm += list(range(3328 + h * 128, 3328 + h * 128 + 128))
        perm += list(range(3840 + h * 128, 3840 + h * 128 + 128))
    perm += list(range(4352, 4368))
    for h in range(8):
        perm += list(range(4880 + h * 128, 4880 + h * 128 + 128))
        perm += list(range(5904 + h * 128, 5904 + h * 128 + 128))
        perm += list(range(6928 + h * 128, 6928 + h * 128 + 128))
        perm += list(range(7952 + h * 128, 7952 + h * 128 + 128))
    assert len(perm) == P_IN and len(set(perm)) == P_IN
    return np.array(perm)


C_IDENT = 0
C_MASK = 128
C_ONESBD = C_MASK + 8 * 128
C_ONES = C_ONESBD + 128
C_HM = C_ONES + 128
C_COS = C_HM + 2
C_SIN = C_COS + 512
NCONST = C_SIN + 512


def _consts():
    c = np.zeros((128, NCONST), np.float32)
    c[:, C_IDENT:C_IDENT + 128] = np.eye(128)
    r = np.arange(128)[:, None]
    q = np.arange(128)[None, :]
    same = (r // 64) == (q // 64)
    us = (same & (r < q)).astype(np.float32)
    ui = (same & (r <= q)).astype(np.float32)
    ls = (same & (r > q)).astype(np.float32)
    li = (same & (r >= q)).astype(np.float32)
    for i, m in enumerate([-us, ui, us, ui, -ls, li, ls, li]):
        c[:, C_MASK + i * 128:C_MASK + (i + 1) * 128] = m
    c[:, C_ONESBD:C_ONESBD + 128] = same.astype(np.float32)
    c[:, C_ONES:C_ONES + 128] = 1.0
    c[:64, C_HM] = 1.0
    c[64:, C_HM + 1] = 1.0
    half = 32
    inv = 1.0 / (10000.0 ** (np.arange(0, half, 2, dtype=np.float32) / half))
    t = (np.arange(8)[None, :] * 128 + np.arange(128)[:, None]).astype(np.float32)
    row = np.floor(t / 64.0)
    col = t - row * 64.0
    cos = np.zeros((128, 8, 4, 16), np.float32)
    sin = np.zeros((128, 8, 4, 16), np.float32)
    for br in range(2):
        for rc, pos in enumerate([row, col]):
            ang = pos[:, :, None] * inv[None, None, :]
            cos[:, :, br * 2 + rc, :] = np.cos(ang)
            sin[:, :, br * 2 + rc, :] = np.sin(ang)
    c[:, C_COS:C_COS + 512] = cos.reshape(128, 512)
    c[:, C_SIN:C_SIN + 512] = sin.reshape(128, 512)
    return c


PF = {}
_n = 0
for _p in range(4):
    for _j in range(3):
        for _tap in range(3):
            PF[("ca_w", _p, _j, _tap)] = _n; _n += 1
        PF[("ca_b", _p, _j)] = _n; _n += 1
    for _d in range(2):
        PF[("w0", _p, _d)] = _n; _n += 1
        PF[("a0", _p, _d)] = _n; _n += 1
    for _nm in ("k_k", "k_a", "omk_a", "r_k", "gn_g", "gn_b"):
        PF[("rw_" + _nm, _p)] = _n; _n += 1
for _h in range(4):
    for _j in range(2):
        for _tap in range(3):
            PF[("cb_w", _h, _j, _tap)] = _n; _n += 1
        PF[("cb_b", _h, _j)] = _n; _n += 1
    PF[("ml_gn_g", _h)] = _n; _n += 1
    PF[("ml_gn_b", _h)] = _n; _n += 1
PF[("subln",)] = _n; _n += 1
NPF = _n

PR_BI = 0
PR_BF = 8
PR_LQ1 = 16
PR_LK1 = 80
PR_LQ2 = 144
PR_LK2 = 208
NPR = 272


def _pack_pfm(inp, l):
    o = np.zeros((128, NPF), np.float32)
    for p in range(4):
        sl = slice(p * 128, p * 128 + 128)
        for j in range(3):
            for tap in range(3):
                o[:, PF[("ca_w", p, j, tap)]] = inp["conv_a_w"][l, tap, j * 512 + p * 128: j * 512 + p * 128 + 128]
            o[:, PF[("ca_b", p, j)]] = inp["conv_a_b"][l, j * 512 + p * 128: j * 512 + p * 128 + 128]
        for d in range(2):
            o[:, PF[("w0", p, d)]] = inp["rwkv_w0"][l, d, sl]
            o[:, PF[("a0", p, d)]] = inp["rwkv_a0"][l, d, sl]
        o[:, PF[("rw_k_k", p)]] = inp["rwkv_k_k"][l, sl]
        o[:, PF[("rw_k_a", p)]] = inp["rwkv_k_a"][l, sl]
        o[:, PF[("rw_r_k", p)]] = inp["rwkv_r_k"][l].reshape(512)[sl]
        o[:, PF[("rw_gn_g", p)]] = inp["rwkv_gn_g"][l, sl]
        o[:, PF[("rw_gn_b", p)]] = inp["rwkv_gn_b"][l, sl]
    for h in range(4):
        sl = slice(h * 128, h * 128 + 128)
        for j in range(2):
            for tap in range(3):
                o[:, PF[("cb_w", h, j, tap)]] = inp["conv_b_w"][l, tap, j * 512 + h * 128: j * 512 + h * 128 + 128]
            o[:, PF[("cb_b", h, j)]] = inp["conv_b_b"][l, j * 512 + h * 128: j * 512 + h * 128 + 128]
        o[:, PF[("ml_gn_g", h)]] = inp["mlstm_gn_g"][l, sl]
        o[:, PF[("ml_gn_b", h)]] = inp["mlstm_gn_b"][l, sl]
    o[:, PF[("subln",)]] = inp["diff_subln_g"][l]
    return o


def _pack_prow(inp, l):
    o = np.zeros((128, NPR), np.float32)
    o[:, PR_BI:PR_BI + 8] = inp["mlstm_b_i"][l].reshape(8)[None, :]
    o[:, PR_BF:PR_BF + 8] = inp["mlstm_b_f"][l].reshape(8)[None, :]
    o[:, PR_LQ1:PR_LQ1 + 64] = inp["diff_lq1"][l][None, :]
    o[:, PR_LK1:PR_LK1 + 64] = inp["diff_lk1"][l][None, :]
    o[:, PR_LQ2:PR_LQ2 + 64] = inp["diff_lq2"][l][None, :]
    o[:, PR_LK2:PR_LK2 + 64] = inp["diff_lk2"][l][None, :]
    return o


ENGS = ("pe", "act", "dve", "pool", "sp")
SAME_ENGINE_SYNC = True
SELF_RAW_ONLY = True


class Buf:
    __slots__ = ("name", "w", "r", "dsem", "excl")

    def __init__(self, name, init=None):
        self.name = name
        self.excl = False
        self.w = dict(init) if init else {}
        self.r = {}
        self.dsem = None


class Sched:
    def __init__(self, nc, es, n_dma_sems=90):
        self.nc = nc
        self.eng = {"pe": nc.tensor, "act": nc.scalar, "dve": nc.vector, "pool": nc.gpsimd, "sp": nc.sync}
        self.cnt = {}
        self.waited = {e: {} for e in ENGS}
        self.sem = {}
        for e in ENGS:
            self.sem[e] = es.enter_context(nc.semaphore("s_" + e))
            self.cnt[e] = 0
        self.free_dsems = [es.enter_context(nc.semaphore("d%d" % i)) for i in range(n_dma_sems)]
        self.n_dsem = 0
        self.nops = {e: 0 for e in ENGS}
        self.nwaits = {e: 0 for e in ENGS}
        self.fence = {}
        self.all_dma_events = {}
        self.recycled = []

    def newbuf(self, name, fenced=True):
        return Buf(name, self.fence if fenced else None)

    def close_scope(self, bufs):
        for b in bufs:
            for dct in (b.w, b.r):
                for k, v in dct.items():
                    if self.fence.get(k, 0) < v:
                        self.fence[k] = v
            if b.dsem is not None:
                self.recycled.append(b.dsem)

    def _dsem_for(self, b):
        if b.dsem is None:
            if self.recycled:
                key = self.recycled.pop()
            else:
                key = "D%d" % self.n_dsem
                self.sem[key] = self.free_dsems[self.n_dsem]
                self.n_dsem += 1
                self.cnt[key] = 0
            b.dsem = key
        return b.dsem

    def _emit_wait(self, ename, k, v):
        self.eng[ename].wait_ge(self.sem[k], v)
        self.nwaits[ename] += 1

    def _waits(self, eng, reads, writes):
        deps = {}
        for b in reads:
            for k, v in b.w.items():
                if deps.get(k, 0) < v:
                    deps[k] = v
        for b in writes:
            for k, v in b.w.items():
                if k == eng and SELF_RAW_ONLY:
                    continue
                if deps.get(k, 0) < v:
                    deps[k] = v
            for k, v in b.r.items():
                if k == eng and SELF_RAW_ONLY:
                    continue
                if deps.get(k, 0) < v:
                    deps[k] = v
        wd = self.waited[eng]
        for k, v in deps.items():
            if k == eng and (eng == "pe" or not SAME_ENGINE_SYNC):
                continue
            if wd.get(k, 0) >= v:
                continue
            wd[k] = v
            self._emit_wait(eng, k, v)

    def op(self, eng, fn, reads=(), writes=()):
        writes = [b for b in writes if b is not None] + [b for b in reads if b is not None and b.excl]
        reads = [b for b in reads if b is not None and not b.excl]
        self._waits(eng, reads, writes)
        self.cnt[eng] += 1
        n = self.cnt[eng]
        inst = fn(self.eng[eng])
        inst.then_inc(self.sem[eng], 1)
        self.nops[eng] += 1
        for b in reads:
            if b.r.get(eng, 0) < n:
                b.r[eng] = n
        for b in writes:
            b.w = {eng: n}
            b.r = {}

    def dma(self, qeng, fn, reads=(), writes=(), sembuf=None):
        reads = [b for b in reads if b is not None]
        writes = [b for b in writes if b is not None]
        self._waits(qeng, reads, writes)
        if sembuf is None:
            sembuf = writes[0] if writes else reads[0]
        key = self._dsem_for(sembuf)
        self.cnt[key] += 16
        n = self.cnt[key]
        inst = fn(self.eng[qeng])
        inst.then_inc(self.sem[key], 16)
        self.nops[qeng] += 1
        self.all_dma_events[key] = n
        for b in reads:
            if b.r.get(key, 0) < n:
                b.r[key] = n
        for b in writes:
            b.w = {key: n}
            b.r = {}

    def finish(self):
        for k, v in self.all_dma_events.items():
            if self.waited["sp"].get(k, 0) < v:
                self.waited["sp"][k] = v
                self._emit_wait("sp", k, v)
        for e in ("pe", "act", "dve", "pool"):
            if self.cnt[e] > 0 and self.waited["sp"].get(e, 0) < self.cnt[e]:
                self._emit_wait("sp", e, self.cnt[e])


class V:
    __slots__ = ("ap", "b")

    def __init__(self, ap, b):
        self.ap = ap
        self.b = b

    def __getitem__(self, idx):
        return V(self.ap[idx], self.b)

    def rr(self, pat, **kw):
        return V(self.ap.rearrange(pat, **kw), self.b)

    def bc(self, shape):
        return V(self.ap.broadcast_to(shape), self.b)

    def us(self, axis):
        return V(self.ap.unsqueeze(axis), self.b)

    def bitcast(self, dt):
        return V(self.ap.bitcast(dt), self.b)


class K:
    def __init__(self, nc, es):
        self.nc = nc
        self.S = Sched(nc, es)
        self.scopes = []

    def open_scope(self):
        es = ExitStack()
        es.__enter__()
        self.scopes.append((es, []))

    def close_scope(self):
        es, bufs = self.scopes.pop()
        self.S.close_scope(bufs)
        es.__exit__(None, None, None)

    _uid = 0

    def sb(self, name, shape, dt=F32):
        es, bufs = self.scopes[-1]
        K._uid += 1
        name = "%s_%d" % (name, K._uid)
        h = es.enter_context(self.nc.sbuf_tensor(name, list(shape), dt))
        b = self.S.newbuf(name)
        bufs.append(b)
        return V(h.ap(), b)

    def ps(self, name, shape, dt=F32):
        es, bufs = self.scopes[-1]
        h = es.enter_context(self.nc.psum_tensor(name, list(shape), dt))
        b = self.S.newbuf(name)
        bufs.append(b)
        return V(h.ap(), b)

    def dram(self, ap, tracked=False, name="dram"):
        return V(ap, self.S.newbuf(name, fenced=False) if tracked else None)

    def act(self, out, in_, func, bias=None, scale=None, accum=None, eng="act"):
        kw = {}
        reads = [in_.b]
        if bias is not None:
            if isinstance(bias, V):
                kw["bias"] = bias.ap
                reads.append(bias.b)
            else:
                kw["bias"] = float(bias)
        if scale is not None:
            if isinstance(scale, V):
                kw["scale"] = scale.ap
                reads.append(scale.b)
            else:
                kw["scale"] = float(scale)
        writes = [out.b]
        if accum is not None:
            kw["accum_out"] = accum.ap
            writes.append(accum.b)
        self.S.op("act", lambda e: e.activation(out=out.ap, in_=in_.ap, func=func, **kw), reads, writes)

    def tt(self, out, in0, in1, op, eng="dve"):
        self.S.op(eng, lambda e: e.tensor_tensor(out=out.ap, in0=in0.ap, in1=in1.ap, op=op), [in0.b, in1.b], [out.b])

    def ts(self, out, in0, s1, op0, s2=None, op1=None, accum=None, eng="dve"):
        reads = [in0.b]
        a1 = s1.ap if isinstance(s1, V) else float(s1)
        if isinstance(s1, V):
            reads.append(s1.b)
        kw = {}
        if s2 is not None:
            a2 = s2.ap if isinstance(s2, V) else float(s2)
            if isinstance(s2, V):
                reads.append(s2.b)
            kw["op1"] = op1
        else:
            a2 = None
        writes = [out.b]
        if accum is not None:
            kw["accum_out"] = accum.ap
            writes.append(accum.b)
            if op1 is not None:
                kw["op1"] = op1
        self.S.op(eng, lambda e: e.tensor_scalar(out=out.ap, in0=in0.ap, scalar1=a1, scalar2=a2, op0=op0, **kw), reads, writes)

    def stt(self, out, in0, scalar, in1, op0, op1):
        reads = [in0.b, in1.b]
        sc = scalar.ap if isinstance(scalar, V) else float(scalar)
        if isinstance(scalar, V):
            reads.append(scalar.b)
        self.S.op("dve", lambda e: e.scalar_tensor_tensor(out=out.ap, in0=in0.ap, scalar=sc, in1=in1.ap, op0=op0, op1=op1), reads, [out.b])

    def copy(self, out, in_, eng="dve"):
        if eng == "act":
            self.act(out, in_, AF.Copy)
        else:
            self.S.op(eng, lambda e: e.tensor_copy(out=out.ap, in_=in_.ap), [in_.b], [out.b])

    def memset(self, out, val, eng="pool"):
        self.S.op(eng, lambda e: e.memset(out.ap, float(val)), [], [out.b])

    def reduce(self, out, in_, op, axis=AX.X, eng="dve"):
        self.S.op(eng, lambda e: e.tensor_reduce(out=out.ap, in_=in_.ap, axis=axis, op=op), [in_.b], [out.b])

    def recip(self, out, in_):
        self.S.op("dve", lambda e: e.reciprocal(out=out.ap, in_=in_.ap), [in_.b], [out.b])

    def scan(self, out, d0, d1, initial, op0, op1):
        self.S.op("dve", lambda e: e.tensor_tensor_scan(out=out.ap, data0=d0.ap, data1=d1.ap, initial=float(initial), op0=op0, op1=op1), [d0.b, d1.b], [out.b])

    def mm(self, out, lhsT, rhs, start=True, stop=True):
        self.S.op("pe", lambda e: e.matmul(out.ap, lhsT=lhsT.ap, rhs=rhs.ap, start=start, stop=stop), [lhsT.b, rhs.b], [out.b])

    def tr(self, out, in_, ident):
        self.S.op("pe", lambda e: e.transpose(out=out.ap, in_=in_.ap, identity=ident.ap), [in_.b, ident.b], [out.b])

    def dma(self, out, in_, q="sp"):
        wr = [out.b]
        rd = [in_.b]
        out_is_dram = "DRAM" in str(out.ap.space).upper()
        sembuf = in_.b if (out_is_dram and in_.b is not None) else out.b
        self.S.dma(q, lambda e: e.dma_start(out=out.ap, in_=in_.ap), rd, wr, sembuf=sembuf)


class _Stop(Exception):
    pass


class Prog:
    stop_stage = None

    def stage(self, name):
        if self.stop_stage is not None and name == self.stop_stage:
            raise _Stop()

    def run_unit(self, fn, *a):
        depth = len(self.k.scopes)
        try:
            fn(*a)
        except _Stop:
            while len(self.k.scopes) > depth:
                self.k.close_scope()

    def __init__(self, dbg=False, layers=(0, 1), groups=(0, 1), parts=None):
        self.dbg = dbg
        self.layers = layers
        self.groups = groups
        self.parts = parts
        self.dumps = {}
        self.bank_i = 0
        self.tbank = 0
        self.abank = 0

    def want(self, part):
        return self.parts is None or part in self.parts

    def declare(self):
        nc = self.nc
        di = lambda n, s: nc.dram_tensor(n, list(s), F32, kind="ExternalInput").ap()
        do = lambda n, s: nc.dram_tensor(n, list(s), F32, kind="ExternalOutput").ap()
        dint = lambda n, s: nc.dram_tensor(n, list(s), F32, kind="Internal").ap()
        self.d_xin = di("xin", [2, TG, D])
        self.d_cT = di("cT", [128, 32])
        self.d_wada = di("w_ada", [2, D, 3 * D])
        self.d_bada = di("b_ada", [2, 3 * D])
        self.d_win = di("w_in", [2, D, P_IN])
        self.d_wout = di("w_out", [2, D, D])
        self.d_pfm = di("pfm", [2, 128, NPF])
        self.d_prow = di("prow", [2, 128, NPR])
        self.d_lng = di("ln_g", [2, D])
        self.d_lnb = di("ln_b", [2, D])
        self.d_wup = di("wup", [2, 2, 128, 512])
        self.d_rw0 = di("rw0", [2, 2, 4, 128, 128])
        self.d_ml0 = di("ml0", [2, 2, 4, 128, 130])
        self.d_mm0 = di("mm0", [2, 128, 8])
        self.d_ck = di("ck", [2, 8, PAST, 128])
        self.d_cv = di("cv", [2, 8, PAST, 128])
        self.d_consts = di("consts", [128, NCONST])
        self.d_yout = do("yout", [2, TG, D])
        self.d_nk = do("nk", [4, 2, 8, TSEQ_P, 128])
        self.d_nv = do("nv", [4, 2, 8, TSEQ_P, 128])
        self.d_nrw = do("nrw", [4, 2, 2, 4, 128, 128])
        self.d_nmc = do("nmc", [4, 2, 2, 4, 128, 130])
        self.d_nmm = do("nmm", [2, 32])
        self.d_x1 = dint("x1s", [2, TG, D])
        self.d_modg = dint("modg", [2, 2, D])

    def dump(self, name, v, shape):
        if not self.dbg:
            return
        ap = self.nc.dram_tensor("dbg_" + name, list(shape), F32, kind="ExternalOutput").ap()
        self.dumps[name] = list(shape)
        k = self.k
        if v.ap.dtype != F32:
            k.open_scope()
            t = k.sb("dbgt_" + name, shape, F32)
            k.copy(t, v)
            k.dma(V(ap, None), t)
            k.close_scope()
        else:
            k.dma(V(ap, None), v)

    def bank(self):
        b = self.PB[self.bank_i % 8]
        self.bank_i += 1
        return b

    def build(self):
        self.nc = nc = bass.Bass("TRN2", target_bir_lowering=False)
        self.declare()
        with ExitStack() as es:
            self.k = k = K(nc, es)
            k.open_scope()
            self.setup()
            for l in self.layers:
                self.layer_setup(l)
                for g in self.groups:
                    self.group(l, g)
            k.S.finish()
            k.close_scope()
        return nc

    def setup(self):
        k = self.k
        self.CON = k.sb("CON", [128, NCONST])
        k.dma(self.CON, V(self.d_consts, None))
        C = self.CON
        self.IDF = C[:, C_IDENT:C_IDENT + 128]
        self.MASK = C[:, C_MASK:C_MASK + 1024].rr("p (a b) -> p a b", a=8)
        self.ONESBD = C[:, C_ONESBD:C_ONESBD + 128]
        self.ONES = C[:, C_ONES:C_ONES + 128]
        self.HM = C[:, C_HM:C_HM + 2]
        self.COS = C[:, C_COS:C_COS + 512].rr("p (i a b) -> p i a b", i=8, a=4)
        self.SIN = C[:, C_SIN:C_SIN + 512].rr("p (i a b) -> p i a b", i=8, a=4)
        self.IDB = k.sb("IDB", [128, 128], BF16)
        k.copy(self.IDB, self.IDF)
        es, bufs = k.scopes[-1]
        h = es.enter_context(self.nc.psum_tensor("PS", [128, 8, 512], F32))
        self.PB = []
        for i in range(8):
            b = k.S.newbuf("bank%d" % i)
            b.excl = True
            bufs.append(b)
            self.PB.append(V(h.ap()[:, i, :], b))
        self.mixT = k.sb("mixT", [128, KC, TG], BF16)
        self.WS = k.sb("WS", [128, KC, 656], BF16)
        self.PFM = k.sb("PFM", [128, NPF])
        self.PROW = k.sb("PROW", [128, NPR])
        self.WUP = k.sb("WUP", [128, 2, 512], BF16)
        self.MODT = k.sb("MODT", [128, 2, 32])
        self.LAM = k.sb("LAM", [128, 2])
        self.SUBG = k.sb("SUBG", [128, 1])
        self.RMASK = k.sb("RMASK", [128, TG])
        k.memset(self.RMASK, 1.0)
        k.memset(self.RMASK.rr("p (c t) -> p c t", t=64)[:, :, 0:1], 0.0)
        self.EPSC = k.sb("EPSC", [128, 4])
        k.memset(self.EPSC[:, 0:1], 1e-12)
        k.memset(self.EPSC[:, 1:2], GN_EPS_A)
        k.memset(self.EPSC[:, 2:3], GN_EPS)
        k.memset(self.EPSC[:, 3:4], 1.0)
        self.EPS12 = self.EPSC[:, 0:1]
        self.EPSA = self.EPSC[:, 1:2]
        self.EPSB = self.EPSC[:, 2:3]
        self.ONE1 = self.EPSC[:, 3:4]
        self.x1bufs = [[k.S.newbuf("x1_%d_%d" % (g, i), fenced=False) for i in range(NT)] for g in range(2)]
        self.modgbuf = [[k.S.newbuf("modg%d%d" % (l, j), fenced=False) for j in range(2)] for l in range(2)]

    def pf(self, *key):
        c = PF[key]
        return self.PFM[:, c:c + 1]

    def layer_setup(self, l):
        k = self.k
        k.dma(self.PFM, V(self.d_pfm[l], None))
        k.dma(self.PROW, V(self.d_prow[l], None))
        for p in range(4):
            k.ts(self.pf("rw_omk_a", p), self.pf("rw_k_a", p), -1.0, ALU.mult, 1.0, ALU.add)
        k.dma(self.WUP, V(self.d_wup[l].rearrange("a p c -> p a c"), None), q="pool")
        lam_init = 0.8 - 0.6 * math.exp(-0.3 * l)
        k.open_scope()
        t = k.sb("lamt", [128, 64])
        s = k.sb("lams", [128, 2])
        k.tt(t, self.PROW[:, PR_LQ1:PR_LQ1 + 64], self.PROW[:, PR_LK1:PR_LK1 + 64], ALU.mult)
        k.reduce(s[:, 0:1], t, ALU.add)
        k.tt(t, self.PROW[:, PR_LQ2:PR_LQ2 + 64], self.PROW[:, PR_LK2:PR_LK2 + 64], ALU.mult)
        k.reduce(s[:, 1:2], t, ALU.add)
        k.act(s, s, AF.Exp)
        k.tt(self.LAM[:, 0:1], s[:, 0:1], s[:, 1:2], ALU.subtract)
        k.ts(self.LAM[:, 0:1], self.LAM[:, 0:1], lam_init, ALU.add)
        k.ts(self.SUBG, self.pf("subln"), 1.0 - lam_init, ALU.mult)
        k.close_scope()
        if self.want("ada"):
            self.ada(l)

    def prefetch(self):
        if self.wi < len(self.wlist):
            c0, ncols = self.wlist[self.wi]
            self.wi += 1
            self.load_w(self.d_win[self.l], c0, ncols)

    def load_w(self, src_rows, c0, ncols):
        k = self.k
        src = src_rows.rearrange("(kc p) c -> p kc c", p=128)[:, :, c0:c0 + ncols]
        k.dma(self.WS[:, :, 0:ncols], V(src, None), q="pool")

    def ada(self, l):
        k = self.k
        k.open_scope()
        ct = k.sb("ada_c", [128, 32])
        cb = k.sb("ada_cb", [128, 32], BF16)
        k.dma(ct, V(self.d_cT, None))
        k.act(cb, ct, AF.Silu)
        brow = k.sb("ada_brow", [1, 512])
        rowt = [k.sb("ada_row%d" % j, [1, 512]) for j in range(2)]
        pm = self.PB[7]
        WA = [self.WS, k.sb("ada_w2", [128, KC, 520], BF16)]
        for s in range(12):
            wa = WA[s % 2]
            srcw = self.d_wada[l].rearrange("(kc p) c -> p kc c", p=128)[:, :, s * 512:(s + 1) * 512]
            k.dma(wa[:, :, 0:512], V(srcw, None), q="pool")
            k.dma(brow, V(self.d_bada[l:l + 1, s * 512:(s + 1) * 512], None))
            for j in range(2):
                pb = self.PB[j + 2 * (s % 2)]
                for kc in range(KC):
                    k.mm(pb[0:1, 0:512], cb[:, kc * 2 + j:kc * 2 + j + 1], wa[:, kc, 0:512], start=(kc == 0), stop=(kc == KC - 1))
                k.tt(rowt[j], pb[0:1, 0:512], brow, ALU.add)
                if s < 8:
                    for q in range(4):
                        c = s * 4 + q
                        k.mm(pm[:, (j * 32 + c) * 2:(j * 32 + c) * 2 + 2], rowt[j][0:1, q * 128:(q + 1) * 128], self.ONES[0:1, 0:2])
                else:
                    k.dma(V(self.d_modg[l, j:j + 1, (s - 8) * 512:(s - 7) * 512], self.modgbuf[l][j]), rowt[j])
        k.copy(self.MODT.rr("p a b -> p (a b)"), pm[:, 0:128].rr("p (c t) -> p c t", t=2)[:, :, 0])
        k.ts(self.MODT[:, :, 16:32], self.MODT[:, :, 16:32], 1.0, ALU.add)
        self.dump("modT%d" % l, self.MODT, [128, 2, 32])
        k.close_scope()

    def group(self, l, g):
        k = self.k
        self.l, self.g = l, g
        self.nseq, self.tseq = (NSEQ_P, TSEQ_P) if g == 0 else (1, TG)
        tag = "%d%d" % (l, g)
        k.open_scope()
        self.uT = k.sb("uT" + tag, [128, KC, TG], BF16)
        self.LW = k.sb("LW" + tag, [128, TG], BF16)
        self.LA = k.sb("LA" + tag, [128, TG], BF16)
        self.OMG = k.sb("OMG" + tag, [128, NT, 8])
        self.CLAMP = k.sb("CLAMP" + tag, [128, NT, 8])
        self.SC = k.sb("SC" + tag, [128, NCH, 8])
        if self.want("u"):
            self.uphase(l, g)
        units = []
        if self.want("lora"):
            units.append((self.unit_lora, (l, g), U_LORA, 256))
        for p in range(4):
            if self.want("rw%d" % p):
                units.append((self.unit_rwkv, (l, g, p), U_RW[p], 512))
        if self.want("mg"):
            units.append((self.unit_mg, (l, g), U_MG, 16))
        for h in range(4):
            if self.want("ml%d" % h):
                units.append((self.unit_mlstm, (l, g, h), U_ML[h], 640))
        for h in range(8):
            if self.want("at%d" % h):
                units.append((self.unit_attn3, (l, g, h), U_AT[h], 512))
        self.wlist = [(u[2], u[3]) for u in units]
        self.wi = 0
        self.prefetch()
        for ui, (fn, args, _c0, _nc) in enumerate(units):
            self.run_unit(fn, *args)
            while self.wi < min(ui + 2, len(self.wlist)):
                self.prefetch()
        k.close_scope()
        if self.want("o"):
            self.phase_o(l, g)

    def xsrc(self, l, g, i):
        if l == 0:
            return V(self.d_xin[g, i * 128:(i + 1) * 128, :], None)
        return V(self.d_x1[g, i * 128:(i + 1) * 128, :], self.x1bufs[g][i])

    def uphase(self, l, g):
        k = self.k
        k.open_scope()
        xts = [k.sb("xt%d" % j, [128, D]) for j in range(2)]
        for i in range(NT):
            xt = xts[i % 2]
            k.dma(xt, self.xsrc(l, g, i))
            for q in range(4):
                pb = self.bank()
                for kk in range(4):
                    kc = q * 4 + kk
                    k.tr(pb[:, kk * 128:(kk + 1) * 128], xt[:, kc * 128:(kc + 1) * 128], self.IDF)
                for kk in range(4):
                    kc = q * 4 + kk
                    k.act(self.uT[:, kc, i * 128:(i + 1) * 128], pb[:, kk * 128:(kk + 1) * 128], AF.Identity,
                          scale=self.MODT[:, g, 16 + kc:17 + kc], bias=self.MODT[:, g, kc:kc + 1])
        k.close_scope()
        self.dump("uT%d%d" % (l, g), self.uT[:, :, 0:256], [128, KC, 256])

    def proj_fm(self, col, evac):
        k = self.k
        for nb in range(2):
            pb = self.bank()
            for kc in range(KC):
                k.mm(pb[:, 0:512], self.WS[:, kc, col:col + 128], self.uT[:, kc, nb * 512:(nb + 1) * 512], start=(kc == 0), stop=(kc == KC - 1))
            evac(nb, pb[:, 0:512])

    def proj_tm(self, col, ncols, evac):
        k = self.k
        for i in range(NT):
            pb = self.bank()
            for kc in range(KC):
                k.mm(pb[:, 0:ncols], self.uT[:, kc, i * 128:(i + 1) * 128], self.WS[:, kc, col:col + ncols], start=(kc == 0), stop=(kc == KC - 1))
            evac(i, pb[:, 0:ncols])

    def pad_evac(self, PRE, j):
        k = self.k
        nseq, tseq = self.nseq, self.tseq

        def ev(nb, ps):
            if nseq == 1:
                k.act(PRE[:, j, 0, 1 + nb * 512:1 + (nb + 1) * 512], ps, AF.Copy)
            else:
                k.act(PRE[:, j, 2 * nb:2 * nb + 2, 1:tseq + 1], ps.rr("p (s t) -> p s t", s=2), AF.Copy)
        return ev

    def conv(self, X, PRE, j, w0, w1, w2, b):
        k = self.k
        nseq, tseq = self.nseq, self.tseq
        xv = X.rr("p (s t) -> p s t", s=nseq)
        k.act(xv, PRE[:, j, :, 1:tseq + 1], AF.Identity, scale=w1, bias=b)
        k.stt(xv, PRE[:, j, :, 0:tseq], w0, xv, ALU.mult, ALU.add)
        k.stt(xv, PRE[:, j, :, 2:tseq + 2], w2, xv, ALU.mult, ALU.add)

    def unit_lora(self, l, g):
        k = self.k
        self.proj_fm(0, lambda nb, ps: k.act(self.LW[:, nb * 512:(nb + 1) * 512], ps, AF.Tanh))
        self.proj_fm(128, lambda nb, ps: k.act(self.LA[:, nb * 512:(nb + 1) * 512], ps, AF.Copy))
        self.prefetch()
        self.dump("LW%d%d" % (l, g), self.LW, [128, TG])

    def unit_rwkv(self, l, g, p):
        k = self.k
        nseq, tseq = self.nseq, self.tseq
        cps = tseq // 64
        tag = "%d%d%d" % (l, g, p)
        k.open_scope()
        GT = k.sb("rGT" + tag, [128, TG])
        BON = k.sb("rBON" + tag, [128, TG])
        KR = [k.sb("rKR%d" % d + tag, [128, NT, 2, 128], BF16) for d in range(2)]
        AH = [k.sb("rAH%d" % d + tag, [128, TG], BF16) for d in range(2)]
        KH = [k.sb("rKH%d" % d + tag, [128, TG], BF16) for d in range(2)]
        VB = k.sb("rVB" + tag, [128, TG], BF16)
        GL = [k.sb("rGL%d" % d + tag, [128, NCH]) for d in range(2)]
        YTM = k.sb("rY" + tag, [128, NT, 128])

        k.open_scope()
        R = k.sb("rR" + tag, [128, TG])
        Kk = k.sb("rK" + tag, [128, TG])
        V32 = k.sb("rV" + tag, [128, TG])
        k.open_scope()
        PRE = k.sb("rPRE" + tag, [128, 3, nseq, tseq + 2])
        k.memset(PRE[:, :, :, 0:1], 0.0)
        k.memset(PRE[:, :, :, tseq + 1:tseq + 2], 0.0)
        for j in range(3):
            self.proj_fm(j * 128, self.pad_evac(PRE, j))
        self.proj_fm(384, lambda nb, ps: k.act(GT[:, nb * 512:(nb + 1) * 512], ps, AF.Silu))
        self.prefetch()
        for j, X in enumerate((R, Kk, V32)):
            self.conv(X, PRE, j, self.pf("ca_w", p, j, 0), self.pf("ca_w", p, j, 1), self.pf("ca_w", p, j, 2), self.pf("ca_b", p, j))
        k.close_scope()
        self.stage("rw_conv")
        B = [k.sb("rB%d" % i + tag, [128, TG]) for i in range(9)]
        k.copy(VB, V32, eng="pool")
        KK, SQ, RS = B[0], B[1], B[2]
        k.ts(KK, Kk, self.pf("rw_k_k", p), ALU.mult)
        k.act(SQ, KK, AF.Square)
        for nb in range(2):
            pb = self.bank()
            k.mm(pb[:, 0:512], self.ONESBD, SQ[:, nb * 512:(nb + 1) * 512])
            k.act(RS[:, nb * 512:(nb + 1) * 512], pb[:, 0:512], AF.Sqrt, bias=self.EPS12)
        k.recip(RS, RS)
        k.tt(KK, KK, RS, ALU.mult)
        self.stage("rw_kk")
        KS = B[8]
        v3 = lambda X: X.rr("p (i t) -> p i t", t=128)
        c3 = lambda X: X.rr("p (c t) -> p c t", t=64)
        for d in range(2):
            SIG, AA, KT, AL, CS, T1, T2 = B[1], B[2], B[3], B[4], B[5], B[6], B[7]
            dr = slice(d * 64, d * 64 + 64)
            for nb in range(2):
                ns = slice(nb * 512, (nb + 1) * 512)
                pb = self.bank()
                k.mm(pb[:, 0:512], self.WUP[dr, 0, p * 128:(p + 1) * 128], self.LW[dr, ns])
                k.act(SIG[:, ns], pb[:, 0:512], AF.Sigmoid, bias=self.pf("w0", p, d))
                pb = self.bank()
                k.mm(pb[:, 0:512], self.WUP[dr, 1, p * 128:(p + 1) * 128], self.LA[dr, ns])
                k.act(AA[:, ns], pb[:, 0:512], AF.Sigmoid, bias=self.pf("a0", p, d))
            k.ts(T1, AA, self.pf("rw_k_a", p), ALU.mult, self.pf("rw_omk_a", p), ALU.add)
            k.tt(KT, T1, Kk, ALU.mult)
            if d == 0:
                k.copy(KS, KT, eng="pool")
            else:
                k.tt(KS, KS, KT, ALU.add, eng="pool")
            k.tt(AL, AA, KK, ALU.mult)
            k.scan(CS, self.RMASK, SIG, 0.0, ALU.mult, ALU.add)
            if d == 1:
                k.tt(c3(T1), c3(CS), c3(CS)[:, :, 63:64].bc([128, NCH, 64]), ALU.subtract)
                k.tt(CS, SIG, T1, ALU.subtract)
            k.tt(T2, CS, SIG, ALU.subtract)
            k.act(T2, T2, AF.Exp, scale=-WDECAY)
            k.tt(KR[d][:, :, 0, :], v3(KK), v3(T2), ALU.mult)
            EP = T1
            k.act(EP, CS, AF.Exp, scale=-WDECAY)
            k.tt(KR[d][:, :, 1, :], v3(R), v3(EP), ALU.mult)
            k.copy(GL[d], c3(EP)[:, :, 63 if d == 0 else 0])
            EM = AA
            k.act(EM, CS, AF.Exp, scale=WDECAY)
            k.tt(AH[d], AL, EM, ALU.mult)
            k.tt(KH[d], KT, EM, ALU.mult)
        k.ts(B[1], R, self.pf("rw_r_k", p), ALU.mult)
        k.tt(B[1], B[1], KS, ALU.mult)
        for nb in range(2):
            ns = slice(nb * 512, (nb + 1) * 512)
            pb = self.bank()
            k.mm(pb[:, 0:512], self.ONESBD, B[1][:, ns])
            k.tt(BON[:, ns], pb[:, 0:512], V32[:, ns], ALU.mult)
        if p == 0:
            self.dump("rw_R%d%d" % (l, g), R, [128, TG])
            self.dump("rw_KK%d%d" % (l, g), KK, [128, TG])
            self.dump("rw_BON%d%d" % (l, g), BON, [128, TG])
            self.dump("rw_GL0%d%d" % (l, g), GL[0], [128, NCH])
            self.dump("rw_GL1%d%d" % (l, g), GL[1], [128, NCH])
        k.close_scope()

        self.stage("rw_prep")
        k.open_scope()
        TMB = k.sb("rTMB" + tag, [128, NT, 5, 128], BF16)
        for i in range(NT):
            pbb = self.bank().bitcast(BF16)
            ts_ = slice(i * 128, (i + 1) * 128)
            for s, src in enumerate((VB, KH[0], KH[1], AH[0], AH[1])):
                k.tr(pbb[:, s * 128:(s + 1) * 128], src[:, ts_], self.IDB)
            k.copy(TMB[:, i].rr("p a b -> p (a b)"), pbb[:, 0:640], eng=("act" if i % 2 else "dve"))
        self.stage("rw_tmb")
        k.memset(YTM, 0.0)
        self.RB = [k.sb("rRB%d" % i + tag, [128, 128], BF16) for i in range(2)]
        self.UB = [k.sb("rUB%d" % i + tag, [128, 128], BF16) for i in range(2)]
        self.HT = [k.sb("rHT%d" % i + tag, [128, 128]) for i in range(2)]
        Hf = {}
        Hb = {}
        for s in range(nseq):
            for d in range(2):
                Hf[(s, d)] = k.sb("rHf%d%d" % (s, d) + tag, [128, 128])
                Hb[(s, d)] = k.sb("rHb%d%d" % (s, d) + tag, [128, 128], BF16)
        for d in range(2):
            k.open_scope()
            GMQ = k.sb("rGMQ%d" % d + tag, [128, NCH, 4, 128], BF16)
            P0 = k.sb("rP0%d" % d + tag, [128, NCH, 128], BF16)
            QA = k.sb("rQA%d" % d + tag, [128, NCH, 128], BF16)
            PA = k.sb("rPA%d" % d + tag, [128, NCH, 128], BF16)
            X = k.sb("rX%d" % d + tag, [128, NCH, 128], BF16)
            R1 = k.sb("rR1%d" % d + tag, [128, NT, 128])
            mP = 4 if d == 0 else 0
            for i in range(NT):
                ts_ = slice(i * 128, (i + 1) * 128)
                for hh in range(2):
                    j = i * 2 + hh
                    hr = slice(hh * 64, hh * 64 + 64)
                    pb = self.bank()
                    krv = KR[d][hr, i].rr("p a b -> p (a b)")
                    k.mm(pb[:, 0:256], AH[d][hr, ts_], krv)
                    k.mm(pb[:, 256:512], KH[d][hr, ts_], krv)
                    k.tt(GMQ[:, j], pb[:, 0:512].rr("p (a b) -> p a b", a=4), self.MASK[:, 4 * d:4 * d + 4, :], ALU.mult)
            for hh in range(2):
                hr = slice(hh * 64, hh * 64 + 64)
                for q in range(2):
                    pbP = self.bank()
                    for ii in range(4):
                        i = q * 4 + ii
                        k.mm(pbP[:, ii * 128:(ii + 1) * 128], KR[d][hr, i, 0, :], AH[d][hr, i * 128:(i + 1) * 128])
                    k.tt(P0[:, q * 8 + hh:q * 8 + 8:2, :], pbP[:, 0:512].rr("p (a b) -> p a b", a=4),
                         self.MASK[:, mP:mP + 1, :].bc([128, 4, 128]), ALU.mult)
            self.stage("rw_gram")
            k.tt(X, GMQ[:, :, 0, :], self.IDB.us(1).bc([128, NCH, 128]), ALU.add)
            Qc, Pc = GMQ[:, :, 0, :], P0
            Qn, Pn = QA, PA
            for lev in range(1, 6):
                for grp in range(4):
                    js = range(grp * 4, grp * 4 + 4)
                    gsl = slice(grp * 4, grp * 4 + 4)
                    pbP = self.bank()
                    for jj, j in enumerate(js):
                        k.mm(pbP[:, jj * 128:(jj + 1) * 128], Qc[:, j, :], Pc[:, j, :])
                    k.copy(Pn[:, gsl, :], pbP[:, 0:512].rr("p (a b) -> p a b", a=4), eng="act")
                    if lev < 5:
                        pbQ = self.bank()
                        for jj, j in enumerate(js):
                            k.mm(pbQ[:, jj * 128:(jj + 1) * 128], Pc[:, j, :], Qc[:, j, :])
                        k.copy(Qn[:, gsl, :], pbQ[:, 0:512].rr("p (a b) -> p a b", a=4), eng="dve")
                    pbX = self.bank()
                    for jj, j in enumerate(js):
                        k.mm(pbX[:, jj * 128:(jj + 1) * 128], Pn[:, j, :], X[:, j, :])
                    k.tt(X[:, gsl, :], X[:, gsl, :], pbX[:, 0:512].rr("p (a b) -> p a b", a=4), ALU.add)
                if lev == 1:
                    Qc, Pc, Qn, Pn = QA, PA, k.sb("rQB%d" % d + tag, [128, NCH, 128], BF16), P0
                else:
                    Qc, Pc, Qn, Pn = Qn, Pn, Qc, Pc
            self.stage("rw_inv")
            for i in range(NT):
                pb = self.bank()
                for hh in range(2):
                    j = i * 2 + hh
                    vs = TMB[:, i, 0, hh * 64:(hh + 1) * 64]
                    k.mm(pb[:, hh * 64:(hh + 1) * 64], GMQ[:, j, 2, :], vs)
                    k.mm(pb[:, 128 + hh * 64:128 + (hh + 1) * 64], GMQ[:, j, 3, :], vs)
                k.copy(R1[:, i, :], pb[:, 0:128], eng="act")
                k.tt(YTM[:, i, :], YTM[:, i, :], pb[:, 128:256], ALU.add)
            self.stage("rw_r1")
            for s in range(nseq):
                if g == 0:
                    k.memset(Hf[(s, d)], 0.0)
                else:
                    k.dma(Hf[(s, d)], V(self.d_rw0[l, d, p], None))
                k.copy(Hb[(s, d)], Hf[(s, d)], eng="act")
            for cs in range(cps):
                for s in range(nseq):
                    c = s * cps + (cs if d == 0 else cps - 1 - cs)
                    i, half = c // 2, c % 2
                    tr_ = slice(half * 64, half * 64 + 64)
                    hf, hb = Hf[(s, d)], Hb[(s, d)]
                    pbR = self.bank()
                    k.mm(pbR[tr_, 0:128], KR[d][:, i, 0, tr_], hb)
                    Rb = self.RB[(s + d) % 2]
                    k.tt(Rb[tr_, :], pbR[tr_, 0:128], R1[tr_, i, :], ALU.add)
                    self.stage("sc_R%d" % c)
                    pbU = self.bank()
                    for hh in range(2):
                        j = i * 2 + hh
                        k.mm(pbU[tr_, hh * 64:(hh + 1) * 64], X[tr_, j, tr_], Rb[tr_, hh * 64:(hh + 1) * 64])
                    Ub = self.UB[(s + d) % 2]
                    k.act(Ub[tr_, :], pbU[tr_, 0:128], AF.Copy, scale=-1.0)
                    self.stage("sc_U%d" % c)
                    pbY = self.bank()
                    k.mm(pbY[tr_, 0:128], KR[d][:, i, 1, tr_], hb)
                    pbH = self.bank()
                    k.mm(pbH[:, 0:128], TMB[tr_, i, 1 + d, :], TMB[tr_, i, 0, :], start=True, stop=False)
                    k.mm(pbH[:, 0:128], TMB[tr_, i, 3 + d, :], Ub[tr_, :], start=False, stop=True)
                    pbY2 = self.bank()
                    for hh in range(2):
                        j = i * 2 + hh
                        k.mm(pbY2[tr_, hh * 64:(hh + 1) * 64], GMQ[tr_, j, 1, tr_], Ub[tr_, hh * 64:(hh + 1) * 64])
                    HT = self.HT[(s + d) % 2]
                    k.tt(HT, pbH[:, 0:128], self.ONESBD, ALU.mult)
                    k.tt(HT, HT, hf, ALU.add)
                    k.ts(hb, HT, GL[d][:, c:c + 1], ALU.mult)
                    k.ts(hf, HT, GL[d][:, c:c + 1], ALU.mult)
                    k.tt(YTM[tr_, i, :], YTM[tr_, i, :], pbY[tr_, 0:128], ALU.add)
                    k.tt(YTM[tr_, i, :], YTM[tr_, i, :], pbY2[tr_, 0:128], ALU.add)
                    self.stage("sc_Y%d" % c)
                    self.stage("sc_H%d" % c)
            self.stage("sc_end")
            if g == 0:
                for s in range(nseq):
                    k.dma(V(self.d_nrw[s, l, d, p], None), Hf[(s, d)])
            self.stage("sc_out%d" % d)
            if p == 0 and d == 0:
                self.dump("rw_X%d%d" % (l, g), X[:, 0:2, :], [128, 2, 128])
                self.dump("rw_R1%d%d" % (l, g), R1, [128, NT, 128])
            k.close_scope()
        if p == 0:
            self.dump("rw_Y%d%d" % (l, g), YTM, [128, NT, 128])
        self.stage("rw_scan")
        YN = k.sb("rYN" + tag, [128, NT, 128])
        ST = k.sb("rST" + tag, [128, 4, 16])
        yv = YTM.rr("p i (h v) -> p (i h) v", h=2)
        ynv = YN.rr("p i (h v) -> p (i h) v", h=2)
        k.reduce(ST[:, 0, :], yv, ALU.add)
        k.act(YN, YTM, AF.Square)
        k.reduce(ST[:, 1, :], ynv, ALU.add)
        k.ts(ST[:, 0, :], ST[:, 0, :], 1.0 / 64, ALU.mult)
        k.tt(ST[:, 2, :], ST[:, 0, :], ST[:, 0, :], ALU.mult)
        k.stt(ST[:, 1, :], ST[:, 1, :], 1.0 / 64, ST[:, 2, :], ALU.mult, ALU.subtract)
        k.act(ST[:, 1, :], ST[:, 1, :], AF.Sqrt, bias=self.EPSA)
        k.recip(ST[:, 1, :], ST[:, 1, :])
        k.tt(ynv, yv, ST[:, 0, :].us(2).bc([128, 16, 64]), ALU.subtract)
        k.tt(ynv, ynv, ST[:, 1, :].us(2).bc([128, 16, 64]), ALU.mult)
        self.stage("rw_ln")
        OUTF = k.sb("rOUT" + tag, [128, TG])
        for q in range(2):
            pb = self.bank()
            for ii in range(4):
                i = q * 4 + ii
                k.tr(pb[:, ii * 128:(ii + 1) * 128], YN[:, i, :], self.IDF)
            k.act(OUTF[:, q * 512:(q + 1) * 512], pb[:, 0:512], AF.Identity, scale=self.pf("rw_gn_g", p), bias=self.pf("rw_gn_b", p))
        k.tt(OUTF, OUTF, BON, ALU.add)
        k.tt(self.mixT[:, p, :], OUTF, GT, ALU.mult)
        if p == 0:
            self.dump("rw_out%d%d" % (l, g), self.mixT[:, 0, :], [128, TG])
        k.close_scope()
        k.close_scope()

    def unit_mg(self, l, g):
        k = self.k
        nseq, tseq = self.nseq, self.tseq
        cps = tseq // 64
        tag = "%d%d" % (l, g)
        k.open_scope()
        GI = k.sb("gGI" + tag, [128, NT, 16])
        self.proj_tm(0, 16, lambda i, ps: k.act(GI[:, i, :], ps, AF.Copy))
        self.prefetch()
        gi = k.sb("ggi" + tag, [128, NT, 8])
        LFN = k.sb("gLFN" + tag, [128, NT, 8])
        NB = k.sb("gNB" + tag, [128, NT, 8])
        AG = k.sb("gAG" + tag, [128, NT, 8])
        k.tt(gi, GI[:, :, 0:8], self.PROW[:, PR_BI:PR_BI + 8].us(1).bc([128, NT, 8]), ALU.add)
        k.tt(LFN, GI[:, :, 8:16], self.PROW[:, PR_BF:PR_BF + 8].us(1).bc([128, NT, 8]), ALU.add)
        k.act(LFN, LFN, AF.Exp, scale=-1.0)
        k.act(LFN, LFN, AF.Ln, bias=self.ONE1)
        self.stage("mg_a")
        pb = self.bank()
        for d in range(2):
            k.mm(pb[:, d * 32:(d + 1) * 32], self.MASK[:, 1 if d == 0 else 5, :], LFN[:, :, d * 4:(d + 1) * 4])
        for d in range(2):
            k.copy(NB[:, :, d * 4:(d + 1) * 4], pb[:, d * 32:(d + 1) * 32].rr("p (i h) -> p i h", h=4))
        k.tt(AG, gi, NB, ALU.add)
        self.stage("mg_b")
        pbt = self.bank()
        k.tr(pbt[0:64, 0:128], AG.rr("p i k -> p (i k)"), self.IDF)
        self.stage("mg_c")
        MXT = k.sb("gMXT" + tag, [64, 2])
        k.reduce(MXT, pbt[0:64, 0:128].rr("p (f t) -> p f t", f=2), ALU.max)
        RH = k.sb("gRH" + tag, [64, 64, 2])
        k.tt(RH, self.IDF[0:64, 0:64].us(2).bc([64, 64, 2]), MXT.us(1).bc([64, 64, 2]), ALU.mult)
        self.stage("mg_d")
        pbm = self.bank()
        k.mm(pbm[:, 0:128], self.ONES[0:64, :], RH.rr("p a b -> p (a b)"))
        MXF = k.sb("gMXF" + tag, [128, NT, 8, 2])
        k.copy(MXF.rr("p i k f -> p (i k f)"), pbm[:, 0:128])
        self.stage("mg_e")
        R2 = k.sb("gR2" + tag, [128, 64, 2])
        k.tt(R2, LFN.rr("p i k -> p (i k)").us(2).bc([128, 64, 2]), self.HM.us(1).bc([128, 64, 2]), ALU.mult)
        pbl = self.bank()
        k.mm(pbl[:, 0:128], self.ONES, R2.rr("p a b -> p (a b)"))
        NBL = k.sb("gNBL" + tag, [128, NT, 8, 2])
        k.copy(NBL.rr("p i k f -> p (i k f)"), pbl[:, 0:128])
        self.stage("mg_f")
        M0 = k.sb("gM0" + tag, [128, nseq, 8])
        MBAR = k.sb("gMBAR" + tag, [128, NT, 2, 8])
        if g == 0:
            k.memset(M0, 0.0)
        else:
            k.dma(M0[:, 0, :], V(self.d_mm0[l], None))
        ipseq = NT // nseq
        mxv = MXF.rr("p (s i) k f -> p s i k f", s=nseq)
        nbv = NBL.rr("p (s i) k f -> p s i k f", s=nseq)
        mbv = MBAR.rr("p (s i) f k -> p s i f k", s=nseq)
        scv = self.SC.rr("p (s i f) k -> p s i f k", s=nseq, f=2)
        for cs in range(cps):
            for d in range(2):
                cc = cs if d == 0 else cps - 1 - cs
                ii, half = cc // 2, cc % 2
                ds = slice(d * 4, d * 4 + 4)
                m0 = M0[:, :, ds]
                mb = mbv[:, :, ii, half, ds]
                k.tt(mb, m0, mxv[:, :, ii, ds, half], ALU.max)
                k.tt(scv[:, :, ii, half, ds], m0, mb, ALU.subtract)
                k.tt(m0, mb, nbv[:, :, ii, ds, half], ALU.subtract)
        self.stage("mg_g")
        k.act(self.SC, self.SC, AF.Exp)
        self.stage("mg_h")
        if g == 0:
            k.dma(V(self.d_nmm[l:l + 1, :], None), M0[0:1].rr("p s k -> p (s k)"))
        self.stage("mg_i")
        MT = k.sb("gMT" + tag, [128, NT, 8])
        k.ts(MT, MBAR[:, :, 0, :], self.HM[:, 0:1], ALU.mult)
        k.stt(MT, MBAR[:, :, 1, :], self.HM[:, 1:2], MT, ALU.mult, ALU.add)
        k.tt(self.OMG, AG, MT, ALU.subtract)
        k.act(self.OMG, self.OMG, AF.Exp)
        k.tt(self.CLAMP, NB, MT, ALU.subtract)
        k.act(self.CLAMP, self.CLAMP, AF.Exp)
        self.stage("mg_j")
        self.dump("mg_AG%d%d" % (l, g), AG, [128, NT, 8])
        self.stage("mg_k")
        self.dump("mg_MBAR%d%d" % (l, g), MBAR.rr("p i f k -> p (i f k)"), [128, NT * 16])
        self.dump("mg_SC%d%d" % (l, g), self.SC, [128, NCH, 8])
        self.dump("mg_OMG%d%d" % (l, g), self.OMG, [128, NT, 8])
        k.close_scope()

    def unit_mlstm(self, l, g, h):
        k = self.k
        nseq, tseq = self.nseq, self.tseq
        cps = tseq // 64
        tag = "%d%d%d" % (l, g, h)
        k.open_scope()
        GT = k.sb("mGT" + tag, [128, TG])
        VT = k.sb("mVT" + tag, [128, NT, 128])
        OT = k.sb("mOT" + tag, [128, NT, 128])
        QB = k.sb("mQB" + tag, [128, TG], BF16)
        KB = k.sb("mKB" + tag, [128, TG], BF16)
        k.open_scope()
        PRE = k.sb("mPRE" + tag, [128, 2, nseq, tseq + 2])
        self.stage("ml_a")
        k.memset(PRE[:, :, :, 0:1], 0.0)
        k.memset(PRE[:, :, :, tseq + 1:tseq + 2], 0.0)
        self.stage("ml_b")
        for j in range(2):
            self.proj_fm(j * 128, self.pad_evac(PRE, j))
        self.stage("ml_c")
        self.proj_fm(256, lambda nb, ps: k.act(GT[:, nb * 512:(nb + 1) * 512], ps, AF.Silu))
        self.stage("ml_d")

        def ev_vo(i, ps):
            k.copy(VT[:, i, :], ps[:, 0:128])
            k.act(OT[:, i, :], ps[:, 128:256], AF.Sigmoid)
        self.proj_tm(384, 256, ev_vo)
        self.prefetch()
        self.stage("ml_proj")
        X = k.sb("mX" + tag, [128, TG])
        for j, dst in enumerate((QB, KB)):
            self.conv(X, PRE, j, self.pf("cb_w", h, j, 0), self.pf("cb_w", h, j, 1), self.pf("cb_w", h, j, 2), self.pf("cb_b", h, j))
            if j == 0:
                k.act(dst, X, AF.Silu)
            else:
                k.act(X, X, AF.Silu)
                k.ts(dst, X, 128.0 ** -0.5, ALU.mult)
        k.close_scope()
        self.stage("ml_conv")
        KTM = k.sb("mKTM" + tag, [128, NT, 128], BF16)
        pbb = self.bank().bitcast(BF16)
        for i in range(NT):
            k.tr(pbb[:, i * 128:(i + 1) * 128], KB[:, i * 128:(i + 1) * 128], self.IDB)
        k.copy(KTM.rr("p i c -> p (i c)"), pbb[:, 0:1024])
        self.stage("ml_ktm")
        MTd = [k.sb("mMT%d" % d + tag, [128, NT, 128], BF16) for d in range(2)]
        for q in range(2):
            pb = self.bank()
            for ii in range(4):
                i = q * 4 + ii
                ts_ = slice(i * 128, (i + 1) * 128)
                k.mm(pb[:, ii * 128:(ii + 1) * 128], KB[:, ts_], QB[:, ts_])
            pv = pb[:, 0:512].rr("p (a b) -> p a b", a=4)
            k.tt(MTd[0][:, q * 4:q * 4 + 4, :], pv, self.MASK[:, 1:2, :].bc([128, 4, 128]), ALU.mult)
            k.tt(MTd[1][:, q * 4:q * 4 + 4, :], pv, self.MASK[:, 5:6, :].bc([128, 4, 128]), ALU.mult)
        self.stage("ml_mt")
        WV = [k.sb("mWV%d" % d + tag, [128, NT, 130], BF16) for d in range(2)]
        HI = [k.sb("mHI%d" % d + tag, [128, NT, 130]) for d in range(2)]
        for d in range(2):
            om = self.OMG[:, :, d * 4 + h:d * 4 + h + 1]
            k.tt(WV[d][:, :, 0:128], VT, om.bc([128, NT, 128]), ALU.mult)
            k.copy(WV[d][:, :, 128:129], om)
            k.memset(WV[d][:, :, 129:130], 0.0)
            for i in range(NT):
                pb = self.bank()
                k.mm(pb[:, 0:130], MTd[d][:, i, :], WV[d][:, i, :])
                k.copy(HI[d][:, i, :], pb[:, 0:130], eng=("act" if i % 2 else "dve"))
        self.stage("ml_hi")
        HS = k.sb("mHS" + tag, [128, NT, 128])
        TOTS = [k.sb("mTOTS%d" % i + tag, [128, NT, 130]) for i in range(2)]
        Z = {}
        Zb = {}
        for s in range(nseq):
            for d in range(2):
                Z[(s, d)] = k.sb("mZ%d%d" % (s, d) + tag, [128, 130])
                Zb[(s, d)] = k.sb("mZb%d%d" % (s, d) + tag, [128, 130], BF16)
                if g == 0:
                    k.memset(Z[(s, d)], 0.0)
                else:
                    k.dma(Z[(s, d)], V(self.d_ml0[l, d, h], None))
        for cs in range(cps):
            for s in range(nseq):
                for d in range(2):
                    c = s * cps + (cs if d == 0 else cps - 1 - cs)
                    i, half = c // 2, c % 2
                    tr_ = slice(half * 64, half * 64 + 64)
                    z, zb = Z[(s, d)], Zb[(s, d)]
                    dh = d * 4 + h
                    k.ts(zb, z, self.SC[:, c, dh:dh + 1], ALU.mult)
                    k.ts(z, z, self.SC[:, c, dh:dh + 1], ALU.mult)
                    pbZ = self.bank()
                    k.mm(pbZ[:, 0:130], KTM[tr_, i, :], WV[d][tr_, i, :])
                    pbS = self.bank()
                    k.mm(pbS[tr_, 0:130], QB[:, c * 64:(c + 1) * 64], zb)
                    k.tt(z, z, pbZ[:, 0:130], ALU.add)
                    k.tt(TOTS[d][tr_, i, :], pbS[tr_, 0:130], HI[d][tr_, i, :], ALU.add)
                    self.stage("ml_c%d_%d" % (c, d))
        DNb = k.sb("mDNb" + tag, [128, 2, NT])
        for d in range(2):
            k.act(DNb[:, d, :], TOTS[d][:, :, 128], AF.Abs)
            k.tt(DNb[:, d, :], DNb[:, d, :], self.CLAMP[:, :, d * 4 + h], ALU.max)
        k.recip(DNb, DNb)
        for d in range(2):
            k.tt(TOTS[d][:, :, 0:128], TOTS[d][:, :, 0:128], DNb[:, d, :].us(2).bc([128, NT, 128]), ALU.mult, eng=("pool" if d else "dve"))
        k.tt(HS, TOTS[0][:, :, 0:128], TOTS[1][:, :, 0:128], ALU.add)
        self.stage("ml_chain")
        if g == 0:
            for s in range(nseq):
                for d in range(2):
                    k.dma(V(self.d_nmc[s, l, d, h], None), Z[(s, d)])
        self.stage("ml_nmc")
        if h == 0:
            self.dump("ml_HS%d%d" % (l, g), HS, [128, NT, 128])
            self.dump("ml_HI%d%d" % (l, g), HI[0], [128, NT, 130])
        self.stage("ml_dump")
        k.tt(HS, HS, OT, ALU.mult)
        HN = k.sb("mHN" + tag, [128, NT, 128])
        ST = k.sb("mST" + tag, [128, 3, NT])
        k.reduce(ST[:, 0, :], HS, ALU.add)
        k.act(HN, HS, AF.Square)
        k.reduce(ST[:, 1, :], HN, ALU.add)
        k.ts(ST[:, 0, :], ST[:, 0, :], 1.0 / 128, ALU.mult)
        k.tt(ST[:, 2, :], ST[:, 0, :], ST[:, 0, :], ALU.mult)
        k.stt(ST[:, 1, :], ST[:, 1, :], 1.0 / 128, ST[:, 2, :], ALU.mult, ALU.subtract)
        k.act(ST[:, 1, :], ST[:, 1, :], AF.Sqrt, bias=self.EPSB)
        k.recip(ST[:, 1, :], ST[:, 1, :])
        k.tt(HN, HS, ST[:, 0, :].us(2).bc([128, NT, 128]), ALU.subtract)
        k.tt(HN, HN, ST[:, 1, :].us(2).bc([128, NT, 128]), ALU.mult)
        OUTF = k.sb("mOUT" + tag, [128, TG])
        for q in range(2):
            pb = self.bank()
            for ii in range(4):
                k.tr(pb[:, ii * 128:(ii + 1) * 128], HN[:, q * 4 + ii, :], self.IDF)
            k.act(OUTF[:, q * 512:(q + 1) * 512], pb[:, 0:512], AF.Identity, scale=self.pf("ml_gn_g", h), bias=self.pf("ml_gn_b", h))
        k.tt(self.mixT[:, 4 + h, :], OUTF, GT, ALU.mult)
        if h == 0:
            self.dump("ml_out%d%d" % (l, g), self.mixT[:, 4, :], [128, TG])
        k.close_scope()

    def attn_prep(self, l, g, h):
        k = self.k
        nseq = self.nseq
        tag = "%d%d%d" % (l, g, h)
        npast = 0 if g == 0 else PAST // 128
        nkt = npast + NT
        c = {"h": h, "nkt": nkt, "npast": npast}
        c["GT"] = GT = k.sb("aGT" + tag, [128, TG])
        c["VBk"] = VBk = k.sb("aVB" + tag, [128, nkt, 128], BF16)
        c["QT"] = QT = k.sb("aQT" + tag, [128, TG], BF16)
        c["KTa"] = KTa = k.sb("aKT" + tag, [128, nkt * 128], BF16)
        self.load_w(self.d_win[l], U_AT[h], 512)
        k.open_scope()
        QKV = k.sb("aQKV" + tag, [128, NT, 384])
        self.proj_tm(0, 384, lambda i, ps: k.copy(QKV[:, i, :], ps, eng=("act" if i % 2 else "dve")))
        self.proj_fm(384, lambda nb, ps: k.act(GT[:, nb * 512:(nb + 1) * 512], ps, AF.Silu))
        if g == 0:
            for s in range(nseq):
                k.dma(V(self.d_nk[s, l, h].rearrange("(i p) c -> p i c", p=128), None), QKV[:, 2 * s:2 * s + 2, 128:256])
                k.dma(V(self.d_nv[s, l, h].rearrange("(i p) c -> p i c", p=128), None), QKV[:, 2 * s:2 * s + 2, 256:384])
        else:
            T = [k.sb("aT%d" % i + tag, [128, NT, 4, 16]) for i in range(4)]
            for off in (0, 128):
                xv = QKV[:, :, off:off + 128].rr("p i (a x t) -> p i a x t", a=4, x=2)
                x1, x2 = xv[:, :, :, 0, :], xv[:, :, :, 1, :]
                k.tt(T[0], x1, self.COS, ALU.mult)
                k.tt(T[1], x2, self.SIN, ALU.mult)
                k.tt(T[2], x2, self.COS, ALU.mult)
                k.tt(T[3], x1, self.SIN, ALU.mult)
                k.tt(x1, T[0], T[1], ALU.subtract)
                k.tt(x2, T[2], T[3], ALU.add)
        QKB = k.sb("aQKB" + tag, [128, NT, 256], BF16)
        k.copy(QKB, QKV[:, :, 0:256])
        k.copy(VBk[:, npast:nkt, :], QKV[:, :, 256:384], eng="pool")
        for which, dst, c0 in ((0, QT, 0), (1, KTa, npast * 128)):
            pbb = self.bank().bitcast(BF16)
            for i in range(NT):
                k.tr(pbb[:, i * 128:(i + 1) * 128], QKB[:, i, which * 128:(which + 1) * 128], self.IDB)
            k.copy(dst[:, c0:c0 + TG], pbb[:, 0:1024], eng=("act" if which else "dve"))
        if g == 1:
            CKB = k.sb("aCKB" + tag, [128, npast, 128], BF16)
            k.dma(CKB, V(self.d_ck[l, h].rearrange("(i p) c -> p i c", p=128), None), q="pool")
            k.dma(VBk[:, 0:npast, :], V(self.d_cv[l, h].rearrange("(i p) c -> p i c", p=128), None), q="pool")
            pbb = self.bank().bitcast(BF16)
            for i in range(npast):
                k.tr(pbb[:, i * 128:(i + 1) * 128], CKB[:, i, :], self.IDB)
            k.copy(KTa[:, 0:npast * 128], pbb[:, 0:npast * 128])
        k.close_scope()
        return c

    def unit_attn2(self, l, g, h0):
        k = self.k
        nseq = self.nseq
        scale = 64.0 ** -0.5
        k.open_scope()
        ctxs = [self.attn_prep(l, g, h0 + j) for j in range(2)]
        for j, c in enumerate(ctxs):
            tag = "%d%d%d" % (l, g, c["h"])
            nkt = c["nkt"]
            c["E"] = [k.sb("aE%d" % b + tag, [128, nkt * 128], BF16) for b in range(2)]
            c["ET"] = [k.sb("aET%d" % b + tag, [128, nkt, 128], BF16) for b in range(2)]
            c["SMq"] = [[k.sb("aSM%d%d" % (a, b) + tag, [128, 4]) for b in range(2)] for a in range(2)]
            c["MXp"] = [k.sb("aMX%d" % b + tag, [128, 4]) for b in range(2)]
            c["NBp"] = [k.sb("aNB%d" % b + tag, [128, 1]) for b in range(2)]
            c["RS"] = k.sb("aRS" + tag, [128, 4])
            c["O2"] = k.sb("aO2" + tag, [128, 128])
            c["OD"] = k.sb("aOD" + tag, [128, 128])
            c["JK"] = k.sb("aJK" + tag, [128, 128])
            c["OT"] = k.sb("aOT" + tag, [128, 128])
            c["abank"] = j * 3
            c["ocol"] = j * 256
        pbO = self.PB[7]
        pbT = self.PB[6]
        items = [(i, br) for i in range(NT) for br in range(2)]

        def keys_of(c, i):
            if g == 0:
                sq = i // (NT // nseq)
                return [2 * sq, 2 * sq + 1]
            return list(range(c["nkt"]))

        def stage_a(c, kidx):
            i, br = items[kidx]
            par = kidx % 2
            kts = keys_of(c, i)
            k0 = kts[0] * 128
            ncols = len(kts) * 128
            chunks = [(c0, min(512, ncols - c0)) for c0 in range(0, ncols, 512)]
            brs = slice(br * 64, br * 64 + 64)
            banks = [self.PB[c["abank"] + ci] for ci in range(len(chunks))]
            for ci, (c0, cn) in enumerate(chunks):
                k.mm(banks[ci][:, 0:cn], c["QT"][brs, i * 128:(i + 1) * 128], c["KTa"][brs, k0 + c0:k0 + c0 + cn])
            for ci, (c0, cn) in enumerate(chunks):
                k.reduce(c["MXp"][par][:, ci:ci + 1], banks[ci][:, 0:cn], ALU.max)
            k.reduce(c["NBp"][par], c["MXp"][par][:, 0:len(chunks)], ALU.max)
            k.ts(c["NBp"][par], c["NBp"][par], -scale, ALU.mult)
            for ci, (c0, cn) in enumerate(chunks):
                k.act(c["E"][par][:, c0:c0 + cn], banks[ci][:, 0:cn], AF.Exp, scale=scale, bias=c["NBp"][par],
                      accum=c["SMq"][i % 2][br][:, ci:ci + 1])

        def stage_b(c, kidx):
            i, br = items[kidx]
            par = kidx % 2
            kts = keys_of(c, i)
            nk = len(kts)
            oc = c["ocol"] + br * 128
            for q0 in range(0, nk, 8):
                qn = min(8, nk - q0)
                pbb = pbT.bitcast(BF16)
                for jj in range(qn):
                    k.tr(pbb[:, jj * 128:(jj + 1) * 128], c["E"][par][:, (q0 + jj) * 128:(q0 + jj + 1) * 128], self.IDB)
                k.copy(c["ET"][par][:, q0:q0 + qn, :].rr("p a b -> p (a b)"), pbb[:, 0:qn * 128], eng=("act" if qn == 8 else "dve"))
            for jj in range(nk):
                k.mm(pbO[:, oc:oc + 128], c["ET"][par][:, jj, :], c["VBk"][:, kts[jj], :], start=(jj == 0), stop=(jj == nk - 1))

        def tail(c, i):
            qp = i % 2
            RS, O2, OD, JK, OTt = c["RS"], c["O2"], c["OD"], c["JK"], c["OT"]
            oc = c["ocol"]
            nch = (len(keys_of(c, i)) * 128 + 511) // 512
            for br in range(2):
                k.reduce(RS[:, br:br + 1], c["SMq"][qp][br][:, 0:nch], ALU.add)
            k.recip(RS[:, 0:2], RS[:, 0:2])
            k.tt(RS[:, 1:2], RS[:, 1:2], self.LAM[:, 0:1], ALU.mult)
            k.act(O2, pbO[:, oc + 128:oc + 256], AF.Identity, scale=RS[:, 1:2])
            k.stt(OD, pbO[:, oc:oc + 128], RS[:, 0:1], O2, ALU.mult, ALU.subtract)
            k.act(JK, OD, AF.Square, accum=RS[:, 2:3])
            k.act(RS[:, 2:3], RS[:, 2:3], AF.Sqrt, scale=1.0 / 128, bias=self.EPSB)
            k.recip(RS[:, 2:3], RS[:, 2:3])
            k.ts(OD, OD, RS[:, 2:3], ALU.mult)
            k.tr(pbT[:, 0:128], OD, self.IDF)
            k.act(OTt, pbT[:, 0:128], AF.Identity, scale=self.SUBG)
            k.tt(self.mixT[:, 8 + c["h"], i * 128:(i + 1) * 128], OTt, c["GT"][:, i * 128:(i + 1) * 128], ALU.mult, eng="pool")

        for c in ctxs:
            stage_a(c, 0)
        for kidx in range(len(items)):
            if kidx + 1 < len(items):
                for c in ctxs:
                    stage_a(c, kidx + 1)
            for c in ctxs:
                stage_b(c, kidx)
                if items[kidx][1] == 1:
                    tail(c, items[kidx][0])
        if h0 == 0:
            self.dump("at_out%d%d" % (l, g), self.mixT[:, 8, :], [128, TG])
        k.close_scope()

    def unit_attn3(self, l, g, h):
        k = self.k
        nseq = self.nseq
        scale = 64.0 ** -0.5
        tag = "%d%d%d" % (l, g, h)
        npast = 0 if g == 0 else PAST // 128
        nkt = npast + NT
        nkl = 2 if g == 0 else nkt
        k.open_scope()
        GT = k.sb("aGT" + tag, [128, TG])
        VP = k.sb("aVP" + tag, [128, nkt, 130], BF16)
        QT = k.sb("aQT" + tag, [128, TG], BF16)
        KTa = k.sb("aKT" + tag, [128, nkt * 128], BF16)
        NB = k.sb("aNB" + tag, [128, 2])
        k.open_scope()
        QKV = k.sb("aQKV" + tag, [128, NT, 384])
        self.proj_tm(0, 384, lambda i, ps: k.copy(QKV[:, i, :], ps, eng=("act" if i % 2 else "dve")))
        self.proj_fm(384, lambda nb, ps: k.act(GT[:, nb * 512:(nb + 1) * 512], ps, AF.Silu))
        self.prefetch()
        if g == 0:
            for s in range(nseq):
                k.dma(V(self.d_nk[s, l, h].rearrange("(i p) c -> p i c", p=128), None), QKV[:, 2 * s:2 * s + 2, 128:256])
                k.dma(V(self.d_nv[s, l, h].rearrange("(i p) c -> p i c", p=128), None), QKV[:, 2 * s:2 * s + 2, 256:384])
        else:
            T = [k.sb("aT%d" % i + tag, [128, NT, 4, 16]) for i in range(4)]
            for off in (0, 128):
                xv = QKV[:, :, off:off + 128].rr("p i (a x t) -> p i a x t", a=4, x=2)
                x1, x2 = xv[:, :, :, 0, :], xv[:, :, :, 1, :]
                k.tt(T[0], x1, self.COS, ALU.mult)
                k.tt(T[1], x2, self.SIN, ALU.mult)
                k.tt(T[2], x2, self.COS, ALU.mult)
                k.tt(T[3], x1, self.SIN, ALU.mult)
                k.tt(x1, T[0], T[1], ALU.subtract)
                k.tt(x2, T[2], T[3], ALU.add)
        QKB = k.sb("aQKB" + tag, [128, NT, 256], BF16)
        k.copy(QKB, QKV[:, :, 0:256])
        k.copy(VP[:, npast:nkt, 0:128], QKV[:, :, 256:384], eng="pool")
        k.memset(VP[:, :, 128:129], 1.0)
        k.memset(VP[:, :, 129:130], 0.0)
        for which, dst, c0 in ((0, QT, 0), (1, KTa, npast * 128)):
            pbb = self.bank().bitcast(BF16)
            for i in range(NT):
                k.tr(pbb[:, i * 128:(i + 1) * 128], QKB[:, i, which * 128:(which + 1) * 128], self.IDB)
            k.copy(dst[:, c0:c0 + TG], pbb[:, 0:1024], eng=("act" if which else "dve"))
        SQ = k.sb("aSQ" + tag, [128, NT, 256])
        N2 = k.sb("aN2" + tag, [128, NT, 4])
        M4 = k.sb("aM4" + tag, [128, 4])
        k.act(SQ, QKV[:, :, 0:256], AF.Square)
        k.reduce(N2, SQ.rr("p i (a d) -> p i a d", a=4), ALU.add)
        k.reduce(M4, N2.rr("p i a -> p a i"), ALU.max)
        if g == 1:
            CKB = k.sb("aCKB" + tag, [128, npast, 128], BF16)
            k.dma(CKB, V(self.d_ck[l, h].rearrange("(i p) c -> p i c", p=128), None), q="pool")
            k.dma(VP[:, 0:npast, 0:128], V(self.d_cv[l, h].rearrange("(i p) c -> p i c", p=128), None), q="pool")
            pbb = self.bank().bitcast(BF16)
            for i in range(npast):
                k.tr(pbb[:, i * 128:(i + 1) * 128], CKB[:, i, :], self.IDB)
            k.copy(KTa[:, 0:npast * 128], pbb[:, 0:npast * 128])
            CSQ = k.sb("aCSQ" + tag, [128, npast, 128])
            CN2 = k.sb("aCN2" + tag, [128, npast, 2])
            CM = k.sb("aCM" + tag, [128, 2])
            k.act(CSQ, CKB, AF.Square)
            k.reduce(CN2, CSQ.rr("p i (a d) -> p i a d", a=2), ALU.add)
            k.reduce(CM, CN2.rr("p i a -> p a i"), ALU.max)
            k.tt(M4[:, 2:4], M4[:, 2:4], CM, ALU.max)
        pbm = self.bank()
        k.tr(pbm[0:4, 0:128], M4, self.IDF)
        MC = k.sb("aMC" + tag, [4, 1])
        k.reduce(MC, pbm[0:4, 0:128], ALU.max)
        RH = k.sb("aRH" + tag, [4, 4])
        k.ts(RH, self.IDF[0:4, 0:4], MC[0:4, 0:1], ALU.mult)
        pbr = self.bank()
        k.mm(pbr[:, 0:4], self.ONES[0:4, :], RH)
        MR = k.sb("aMR" + tag, [128, 4])
        k.copy(MR, pbr[:, 0:4])
        k.tt(NB, MR[:, 0:2], MR[:, 2:4], ALU.mult)
        k.act(NB, NB, AF.Sqrt)
        k.ts(NB, NB, -scale, ALU.mult)
        k.close_scope()
        ETs = [k.sb("aET%d" % b + tag, [128, nkl, TG], BF16) for b in range(2)]
        O1S = k.sb("aO1" + tag, [128, NT, 130])
        RS = k.sb("aRS" + tag, [128, 4])
        O2 = k.sb("aO2" + tag, [128, 128])
        OD = k.sb("aOD" + tag, [128, 128])
        JK = k.sb("aJK" + tag, [128, 128])
        OTt = k.sb("aOT" + tag, [128, 128])
        qblk = 512 if g == 1 else TSEQ_P

        def qk_items(br):
            brs = slice(br * 64, br * 64 + 64)
            for qb in range(TG // qblk):
                qs = slice(qb * qblk, (qb + 1) * qblk)
                kt0 = 0 if g == 1 else 2 * qb
                for j in range(nkl):
                    yield (brs, qs, kt0, j)

        def qk_exp(br, it):
            brs, qs, kt0, j = it
            pb = self.PB[self.abank % 6]
            self.abank += 1
            k.mm(pb[:, 0:qblk], KTa[brs, (kt0 + j) * 128:(kt0 + j + 1) * 128], QT[brs, qs])
            k.act(ETs[br][:, j, qs], pb[:, 0:qblk], AF.Exp, scale=scale, bias=NB[:, br:br + 1])

        def pv(br, i):
            ET = ETs[br]
            kt0 = 0 if g == 1 else 2 * (i // 2)
            pbO = self.PB[6 + i % 2]
            for j in range(nkl):
                k.mm(pbO[:, 0:130], ET[:, j, i * 128:(i + 1) * 128], VP[:, kt0 + j, :], start=(j == 0), stop=(j == nkl - 1))
            if br == 0:
                k.copy(O1S[:, i, :], pbO[:, 0:130], eng=("act" if i % 2 else "dve"))
            else:
                k.copy(RS[:, 0:1], O1S[:, i, 128:129])
                k.copy(RS[:, 1:2], pbO[:, 128:129])
                k.recip(RS[:, 0:2], RS[:, 0:2])
                k.tt(RS[:, 1:2], RS[:, 1:2], self.LAM[:, 0:1], ALU.mult)
                k.act(O2, pbO[:, 0:128], AF.Identity, scale=RS[:, 1:2])
                k.stt(OD, O1S[:, i, 0:128], RS[:, 0:1], O2, ALU.mult, ALU.subtract)
                k.act(JK, OD, AF.Square, accum=RS[:, 2:3])
                k.act(RS[:, 2:3], RS[:, 2:3], AF.Sqrt, scale=1.0 / 128, bias=self.EPSB)
                k.recip(RS[:, 2:3], RS[:, 2:3])
                k.ts(OD, OD, RS[:, 2:3], ALU.mult)
                pbt = self.PB[self.abank % 6]
                self.abank += 1
                k.tr(pbt[:, 0:128], OD, self.IDF)
                k.act(OTt, pbt[:, 0:128], AF.Identity, scale=self.SUBG)
                k.tt(self.mixT[:, 8 + h, i * 128:(i + 1) * 128], OTt, GT[:, i * 128:(i + 1) * 128], ALU.mult, eng="pool")

        for it in qk_items(0):
            qk_exp(0, it)
        its1 = list(qk_items(1))
        per = max(1, len(its1) // NT)
        nxt = 0
        for n, it in enumerate(its1):
            qk_exp(1, it)
            if (n + 1) % per == 0 and nxt < NT:
                pv(0, nxt)
                nxt += 1
        while nxt < NT:
            pv(0, nxt)
            nxt += 1
        for i in range(NT):
            pv(1, i)
        if h == 0:
            self.dump("at_out%d%d" % (l, g), self.mixT[:, 8, :], [128, TG])
        k.close_scope()

    def phase_o(self, l, g):
        k = self.k
        k.open_scope()
        WOs = [k.sb("oWO%d" % s, [128, KC, 520], BF16) for s in range(4)]
        for s in range(4):
            src = self.d_wout[l].rearrange("(kc p) c -> p kc c", p=128)[:, :, s * 512:(s + 1) * 512]
            k.dma(WOs[s][:, :, 0:512], V(src, None), q="pool")
        GBC = k.sb("oGBC", [128, D])
        LG = k.sb("oLG", [128, D])
        LB = k.sb("oLB", [128, D])
        k.dma(GBC, V(self.d_modg[l, g:g + 1, :].partition_broadcast(128).rearrange("p a d -> p (a d)"), self.modgbuf[l][g]))
        k.dma(LG, V(self.d_lng[l:l + 1, :].partition_broadcast(128).rearrange("p a d -> p (a d)"), None))
        k.dma(LB, V(self.d_lnb[l:l + 1, :].partition_broadcast(128).rearrange("p a d -> p (a d)"), None))
        XT = k.sb("oXT", [128, D])
        VV = k.sb("oVV", [128, D])
        JK = k.sb("oJK", [128, D], BF16)
        ST = k.sb("oST", [128, 4])
        for i in range(NT):
            k.dma(XT, self.xsrc(l, g, i))
            for s in range(4):
                pb = self.bank()
                for kc in range(KC):
                    k.mm(pb[:, 0:512], self.mixT[:, kc, i * 128:(i + 1) * 128], WOs[s][:, kc, 0:512], start=(kc == 0), stop=(kc == KC - 1))
                k.tt(VV[:, s * 512:(s + 1) * 512], pb[:, 0:512], GBC[:, s * 512:(s + 1) * 512], ALU.mult)
            k.stt(VV, XT, ALPHA, VV, ALU.mult, ALU.add)
            k.act(JK, VV, AF.Identity, accum=ST[:, 0:1])
            k.act(JK, VV, AF.Square, accum=ST[:, 1:2])
            k.ts(ST[:, 0:1], ST[:, 0:1], 1.0 / D, ALU.mult)
            k.tt(ST[:, 2:3], ST[:, 0:1], ST[:, 0:1], ALU.mult)
            k.stt(ST[:, 1:2], ST[:, 1:2], 1.0 / D, ST[:, 2:3], ALU.mult, ALU.subtract)
            k.act(ST[:, 1:2], ST[:, 1:2], AF.Sqrt, bias=self.EPSB)
            k.recip(ST[:, 1:2], ST[:, 1:2])
            k.ts(VV, VV, ST[:, 0:1], ALU.subtract, ST[:, 1:2], ALU.mult)
            k.tt(VV, VV, LG, ALU.mult)
            k.tt(VV, VV, LB, ALU.add)
            if l == DEPTH - 1:
                dst = V(self.d_yout[g, i * 128:(i + 1) * 128, :], None)
            else:
                dst = V(self.d_x1[g, i * 128:(i + 1) * 128, :], self.x1bufs[g][i])
            k.dma(dst, VV)
        k.close_scope()


def _shared_inputs(inp):
    perm = _perm_cols()
    sh = {}
    sh["w_ada"] = np.ascontiguousarray(inp["w_ada"], dtype=np.float32)
    sh["b_ada"] = np.ascontiguousarray(inp["b_ada"], dtype=np.float32)
    sh["w_in"] = np.ascontiguousarray(inp["w_in"][:, :, perm], dtype=np.float32)
    sh["w_out"] = np.ascontiguousarray(inp["w_out"], dtype=np.float32)
    sh["pfm"] = np.stack([_pack_pfm(inp, l) for l in range(DEPTH)])
    sh["prow"] = np.stack([_pack_prow(inp, l) for l in range(DEPTH)])
    sh["ln_g"] = np.ascontiguousarray(inp["ln_g"], dtype=np.float32)
    sh["ln_b"] = np.ascontiguousarray(inp["ln_b"], dtype=np.float32)
    wup = np.zeros((DEPTH, 2, 128, 512), np.float32)
    wup[:, 0] = inp["rwkv_w_up"].reshape(DEPTH, 128, 512)
    wup[:, 1] = inp["rwkv_a_up"].reshape(DEPTH, 128, 512)
    sh["wup"] = wup
    sh["consts"] = _consts()
    return sh


def _core_inputs(inp, c, sh):
    sb = c % 2
    m = dict(sh)
    xin = np.empty((2, TG, D), np.float32)
    xin[0] = inp["x_prompt"][4 * c:4 * c + 4].reshape(TG, D)
    xin[1] = inp["x_sample"][sb]
    m["xin"] = xin
    cT = np.empty((128, KC, 2), np.float32)
    cT[:, :, 0] = inp["c_ctx"].reshape(KC, 128).T
    cT[:, :, 1] = inp["c"][sb].reshape(KC, 128).T
    m["cT"] = cT.reshape(128, 32)
    rw = inp["state_rwkv"][sb]
    rw0 = np.zeros((DEPTH, 2, 4, 128, 128), np.float32)
    for p in range(4):
        for hh in range(2):
            rw0[:, :, p, hh * 64:(hh + 1) * 64, hh * 64:(hh + 1) * 64] = np.swapaxes(rw[:, :, 2 * p + hh], -1, -2)
    m["rw0"] = rw0
    ml0 = np.zeros((DEPTH, 2, 4, 128, 130), np.float32)
    ml0[..., 0:128] = np.swapaxes(inp["state_mlstm_c"][sb], -1, -2)
    ml0[..., 128] = inp["state_mlstm_n"][sb]
    m["ml0"] = ml0
    m["mm0"] = np.ascontiguousarray(np.broadcast_to(inp["state_mlstm_m"][sb].reshape(DEPTH, 1, 8), (DEPTH, 128, 8)), dtype=np.float32)
    m["ck"] = np.ascontiguousarray(inp["cache_attn_k"][sb], dtype=np.float32)
    m["cv"] = np.ascontiguousarray(inp["cache_attn_v"][sb], dtype=np.float32)
    return m


def _assemble(results):
    B = 8 * NSEQ_P
    y_prompt = np.empty((B, TSEQ_P, D), np.float32)
    y_sample = np.empty((2, TG, D), np.float32)
    new_k = np.empty((B, DEPTH, 8, TSEQ_P, 128), np.float32)
    new_v = np.empty((B, DEPTH, 8, TSEQ_P, 128), np.float32)
    new_rw = np.empty((B, DEPTH, 2, 8, 64, 64), np.float32)
    new_c = np.empty((B, DEPTH, 2, 4, 128, 128), np.float32)
    new_n = np.empty((B, DEPTH, 2, 4, 128), np.float32)
    new_m = np.empty((B, DEPTH, 2, 4), np.float32)
    for c, r in enumerate(results):
        bs = slice(4 * c, 4 * c + 4)
        y_prompt[bs] = r["yout"][0].reshape(4, TSEQ_P, D)
        if c < 2:
            y_sample[c] = r["yout"][1]
        new_k[bs] = r["nk"]
        new_v[bs] = r["nv"]
        nrw = r["nrw"]
        for p in range(4):
            for hh in range(2):
                blk = nrw[:, :, :, p, hh * 64:(hh + 1) * 64, hh * 64:(hh + 1) * 64]
                new_rw[bs, :, :, 2 * p + hh] = np.swapaxes(blk, -1, -2)
        nmc = r["nmc"]
        new_c[bs] = np.swapaxes(nmc[..., 0:128], -1, -2)
        new_n[bs] = nmc[..., 128]
        new_m[bs] = np.transpose(r["nmm"].reshape(DEPTH, 4, 2, 4), (1, 0, 2, 3))
    return (y_prompt, y_sample, new_k, new_v, new_rw, new_c, new_n, new_m)


def kernel(**inputs):
    inp = {k: np.asarray(v) for k, v in inputs.items()}
    prog = Prog()
    nc = prog.build()
    sh = _shared_inputs(inp)
    in_maps = [_core_inputs(inp, c, sh) for c in range(8)]
    res = run_bass_kernel_spmd(nc, in_maps, core_ids=list(range(8)))
    return _assemble(res.results)
```
